# Optimizing a Trainium2 kernel written in Bass

```python
import jax
import jax.numpy as jnp
from jax import lax
import numpy as np

D_MODEL = 1024
BATCH = 4
SEQ = 4096
DEPTH = 2

CTX_LEN = 256
GRID_W = 64
EPS = 1e-6

NA_HEADS = 8
NA_HEAD_DIM = 64
NA_WIN_ROWS = 8
NA_WIN_COLS = 16

GLA_HEADS = 4
GLA_DK = 64
GLA_DV = 128
GLA_RANK = 16
GLA_TAU = 16.0

RET_HEADS = 8
RET_DK = 128
RET_DV = 256

CHUNK = 64
ROPE_BASE = 10000.0

D_FF = 2816
CONV_W = 3

N_EVEN = (DEPTH + 1) // 2
N_ODD = DEPTH // 2

NA_W = NA_HEADS * NA_HEAD_DIM
GLA_QK_W = GLA_HEADS * GLA_DK
GLA_V_W = GLA_HEADS * GLA_DV
EVEN_SIZES = (NA_W, NA_W, NA_W, GLA_QK_W, GLA_QK_W, GLA_V_W, GLA_V_W, GLA_RANK, GLA_RANK)
EVEN_IN = sum(EVEN_SIZES)
EVEN_OUT = NA_W + GLA_V_W
RET_QK_W = RET_HEADS * RET_DK
RET_V_W = RET_HEADS * RET_DV
ODD_SIZES = (RET_QK_W, RET_QK_W, RET_V_W, RET_V_W)
ODD_IN = sum(ODD_SIZES)
ODD_OUT = RET_V_W

kernel_name = "hybrid_na_gla_retention_dit"


def rms_norm(x, g):
    xf = x.astype(jnp.float32)
    y = xf * lax.rsqrt(jnp.mean(xf * xf, axis=-1, keepdims=True) + EPS)
    return (y * g.astype(jnp.float32)).astype(x.dtype)


def group_norm_heads(o, g):
    mu = jnp.mean(o, axis=-1, keepdims=True)
    var = jnp.mean(jnp.square(o - mu), axis=-1, keepdims=True)
    return (o - mu) * lax.rsqrt(var + EPS) * g.astype(jnp.float32)


def split_cols(p, sizes):
    return jnp.split(p, np.cumsum(sizes)[:-1].tolist(), axis=-1)


def to_heads(t, n_heads):
    return t.reshape(t.shape[0], t.shape[1], n_heads, -1)


def to_bhtd(t, n_heads):
    return jnp.swapaxes(to_heads(t, n_heads), 1, 2).astype(jnp.float32)


def rev_time(a):
    return jnp.flip(a, axis=2)


def axial_rope_tables(n_tokens, rot_dim):
    pos = jnp.arange(n_tokens, dtype=jnp.int32)
    row = (pos // GRID_W).astype(jnp.float32)
    col = (pos % GRID_W).astype(jnp.float32)
    axis_dim = rot_dim // 2
    inv_freq = ROPE_BASE ** (-jnp.arange(0, axis_dim, 2, dtype=jnp.float32) / axis_dim)
    ang = jnp.concatenate([row[:, None] * inv_freq, col[:, None] * inv_freq], axis=-1)
    return jnp.cos(ang), jnp.sin(ang)


def apply_axial_rope(x, cos, sin):
    b, t, h, dim = x.shape
    nf = dim // 4
    xr = x.astype(jnp.float32).reshape(b, t, h, 2, 2, nf)
    x1, x2 = xr[..., 0, :], xr[..., 1, :]
    c = cos.reshape(t, 1, 2, nf)
    s = sin.reshape(t, 1, 2, nf)
    out = jnp.stack([x1 * c - x2 * s, x2 * c + x1 * s], axis=-2)
    return out.reshape(b, t, h, dim).astype(x.dtype)


def neighbourhood_attention(q, k, v, kc, vc, rpb):
    bsz, seq, nh, d = q.shape
    rows = seq // GRID_W
    wr = min(NA_WIN_ROWS, rows)
    wc = NA_WIN_COLS
    qg = (q * (d ** -0.5)).reshape(bsz, rows, GRID_W, nh, d)
    kg = k.reshape(bsz, rows, GRID_W, nh, d)
    vg = v.reshape(bsz, rows, GRID_W, nh, d)
    cols = jnp.arange(GRID_W)
    col_start = jnp.clip(cols - wc // 2, 0, GRID_W - wc)
    col_idx = col_start[:, None] + jnp.arange(wc)[None, :]
    col_off = col_idx - cols[:, None] + (NA_WIN_COLS - 1)
    bias_c = rpb[:, :, col_off]

    def row_block(r):
        r0 = jnp.clip(r - wr // 2, 0, rows - wr)
        k_strip = lax.dynamic_slice_in_dim(kg, r0, wr, axis=1)
        v_strip = lax.dynamic_slice_in_dim(vg, r0, wr, axis=1)
        k_win = k_strip[:, :, col_idx]
        v_win = v_strip[:, :, col_idx]
        q_r = lax.dynamic_index_in_dim(qg, r, axis=1, keepdims=False)
        row_off = r0 + jnp.arange(wr) - r + (NA_WIN_ROWS - 1)
        bias = jnp.transpose(bias_c[:, row_off], (0, 2, 1, 3))
        s_loc = jnp.einsum('bqhd,biqjhd->bhqij', q_r, k_win) + bias[None]
        s_ctx = jnp.einsum('bqhd,bkhd->bhqk', q_r, kc)
        s = jnp.concatenate([s_loc.reshape(bsz, nh, GRID_W, wr * wc), s_ctx], axis=-1)
        p = jax.nn.softmax(s.astype(jnp.float32), axis=-1).astype(v.dtype)
        p_loc = p[..., :wr * wc].reshape(bsz, nh, GRID_W, wr, wc)
        p_ctx = p[..., wr * wc:]
        return (jnp.einsum('bhqij,biqjhd->bqhd', p_loc, v_win)
                + jnp.einsum('bhqk,bkhd->bqhd', p_ctx, vc))

    out = lax.map(row_block, jnp.arange(rows))
    return jnp.moveaxis(out, 0, 1).reshape(bsz, seq, nh * d)


def context_attention(qc, kc, vc):
    scale = qc.shape[-1] ** -0.5
    s = jnp.einsum('bqhd,bkhd->bhqk', qc * scale, kc).astype(jnp.float32)
    p = jax.nn.softmax(s, axis=-1).astype(vc.dtype)
    o = jnp.einsum('bhqk,bkhd->bqhd', p, vc)
    return o.reshape(o.shape[0], o.shape[1], -1)


def gla_chunked(q, k, v, g, state0):
    bsz, nh, t, dk = q.shape
    dv = v.shape[-1]
    n = t // CHUNK
    q = q.reshape(bsz, nh, n, CHUNK, dk)
    k = k.reshape(bsz, nh, n, CHUNK, dk)
    v = v.reshape(bsz, nh, n, CHUNK, dv)
    b = jnp.cumsum(g.reshape(bsz, nh, n, CHUNK, dk), axis=3)
    b_last = b[:, :, :, -1:, :]
    q_in = q * jnp.exp(b)
    k_in = k * jnp.exp(-b)
    mask = jnp.tril(jnp.ones((CHUNK, CHUNK), dtype=bool))
    att = jnp.where(mask, jnp.einsum('bhnid,bhnjd->bhnij', q_in, k_in), 0.0)
    o_intra = jnp.einsum('bhnij,bhnjv->bhniv', att, v)
    chunk_kv = jnp.einsum('bhnjd,bhnjv->bhndv', k * jnp.exp(b_last - b), v)
    chunk_decay = jnp.exp(b_last[:, :, :, 0, :])

    def step(s, inp):
        dec, kv = inp
        return dec[..., None] * s + kv, s

    s_final, s_start = lax.scan(step, state0, (jnp.moveaxis(chunk_decay, 2, 0), jnp.moveaxis(chunk_kv, 2, 0)))
    s_start = jnp.moveaxis(s_start, 0, 2)
    o_inter = jnp.einsum('bhnid,bhndv->bhniv', q_in, s_start)
    return (o_intra + o_inter).reshape(bsz, nh, t, dv), s_final


def retention_chunked(q, k, v, log_gamma, state0):
    bsz, nh, t, dk = q.shape
    dv = v.shape[-1]
    n = t // CHUNK
    q = q.reshape(bsz, nh, n, CHUNK, dk)
    k = k.reshape(bsz, nh, n, CHUNK, dk)
    v = v.reshape(bsz, nh, n, CHUNK, dv)
    idx = jnp.arange(CHUNK, dtype=jnp.float32)
    lg = log_gamma.astype(jnp.float32)[:, None]
    diff = idx[:, None] - idx[None, :]
    decay = jnp.where(diff >= 0, jnp.exp(lg[:, :, None] * jnp.maximum(diff, 0.0)), 0.0)
    att = jnp.einsum('bhnid,bhnjd->bhnij', q, k) * decay[None, :, None]
    o_intra = jnp.einsum('bhnij,bhnjv->bhniv', att, v)
    q_dec = jnp.exp(lg * (idx + 1.0))[None, :, None, :, None]
    k_dec = jnp.exp(lg * (CHUNK - 1.0 - idx))[None, :, None, :, None]
    chunk_kv = jnp.einsum('bhnjd,bhnjv->bhndv', k * k_dec, v)
    chunk_decay = jnp.exp(lg * CHUNK)[None, :, :, None]

    def step(s, kv):
        return chunk_decay * s + kv, s

    s_final, s_start = lax.scan(step, state0, jnp.moveaxis(chunk_kv, 2, 0))
    s_start = jnp.moveaxis(s_start, 0, 2)
    o_inter = jnp.einsum('bhnid,bhndv->bhniv', q * q_dec, s_start)
    return (o_intra + o_inter).reshape(bsz, nh, t, dv), s_final


def na_gla_mixer(h, hc, w_in, w_out, rpb, w_a_fwd, b_a_fwd, w_a_bwd, b_a_bwd, norm_g, need_ctx):
    f32 = jnp.float32
    bsz = h.shape[0]
    na_q, na_k, na_v, gq, gk, gv, gr, ga_f, ga_b = split_cols(h @ w_in, EVEN_SIZES)
    na_qc, na_kc, na_vc, gqc, gkc, gvc, grc, gac_f, gac_b = split_cols(hc @ w_in, EVEN_SIZES)

    kc_na, vc_na = to_heads(na_kc, NA_HEADS), to_heads(na_vc, NA_HEADS)
    a_lat = neighbourhood_attention(to_heads(na_q, NA_HEADS), to_heads(na_k, NA_HEADS),
                                    to_heads(na_v, NA_HEADS), kc_na, vc_na, rpb)

    def log_gate(a, w, b):
        z = (a @ w + b).astype(f32)
        return to_bhtd(jax.nn.log_sigmoid(z), GLA_HEADS) / GLA_TAU

    def gla_qkvg(q, k, v, a_f, a_b):
        return (to_bhtd(q, GLA_HEADS) * GLA_DK ** -0.5, to_bhtd(k, GLA_HEADS), to_bhtd(v, GLA_HEADS),
                log_gate(a_f, w_a_fwd, b_a_fwd), log_gate(a_b, w_a_bwd, b_a_bwd))

    q, k, v, g_f, g_b = gla_qkvg(gq, gk, gv, ga_f, ga_b)
    qc, kc, vc, gc_f, gc_b = gla_qkvg(gqc, gkc, gvc, gac_f, gac_b)
    zero = jnp.zeros((bsz, GLA_HEADS, GLA_DK, GLA_DV), f32)
    oc_f, s_f = gla_chunked(qc, kc, vc, gc_f, zero)
    oc_b, s_b = gla_chunked(rev_time(qc), rev_time(kc), rev_time(vc), rev_time(gc_b), zero)
    o_f, _ = gla_chunked(q, k, v, g_f, s_f)
    o_b, _ = gla_chunked(rev_time(q), rev_time(k), rev_time(v), rev_time(g_b), s_b)

    def gla_readout(o, r):
        o = rms_norm(jnp.swapaxes(o, 1, 2), norm_g)
        o = o.reshape(o.shape[0], o.shape[1], -1) * jax.nn.silu(r.astype(f32))
        return o.astype(h.dtype)

    y = jnp.concatenate([a_lat, gla_readout(o_f + rev_time(o_b), gr)], axis=-1) @ w_out
    if not need_ctx:
        return y, None
    a_ctx = context_attention(to_heads(na_qc, NA_HEADS), kc_na, vc_na)
    yc = jnp.concatenate([a_ctx, gla_readout(oc_f + rev_time(oc_b), grc)], axis=-1) @ w_out
    return y, yc


def retention_mixer(h, hc, w_in, w_out, decay_fwd, decay_bwd, norm_g, rope_cos, rope_sin, need_ctx):
    f32 = jnp.float32
    bsz = h.shape[0]
    q, k, v, g = split_cols(h @ w_in, ODD_SIZES)
    qc, kc, vc, gc = split_cols(hc @ w_in, ODD_SIZES)
    q = jnp.swapaxes(apply_axial_rope(to_heads(q, RET_HEADS), rope_cos, rope_sin), 1, 2).astype(f32)
    k = jnp.swapaxes(apply_axial_rope(to_heads(k, RET_HEADS), rope_cos, rope_sin), 1, 2).astype(f32) * RET_DK ** -0.5
    v = to_bhtd(v, RET_HEADS)
    qc = to_bhtd(qc, RET_HEADS)
    kc = to_bhtd(kc, RET_HEADS) * RET_DK ** -0.5
    vc = to_bhtd(vc, RET_HEADS)
    lg_f = jax.nn.log_sigmoid(decay_fwd.astype(f32))
    lg_b = jax.nn.log_sigmoid(decay_bwd.astype(f32))
    zero = jnp.zeros((bsz, RET_HEADS, RET_DK, RET_DV), f32)
    oc_f, s_f = retention_chunked(qc, kc, vc, lg_f, zero)
    oc_b, s_b = retention_chunked(rev_time(qc), rev_time(kc), rev_time(vc), lg_b, zero)
    o_f, _ = retention_chunked(q, k, v, lg_f, s_f)
    o_b, _ = retention_chunked(rev_time(q), rev_time(k), rev_time(v), lg_b, s_b)

    def readout(o, gate):
        o = group_norm_heads(jnp.swapaxes(o, 1, 2), norm_g)
        o = o.reshape(o.shape[0], o.shape[1], -1) * jax.nn.silu(gate.astype(f32))
        return o.astype(h.dtype) @ w_out

    y = readout(o_f + rev_time(o_b), g)
    if not need_ctx:
        return y, None
    return y, readout(oc_f + rev_time(oc_b), gc)


def conv_ffn(h, w_up, conv_w, conv_b, w_down):
    u = h @ w_up
    t = u.shape[1]
    pad = CONV_W // 2
    up = jnp.pad(u, ((0, 0), (pad, pad), (0, 0)))
    u = sum(up[:, i:i + t] * conv_w[i] for i in range(CONV_W)) + conv_b
    a, b = jnp.split(u, 2, axis=-1)
    return (jax.nn.silu(a) * b) @ w_down


def setup_inputs(seed: int = 0) -> dict:
    key = jax.random.key(seed)
    keys = iter(jax.random.split(key, 32))
    f32 = jnp.float32

    def normal(shape, scale):
        return jax.random.normal(next(keys), shape, f32) * scale

    def gain(shape):
        return 1.0 + normal(shape, 0.02)

    D = D_MODEL
    ret_decay_init = jnp.log(2.0 ** (5.0 + jnp.arange(RET_HEADS, dtype=f32)) - 1.0)
    return {
        "x": normal((BATCH, SEQ, D), 1.0),
        "c": normal((BATCH, D), 1.0),
        "ctx": normal((BATCH, CTX_LEN, D), 1.0),
        "c_ctx": normal((D,), 1.0),
        "w_mod": normal((DEPTH, D, 6 * D), 0.5 * D ** -0.5),
        "b_mod": normal((DEPTH, 6 * D), 0.02),
        "norm1_g": gain((DEPTH, D)),
        "norm2_g": gain((DEPTH, D)),
        "na_gla_w_in": normal((N_EVEN, D, EVEN_IN), D ** -0.5),
        "na_gla_w_out": normal((N_EVEN, EVEN_OUT, D), EVEN_OUT ** -0.5),
        "na_rpb": normal((N_EVEN, NA_HEADS, 2 * NA_WIN_ROWS - 1, 2 * NA_WIN_COLS - 1), 0.1),
        "gla_w_a_fwd": normal((N_EVEN, GLA_RANK, GLA_QK_W), GLA_RANK ** -0.5),
        "gla_b_a_fwd": normal((N_EVEN, GLA_QK_W), 0.1),
        "gla_w_a_bwd": normal((N_EVEN, GLA_RANK, GLA_QK_W), GLA_RANK ** -0.5),
        "gla_b_a_bwd": normal((N_EVEN, GLA_QK_W), 0.1),
        "gla_norm_g": gain((N_EVEN, GLA_DV)),
        "ret_w_in": normal((N_ODD, D, ODD_IN), D ** -0.5),
        "ret_w_out": normal((N_ODD, ODD_OUT, D), ODD_OUT ** -0.5),
        "ret_decay_fwd": ret_decay_init + normal((N_ODD, RET_HEADS), 0.05),
        "ret_decay_bwd": ret_decay_init + normal((N_ODD, RET_HEADS), 0.05),
        "ret_norm_g": gain((N_ODD, RET_HEADS, RET_DV)),
        "ffn_w_up": normal((DEPTH, D, 2 * D_FF), D ** -0.5),
        "ffn_conv_w": normal((DEPTH, CONV_W, 2 * D_FF), CONV_W ** -0.5),
        "ffn_conv_b": normal((DEPTH, 2 * D_FF), 0.02),
        "ffn_w_down": normal((DEPTH, D_FF, D), D_FF ** -0.5),
        "final_norm_g": gain((D,)),
    }


def reference(x, c, ctx, c_ctx, w_mod, b_mod, norm1_g, norm2_g, na_gla_w_in, na_gla_w_out, na_rpb,
              gla_w_a_fwd, gla_b_a_fwd, gla_w_a_bwd, gla_b_a_bwd, gla_norm_g,
              ret_w_in, ret_w_out, ret_decay_fwd, ret_decay_bwd, ret_norm_g,
              ffn_w_up, ffn_conv_w, ffn_conv_b, ffn_w_down, final_norm_g):
    seq = x.shape[1]
    rope_cos, rope_sin = axial_rope_tables(seq, RET_DK)
    silu_c = jax.nn.silu(c)
    silu_cc = jax.nn.silu(c_ctx)
    xc = ctx
    for layer in range(DEPTH):
        need_ctx = layer < DEPTH - 1
        mod = silu_c @ w_mod[layer] + b_mod[layer]
        mod_c = silu_cc @ w_mod[layer] + b_mod[layer]
        sh1, sc1, g1, sh2, sc2, g2 = jnp.split(mod[:, None, :], 6, axis=-1)
        sh1c, sc1c, g1c, sh2c, sc2c, g2c = jnp.split(mod_c, 6, axis=-1)
        h = rms_norm(x, norm1_g[layer]) * (1.0 + sc1) + sh1
        hc = rms_norm(xc, norm1_g[layer]) * (1.0 + sc1c) + sh1c
        j = layer // 2
        if layer % 2 == 0:
            y, yc = na_gla_mixer(h, hc, na_gla_w_in[j], na_gla_w_out[j], na_rpb[j],
                                 gla_w_a_fwd[j], gla_b_a_fwd[j], gla_w_a_bwd[j], gla_b_a_bwd[j],
                                 gla_norm_g[j], need_ctx)
        else:
            y, yc = retention_mixer(h, hc, ret_w_in[j], ret_w_out[j], ret_decay_fwd[j], ret_decay_bwd[j],
                                    ret_norm_g[j], rope_cos, rope_sin, need_ctx)
        x = x + g1 * y
        h2 = rms_norm(x, norm2_g[layer]) * (1.0 + sc2) + sh2
        x = x + g2 * conv_ffn(h2, ffn_w_up[layer], ffn_conv_w[layer], ffn_conv_b[layer], ffn_w_down[layer])
        if need_ctx:
            xc = xc + g1c * yc
            hc2 = rms_norm(xc, norm2_g[layer]) * (1.0 + sc2c) + sh2c
            xc = xc + g2c * conv_ffn(hc2, ffn_w_up[layer], ffn_conv_w[layer], ffn_conv_b[layer], ffn_w_down[layer])
    return rms_norm(x, final_norm_g)
```

```python
import os
import ml_dtypes
from concourse.bass_utils import run_bass_kernel_spmd

from contextlib import ExitStack
import numpy as np
import concourse.bass as bass
import concourse.mybir as mybir

F32 = mybir.dt.float32
BF16 = mybir.dt.bfloat16
AF = mybir.ActivationFunctionType
ALU = mybir.AluOpType
AX = mybir.AxisListType

ENGS = ("pe", "act", "dve", "pool", "sp")
NDSEM = 12


def _region(ap):
    t = ap.tensor
    name = t.name
    dims = list(ap.ap)
    off = int(ap.offset)
    sp = str(ap.space) if hasattr(ap, "space") else ""
    if "DRAM" in sp.upper() or type(t).__name__.startswith("DRam"):
        ext = sum((int(c) - 1) * abs(int(s)) for s, c in dims)
        return (name, 0, 1, off, off + ext + 1)
    if type(t).__name__.startswith("PSum"):
        return (name, 0, 128, 0, 1 << 40)
    pstep, pcnt = int(dims[0][0]), int(dims[0][1])
    if pstep == 0:
        pstep = 1 << 40
    p0 = off // pstep
    f0 = off % pstep
    ext = sum((int(c) - 1) * abs(int(s)) for s, c in dims[1:])
    return (name, p0, p0 + pcnt, f0, f0 + ext + 1)


def _overlap(a, b):
    return a[1] < b[2] and b[1] < a[2] and a[3] < b[4] and b[3] < a[4]


def _covers(a, b):
    return a[1] <= b[1] and a[2] >= b[2] and a[3] <= b[3] and a[4] >= b[4]


class KB:
    def __init__(self):
        self.nc = bass.Bass("TRN2", target_bir_lowering=False)
        self.es = ExitStack()
        self.ops = []
        self.recs = {}
        self.n_alloc = 0
        self.fence = None
        self.fenced = set()
        self.stack = [self.es]

    def sb(self, name, shape, dt=F32):
        return self.stack[-1].enter_context(self.nc.sbuf_tensor(name, list(shape), dt))

    def barrier(self):
        last = {}
        f = set()
        for i, o in enumerate(self.ops):
            if o["dma"]:
                f.add(i)
            else:
                last[o["eng"]] = i
        f.update(last.values())
        if self.fence is not None:
            f = {i for i in f if i > self.fence_at or not self.ops[i]["dma"]}
        self.fence = f
        self.fence_at = len(self.ops)
        self.fenced = set()

    def open_scope(self):
        self.stack.append(ExitStack())

    def close_scope(self):
        self.barrier()
        self.stack.pop().close()

    def ps(self, name, shape, dt=F32):
        return self.es.enter_context(self.nc.psum_tensor(name, list(shape), dt))

    def dram(self, name, shape, dt=F32, kind="ExternalInput"):
        return self.nc.dram_tensor(name, list(shape), dt, kind=kind).ap()

    def op(self, eng, fn, reads, writes, dma=False):
        idx = len(self.ops)
        deps = set()
        rr = [_region(a) for a in reads if a is not None and hasattr(a, "tensor")]
        ww = [_region(a) for a in writes if a is not None and hasattr(a, "tensor")]
        for r in rr:
            for (g, oi, isw) in self.recs.get(r[0], ()):
                if isw and _overlap(r, g):
                    deps.add(oi)
        for w in ww:
            for (g, oi, isw) in self.recs.get(w[0], ()):
                if _overlap(w, g):
                    deps.add(oi)
        for w in ww:
            lst = self.recs.setdefault(w[0], [])
            lst[:] = [x for x in lst if not _covers(w, x[0])]
            lst.append((w, idx, True))
        for r in rr:
            lst = self.recs.setdefault(r[0], [])
            lst[:] = [x for x in lst if not ((not x[2]) and x[1] < idx and self.ops[x[1]]["eng"] == eng
                                             and not self.ops[x[1]]["dma"] and not dma and _covers(r, x[0]))]
            lst.append((r, idx, False))
        if self.fence is not None and eng not in self.fenced:
            deps.update(self.fence)
            self.fenced.add(eng)
        deps.discard(idx)
        self.ops.append(dict(eng=eng, fn=fn, deps=deps, dma=dma, rr=rr, ww=ww))
        return idx

    def dma(self, out, in_, q="sp"):
        return self.op(q, lambda e: e.dma_start(out=out, in_=in_), [in_], [out], dma=True)

    def matmul(self, out, lhsT, rhs, start=True, stop=True):
        return self.op("pe", lambda e: e.matmul(out, lhsT, rhs, start=start, stop=stop), [lhsT, rhs], [out])

    def transpose(self, out, in_, ident):
        return self.op("pe", lambda e: e.transpose(out, in_, ident), [in_, ident], [out])

    def act(self, out, in_, func, bias=None, scale=None, accum_out=None, eng="act"):
        kw = {}
        if bias is not None:
            kw["bias"] = bias
        if scale is not None:
            kw["scale"] = scale
        if accum_out is not None:
            kw["accum_out"] = accum_out
        return self.op(eng, lambda e: e.activation(out, in_, func, **kw), [in_, bias, scale], [out, accum_out])

    def tt(self, out, in0, in1, op, eng="dve"):
        return self.op(eng, lambda e: e.tensor_tensor(out, in0, in1, op), [in0, in1], [out])

    def ts(self, out, in0, s1, op0, s2=None, op1=None, accum_out=None, eng="dve"):
        def f(e):
            kw = {}
            if accum_out is not None:
                kw["accum_out"] = accum_out
            if op1 is None:
                return e.tensor_scalar(out, in0, s1, None, op0, **kw)
            return e.tensor_scalar(out, in0, s1, s2, op0, op1, **kw)
        return self.op(eng, f, [in0, s1, s2], [out, accum_out])

    def stt(self, out, in0, scalar, in1, op0, op1, eng="dve"):
        return self.op(eng, lambda e: e.scalar_tensor_tensor(out, in0, scalar, in1, op0, op1), [in0, scalar, in1], [out])

    def copy(self, out, in_, eng="dve"):
        if eng == "act":
            return self.op("act", lambda e: e.copy(out, in_), [in_], [out])
        return self.op(eng, lambda e: e.tensor_copy(out, in_), [in_], [out])

    def memset(self, ap, val, eng="dve"):
        return self.op(eng, lambda e: e.memset(ap, val), [], [ap])

    def recip(self, out, in_):
        return self.op("dve", lambda e: e.reciprocal(out, in_), [in_], [out])

    def reduce(self, out, in_, op=ALU.add, axis=AX.X, eng="dve"):
        return self.op(eng, lambda e: e.tensor_reduce(out, in_, axis, op), [in_], [out])

    def finish(self, out_aps):
        nc = self.nc
        ops = self.ops
        out_names = {a.tensor.name for a in out_aps}
        final_deps = set()
        for i, o in enumerate(ops):
            if o["dma"] and any(w[0] in out_names for w in o["ww"]):
                final_deps.add(i)
        ops.append(dict(eng="sp", fn=None, deps=final_deps, dma=False, rr=[], ww=[]))
        needs_sig = [False] * len(ops)
        for o in ops:
            for d in o["deps"]:
                if ops[d]["dma"]:
                    continue
                if ops[d]["eng"] == o["eng"] and not o["dma"] and o["eng"] == "pe":
                    continue
                needs_sig[d] = True
        sems = {e: self.es.enter_context(nc.semaphore("s_" + e)) for e in ENGS}
        dsems = {e: [self.es.enter_context(nc.semaphore("d_%s_%d" % (e, i))) for i in range(NDSEM)]
                 for e in ("sp", "act", "pool")}
        cnt = {e: 0 for e in ENGS}
        dcnt = {e: 0 for e in dsems}
        sig = [None] * len(ops)
        prevdma = [None] * len(ops)
        for i, o in enumerate(ops):
            if o["dma"]:
                q = o["eng"]
                n = dcnt[q]
                dcnt[q] += 1
                s = dsems[q][n % NDSEM]
                sig[i] = (s, 16 * (n // NDSEM + 1))
                if n >= NDSEM:
                    prevdma[i] = (s, 16 * (n // NDSEM))
            elif needs_sig[i]:
                cnt[o["eng"]] += 1
                sig[i] = (sems[o["eng"]], cnt[o["eng"]])
        per_eng = {e: [] for e in ENGS}
        for i, o in enumerate(ops):
            per_eng[o["eng"]].append(i)
        self.stats = {e: len(per_eng[e]) for e in ENGS}
        self.stats["sig"] = dict(cnt)

        def emit(ename):
            def body(eng):
                seen = {}
                for i in per_eng[ename]:
                    o = ops[i]
                    waits = {}
                    for d in o["deps"]:
                        po = ops[d]
                        if (not po["dma"]) and po["eng"] == ename and ename == "pe" and not o["dma"]:
                            continue
                        s, v = sig[d]
                        key = id(s)
                        if waits.get(key, (None, 0))[1] < v:
                            waits[key] = (s, v)
                    if prevdma[i] is not None:
                        s, v = prevdma[i]
                        key = id(s)
                        if waits.get(key, (None, 0))[1] < v:
                            waits[key] = (s, v)
                    for key, (s, v) in waits.items():
                        if seen.get(key, 0) >= v:
                            continue
                        eng.wait_ge(s, v)
                        seen[key] = v
                    if o["fn"] is None:
                        continue
                    ins = o["fn"](eng)
                    if sig[i] is not None:
                        s, v = sig[i]
                        ins.then_inc(s, 16 if o["dma"] else 1)
            return body

        with nc.Block() as block:
            block.tensor(emit("pe"))
            block.scalar(emit("act"))
            block.vector(emit("dve"))
            block.gpsimd(emit("pool"))
            block.sync(emit("sp"))
        self.es.close()
        return nc

D = 1024; DFF = 2816; NFF = 22; EPS = 1e-6; NT = 34


def col_tiles(cols):
    nt = (cols + 511) // 512
    base = cols // nt
    res = []; s = 0
    for i in range(nt):
        e = s + base + (1 if i < cols % nt else 0)
        res.append((s, e)); s = e
    return res


def load_cast_rows(k, dst, src, nk, width, q="pool", split=1):
    v = src.rearrange("(k p) c -> p k c", p=128)
    step = max(1, nk // split)
    for a in range(0, nk, step):
        b = min(nk, a + step)
        k.dma(dst[:, a:b, :], v[:, a:b, :], q=q)


def build_F(layer_has_ctx, Fdim, last, segs):
    k = KB()
    KF = Fdim // 128
    NMAX = max(n for n, _ in segs); CMAX = NMAX + 2
    d_x = [k.dram("xT_%d" % i, [D, n + 2]) for i, (n, _) in enumerate(segs)]
    d_o = [k.dram("oT_%d" % i, [Fdim, n + 2], BF16) for i, (n, _) in enumerate(segs)]
    d_hm = [k.dram("hm_%d" % i, [128, 2]) for i, (n, _) in enumerate(segs)]
    d_y = [k.dram("yT_%d" % i, [D, n], kind="ExternalOutput") for i, (n, _) in enumerate(segs)]
    d_cvec = k.dram("cvec", [128, 8, 2])
    d_wmod = k.dram("wmod", [D, 4096]); d_bmod = k.dram("bmod", [128, 32])
    d_n2g = k.dram("n2g", [128, 8]); d_fg = k.dram("fg", [128, 8])
    d_cw = k.dram("convw", [128, 3, 44]); d_cb = k.dram("convb", [128, 44])
    d_wout = k.dram("wout", [Fdim, D]); d_wup = k.dram("wup", [D, 2 * DFF]); d_wdn = k.dram("wdn", [DFF, D])
    ones_bf = k.sb("ones_bf", [128, 128], BF16)
    cvec = k.sb("cvec_s", [128, 8, 2]); scb = k.sb("scb", [128, 8, 2], BF16)
    bmod = k.sb("bmod_s", [128, 32]); modv = k.sb("modv", [128, 32, 2])
    n2g = k.sb("n2g_s", [128, 8]); fg = k.sb("fg_s", [128, 8]); gm2 = k.sb("gm2", [128, 8, 2])
    cw = k.sb("cw_s", [128, 3, 44]); cb = k.sb("cb_s", [128, 44])
    wmb = [k.sb("wmb%d" % i, [128, 8, 512], BF16) for i in range(2)]
    wout = k.sb("wout_s", [128, KF, D], BF16)
    wdn = k.sb("wdn_s", [128, NFF, D], BF16)
    oT = k.sb("oT_s", [128, KF, CMAX], BF16)
    x1 = k.sb("x1T", [128, 8, CMAX])
    sq = k.sb("sq", [128, 8, CMAX], BF16)
    rstd = k.sb("rstd", [128, CMAX]); tmp = k.sb("tmp", [128, CMAX])
    h2 = k.sb("h2T", [128, 8, CMAX], BF16)
    hm = k.sb("hm_s", [128, 2])
    wua = [k.sb("wua%d" % i, [128, 8, 256], BF16) for i in range(2)]
    wub = [k.sb("wub%d" % i, [128, 8, 256], BF16) for i in range(2)]
    u = [k.sb("u%d" % i, [128, 2, CMAX]) for i in range(2)]
    va = k.sb("va", [128, NMAX]); vb = k.sb("vb", [128, NMAX]); sa = k.sb("sa", [128, NMAX])
    tT = k.sb("tT", [128, NFF, NMAX], BF16)
    P = [k.ps("P%d" % i, [128, 512]) for i in range(8)]
    pctr = [0]

    def nextP():
        p = P[pctr[0] % 8]; pctr[0] += 1
        return p

    k.memset(ones_bf[:], 1.0)
    k.dma(cvec[:], d_cvec); k.dma(bmod[:], d_bmod); k.dma(n2g[:], d_n2g); k.dma(fg[:], d_fg)
    k.dma(cw[:], d_cw); k.dma(cb[:], d_cb)
    k.act(scb[:], cvec[:], AF.Silu)
    pm = nextP()
    for g in range(8):
        wb = wmb[g % 2]
        k.dma(wb[:], d_wmod.rearrange("(k p) c -> p k c", p=128)[:, :, g * 512:(g + 1) * 512], q="pool")
        for c4 in range(4):
            cc = g * 4 + c4
            for kk in range(8):
                k.matmul(pm[:, cc * 2:cc * 2 + 2], wb[:, kk, c4 * 128:(c4 + 1) * 128], scb[:, kk, :],
                         start=(kk == 0), stop=(kk == 7))
    pmv = pm[:, 0:64].rearrange("p (c j) -> p c j", j=2)
    for j in range(2):
        k.tt(modv[:, :, j], pmv[:, :, j], bmod[:], ALU.add)
    for j in range(2):
        k.ts(gm2[:, :, j], modv[:, 16:24, j], 1.0, ALU.add)
        k.tt(gm2[:, :, j], gm2[:, :, j], n2g[:], ALU.mult)
    load_cast_rows(k, wout, d_wout, KF, D, split=4)
    load_cast_rows(k, wdn, d_wdn, NFF, D, split=4)
    wupv = d_wup.rearrange("(k p) c -> p k c", p=128)

    for si, (n, j) in enumerate(segs):
        cols = n + 2
        tiles = col_tiles(cols)
        k.dma(x1[:, :, 0:cols], d_x[si].rearrange("(k p) t -> p k t", p=128))
        k.dma(oT[:, :, 0:cols], d_o[si].rearrange("(k p) t -> p k t", p=128))
        k.dma(hm[:], d_hm[si])
        for fc in range(8):
            for (a, b) in tiles:
                p = nextP()
                for kk in range(KF):
                    k.matmul(p[:, 0:b - a], wout[:, kk, fc * 128:(fc + 1) * 128], oT[:, kk, a:b],
                             start=(kk == 0), stop=(kk == KF - 1))
                k.stt(x1[:, fc, a:b], p[:, 0:b - a], modv[:, 0 + fc, j:j + 1], x1[:, fc, a:b], ALU.mult, ALU.add)
        for kk in range(8):
            k.act(sq[:, kk, 0:cols], x1[:, kk, 0:cols], AF.Square)
        for (a, b) in tiles:
            p = nextP()
            for kk in range(8):
                k.matmul(p[:, 0:b - a], ones_bf[:], sq[:, kk, a:b], start=(kk == 0), stop=(kk == 7))
            k.act(tmp[:, a:b], p[:, 0:b - a], AF.Sqrt, bias=EPSB[0], scale=1.0 / D)
        k.recip(rstd[:, 0:cols], tmp[:, 0:cols])
        for kk in range(8):
            k.stt(tmp[:, 0:cols], x1[:, kk, 0:cols], gm2[:, kk, j:j + 1], rstd[:, 0:cols], ALU.mult, ALU.mult)
            k.act(h2[:, kk, 0:cols], tmp[:, 0:cols], AF.Identity, bias=modv[:, 8 + kk, j:j + 1], scale=1.0)
        k.ts(h2[:, :, 0:1], h2[:, :, 0:1], hm[:, 0:1], ALU.mult)
        k.ts(h2[:, :, cols - 1:cols], h2[:, :, cols - 1:cols], hm[:, 1:2], ALU.mult)
        for g in range(11):
            wa = wua[g % 2]; wb_ = wub[g % 2]
            k.dma(wa[:], wupv[:, :, g * 256:(g + 1) * 256], q="pool")
            k.dma(wb_[:], wupv[:, :, DFF + g * 256:DFF + (g + 1) * 256], q="pool")
            for c2 in range(2):
                c = g * 2 + c2
                ub = u[c % 2]
                for half, w in ((0, wa), (1, wb_)):
                    for (a, b) in tiles:
                        p = nextP()
                        for kk in range(8):
                            k.matmul(p[:, 0:b - a], w[:, kk, c2 * 128:(c2 + 1) * 128], h2[:, kk, a:b],
                                     start=(kk == 0), stop=(kk == 7))
                        k.copy(ub[:, half, a:b], p[:, 0:b - a], eng="act")
                ca = c; cbi = NFF + c
                k.act(va[:, 0:n], ub[:, 0, 1:n + 1], AF.Identity, bias=cb[:, ca:ca + 1], scale=cw[:, 1, ca:ca + 1])
                k.stt(va[:, 0:n], ub[:, 0, 0:n], cw[:, 0, ca:ca + 1], va[:, 0:n], ALU.mult, ALU.add)
                k.stt(va[:, 0:n], ub[:, 0, 2:n + 2], cw[:, 2, ca:ca + 1], va[:, 0:n], ALU.mult, ALU.add)
                k.act(vb[:, 0:n], ub[:, 1, 1:n + 1], AF.Identity, bias=cb[:, cbi:cbi + 1], scale=cw[:, 1, cbi:cbi + 1])
                k.stt(vb[:, 0:n], ub[:, 1, 0:n], cw[:, 0, cbi:cbi + 1], vb[:, 0:n], ALU.mult, ALU.add)
                k.stt(vb[:, 0:n], ub[:, 1, 2:n + 2], cw[:, 2, cbi:cbi + 1], vb[:, 0:n], ALU.mult, ALU.add)
                k.act(sa[:, 0:n], va[:, 0:n], AF.Silu)
                k.tt(tT[:, c, 0:n], sa[:, 0:n], vb[:, 0:n], ALU.mult)
        for fc in range(8):
            p = nextP()
            for c in range(NFF):
                k.matmul(p[:, 0:n], wdn[:, c, fc * 128:(fc + 1) * 128], tT[:, c, 0:n], start=(c == 0), stop=(c == NFF - 1))
            k.stt(x1[:, fc, 1:n + 1], p[:, 0:n], modv[:, 24 + fc, j:j + 1], x1[:, fc, 1:n + 1], ALU.mult, ALU.add)
        if last:
            for kk in range(8):
                k.act(sq[:, kk, 0:n], x1[:, kk, 1:n + 1], AF.Square)
            p = nextP()
            for kk in range(8):
                k.matmul(p[:, 0:n], ones_bf[:], sq[:, kk, 0:n], start=(kk == 0), stop=(kk == 7))
            k.act(tmp[:, 0:n], p[:, 0:n], AF.Sqrt, bias=EPSB[0], scale=1.0 / D)
            k.recip(rstd[:, 0:n], tmp[:, 0:n])
            for kk in range(8):
                k.stt(x1[:, kk, 1:n + 1], x1[:, kk, 1:n + 1], fg[:, kk:kk + 1], rstd[:, 0:n], ALU.mult, ALU.mult)
        k.dma(d_y[si].rearrange("(k p) t -> p k t", p=128), x1[:, :, 1:n + 1])
    nc = k.finish(d_y)
    return nc, k

EPSB = [EPS]


NT = 34


def emit_mod(k, nextP, d_cvec, d_wmod, d_bmod, ncols, wmb, name="m"):
    ncc = ncols // 128
    cvec = k.sb(name + "cvec", [128, 8, 2]); scb = k.sb(name + "scb", [128, 8, 2], BF16)
    bmod = k.sb(name + "bmod", [128, ncc]); modv = k.sb(name + "modv", [128, ncc, 2])
    k.dma(cvec[:], d_cvec); k.dma(bmod[:], d_bmod)
    k.act(scb[:], cvec[:], AF.Silu)
    pm = nextP()
    for g in range(ncols // 512):
        wb = wmb[g % len(wmb)]
        k.dma(wb[:], d_wmod.rearrange("(k p) c -> p k c", p=128)[:, :, g * 512:(g + 1) * 512], q="pool")
        for c4 in range(4):
            cc = g * 4 + c4
            for kk in range(8):
                k.matmul(pm[:, cc * 2:cc * 2 + 2], wb[:, kk, c4 * 128:(c4 + 1) * 128], scb[:, kk, :],
                         start=(kk == 0), stop=(kk == 7))
    pmv = pm[:, 0:2 * ncc].rearrange("p (c j) -> p c j", j=2)
    for j in range(2):
        k.tt(modv[:, :, j], pmv[:, :, j], bmod[:], ALU.add)
    return modv


def emit_hT(k, nextP, hT, d_xT, d_xcT, modv, n1g, ones_bf, xt, sq, rstd, tmp, name=""):
    gm1 = k.sb(name + "gm1", [128, 8, 2]); tmp2 = k.sb(name + "tmp2", [128, 256])
    for j in range(2):
        k.ts(gm1[:, :, j], modv[:, 8:16, j], 1.0, ALU.add)
        k.tt(gm1[:, :, j], gm1[:, :, j], n1g[:], ALU.mult)
    W = 256
    jobs = [(d_xcT, 0, 0, 1)] + [(d_xT, i * W, 256 + i * W, 0) for i in range(4096 // W)]
    for ji, (src, c0, h0, j) in enumerate(jobs):
        x = xt[ji % len(xt)]
        k.dma(x[:], src.rearrange("(k p) t -> p k t", p=128)[:, :, c0:c0 + W])
        for kk in range(8):
            k.act(sq[:, kk, :], x[:, kk, :], AF.Square)
        p = nextP()
        for kk in range(8):
            k.matmul(p[:, 0:W], ones_bf[:], sq[:, kk, :], start=(kk == 0), stop=(kk == 7))
        k.ts(tmp[:, 0:W], p[:, 0:W], 1.0 / D, ALU.mult, EPS, ALU.add)
        k.act(tmp[:, 0:W], tmp[:, 0:W], AF.Sqrt)
        k.recip(rstd[:, 0:W], tmp[:, 0:W])
        for kk in range(8):
            tb = tmp if kk % 2 == 0 else sq[:, 0:4, :].bitcast(F32).rearrange("p a w -> p (a w)") if False else (tmp if kk % 2 == 0 else tmp2)
            k.stt(tb[:, 0:W], x[:, kk, :], gm1[:, kk, j:j + 1], rstd[:, 0:W], ALU.mult, ALU.mult)
            k.act(hT[:, kk, h0:h0 + W], tb[:, 0:W], AF.Identity, bias=modv[:, kk, j:j + 1], scale=1.0)


def rope(k, out_bf, x, cos_t, sin_t, tA, tB):
    xv = x.rearrange("p (a h f) -> p a h f", a=2, h=2)
    ov = out_bf.rearrange("p (a h f) -> p a h f", a=2, h=2)
    Av = tA.rearrange("p (a h f) -> p a h f", a=2, h=2)
    Bv = tB.rearrange("p (a h f) -> p a h f", a=2, h=2)
    cb = cos_t.rearrange("p (a f) -> p a f", a=2).unsqueeze(2).to_broadcast([128, 2, 2, 32])
    sv = sin_t.rearrange("p (a f) -> p a f", a=2)
    k.tt(Av, xv, cb, ALU.mult)
    k.tt(Bv[:, :, 0, :], xv[:, :, 1, :], sv, ALU.mult)
    k.tt(Bv[:, :, 1, :], xv[:, :, 0, :], sv, ALU.mult)
    k.tt(ov[:, :, 0, :], Av[:, :, 0, :], Bv[:, :, 0, :], ALU.subtract)
    k.tt(ov[:, :, 1, :], Av[:, :, 1, :], Bv[:, :, 1, :], ALU.add)


def make_consts():
    j = np.arange(128)[:, None].astype(np.float32); i = np.arange(128)[None, :].astype(np.float32)
    c = {}
    c["mask_f"] = (i >= j).astype(np.float32)
    c["mask_b"] = (j >= i).astype(np.float32)
    c["dm_f"] = np.maximum(i - j, 0.0); c["dm_b"] = np.maximum(j - i, 0.0)
    c["row_f"] = np.broadcast_to(i + 1.0, (128, 128)).copy()
    c["row_b"] = np.broadcast_to(128.0 - i, (128, 128)).copy()
    pc = np.zeros((128, 4), np.float32)
    pc[:, 0] = 127.0 - np.arange(128)
    pc[:, 1] = np.arange(128)
    pc[:, 2] = 128.0
    c["pcols"] = pc
    return {kk: np.ascontiguousarray(v.astype(np.float32)) for kk, v in c.items()}


def rope_tables():
    pos = np.arange(4096)
    row = (pos // 64).astype(np.float32); col = (pos % 64).astype(np.float32)
    inv = (10000.0 ** (-np.arange(0, 64, 2, dtype=np.float32) / 64.0)).astype(np.float32)
    ang = np.concatenate([row[:, None] * inv, col[:, None] * inv], axis=-1).astype(np.float32)
    cos = np.cos(ang).astype(np.float32); sin = np.sin(ang).astype(np.float32)
    cs = np.ascontiguousarray(cos.reshape(32, 128, 64).transpose(1, 0, 2))
    sn = np.ascontiguousarray(sin.reshape(32, 128, 64).transpose(1, 0, 2))
    return cs, sn


def build_M1():
    k = KB()
    DK = 128; DV = 256; NH = 4
    d_xT = k.dram("xT", [D, 4096]); d_xcT = k.dram("xcT", [D, 256])
    d_cvec = k.dram("cvec", [128, 8, 2]); d_wmod = k.dram("wmod", [D, 2048]); d_bmod = k.dram("bmod", [128, 16])
    d_n1g = k.dram("n1g", [128, 8])
    d_win = k.dram("win", [NH, D, 768])
    d_dec = k.dram("dec", [128, 8])
    d_ng = k.dram("ng", [128, NH, DV])
    d_cos = k.dram("cos", [128, 32, 64]); d_sin = k.dram("sin", [128, 32, 64])
    cn = {nm: k.dram(nm, [128, 128]) for nm in ("mask_f", "mask_b", "dm_f", "dm_b", "row_f", "row_b")}
    d_pcols = k.dram("pcols", [128, 4]); d_ident = k.dram("ident", [128, 128])
    d_oT = k.dram("oT", [NH * DV, 4096], BF16, kind="ExternalOutput")

    P = [k.ps("P%d" % i, [128, 512]) for i in range(6)]
    PT = [k.ps("PT%d" % i, [128, 1024], BF16) for i in range(2)]
    pc = [0, 0]

    def nextP():
        p = P[pc[0] % 6]; pc[0] += 1; return p

    def nextPT():
        p = PT[pc[1] % 2]; pc[1] += 1; return p

    ones_bf = k.sb("ones_bf", [128, 128], BF16); k.memset(ones_bf[:], 1.0)
    identf = k.sb("identf", [128, 128]); ident = k.sb("ident_s", [128, 128], BF16)
    k.dma(identf[:], d_ident); k.copy(ident[:], identf[:])
    n1g = k.sb("n1g_s", [128, 8]); k.dma(n1g[:], d_n1g)
    whd = [k.sb("whd0", [128, 8, 768], BF16)]
    wmb = [whd[0][:, :, 0:512]]
    import os
    if 'nomod' in os.environ.get('M1_SKIP', ''):
        modv = k.sb("mmodv", [128, 16, 2]); k.memset(modv[:], 0.1)
    else:
        modv = emit_mod(k, nextP, d_cvec, d_wmod, d_bmod, 2048, wmb)
    hT = k.sb("hT", [128, 8, NT * 128], BF16)
    xt = [k.sb("xt0", [128, 8, 256])]
    sq = k.sb("sq", [128, 8, 256], BF16); rstd = k.sb("rstd", [128, 256]); tmp = k.sb("tmp", [128, 256])
    import os
    if 'nohT' in os.environ.get('M1_SKIP', ''):
        k.memset(hT[:, :, 0:512], 0.01)
    else:
        emit_hT(k, nextP, hT, d_xT, d_xcT, modv, n1g, ones_bf, xt, sq, rstd, tmp)

    cs = {nm: k.sb(nm + "_s", [128, 128]) for nm in cn}
    for nm in cn:
        k.dma(cs[nm][:], cn[nm])
    pcols = k.sb("pcols_s", [128, 4]); k.dma(pcols[:], d_pcols)
    cos = k.sb("cos_s", [128, 32, 64]); sin = k.sb("sin_s", [128, 32, 64])
    ng = k.sb("ng_s", [128, NH, DV])
    if 'nocs' not in os.environ.get('M1_SKIP', ''):
        k.dma(cos[:], d_cos); k.dma(sin[:], d_sin)
        k.dma(ng[:], d_ng)
    dec = k.sb("dec_s", [128, 8]); k.dma(dec[:], d_dec)
    lg = k.sb("lg", [128, 8]); e1 = k.sb("e1", [128, 8])
    k.act(e1[:], dec[:], AF.Exp, scale=-1.0)
    k.act(e1[:], e1[:], AF.Ln, bias=1.0, scale=1.0)
    k.ts(lg[:], e1[:], -1.0, ALU.mult)

    kb = k.sb("kb", [128, NT, DK], BF16); qT = k.sb("qT", [128, NT, 128], BF16); kT = k.sb("kT", [128, NT, 128], BF16)
    vb = k.sb("vb", [128, NT, DV], BF16); gs = k.sb("gs", [128, 32, DV], BF16)
    oacc = k.sb("oacc", [128, 32, DV], BF16)
    oTh = [k.sb("oTh%d" % i, [128, 2, 512], BF16) for i in range(2)]
    qf = k.sb("qf", [128, 128]); kf = k.sb("kf", [128, 128]); ksc = k.sb("ksc", [128, 128])
    tA = k.sb("tA", [128, 128]); tB = k.sb("tB", [128, 128])
    DM = [k.sb("DM%d" % i, [128, 128]) for i in range(2)]
    EBr = [k.sb("EBr%d" % i, [128, 128]) for i in range(2)]
    ERc = k.sb("ERc", [128, 2]); dcol = k.sb("dcol", [128, 2])
    S = k.sb("S", [128, DV]); Sb = k.sb("Sb", [128, DV], BF16)
    AT = k.sb("AT", [128, 128], BF16); qin = k.sb("qin", [128, 128], BF16); kk_ = k.sb("kk", [128, 128], BF16)
    st6 = k.sb("st6", [128, 6]); mv = k.sb("mv", [128, 2]); rs = k.sb("rs", [128, 1]); on = k.sb("on", [128, DV])
    obf = k.sb("obf", [128, DV], BF16)

    import os
    STOP = float(os.environ.get('M1_STOP', '99')); NTL = int(os.environ.get('M1_NT', '34'))
    gf = k.sb("gf", [128, DV])
    for hl in range(NH):
        w = whd[0]
        k.dma(w[:], d_win[hl].rearrange("(k p) c -> p k c", p=128), q="pool")
        for di, (dmn, mkn, rown) in enumerate((("dm_f", "mask_f", "row_f"), ("dm_b", "mask_b", "row_b"))):
            lgc = lg[:, di * 4 + hl:di * 4 + hl + 1]
            k.act(DM[di][:], cs[dmn][:], AF.Exp, scale=lgc)
            k.tt(DM[di][:], DM[di][:], cs[mkn][:], ALU.mult)
            k.act(EBr[di][:], cs[rown][:], AF.Exp, scale=lgc)
            k.act(ERc[:, di:di + 1], pcols[:, di:di + 1], AF.Exp, scale=lgc)
            k.act(dcol[:, di:di + 1], pcols[:, 2:3], AF.Exp, scale=lgc)
        for t in range(NT):
            pa = nextP(); pb = nextP()
            for kk in range(8):
                k.matmul(pa[:, 0:512], hT[:, kk, t * 128:(t + 1) * 128], w[:, kk, 0:512], start=(kk == 0), stop=(kk == 7))
            for kk in range(8):
                k.matmul(pb[:, 0:256], hT[:, kk, t * 128:(t + 1) * 128], w[:, kk, 512:768], start=(kk == 0), stop=(kk == 7))
            k.ts(ksc[:], pa[:, 128:256], float(DK) ** -0.5, ALU.mult)
            if t >= 2:
                rope(k, qf[:], pa[:, 0:128], cos[:, t - 2, :], sin[:, t - 2, :], tA[:], tB[:])
                rope(k, kf[:], ksc[:], cos[:, t - 2, :], sin[:, t - 2, :], tA[:], tB[:])
                ksrc = kf
                k.copy(gf[:], pa[:, 256:512])
                k.act(gs[:, t - 2, :], gf[:], AF.Silu)
            else:
                k.copy(qf[:], pa[:, 0:128])
                ksrc = ksc
            k.copy(kb[:, t, :], ksrc[:], eng="pool")
            k.copy(vb[:, t, :], pb[:, 0:256])
            pt = nextP()
            k.transpose(pt[:, 0:128], qf[:], identf[:])
            k.transpose(pt[:, 128:256], ksrc[:], identf[:])
            k.copy(qT[:, t, :], pt[:, 0:128])
            k.copy(kT[:, t, :], pt[:, 128:256])
        for di in (1, 0):
            order = [1, 0] + list(range(33, 1, -1)) if di == 1 else list(range(NT))
            k.memset(S[:], 0.0); k.memset(Sb[:], 0.0)
            for t in order:
                if t >= 2:
                    pat = nextP()
                    k.matmul(pat[:, 0:128], kT[:, t, :], qT[:, t, :])
                    k.tt(AT[:], pat[:, 0:128], DM[di][:], ALU.mult)
                    k.tt(qin[:], qT[:, t, :], EBr[di][:], ALU.mult, eng="pool")
                    po = nextP()
                    k.matmul(po[:, 0:DV], AT[:], vb[:, t, :], start=True, stop=False)
                    k.matmul(po[:, 0:DV], qin[:], Sb[:], start=False, stop=True)
                k.ts(kk_[:], kb[:, t, :], ERc[:, di:di + 1], ALU.mult, eng="pool")
                pkv = nextP()
                k.matmul(pkv[:, 0:DV], kk_[:], vb[:, t, :])
                if t >= 2:
                    if di == 1:
                        k.copy(oacc[:, t - 2, :], po[:, 0:DV])
                    else:
                        k.tt(on[:], po[:, 0:DV], oacc[:, t - 2, :], ALU.add)
                        k.op("dve", lambda e, a=st6, b=on: e.bn_stats(a[:], b[:]), [on[:]], [st6[:]])
                        k.op("dve", lambda e, a=mv, b=st6: e.bn_aggr(a[:], b[:]), [st6[:]], [mv[:]])
                        k.ts(rs[:], mv[:, 1:2], EPS, ALU.add)
                        k.act(rs[:], rs[:], AF.Sqrt)
                        k.recip(rs[:], rs[:])
                        k.ts(on[:], on[:], mv[:, 0:1], ALU.subtract, rs[:, 0:1], ALU.mult)
                        k.tt(on[:], on[:], ng[:, hl, :], ALU.mult, eng="pool")
                        k.tt(obf[:], on[:], gs[:, t - 2, :], ALU.mult, eng="pool")
                        pt = nextPT()
                        k.transpose(pt[:, 0:128], obf[:, 0:128], ident[:])
                        k.transpose(pt[:, 128:256], obf[:, 128:256], ident[:])
                        lt = t - 2
                        ob = oTh[(lt // 4) % 2]
                        k.copy(ob[:, :, (lt % 4) * 128:(lt % 4 + 1) * 128], pt[:, 0:256].rearrange("p (c t) -> p c t", c=2))
                        if lt % 4 == 3:
                            k.dma(d_oT[hl * DV:(hl + 1) * DV, (lt // 4) * 512:(lt // 4 + 1) * 512].rearrange("(c p) t -> p c t", p=128), ob[:])
                k.stt(S[:], S[:], dcol[:, di:di + 1], pkv[:, 0:DV], ALU.mult, ALU.add)
                k.copy(Sb[:], S[:], eng="act")
    nc = k.finish([d_oT])
    return nc, k


NEG = -30000.0


def na_configs():
    cfgs = []; plan = {}
    for g in range(32):
        lst = []
        for u in range(32):
            key = []
            for kh in range(2):
                for qh in range(2):
                    r = 2 * g + qh; kr = 2 * u + kh
                    r0 = min(max(r - 4, 0), 56)
                    key.append(kr - r + 7 if r0 <= kr <= r0 + 7 else None)
            key = tuple(key)
            if all(x is None for x in key):
                continue
            if key not in cfgs:
                cfgs.append(key)
            lst.append((u, cfgs.index(key)))
        plan[g] = lst
    return cfgs, plan


def build_M0():
    k = KB()
    d_xT = k.dram("xT", [D, 4096]); d_xcT = k.dram("xcT", [D, 256])
    d_cvec = k.dram("cvec", [128, 8, 2]); d_wmod = k.dram("wmod", [D, 2048]); d_bmod = k.dram("bmod", [128, 16])
    d_n1g = k.dram("n1g", [128, 8])
    d_wna = k.dram("wna", [D, 768]); d_wgl = k.dram("wgl", [D, 768]); d_wa = k.dram("wa", [D, 32])
    d_waaug = k.dram("waaug", [17, 2, 128])
    d_rpbT = k.dram("rpbT", [128, 4, 15, 64]); d_colmask = k.dram("colmask", [128, 64])
    d_gng = k.dram("gng", [128, 128])
    d_maskf = k.dram("mask_f", [128, 128]); d_maskb = k.dram("mask_b", [128, 128])
    d_ident = k.dram("ident", [128, 128])
    d_oT = k.dram("oT", [512, NT * 128], BF16, kind="ExternalOutput")

    P = [k.ps("P%d" % i, [128, 512]) for i in range(6)]
    PT = [k.ps("PT%d" % i, [128, 1024], BF16) for i in range(2)]
    pc = [0, 0]

    NRR = [6]

    def nextP():
        p = P[pc[0] % NRR[0]]; pc[0] += 1; return p

    def nextPT():
        p = PT[pc[1] % 2]; pc[1] += 1; return p

    ones_bf = k.sb("ones_bf", [128, 128], BF16); k.memset(ones_bf[:], 1.0)
    identf = k.sb("identf", [128, 128]); ident = k.sb("ident_s", [128, 128], BF16)
    k.dma(identf[:], d_ident); k.copy(ident[:], identf[:])
    n1g = k.sb("n1g_s", [128, 8]); k.dma(n1g[:], d_n1g)
    hT = k.sb("hT", [128, 8, NT * 128], BF16)
    oTs = [k.sb("oTs%d" % i, [128, 2, 512], BF16) for i in range(2)]
    k.open_scope()
    wmb = [k.sb("wmb0", [128, 8, 512], BF16)]
    modv = emit_mod(k, nextP, d_cvec, d_wmod, d_bmod, 2048, wmb)
    xt = [k.sb("xt0", [128, 8, 256]), k.sb("xt1", [128, 8, 256])]
    sq = k.sb("sq", [128, 8, 256], BF16); rstd = k.sb("rstd", [128, 256]); tmp = k.sb("tmp", [128, 256])
    emit_hT(k, nextP, hT, d_xT, d_xcT, modv, n1g, ones_bf, xt, sq, rstd, tmp)
    k.close_scope()

    cfgs, plan = na_configs()
    k.open_scope()
    wna = k.sb("wna_s", [128, 8, 768], BF16)
    k.dma(wna[:], d_wna.rearrange("(k p) c -> p k c", p=128), q="pool")
    QT = k.sb("QT", [64, 4, NT * 128], BF16); KT = k.sb("KT", [64, 4, NT * 128], BF16)
    Vaug = k.sb("Vaug", [128, NT, 4, 65], BF16)
    k.memset(Vaug[:], 1.0)
    BT = k.sb("BT", [128, len(cfgs), 4, 128])
    k.open_scope()
    Btab = k.sb("Btab", [128, 4, 15, 64]); cmask = k.sb("cmask", [128, 64])
    k.dma(Btab[:], d_rpbT); k.dma(cmask[:], d_colmask)
    for h in range(4):
        k.tt(Btab[:, h, :, :], Btab[:, h, :, :], cmask[:].unsqueeze(1).to_broadcast([128, 15, 64]), ALU.add)
    for ci, key in enumerate(cfgs):
        bi = 0
        for kh in range(2):
            for qh in range(2):
                roff = key[bi]; bi += 1
                dst = BT[kh * 64:(kh + 1) * 64, ci, :, qh * 64:(qh + 1) * 64]
                if roff is None:
                    k.memset(dst, NEG, eng="pool")
                else:
                    k.copy(dst, Btab[kh * 64:(kh + 1) * 64, :, roff, :], eng="pool")
    k.close_scope()
    qs = k.sb("qs", [128, 256]); ks_ = k.sb("ks", [128, 256])
    for t in range(NT):
        pa = nextP(); pb = nextP()
        for kk in range(8):
            k.matmul(pa[:, 0:512], hT[:, kk, t * 128:(t + 1) * 128], wna[:, kk, 0:512], start=(kk == 0), stop=(kk == 7))
        for kk in range(8):
            k.matmul(pb[:, 0:256], hT[:, kk, t * 128:(t + 1) * 128], wna[:, kk, 512:768], start=(kk == 0), stop=(kk == 7))
        k.ts(qs[:], pa[:, 0:256], 0.125, ALU.mult)
        k.copy(ks_[:], pa[:, 256:512])
        k.copy(Vaug[:, t, :, 0:64], pb[:, 0:256].rearrange("p (h d) -> p h d", h=4))
        pt = nextP(); pt2 = nextP()
        for h in range(4):
            k.transpose(pt[0:64, h * 128:(h + 1) * 128], qs[:, h * 64:(h + 1) * 64], identf[:])
        for h in range(4):
            k.transpose(pt2[0:64, h * 128:(h + 1) * 128], ks_[:, h * 64:(h + 1) * 64], identf[:])
        k.copy(QT[:, :, t * 128:(t + 1) * 128], pt[0:64, 0:512].rearrange("p (c t) -> p c t", c=4))
        k.copy(KT[:, :, t * 128:(t + 1) * 128], pt2[0:64, 0:512].rearrange("p (c t) -> p c t", c=4))
    import os
    STOP = float(os.environ.get("M0_STOP", "99"))
    if STOP <= 2:
        k.close_scope(); return k.finish([d_oT]), k
    sc = [k.sb("sc%d" % i, [128, 512]) for i in range(2)]
    PTb = [k.sb("PTb%d" % i, [128, 512], BF16) for i in range(2)]
    rden = k.sb("rden", [128, 4, 1]); obf = k.sb("obf", [128, 256], BF16)
    it = [0]
    NRR[0] = 4
    for qt in range(NT if STOP > 2.5 else int(os.environ.get("M0_NQ", "1"))):
        if qt < 2:
            keys = [(0, None), (1, None)]
        else:
            keys = [(0, None), (1, None)] + [(u + 2, ci) for (u, ci) in plan[qt - 2]]
        po = P[4 + qt % 2]
        for ki, (kt, ci) in enumerate(keys):
            ps = nextP()
            for h in range(4):
                k.matmul(ps[:, h * 128:(h + 1) * 128], KT[:, h, kt * 128:(kt + 1) * 128],
                         QT[:, h, qt * 128:(qt + 1) * 128])
            s_ = sc[it[0] % 2]; p_ = PTb[it[0] % 2]; it[0] += 1
            if ci is None:
                k.copy(s_[:], ps[:, 0:512])
            else:
                k.tt(s_[:], ps[:, 0:512], BT[:, ci, :, :].rearrange("p h q -> p (h q)"), ALU.add)
            k.act(p_[:], s_[:], AF.Exp)
            for h in range(4):
                k.matmul(po[:, h * 65:(h + 1) * 65], p_[:, h * 128:(h + 1) * 128], Vaug[:, kt, h, :],
                         start=(ki == 0 and h == 0), stop=(ki == len(keys) - 1 and h == 3))
        pov = po[:, 0:260].rearrange("p (h e) -> p h e", e=65)
        k.recip(rden[:], pov[:, :, 64:65])
        k.tt(obf[:].rearrange("p (h d) -> p h d", h=4), pov[:, :, 0:64], rden[:].to_broadcast([128, 4, 64]), ALU.mult)
        ptt = nextPT()
        k.transpose(ptt[:, 0:128], obf[:, 0:128], ident[:])
        k.transpose(ptt[:, 128:256], obf[:, 128:256], ident[:])
        ob = oTs[(qt // 4) % 2]
        k.copy(ob[:, :, (qt % 4) * 128:(qt % 4 + 1) * 128], ptt[:, 0:256].rearrange("p (c t) -> p c t", c=2))
        if qt % 4 == 3 or qt == NT - 1:
            q0 = (qt // 4) * 4; n = qt - q0 + 1
            k.dma(d_oT[0:256, q0 * 128:(qt + 1) * 128].rearrange("(c p) t -> p c t", p=128), ob[:, :, 0:n * 128])
    NRR[0] = 6
    k.close_scope()
    if STOP <= 3:
        return k.finish([d_oT]), k

    k.open_scope()
    waaugf = k.sb("waaugf", [17, 2, 128]); waaug = k.sb("waaug_s", [17, 2, 128], BF16)
    k.dma(waaugf[:], d_waaug); k.copy(waaug[:], waaugf[:])
    gng = k.sb("gng_s", [128, 128]); k.dma(gng[:], d_gng)
    mk = [k.sb("mkf", [128, 128]), k.sb("mkb", [128, 128])]
    k.dma(mk[0][:], d_maskf); k.dma(mk[1][:], d_maskb)
    mks = [k.sb("mksf", [128, 128]), k.sb("mksb", [128, 128])]
    k.ts(mks[0][:], mk[0][:], -1.0, ALU.mult, 1.0, ALU.add)
    k.ts(mks[1][:], mk[1][:], -1.0, ALU.mult, 1.0, ALU.add)
    qTg = k.sb("qTg", [64, NT, 128], BF16); kTg = k.sb("kTg", [64, NT, 128], BF16)
    kbg = k.sb("kbg", [128, NT, 64], BF16); vg = k.sb("vg", [128, NT, 128], BF16)
    rsl = k.sb("rsl", [128, NT, 128], BF16); sp = k.sb("sp", [128, NT, 2, 64])
    oacc = k.sb("oaccg", [128, NT, 128], BF16)
    wgl = k.sb("wgl_s", [128, 8, 384], BF16); wa = k.sb("wa_s", [128, 8, 32], BF16)
    k.dma(wa[:], d_wa.rearrange("(k p) c -> p k c", p=128), q="pool")
    aT = k.sb("aT", [17, 2, 128], BF16); k.memset(aT[:], 1.0)
    qf = k.sb("qfg", [128, 64]); kf = k.sb("kfg", [128, 64]); rf = k.sb("rfg", [128, 128]); zf = k.sb("zfg", [128, 128])
    S = k.sb("Sg", [64, 128]); Sb = k.sb("Sbg", [64, 128], BF16)
    EBT = k.sb("EBT", [64, 128]); ENBT = k.sb("ENBT", [64, 128]); ER = k.sb("ERg", [128, 64])
    bcs = k.sb("bcs", [64, 128]); rsb = k.sb("rsb", [128, 64])
    qin = k.sb("qing", [64, 128], BF16); kin = k.sb("king", [64, 128], BF16); kkg = k.sb("kkg", [128, 64], BF16)
    AT = k.sb("ATg", [128, 128], BF16)
    on = k.sb("ong", [128, 128]); st6 = k.sb("st6g", [128, 6]); mv = k.sb("mvg", [128, 2]); ms = k.sb("msg", [128, 1])
    obg = k.sb("obg", [128, 128], BF16)
    oTg = [k.sb("oTg%d" % i, [128, 512], BF16) for i in range(2)]
    dwg = d_wgl.rearrange("(k p) c -> p k c", p=128)
    for gh in range(2):
        k.dma(wgl[:, :, 0:64], dwg[:, :, gh * 64:(gh + 1) * 64], q="pool")
        k.dma(wgl[:, :, 64:128], dwg[:, :, 128 + gh * 64:128 + (gh + 1) * 64], q="pool")
        k.dma(wgl[:, :, 128:256], dwg[:, :, 256 + gh * 128:256 + (gh + 1) * 128], q="pool")
        k.dma(wgl[:, :, 256:384], dwg[:, :, 512 + gh * 128:512 + (gh + 1) * 128], q="pool")
        for t in range(NT):
            pa = nextP(); pz = nextP()
            for kk in range(8):
                k.matmul(pa[:, 0:384], hT[:, kk, t * 128:(t + 1) * 128], wgl[:, kk, 0:384], start=(kk == 0), stop=(kk == 7))
            for di in range(2):
                for kk in range(8):
                    k.matmul(pz[0:16, di * 128:(di + 1) * 128], wa[:, kk, di * 16:(di + 1) * 16], hT[:, kk, t * 128:(t + 1) * 128],
                             start=(kk == 0), stop=(kk == 7))
            k.copy(aT[0:16, :, :], pz[0:16, 0:256].rearrange("p (a t) -> p a t", a=2))
            pz2 = nextP()
            for di in range(2):
                k.matmul(pz2[:, di * 64:(di + 1) * 64], aT[:, di, :], waaug[:, di, gh * 64:(gh + 1) * 64])
            k.copy(zf[:], pz2[:, 0:128])
            k.act(zf[:], zf[:], AF.Exp, scale=-1.0)
            k.act(sp[:, t, :, :].rearrange("p a d -> p (a d)"), zf[:], AF.Ln, bias=1.0, scale=1.0)
            k.ts(qf[:], pa[:, 0:64], 0.125, ALU.mult)
            k.copy(kf[:], pa[:, 64:128])
            k.copy(kbg[:, t, :], kf[:], eng="pool")
            k.copy(rf[:], pa[:, 128:256])
            k.act(rsl[:, t, :], rf[:], AF.Silu)
            k.copy(vg[:, t, :], pa[:, 256:384])
            pt = nextP()
            k.transpose(pt[0:64, 0:128], qf[:], identf[:])
            k.transpose(pt[0:64, 128:256], kf[:], identf[:])
            k.copy(qTg[:, t, :], pt[0:64, 0:128])
            k.copy(kTg[:, t, :], pt[0:64, 128:256])
        for di in (1, 0):
            order = ([1, 0] + list(range(33, 1, -1))) if di == 1 else list(range(NT))
            k.memset(S[:], 0.0); k.memset(Sb[:], 0.0)
            for t in order:
                spt = sp[:, t, di, :]
                pbc = nextP()
                k.matmul(pbc[0:64, 0:128], spt, mk[di][:])
                k.matmul(pbc[:, 128:192], mks[di][:], spt)
                k.copy(bcs[:], pbc[0:64, 0:128]); k.copy(rsb[:], pbc[:, 128:192])
                k.act(EBT[:], bcs[:], AF.Exp, scale=-1.0 / 16.0)
                k.act(ENBT[:], bcs[:], AF.Exp, scale=1.0 / 16.0)
                k.act(ER[:], rsb[:], AF.Exp, scale=-1.0 / 16.0)
                k.tt(qin[:], qTg[:, t, :], EBT[:], ALU.mult)
                k.tt(kin[:], kTg[:, t, :], ENBT[:], ALU.mult, eng="pool")
                k.tt(kkg[:], kbg[:, t, :], ER[:], ALU.mult, eng="pool")
                pat = nextP()
                k.matmul(pat[:, 0:128], kin[:], qin[:])
                k.tt(AT[:], pat[:, 0:128], mk[di][:], ALU.mult)
                po = nextP()
                k.matmul(po[:, 0:128], AT[:], vg[:, t, :], start=True, stop=False)
                k.matmul(po[:, 0:128], qin[:], Sb[:], start=False, stop=True)
                pkv = nextP()
                k.matmul(pkv[0:64, 0:128], kkg[:], vg[:, t, :])
                if di == 1:
                    k.copy(oacc[:, t, :], po[:, 0:128])
                else:
                    k.tt(on[:], po[:, 0:128], oacc[:, t, :], ALU.add)
                    k.op("dve", lambda e, a=st6, b=on: e.bn_stats(a[:], b[:]), [on[:]], [st6[:]])
                    k.op("dve", lambda e, a=mv, b=st6: e.bn_aggr(a[:], b[:]), [st6[:]], [mv[:]])
                    k.stt(ms[:], mv[:, 0:1], mv[:, 0:1], mv[:, 1:2], ALU.mult, ALU.add)
                    k.ts(ms[:], ms[:], EPS, ALU.add)
                    k.act(ms[:], ms[:], AF.Sqrt)
                    k.recip(ms[:], ms[:])
                    k.stt(on[:], on[:], ms[:, 0:1], gng[:], ALU.mult, ALU.mult)
                    k.tt(obg[:], on[:], rsl[:, t, :], ALU.mult, eng="pool")
                    ptt = nextPT()
                    k.transpose(ptt[:, 0:128], obg[:], ident[:])
                    ob = oTg[(t // 4) % 2]
                    k.copy(ob[:, (t % 4) * 128:(t % 4 + 1) * 128], ptt[:, 0:128])
                    if t % 4 == 3 or t == NT - 1:
                        q0 = (t // 4) * 4; n = t - q0 + 1
                        k.dma(d_oT[256 + gh * 128:256 + (gh + 1) * 128, q0 * 128:(t + 1) * 128], ob[:, 0:n * 128])
                dci = 127 if di == 0 else 0
                k.stt(S[:], S[:], EBT[:, dci:dci + 1], pkv[0:64, 0:128], ALU.mult, ALU.add)
                k.copy(Sb[:], S[:], eng="act")
    k.close_scope()
    nc = k.finish([d_oT])
    return nc, k

BF = ml_dtypes.bfloat16

def pk(v):
    return np.ascontiguousarray(v.reshape(-1, 128).T)

def seg_cols(arrT, t0, n, T):
    out = np.zeros((arrT.shape[0], n + 2), arrT.dtype)
    lo = max(t0 - 1, 0); hi = min(t0 + n + 1, T)
    out[:, lo - (t0 - 1): hi - (t0 - 1)] = arrT[:, lo:hi]
    hm = np.array([1.0 if t0 - 1 >= 0 else 0.0, 1.0 if t0 + n < T else 0.0], np.float32)
    return out, np.ascontiguousarray(np.broadcast_to(hm, (128, 2)))

def prep_F(inp, L, b, th, xT, oT, xcT=None, ocT=None, wout=None):
    m = {}
    segs = []
    for s in range(4):
        t0 = th * 2048 + s * 512
        m["xT_%d" % s], m["hm_%d" % s] = seg_cols(xT, t0, 512, 4096)
        m["oT_%d" % s], _ = seg_cols(oT, t0, 512, 4096)
        segs.append((512, 0))
    if xcT is not None:
        m["xT_4"], m["hm_4"] = seg_cols(xcT, 0, 256, 256)
        m["oT_4"], _ = seg_cols(ocT, 0, 256, 256)
        segs.append((256, 1))
    cv = np.stack([pk(inp["c"][b]), pk(inp["c_ctx"])], axis=-1)
    m["cvec"] = np.ascontiguousarray(cv.astype(np.float32))
    m["wmod"] = np.ascontiguousarray(inp["w_mod"][L][:, 2048:6144])
    m["bmod"] = pk(inp["b_mod"][L][2048:6144])
    m["n2g"] = pk(inp["norm2_g"][L]); m["fg"] = pk(inp["final_norm_g"])
    cw = inp["ffn_conv_w"][L]
    m["convw"] = np.ascontiguousarray(np.stack([pk(cw[i]) for i in range(3)], axis=1))
    m["convb"] = pk(inp["ffn_conv_b"][L])
    m["wout"] = wout
    m["wup"] = inp["ffn_w_up"][L]; m["wdn"] = inp["ffn_w_down"][L]
    return m, segs


_C = make_consts(); _COS, _SIN = rope_tables()

def prep_M_common(inp, L, b, xT, xcT):
    m = {"xT": np.ascontiguousarray(xT), "xcT": np.ascontiguousarray(xcT)}
    cv = np.stack([pk(inp["c"][b]), pk(inp["c_ctx"])], axis=-1)
    m["cvec"] = np.ascontiguousarray(cv.astype(np.float32))
    m["wmod"] = np.ascontiguousarray(inp["w_mod"][L][:, 0:2048])
    m["bmod"] = pk(inp["b_mod"][L][0:2048])
    m["n1g"] = pk(inp["norm1_g"][L])
    m["ident"] = np.eye(128, dtype=np.float32)
    return m

def prep_M1(inp, b, hh, xT, xcT):
    m = prep_M_common(inp, 1, b, xT, xcT)
    w = inp["ret_w_in"][0]
    heads = [hh * 4 + i for i in range(4)]
    m["win"] = np.ascontiguousarray(np.stack([np.concatenate([
        w[:, h * 128:(h + 1) * 128], w[:, 1024 + h * 128:1024 + (h + 1) * 128],
        w[:, 4096 + h * 256:4096 + (h + 1) * 256], w[:, 2048 + h * 256:2048 + (h + 1) * 256]], axis=1) for h in heads]))
    dec = np.concatenate([inp["ret_decay_fwd"][0][heads], inp["ret_decay_bwd"][0][heads]])
    m["dec"] = np.ascontiguousarray(np.broadcast_to(dec, (128, 8)).astype(np.float32))
    m["ng"] = np.ascontiguousarray(np.broadcast_to(inp["ret_norm_g"][0][heads], (128, 4, 256)).astype(np.float32))
    m["cos"] = _COS; m["sin"] = _SIN
    for kk in ("mask_f", "mask_b", "dm_f", "dm_b", "row_f", "row_b", "pcols"):
        m[kk] = _C[kk]
    return m

def _na_tables(rpb_heads):
    c = np.arange(64); c0 = np.clip(c - 8, 0, 48)
    kc = np.arange(64)
    allowed = (kc[:, None] >= c0[None, :]) & (kc[:, None] < c0[None, :] + 16)
    off = np.clip(kc[:, None] - c[None, :] + 15, 0, 30)
    g = rpb_heads[:, :, off]
    g = np.where(allowed[None, None], g, 0.0).astype(np.float32)
    g = np.transpose(g, (2, 0, 1, 3))
    rpbT = np.ascontiguousarray(np.concatenate([g, g], axis=0))
    cm = np.where(allowed, 0.0, -30000.0).astype(np.float32)
    return rpbT, np.ascontiguousarray(np.concatenate([cm, cm], axis=0))

def prep_M0(inp, b, hh, xT, xcT):
    m = prep_M_common(inp, 0, b, xT, xcT)
    w = inp["na_gla_w_in"][0]
    nh = [hh * 4 + i for i in range(4)]; gh = [hh * 2 + i for i in range(2)]
    m["wna"] = np.ascontiguousarray(np.concatenate(
        [w[:, h * 64:(h + 1) * 64] for h in nh] + [w[:, 512 + h * 64:512 + (h + 1) * 64] for h in nh] +
        [w[:, 1024 + h * 64:1024 + (h + 1) * 64] for h in nh], axis=1))
    m["wgl"] = np.ascontiguousarray(np.concatenate(
        [w[:, 1536 + h * 64:1536 + (h + 1) * 64] for h in gh] + [w[:, 1792 + h * 64:1792 + (h + 1) * 64] for h in gh] +
        [w[:, 2560 + h * 128:2560 + (h + 1) * 128] for h in gh] + [w[:, 2048 + h * 128:2048 + (h + 1) * 128] for h in gh], axis=1))
    m["wa"] = np.ascontiguousarray(w[:, 3072:3104])
    gc = slice(hh * 128, (hh + 1) * 128)
    wa = np.zeros((17, 2, 128), np.float32)
    wa[0:16, 0] = inp["gla_w_a_fwd"][0][:, gc]; wa[16, 0] = inp["gla_b_a_fwd"][0][gc]
    wa[0:16, 1] = inp["gla_w_a_bwd"][0][:, gc]; wa[16, 1] = inp["gla_b_a_bwd"][0][gc]
    m["waaug"] = wa
    m["rpbT"], m["colmask"] = _na_tables(inp["na_rpb"][0][nh])
    m["gng"] = np.ascontiguousarray(np.broadcast_to(inp["gla_norm_g"][0], (128, 128)).astype(np.float32))
    m["mask_f"] = _C["mask_f"]; m["mask_b"] = _C["mask_b"]
    return m


class Env:
    pass


def make_env(k):
    e = Env(); e.k = k
    e.P = [k.ps("P%d" % i, [128, 512]) for i in range(6)]
    e.PT = [k.ps("PT%d" % i, [128, 1024], BF16) for i in range(2)]
    e.pc = [0, 0]; e.NRR = [6]

    def nextP():
        p = e.P[e.pc[0] % e.NRR[0]]; e.pc[0] += 1; return p

    def nextPT():
        p = e.PT[e.pc[1] % 2]; e.pc[1] += 1; return p
    e.nextP = nextP; e.nextPT = nextPT
    e.ones_bf = k.sb("ones_bf", [128, 128], BF16); k.memset(e.ones_bf[:], 1.0)
    e.identf = k.sb("identf", [128, 128]); e.ident = k.sb("ident_s", [128, 128], BF16)
    d_ident = k.dram("ident", [128, 128])
    k.dma(e.identf[:], d_ident); k.copy(e.ident[:], e.identf[:])
    return e


def phase_hT(e, pfx, d_xT, d_xcT, d_cvec, d_wmod, d_bmod, d_n1g, hT):
    k = e.k
    k.open_scope()
    n1g = k.sb(pfx + "n1g_s", [128, 8]); k.dma(n1g[:], d_n1g)
    wmb = [k.sb(pfx + "wmb0", [128, 8, 512], BF16)]
    modv = emit_mod(k, e.nextP, d_cvec, d_wmod, d_bmod, 2048, wmb, name=pfx + "m")
    xt = [k.sb(pfx + "xt0", [128, 8, 256]), k.sb(pfx + "xt1", [128, 8, 256])]
    sq = k.sb(pfx + "sq", [128, 8, 256], BF16); rstd = k.sb(pfx + "rstd", [128, 256]); tmp = k.sb(pfx + "tmp", [128, 256])
    emit_hT(k, e.nextP, hT, d_xT, d_xcT, modv, n1g, e.ones_bf, xt, sq, rstd, tmp, name=pfx)
    k.close_scope()


def phase_NA(e, pfx, hT, d_wna, d_rpbT, d_colmask, d_oT, row0):
    k = e.k; nextP = e.nextP; nextPT = e.nextPT; identf = e.identf; ident = e.ident; P = e.P
    cfgs, plan = na_configs()
    k.open_scope()
    oTs = [k.sb(pfx + "oTs%d" % i, [128, 2, 512], BF16) for i in range(2)]
    wna = k.sb(pfx + "wna_s", [128, 8, 768], BF16)
    k.dma(wna[:], d_wna.rearrange("(k p) c -> p k c", p=128), q="pool")
    QT = k.sb(pfx + "QT", [64, 4, NT * 128], BF16); KT = k.sb(pfx + "KT", [64, 4, NT * 128], BF16)
    Vaug = k.sb(pfx + "Vaug", [128, NT, 4, 65], BF16)
    k.memset(Vaug[:], 1.0)
    BT = k.sb(pfx + "BT", [128, len(cfgs), 4, 128])
    k.open_scope()
    Btab = k.sb(pfx + "Btab", [128, 4, 15, 64]); cmask = k.sb(pfx + "cmask", [128, 64])
    k.dma(Btab[:], d_rpbT); k.dma(cmask[:], d_colmask)
    for h in range(4):
        k.tt(Btab[:, h, :, :], Btab[:, h, :, :], cmask[:].unsqueeze(1).to_broadcast([128, 15, 64]), ALU.add)
    for ci, key in enumerate(cfgs):
        bi = 0
        for kh in range(2):
            for qh in range(2):
                roff = key[bi]; bi += 1
                dst = BT[kh * 64:(kh + 1) * 64, ci, :, qh * 64:(qh + 1) * 64]
                if roff is None:
                    k.memset(dst, NEG, eng="pool")
                else:
                    k.copy(dst, Btab[kh * 64:(kh + 1) * 64, :, roff, :], eng="pool")
    k.close_scope()
    qs = k.sb(pfx + "qs", [128, 256]); ks_ = k.sb(pfx + "ks", [128, 256])
    for t in range(NT):
        pa = nextP(); pb = nextP()
        for kk in range(8):
            k.matmul(pa[:, 0:512], hT[:, kk, t * 128:(t + 1) * 128], wna[:, kk, 0:512], start=(kk == 0), stop=(kk == 7))
        for kk in range(8):
            k.matmul(pb[:, 0:256], hT[:, kk, t * 128:(t + 1) * 128], wna[:, kk, 512:768], start=(kk == 0), stop=(kk == 7))
        k.ts(qs[:], pa[:, 0:256], 0.125, ALU.mult)
        k.copy(ks_[:], pa[:, 256:512])
        k.copy(Vaug[:, t, :, 0:64], pb[:, 0:256].rearrange("p (h d) -> p h d", h=4))
        pt = nextP(); pt2 = nextP()
        for h in range(4):
            k.transpose(pt[0:64, h * 128:(h + 1) * 128], qs[:, h * 64:(h + 1) * 64], identf[:])
        for h in range(4):
            k.transpose(pt2[0:64, h * 128:(h + 1) * 128], ks_[:, h * 64:(h + 1) * 64], identf[:])
        k.copy(QT[:, :, t * 128:(t + 1) * 128], pt[0:64, 0:512].rearrange("p (c t) -> p c t", c=4))
        k.copy(KT[:, :, t * 128:(t + 1) * 128], pt2[0:64, 0:512].rearrange("p (c t) -> p c t", c=4))
    sc = [k.sb(pfx + "sc%d" % i, [128, 512]) for i in range(2)]
    PTb = [k.sb(pfx + "PTb%d" % i, [128, 512], BF16) for i in range(2)]
    rden = k.sb(pfx + "rden", [128, 4, 1]); obf = k.sb(pfx + "obf", [128, 256], BF16)
    it = [0]
    e.NRR[0] = 4
    for qt in range(NT):
        if qt < 2:
            keys = [(0, None), (1, None)]
        else:
            keys = [(0, None), (1, None)] + [(u + 2, ci) for (u, ci) in plan[qt - 2]]
        po = P[4 + qt % 2]
        for ki, (kt, ci) in enumerate(keys):
            ps = nextP()
            for h in range(4):
                k.matmul(ps[:, h * 128:(h + 1) * 128], KT[:, h, kt * 128:(kt + 1) * 128], QT[:, h, qt * 128:(qt + 1) * 128])
            s_ = sc[it[0] % 2]; p_ = PTb[it[0] % 2]; it[0] += 1
            if ci is None:
                k.copy(s_[:], ps[:, 0:512])
            else:
                k.tt(s_[:], ps[:, 0:512], BT[:, ci, :, :].rearrange("p h q -> p (h q)"), ALU.add)
            k.act(p_[:], s_[:], AF.Exp)
            for h in range(4):
                k.matmul(po[:, h * 65:(h + 1) * 65], p_[:, h * 128:(h + 1) * 128], Vaug[:, kt, h, :],
                         start=(ki == 0 and h == 0), stop=(ki == len(keys) - 1 and h == 3))
        pov = po[:, 0:260].rearrange("p (h e) -> p h e", e=65)
        k.recip(rden[:], pov[:, :, 64:65])
        k.tt(obf[:].rearrange("p (h d) -> p h d", h=4), pov[:, :, 0:64], rden[:].to_broadcast([128, 4, 64]), ALU.mult)
        ptt = nextPT()
        k.transpose(ptt[:, 0:128], obf[:, 0:128], ident[:])
        k.transpose(ptt[:, 128:256], obf[:, 128:256], ident[:])
        ob = oTs[(qt // 4) % 2]
        k.copy(ob[:, :, (qt % 4) * 128:(qt % 4 + 1) * 128], ptt[:, 0:256].rearrange("p (c t) -> p c t", c=2))
        if qt % 4 == 3 or qt == NT - 1:
            q0 = (qt // 4) * 4; n = qt - q0 + 1
            k.dma(d_oT[row0:row0 + 256, q0 * 128:(qt + 1) * 128].rearrange("(c p) t -> p c t", p=128), ob[:, :, 0:n * 128])
    e.NRR[0] = 6
    k.close_scope()


def phase_GLA(e, pfx, hT, d_wgl, d_wa, d_waaug, d_gng, d_maskf, d_maskb, d_oT, row0, ghs):
    k = e.k; nextP = e.nextP; nextPT = e.nextPT; identf = e.identf; ident = e.ident
    k.open_scope()
    waaugf = k.sb(pfx + "waaugf", [17, 2, 256]); waaug = k.sb(pfx + "waaug_s", [17, 2, 256], BF16)
    k.dma(waaugf[:], d_waaug); k.copy(waaug[:], waaugf[:])
    gng = k.sb(pfx + "gng_s", [128, 128]); k.dma(gng[:], d_gng)
    mk = [k.sb(pfx + "mkf", [128, 128]), k.sb(pfx + "mkb", [128, 128])]
    k.dma(mk[0][:], d_maskf); k.dma(mk[1][:], d_maskb)
    mks = [k.sb(pfx + "mksf", [128, 128]), k.sb(pfx + "mksb", [128, 128])]
    k.ts(mks[0][:], mk[0][:], -1.0, ALU.mult, 1.0, ALU.add)
    k.ts(mks[1][:], mk[1][:], -1.0, ALU.mult, 1.0, ALU.add)
    qTg = k.sb(pfx + "qTg", [64, NT, 128], BF16); kTg = k.sb(pfx + "kTg", [64, NT, 128], BF16)
    kbg = k.sb(pfx + "kbg", [128, NT, 64], BF16); vg = k.sb(pfx + "vg", [128, NT, 128], BF16)
    rsl = k.sb(pfx + "rsl", [128, NT, 128], BF16); sp = k.sb(pfx + "sp", [128, NT, 2, 64])
    oacc = k.sb(pfx + "oaccg", [128, NT, 128], BF16)
    wgl = k.sb(pfx + "wgl_s", [128, 8, 384], BF16); wa = k.sb(pfx + "wa_s", [128, 8, 32], BF16)
    k.dma(wa[:], d_wa.rearrange("(k p) c -> p k c", p=128), q="pool")
    aT = k.sb(pfx + "aT", [17, 2, 128], BF16); k.memset(aT[:], 1.0)
    qf = k.sb(pfx + "qfg", [128, 64]); kf = k.sb(pfx + "kfg", [128, 64]); rf = k.sb(pfx + "rfg", [128, 128]); zf = k.sb(pfx + "zfg", [128, 128])
    R2 = range(2)
    S2 = [k.sb(pfx + "Sg%d" % i, [64, 128]) for i in R2]; Sb2 = [k.sb(pfx + "Sbg%d" % i, [64, 128], BF16) for i in R2]
    EBT2 = [k.sb(pfx + "EBT%d" % i, [64, 128]) for i in R2]; ENBT2 = [k.sb(pfx + "ENBT%d" % i, [64, 128]) for i in R2]
    ER2 = [k.sb(pfx + "ERg%d" % i, [128, 64]) for i in R2]
    bcs2 = [k.sb(pfx + "bcs%d" % i, [64, 128]) for i in R2]; rsb2 = [k.sb(pfx + "rsb%d" % i, [128, 64]) for i in R2]
    qin2 = [k.sb(pfx + "qing%d" % i, [64, 128], BF16) for i in R2]; kin2 = [k.sb(pfx + "king%d" % i, [64, 128], BF16) for i in R2]
    kkg2 = [k.sb(pfx + "kkg%d" % i, [128, 64], BF16) for i in R2]
    AT2 = [k.sb(pfx + "ATg%d" % i, [128, 128], BF16) for i in R2]
    oT4 = [k.sb(pfx + "oT4g%d" % i, [128, 128], BF16) for i in range(4)]; oc = [0]
    on = k.sb(pfx + "ong", [128, 128]); st6 = k.sb(pfx + "st6g", [128, 6]); mv = k.sb(pfx + "mvg", [128, 2]); ms = k.sb(pfx + "msg", [128, 1])
    obg = k.sb(pfx + "obg", [128, 128], BF16)
    dwg = d_wgl.rearrange("(k p) c -> p k c", p=128)
    for (gl, gglob, rofs) in ghs:
        k.dma(wgl[:, :, 0:64], dwg[:, :, gl * 64:(gl + 1) * 64], q="pool")
        k.dma(wgl[:, :, 64:128], dwg[:, :, 128 + gl * 64:128 + (gl + 1) * 64], q="pool")
        k.dma(wgl[:, :, 128:256], dwg[:, :, 256 + gl * 128:256 + (gl + 1) * 128], q="pool")
        k.dma(wgl[:, :, 256:384], dwg[:, :, 512 + gl * 128:512 + (gl + 1) * 128], q="pool")
        for t in range(NT):
            pa = nextP(); pz = nextP()
            for kk in range(8):
                k.matmul(pa[:, 0:384], hT[:, kk, t * 128:(t + 1) * 128], wgl[:, kk, 0:384], start=(kk == 0), stop=(kk == 7))
            for di in range(2):
                for kk in range(8):
                    k.matmul(pz[0:16, di * 128:(di + 1) * 128], wa[:, kk, di * 16:(di + 1) * 16], hT[:, kk, t * 128:(t + 1) * 128],
                             start=(kk == 0), stop=(kk == 7))
            k.copy(aT[0:16, :, :], pz[0:16, 0:256].rearrange("p (a t) -> p a t", a=2))
            pz2 = nextP()
            for di in range(2):
                k.matmul(pz2[:, di * 64:(di + 1) * 64], aT[:, di, :], waaug[:, di, gglob * 64:(gglob + 1) * 64])
            k.copy(zf[:], pz2[:, 0:128])
            k.act(zf[:], zf[:], AF.Exp, scale=-1.0)
            k.act(sp[:, t, :, :].rearrange("p a d -> p (a d)"), zf[:], AF.Ln, bias=1.0, scale=1.0)
            k.ts(qf[:], pa[:, 0:64], 0.125, ALU.mult)
            k.copy(kf[:], pa[:, 64:128])
            k.copy(kbg[:, t, :], kf[:], eng="pool")
            k.copy(rsl[:, t, :], pa[:, 128:256])
            k.copy(vg[:, t, :], pa[:, 256:384])
            pt = nextP()
            k.transpose(pt[0:64, 0:128], qf[:], identf[:])
            k.transpose(pt[0:64, 128:256], kf[:], identf[:])
            k.copy(qTg[:, t, :], pt[0:64, 0:128])
            k.copy(kTg[:, t, :], pt[0:64, 128:256])
        for t in range(NT):
            k.act(rsl[:, t, :], rsl[:, t, :], AF.Silu)
        ordB = [1, 0] + list(range(33, 1, -1)); ordF = list(range(NT))
        for di in range(2):
            k.memset(S2[di][:], 0.0); k.memset(Sb2[di][:], 0.0)
        have = set()
        for i in range(NT):
            for di, t in ((1, ordB[i]), (0, ordF[i])):
                S = S2[di]; Sb = Sb2[di]; AT = AT2[di]; qin = qin2[di]; kin = kin2[di]; kkg = kkg2[di]
                EBT = EBT2[di]; ENBT = ENBT2[di]; ER = ER2[di]; bcs = bcs2[di]; rsb = rsb2[di]
                spt = sp[:, t, di, :]
                pbc = nextP()
                k.matmul(pbc[0:64, 0:128], spt, mk[di][:])
                k.matmul(pbc[:, 128:192], mks[di][:], spt)
                k.copy(bcs[:], pbc[0:64, 0:128]); k.copy(rsb[:], pbc[:, 128:192])
                k.act(EBT[:], bcs[:], AF.Exp, scale=-1.0 / 16.0)
                k.act(ENBT[:], bcs[:], AF.Exp, scale=1.0 / 16.0)
                k.act(ER[:], rsb[:], AF.Exp, scale=-1.0 / 16.0)
                k.tt(qin[:], qTg[:, t, :], EBT[:], ALU.mult)
                k.tt(kin[:], kTg[:, t, :], ENBT[:], ALU.mult)
                k.tt(kkg[:], kbg[:, t, :], ER[:], ALU.mult)
                pat = nextP()
                k.matmul(pat[:, 0:128], kin[:], qin[:])
                k.tt(AT[:], pat[:, 0:128], mk[di][:], ALU.mult)
                po = nextP()
                k.matmul(po[:, 0:128], AT[:], vg[:, t, :], start=True, stop=False)
                k.matmul(po[:, 0:128], qin[:], Sb[:], start=False, stop=True)
                pkv = nextP()
                k.matmul(pkv[0:64, 0:128], kkg[:], vg[:, t, :])
                if t not in have:
                    k.copy(oacc[:, t, :], po[:, 0:128]); have.add(t)
                else:
                    k.tt(on[:], po[:, 0:128], oacc[:, t, :], ALU.add)
                    k.op("dve", lambda e_, a=st6, b=on: e_.bn_stats(a[:], b[:]), [on[:]], [st6[:]])
                    k.op("dve", lambda e_, a=mv, b=st6: e_.bn_aggr(a[:], b[:]), [st6[:]], [mv[:]])
                    k.stt(ms[:], mv[:, 0:1], mv[:, 0:1], mv[:, 1:2], ALU.mult, ALU.add)
                    k.ts(ms[:], ms[:], EPS, ALU.add)
                    k.act(ms[:], ms[:], AF.Ln)
                    k.act(ms[:], ms[:], AF.Exp, scale=-0.5)
                    k.stt(on[:], on[:], ms[:, 0:1], gng[:], ALU.mult, ALU.mult)
                    k.tt(obg[:], on[:], rsl[:, t, :], ALU.mult, eng="pool")
                    ptt = nextPT()
                    k.transpose(ptt[:, 0:128], obg[:], ident[:])
                    ob = oT4[oc[0] % 4]; oc[0] += 1
                    k.copy(ob[:], ptt[:, 0:128])
                    k.dma(d_oT[row0 + rofs:row0 + rofs + 128, t * 128:(t + 1) * 128], ob[:])
                dci = 127 if di == 0 else 0
                k.stt(S[:], S[:], EBT[:, dci:dci + 1], pkv[0:64, 0:128], ALU.mult, ALU.add)
                k.copy(Sb[:], S[:], eng="act")
    k.close_scope()


def phase_RET(e, pfx, hT, d_win, d_dec, d_ng, d_cos, d_sin, cn, d_pcols, d_oT, NH):
    k = e.k; nextP = e.nextP; nextPT = e.nextPT; identf = e.identf; ident = e.ident
    DK = 128; DV = 256
    k.open_scope()
    cs = {nm: k.sb(pfx + nm + "_s", [128, 128]) for nm in cn}
    for nm in cn:
        k.dma(cs[nm][:], cn[nm])
    pcols = k.sb(pfx + "pcols_s", [128, 4]); k.dma(pcols[:], d_pcols)
    cos = k.sb(pfx + "cos_s", [128, 32, 128]); sin = k.sb(pfx + "sin_s", [128, 32, 128])
    k.dma(cos[:], d_cos); k.dma(sin[:], d_sin)
    ng = k.sb(pfx + "ng_s", [128, 2, DV])
    dec = k.sb(pfx + "dec_s", [128, 2 * NH]); k.dma(dec[:], d_dec)
    lg = k.sb(pfx + "lg", [128, 2 * NH]); e1 = k.sb(pfx + "e1", [128, 2 * NH])
    k.act(e1[:], dec[:], AF.Exp, scale=-1.0)
    k.act(e1[:], e1[:], AF.Ln, bias=1.0, scale=1.0)
    k.ts(lg[:], e1[:], -1.0, ALU.mult)
    w = k.sb(pfx + "whd0", [128, 8, 768], BF16)
    kb = k.sb(pfx + "kb", [128, NT, DK], BF16); qT = k.sb(pfx + "qT", [128, NT, 128], BF16); kT = k.sb(pfx + "kT", [128, NT, 128], BF16)
    vb = k.sb(pfx + "vb", [128, NT, DV], BF16); gs = k.sb(pfx + "gs", [128, 32, DV], BF16)
    oacc = k.sb(pfx + "oacc", [128, 32, DV], BF16)
    qkL = [k.sb(pfx + "qk%d" % i, [128, 256]) for i in range(2)]
    tAL = [k.sb(pfx + "tA%d" % i, [128, 256]) for i in range(1)] * 2; tBL = [k.sb(pfx + "tB%d" % i, [128, 256]) for i in range(1)] * 2
    DM = [k.sb(pfx + "DM%d" % i, [128, 128]) for i in range(2)]
    EBr = [k.sb(pfx + "EBr%d" % i, [128, 128]) for i in range(2)]
    ERc = k.sb(pfx + "ERc", [128, 2]); dcol = k.sb(pfx + "dcol", [128, 2])
    S2 = [k.sb(pfx + "S%d" % i, [128, DV]) for i in range(2)]; Sb2 = [k.sb(pfx + "Sb%d" % i, [128, DV], BF16) for i in range(2)]
    AT2 = [k.sb(pfx + "AT%d" % i, [128, 128], BF16) for i in range(2)]; qin2 = [k.sb(pfx + "qin%d" % i, [128, 128], BF16) for i in range(2)]
    kk2 = [k.sb(pfx + "kk%d" % i, [128, 128], BF16) for i in range(2)]
    oT4 = [k.sb(pfx + "oT4%d" % i, [128, 2, 128], BF16) for i in range(2)] * 2; oc = [0]
    R2_ = range(2)
    st6L = [k.sb(pfx + "st6%d" % i, [128, 6]) for i in R2_]; mvL = [k.sb(pfx + "mv%d" % i, [128, 2]) for i in R2_]
    rsL = [k.sb(pfx + "rs%d" % i, [128, 1]) for i in R2_]; nbL = [k.sb(pfx + "nb%d" % i, [128, 1]) for i in R2_]
    onL = [k.sb(pfx + "on%d" % i, [128, DV]) for i in R2_]; obfL = [k.sb(pfx + "obf%d" % i, [128, DV], BF16) for i in R2_]
    pend = [None]; rc = [0]
    gfL = tBL
    for hl in range(NH):
        k.dma(w[:], d_win[hl].rearrange("(k p) c -> p k c", p=128), q="pool")
        k.dma(ng[:, hl % 2, :], d_ng[:, hl, :])
        for di, (dmn, mkn, rown) in enumerate((("dm_f", "mask_f", "row_f"), ("dm_b", "mask_b", "row_b"))):
            lgc = lg[:, di * NH + hl:di * NH + hl + 1]
            k.act(DM[di][:], cs[dmn][:], AF.Exp, scale=lgc)
            k.tt(DM[di][:], DM[di][:], cs[mkn][:], ALU.mult)
            k.act(EBr[di][:], cs[rown][:], AF.Exp, scale=lgc)
            k.act(ERc[:, di:di + 1], pcols[:, di:di + 1], AF.Exp, scale=lgc)
            k.act(dcol[:, di:di + 1], pcols[:, 2:3], AF.Exp, scale=lgc)
        for t in range(NT):
            pa = nextP(); pb = nextP()
            for kk in range(8):
                k.matmul(pa[:, 0:512], hT[:, kk, t * 128:(t + 1) * 128], w[:, kk, 0:512], start=(kk == 0), stop=(kk == 7))
            for kk in range(8):
                k.matmul(pb[:, 0:256], hT[:, kk, t * 128:(t + 1) * 128], w[:, kk, 512:768], start=(kk == 0), stop=(kk == 7))
            qk = qkL[t % 2]; gf = gfL[t % 2]
            SC = float(DK) ** -0.5
            if t >= 2:
                xv = pa[:, 0:256].rearrange("p (g h f) -> p g h f", g=4, h=2)
                ov = qk[:].rearrange("p (g h f) -> p g h f", g=4, h=2)
                Av = tAL[t % 2][:].rearrange("p (g h f) -> p g h f", g=4, h=2)
                Bv = tBL[t % 2][:].rearrange("p (g h f) -> p g h f", g=4, h=2)
                cb_ = cos[:, t - 2, :].rearrange("p (g f) -> p g f", g=4).unsqueeze(2).to_broadcast([128, 4, 2, 32])
                sv = sin[:, t - 2, :].rearrange("p (g f) -> p g f", g=4)
                k.tt(Av, xv, cb_, ALU.mult)
                k.tt(Bv[:, :, 0, :], xv[:, :, 1, :], sv, ALU.mult)
                k.tt(Bv[:, :, 1, :], xv[:, :, 0, :], sv, ALU.mult)
                k.tt(ov[:, :, 0, :], Av[:, :, 0, :], Bv[:, :, 0, :], ALU.subtract)
                k.tt(ov[:, :, 1, :], Av[:, :, 1, :], Bv[:, :, 1, :], ALU.add)
                k.copy(gf[:], pa[:, 256:512])
                k.act(gs[:, t - 2, :], gf[:], AF.Silu)
            else:
                k.copy(qk[:], pa[:, 0:256])
            k.ts(kb[:, t, :], qk[:, 128:256], SC, ALU.mult, eng="pool")
            k.copy(vb[:, t, :], pb[:, 0:256])
            pt = nextP()
            k.transpose(pt[:, 0:128], qk[:, 0:128], identf[:])
            k.transpose(pt[:, 128:256], qk[:, 128:256], identf[:])
            k.copy(qT[:, t, :], pt[:, 0:128])
            k.ts(kT[:, t, :], pt[:, 128:256], SC, ALU.mult)
        ordB = [1, 0] + list(range(33, 1, -1)); ordF = list(range(NT))
        for di in range(2):
            k.memset(S2[di][:], 0.0); k.memset(Sb2[di][:], 0.0)
        have = set()
        for i in range(NT):
            for di, t in ((1, ordB[i]), (0, ordF[i])):
                S = S2[di]; Sb = Sb2[di]; AT = AT2[di]; qin = qin2[di]; kk_ = kk2[di]
                if t >= 2:
                    pat = nextP()
                    k.matmul(pat[:, 0:128], kT[:, t, :], qT[:, t, :])
                    k.tt(AT[:], pat[:, 0:128], DM[di][:], ALU.mult)
                    k.tt(qin[:], qT[:, t, :], EBr[di][:], ALU.mult)
                    po = nextP()
                    k.matmul(po[:, 0:DV], AT[:], vb[:, t, :], start=True, stop=False)
                    k.matmul(po[:, 0:DV], qin[:], Sb[:], start=False, stop=True)
                k.ts(kk_[:], kb[:, t, :], ERc[:, di:di + 1], ALU.mult)
                pkv = nextP()
                k.matmul(pkv[:, 0:DV], kk_[:], vb[:, t, :])
                k.stt(S[:], S[:], dcol[:, di:di + 1], pkv[:, 0:DV], ALU.mult, ALU.add)
                k.copy(Sb[:], S[:], eng="act")
                if pend[0] is not None:
                    pend[0](); pend[0] = None
                if t >= 2:
                    if t not in have:
                        k.copy(oacc[:, t - 2, :], po[:, 0:DV]); have.add(t)
                    else:
                        par = rc[0] % 2; rc[0] += 1
                        onb = onL[par]
                        k.tt(onb[:], po[:, 0:DV], oacc[:, t - 2, :], ALU.add)

                        def readout(onb=onb, par=par, t=t, hl=hl):
                            st6_ = st6L[par]; mv_ = mvL[par]; rs_ = rsL[par]; nb_ = nbL[par]; obf_ = obfL[par]
                            k.op("dve", lambda e_, a=st6_, b_=onb: e_.bn_stats(a[:], b_[:]), [onb[:]], [st6_[:]])
                            k.op("dve", lambda e_, a=mv_, b_=st6_: e_.bn_aggr(a[:], b_[:]), [st6_[:]], [mv_[:]])
                            k.ts(rs_[:], mv_[:, 1:2], EPS, ALU.add)
                            k.act(rs_[:], rs_[:], AF.Sqrt)
                            k.recip(rs_[:], rs_[:])
                            k.stt(nb_[:], mv_[:, 0:1], -1.0, rs_[:], ALU.mult, ALU.mult)
                            k.act(onb[:], onb[:], AF.Identity, bias=nb_[:, 0:1], scale=rs_[:, 0:1])
                            k.tt(onb[:], onb[:], ng[:, hl % 2, :], ALU.mult, eng="pool")
                            k.tt(obf_[:], onb[:], gs[:, t - 2, :], ALU.mult, eng="pool")
                            ptt = nextPT()
                            k.transpose(ptt[:, 0:128], obf_[:, 0:128], ident[:])
                            k.transpose(ptt[:, 128:256], obf_[:, 128:256], ident[:])
                            lt = t - 2
                            ob = oT4[oc[0] % 4]; oc[0] += 1
                            k.copy(ob[:], ptt[:, 0:256].rearrange("p (c t) -> p c t", c=2))
                            k.dma(d_oT[hl * DV:(hl + 1) * DV, lt * 128:(lt + 1) * 128].rearrange("(c p) t -> p c t", p=128), ob[:])
                        pend[0] = readout
        if pend[0] is not None:
            pend[0](); pend[0] = None
    k.close_scope()


def phase_F(e, pfx, Fdim, last, segs, xsrc, osrc, ydst, d_cvec, d_wmod, d_bmod, d_n2g, d_fg, d_cw, d_cb, d_wout, d_wup, d_wdn):
    k = e.k; nextP = e.nextP; ones_bf = e.ones_bf
    KF = Fdim // 128
    NMAX = max(s[0] for s in segs); CMAX = NMAX + 2
    k.open_scope()
    cvec = k.sb(pfx + "cvec_s", [128, 8, 2]); scb = k.sb(pfx + "scb", [128, 8, 2], BF16)
    bmod = k.sb(pfx + "bmod_s", [128, 32]); modv = k.sb(pfx + "modv", [128, 32, 2])
    n2g = k.sb(pfx + "n2g_s", [128, 8]); fg = k.sb(pfx + "fg_s", [128, 8]); gm2 = k.sb(pfx + "gm2", [128, 8, 2])
    cw = k.sb(pfx + "cw_s", [128, 3, 44]); cb = k.sb(pfx + "cb_s", [128, 44])
    wout = k.sb(pfx + "wout_s", [128, KF, D], BF16)
    wdn = k.sb(pfx + "wdn_s", [128, NFF, D], BF16)
    oT = k.sb(pfx + "oT_s", [128, KF, CMAX], BF16)
    x1 = k.sb(pfx + "x1T", [128, 8, CMAX])
    sq = k.sb(pfx + "sq", [128, 8, CMAX], BF16)
    rstd = k.sb(pfx + "rstd", [128, CMAX]); tmp = k.sb(pfx + "tmp", [128, CMAX]); tmpb = k.sb(pfx + "tmpb", [128, CMAX])
    h2 = k.sb(pfx + "h2T", [128, 8, CMAX], BF16)
    wua = [k.sb(pfx + "wua%d" % i, [128, 8, 256], BF16) for i in range(2)]
    wub = [k.sb(pfx + "wub%d" % i, [128, 8, 256], BF16) for i in range(2)]
    u = [k.sb(pfx + "u%d" % i, [128, 2, CMAX]) for i in range(2)]
    vaL = [k.sb(pfx + "va%d" % i, [128, NMAX]) for i in range(2)]; vbL = [k.sb(pfx + "vb%d" % i, [128, NMAX]) for i in range(2)]
    saL = [k.sb(pfx + "sa%d" % i, [128, NMAX]) for i in range(2)]
    tT = k.sb(pfx + "tT", [128, NFF, NMAX], BF16)
    k.dma(cvec[:], d_cvec); k.dma(bmod[:], d_bmod); k.dma(n2g[:], d_n2g); k.dma(fg[:], d_fg)
    k.dma(cw[:], d_cw); k.dma(cb[:], d_cb)
    k.act(scb[:], cvec[:], AF.Silu)
    pm = nextP()
    for g in range(8):
        wb = wua[g % 2][:, :, :]
        wb2 = wub[g % 2][:, :, :]
        k.dma(wb, d_wmod.rearrange("(k p) c -> p k c", p=128)[:, :, g * 512:g * 512 + 256], q="pool")
        k.dma(wb2, d_wmod.rearrange("(k p) c -> p k c", p=128)[:, :, g * 512 + 256:(g + 1) * 512], q="pool")
        for c4 in range(4):
            cc = g * 4 + c4
            src = wb if c4 < 2 else wb2
            for kk in range(8):
                k.matmul(pm[:, cc * 2:cc * 2 + 2], src[:, kk, (c4 % 2) * 128:(c4 % 2 + 1) * 128], scb[:, kk, :],
                         start=(kk == 0), stop=(kk == 7))
    pmv = pm[:, 0:64].rearrange("p (c j) -> p c j", j=2)
    for j in range(2):
        k.tt(modv[:, :, j], pmv[:, :, j], bmod[:], ALU.add)
    for j in range(2):
        k.ts(gm2[:, :, j], modv[:, 16:24, j], 1.0, ALU.add)
        k.tt(gm2[:, :, j], gm2[:, :, j], n2g[:], ALU.mult)
    load_cast_rows(k, wout, d_wout, KF, D, split=4)
    load_cast_rows(k, wdn, d_wdn, NFF, D, split=4)
    wupv = d_wup.rearrange("(k p) c -> p k c", p=128)
    wctr = [0]
    k.dma(wua[0][:], wupv[:, :, 0:256], q="pool")
    k.dma(wub[0][:], wupv[:, :, DFF:DFF + 256], q="pool")

    for si, (n, j, kind, t0) in enumerate(segs):
        cols = n + 2
        tiles = col_tiles(cols)
        xs, T = xsrc[kind]; os_, oc0, _ = osrc[kind]
        lo = max(t0 - 1, 0); hi = min(t0 + n + 1, T)
        c_lo = lo - (t0 - 1); c_hi = hi - (t0 - 1)
        if c_lo > 0:
            k.memset(x1[:, :, 0:1], 0.0); k.memset(oT[:, :, 0:1], 0.0)
        if c_hi < cols:
            k.memset(x1[:, :, cols - 1:cols], 0.0); k.memset(oT[:, :, cols - 1:cols], 0.0)
        k.dma(x1[:, :, c_lo:c_hi], xs.rearrange("(k p) t -> p k t", p=128)[:, :, lo:hi])
        k.dma(oT[:, :, c_lo:c_hi], os_.rearrange("(k p) t -> p k t", p=128)[:, :, oc0 + lo:oc0 + hi])
        for fc in range(8):
            for (a, b) in tiles:
                p = nextP()
                for kk in range(KF):
                    k.matmul(p[:, 0:b - a], wout[:, kk, fc * 128:(fc + 1) * 128], oT[:, kk, a:b],
                             start=(kk == 0), stop=(kk == KF - 1))
                k.stt(x1[:, fc, a:b], p[:, 0:b - a], modv[:, 0 + fc, j:j + 1], x1[:, fc, a:b], ALU.mult, ALU.add)
        for kk in range(8):
            k.act(sq[:, kk, 0:cols], x1[:, kk, 0:cols], AF.Square)
        for (a, b) in tiles:
            p = nextP()
            for kk in range(8):
                k.matmul(p[:, 0:b - a], ones_bf[:], sq[:, kk, a:b], start=(kk == 0), stop=(kk == 7))
            k.ts(tmp[:, a:b], p[:, 0:b - a], 1.0 / D, ALU.mult, EPS, ALU.add)
        k.act(tmp[:, 0:cols], tmp[:, 0:cols], AF.Sqrt)
        k.recip(rstd[:, 0:cols], tmp[:, 0:cols])
        for kk in range(8):
            tb = tmp if kk % 2 == 0 else tmpb
            k.stt(tb[:, 0:cols], x1[:, kk, 0:cols], gm2[:, kk, j:j + 1], rstd[:, 0:cols], ALU.mult, ALU.mult)
            k.act(h2[:, kk, 0:cols], tb[:, 0:cols], AF.Identity, bias=modv[:, 8 + kk, j:j + 1], scale=1.0)
        if c_lo > 0:
            k.memset(h2[:, :, 0:1], 0.0)
        if c_hi < cols:
            k.memset(h2[:, :, cols - 1:cols], 0.0)
        for g in range(11):
            wa = wua[wctr[0] % 2]; wb_ = wub[wctr[0] % 2]
            wctr[0] += 1
            gn = g + 1 if g < 10 else (0 if si + 1 < len(segs) else None)
            if gn is not None:
                k.dma(wua[wctr[0] % 2][:], wupv[:, :, gn * 256:(gn + 1) * 256], q="pool")
                k.dma(wub[wctr[0] % 2][:], wupv[:, :, DFF + gn * 256:DFF + (gn + 1) * 256], q="pool")
            for c2 in range(2):
                c = g * 2 + c2
                ub = u[c % 2]
                for half, w in ((0, wa), (1, wb_)):
                    for (a, b) in tiles:
                        p = nextP()
                        for kk in range(8):
                            k.matmul(p[:, 0:b - a], w[:, kk, c2 * 128:(c2 + 1) * 128], h2[:, kk, a:b],
                                     start=(kk == 0), stop=(kk == 7))
                        k.copy(ub[:, half, a:b], p[:, 0:b - a])
                ca = c; cbi = NFF + c
                va = vaL[c % 2]; vb = vbL[c % 2]; sa = saL[c % 2]
                k.act(va[:, 0:n], ub[:, 0, 1:n + 1], AF.Identity, bias=cb[:, ca:ca + 1], scale=cw[:, 1, ca:ca + 1])
                k.stt(va[:, 0:n], ub[:, 0, 0:n], cw[:, 0, ca:ca + 1], va[:, 0:n], ALU.mult, ALU.add)
                k.stt(va[:, 0:n], ub[:, 0, 2:n + 2], cw[:, 2, ca:ca + 1], va[:, 0:n], ALU.mult, ALU.add)
                k.act(vb[:, 0:n], ub[:, 1, 1:n + 1], AF.Identity, bias=cb[:, cbi:cbi + 1], scale=cw[:, 1, cbi:cbi + 1])
                k.stt(vb[:, 0:n], ub[:, 1, 0:n], cw[:, 0, cbi:cbi + 1], vb[:, 0:n], ALU.mult, ALU.add)
                k.stt(vb[:, 0:n], ub[:, 1, 2:n + 2], cw[:, 2, cbi:cbi + 1], vb[:, 0:n], ALU.mult, ALU.add)
                k.act(sa[:, 0:n], va[:, 0:n], AF.Silu)
                k.tt(tT[:, c, 0:n], sa[:, 0:n], vb[:, 0:n], ALU.mult, eng="pool")
        for fc in range(8):
            p = nextP()
            for c in range(NFF):
                k.matmul(p[:, 0:n], wdn[:, c, fc * 128:(fc + 1) * 128], tT[:, c, 0:n], start=(c == 0), stop=(c == NFF - 1))
            k.stt(x1[:, fc, 1:n + 1], p[:, 0:n], modv[:, 24 + fc, j:j + 1], x1[:, fc, 1:n + 1], ALU.mult, ALU.add)
        if last:
            for kk in range(8):
                k.act(sq[:, kk, 0:n], x1[:, kk, 1:n + 1], AF.Square)
            p = nextP()
            for kk in range(8):
                k.matmul(p[:, 0:n], ones_bf[:], sq[:, kk, 0:n], start=(kk == 0), stop=(kk == 7))
            k.ts(tmp[:, 0:n], p[:, 0:n], 1.0 / D, ALU.mult, EPS, ALU.add)
            k.act(tmp[:, 0:n], tmp[:, 0:n], AF.Sqrt)
            k.recip(rstd[:, 0:n], tmp[:, 0:n])
            for kk in range(8):
                k.stt(x1[:, kk, 1:n + 1], x1[:, kk, 1:n + 1], fg[:, kk:kk + 1], rstd[:, 0:n], ALU.mult, ALU.mult)
        k.dma(ydst[kind].rearrange("(k p) t -> p k t", p=128)[:, :, t0:t0 + n], x1[:, :, 1:n + 1])
    k.close_scope()


def build_fused():
    k = KB(); nc = k.nc
    e = make_env(k)
    d_xT = k.dram("xT", [D, 4096]); d_xcT = k.dram("xcT", [D, 256]); d_cvec = k.dram("cvec", [128, 8, 2])
    cn = {nm: k.dram(nm, [128, 128]) for nm in ("mask_f", "mask_b", "dm_f", "dm_b", "row_f", "row_b")}
    d_pcols = k.dram("pcols", [128, 4]); d_cos = k.dram("cos", [128, 32, 128]); d_sin = k.dram("sin", [128, 32, 128])
    L = []
    for l in range(2):
        L.append(dict(wmodA=k.dram("wmodA%d" % l, [D, 2048]), bmodA=k.dram("bmodA%d" % l, [128, 16]), n1g=k.dram("n1g%d" % l, [128, 8]),
                      wmodB=k.dram("wmodB%d" % l, [D, 4096]), bmodB=k.dram("bmodB%d" % l, [128, 32]), n2g=k.dram("n2g%d" % l, [128, 8]),
                      cw=k.dram("convw%d" % l, [128, 3, 44]), cb=k.dram("convb%d" % l, [128, 44]),
                      wout=k.dram("wout%d" % l, [1024 * (l + 1), D]), wup=k.dram("wup%d" % l, [D, 2 * DFF]), wdn=k.dram("wdn%d" % l, [DFF, D])))
    d_fg = k.dram("fg", [128, 8])
    d_wna = k.dram("wna", [2, D, 768]); d_wgl = k.dram("wgl", [2, D, 768]); d_wa = k.dram("wa", [D, 32])
    d_waaug = k.dram("waaug", [17, 2, 256]); d_rpbT = k.dram("rpbT", [2, 128, 4, 15, 64]); d_colmask = k.dram("colmask", [128, 64])
    d_gng = k.dram("gng", [128, 128])
    d_win = k.dram("win", [8, D, 768]); d_dec = k.dram("dec", [128, 16]); d_ng = k.dram("ng", [128, 8, 256])
    d_y = k.dram("yT", [D, 4096], kind="ExternalOutput")
    o0T = nc.dram_tensor("o0T", [1024, NT * 128], BF16, kind="Internal").ap()
    x2T = nc.dram_tensor("x2T", [D, 4096], F32, kind="Internal").ap()
    xc2T = nc.dram_tensor("xc2T", [D, 256], F32, kind="Internal").ap()
    o1T = nc.dram_tensor("o1T", [2048, 4096], BF16, kind="Internal").ap()
    xcdump = nc.dram_tensor("xcdump", [D, 256], F32, kind="Internal").ap()

    k.open_scope()
    hT = k.sb("hT0", [128, 8, NT * 128], BF16)
    phase_hT(e, "a0", d_xT, d_xcT, d_cvec, L[0]["wmodA"], L[0]["bmodA"], L[0]["n1g"], hT)
    for hh in range(2):
        phase_NA(e, "na%d" % hh, hT, d_wna[hh], d_rpbT[hh], d_colmask, o0T, hh * 512)
        phase_GLA(e, "gl%d" % hh, hT, d_wgl[hh], d_wa, d_waaug, d_gng, cn["mask_f"], cn["mask_b"], o0T, hh * 512 + 256,
                  [(0, hh * 2, 0), (1, hh * 2 + 1, 128)])
    k.close_scope()
    segs0 = [(512, 0, "lat", s * 512) for s in range(8)] + [(256, 1, "ctx", 0)]
    phase_F(e, "f0", 1024, False, segs0,
            {"lat": (d_xT, 4096), "ctx": (d_xcT, 256)}, {"lat": (o0T, 256, 4096), "ctx": (o0T, 0, 256)},
            {"lat": x2T, "ctx": xc2T}, d_cvec, L[0]["wmodB"], L[0]["bmodB"], L[0]["n2g"], d_fg, L[0]["cw"], L[0]["cb"],
            L[0]["wout"], L[0]["wup"], L[0]["wdn"])
    k.open_scope()
    hT = k.sb("hT1", [128, 8, NT * 128], BF16)
    phase_hT(e, "a1", x2T, xc2T, d_cvec, L[1]["wmodA"], L[1]["bmodA"], L[1]["n1g"], hT)
    phase_RET(e, "rt", hT, d_win, d_dec, d_ng, d_cos, d_sin, cn, d_pcols, o1T, 8)
    k.close_scope()
    segs1 = [(512, 0, "lat", s * 512) for s in range(8)]
    phase_F(e, "f1", 2048, True, segs1,
            {"lat": (x2T, 4096)}, {"lat": (o1T, 0, 4096)}, {"lat": d_y},
            d_cvec, L[1]["wmodB"], L[1]["bmodB"], L[1]["n2g"], d_fg, L[1]["cw"], L[1]["cb"],
            L[1]["wout"], L[1]["wup"], L[1]["wdn"])
    ncc = k.finish([d_y])
    return ncc, k


_PROG = []


def _maps(inp):
    B = 4
    shared = {}
    shared["ident"] = np.eye(128, dtype=np.float32)
    for kk in ("mask_f", "mask_b", "dm_f", "dm_b", "row_f", "row_b", "pcols"):
        shared[kk] = _C[kk]
    shared["cos"] = np.ascontiguousarray(np.concatenate([_COS, _COS], axis=2)); shared["sin"] = np.ascontiguousarray(np.concatenate([_SIN, _SIN], axis=2))
    for l in range(2):
        shared["wmodA%d" % l] = np.ascontiguousarray(inp["w_mod"][l][:, 0:2048])
        shared["bmodA%d" % l] = pk(inp["b_mod"][l][0:2048])
        shared["n1g%d" % l] = pk(inp["norm1_g"][l])
        shared["wmodB%d" % l] = np.ascontiguousarray(inp["w_mod"][l][:, 2048:6144])
        shared["bmodB%d" % l] = pk(inp["b_mod"][l][2048:6144])
        shared["n2g%d" % l] = pk(inp["norm2_g"][l])
        cw = inp["ffn_conv_w"][l]
        shared["convw%d" % l] = np.ascontiguousarray(np.stack([pk(cw[i]) for i in range(3)], axis=1))
        shared["convb%d" % l] = pk(inp["ffn_conv_b"][l])
        shared["wup%d" % l] = inp["ffn_w_up"][l]; shared["wdn%d" % l] = inp["ffn_w_down"][l]
    perm = np.concatenate([np.arange(0, 256), np.arange(512, 768), np.arange(256, 512), np.arange(768, 1024)])
    shared["wout0"] = np.ascontiguousarray(inp["na_gla_w_out"][0][perm])
    shared["wout1"] = np.ascontiguousarray(inp["ret_w_out"][0])
    shared["fg"] = pk(inp["final_norm_g"])
    w = inp["na_gla_w_in"][0]
    wna = []; wgl = []; rp = []
    for hh in range(2):
        nh = [hh * 4 + i for i in range(4)]; gh = [hh * 2 + i for i in range(2)]
        wna.append(np.concatenate([w[:, h * 64:(h + 1) * 64] for h in nh] + [w[:, 512 + h * 64:512 + (h + 1) * 64] for h in nh] +
                                  [w[:, 1024 + h * 64:1024 + (h + 1) * 64] for h in nh], axis=1))
        wgl.append(np.concatenate([w[:, 1536 + h * 64:1536 + (h + 1) * 64] for h in gh] + [w[:, 1792 + h * 64:1792 + (h + 1) * 64] for h in gh] +
                                  [w[:, 2560 + h * 128:2560 + (h + 1) * 128] for h in gh] + [w[:, 2048 + h * 128:2048 + (h + 1) * 128] for h in gh], axis=1))
        r_, cm = _na_tables(inp["na_rpb"][0][nh]); rp.append(r_)
    shared["wna"] = np.ascontiguousarray(np.stack(wna)); shared["wgl"] = np.ascontiguousarray(np.stack(wgl))
    shared["rpbT"] = np.ascontiguousarray(np.stack(rp)); shared["colmask"] = cm
    shared["wa"] = np.ascontiguousarray(w[:, 3072:3104])
    wa = np.zeros((17, 2, 256), np.float32)
    wa[0:16, 0] = inp["gla_w_a_fwd"][0]; wa[16, 0] = inp["gla_b_a_fwd"][0]
    wa[0:16, 1] = inp["gla_w_a_bwd"][0]; wa[16, 1] = inp["gla_b_a_bwd"][0]
    shared["waaug"] = wa
    shared["gng"] = np.ascontiguousarray(np.broadcast_to(inp["gla_norm_g"][0], (128, 128)).astype(np.float32))
    wr = inp["ret_w_in"][0]
    shared["win"] = np.ascontiguousarray(np.stack([np.concatenate([
        wr[:, h * 128:(h + 1) * 128], wr[:, 1024 + h * 128:1024 + (h + 1) * 128],
        wr[:, 4096 + h * 256:4096 + (h + 1) * 256], wr[:, 2048 + h * 256:2048 + (h + 1) * 256]], axis=1) for h in range(8)]))
    dec = np.concatenate([inp["ret_decay_fwd"][0], inp["ret_decay_bwd"][0]])
    shared["dec"] = np.ascontiguousarray(np.broadcast_to(dec, (128, 16)).astype(np.float32))
    shared["ng"] = np.ascontiguousarray(np.broadcast_to(inp["ret_norm_g"][0], (128, 8, 256)).astype(np.float32))
    maps = []
    for core in range(8):
        b = core // 2
        m = dict(shared)
        m["xT"] = np.ascontiguousarray(inp["x"][b].T); m["xcT"] = np.ascontiguousarray(inp["ctx"][b].T)
        m["cvec"] = np.ascontiguousarray(np.stack([pk(inp["c"][b]), pk(inp["c_ctx"])], axis=-1).astype(np.float32))
        maps.append(m)
    return maps


def kernel(**inp):
    inp = {k_: np.asarray(v) for k_, v in inp.items()}
    if not _PROG:
        _PROG.append(build_fused()[0])
    res = run_bass_kernel_spmd(_PROG[0], _maps(inp), core_ids=list(range(8))).results
    out = np.empty((4, 4096, 1024), np.float32)
    for b in range(4):
        out[b, 0:2048] = res[2 * b]["yT"][:, 0:2048].T
        out[b, 2048:4096] = res[2 * b + 1]["yT"][:, 2048:4096].T
    return out
```

```python
import os
import ml_dtypes
from concourse.bass_utils import run_bass_kernel_spmd

from contextlib import ExitStack
import numpy as np
import concourse.bass as bass
import concourse.mybir as mybir

F32 = mybir.dt.float32
BF16 = mybir.dt.bfloat16
AF = mybir.ActivationFunctionType
ALU = mybir.AluOpType
AX = mybir.AxisListType

ENGS = ("pe", "act", "dve", "pool", "sp")
NDSEM = 12


def _region(ap):
    t = ap.tensor
    name = t.name
    dims = list(ap.ap)
    off = int(ap.offset)
    sp = str(ap.space) if hasattr(ap, "space") else ""
    if "DRAM" in sp.upper() or type(t).__name__.startswith("DRam"):
        ext = sum((int(c) - 1) * abs(int(s)) for s, c in dims)
        return (name, 0, 1, off, off + ext + 1)
    if type(t).__name__.startswith("PSum"):
        return (name, 0, 128, 0, 1 << 40)
    pstep, pcnt = int(dims[0][0]), int(dims[0][1])
    if pstep == 0:
        pstep = 1 << 40
    p0 = off // pstep
    f0 = off % pstep
    ext = sum((int(c) - 1) * abs(int(s)) for s, c in dims[1:])
    return (name, p0, p0 + pcnt, f0, f0 + ext + 1)


def _overlap(a, b):
    return a[1] < b[2] and b[1] < a[2] and a[3] < b[4] and b[3] < a[4]


def _covers(a, b):
    return a[1] <= b[1] and a[2] >= b[2] and a[3] <= b[3] and a[4] >= b[4]


class KB:
    def __init__(self):
        self.nc = bass.Bass("TRN2", target_bir_lowering=False)
        self.es = ExitStack()
        self.ops = []
        self.recs = {}
        self.n_alloc = 0
        self.fence = None
        self.fenced = set()
        self.stack = [self.es]

    def sb(self, name, shape, dt=F32):
        return self.stack[-1].enter_context(self.nc.sbuf_tensor(name, list(shape), dt))

    def barrier(self):
        last = {}
        f = set()
        for i, o in enumerate(self.ops):
            if o["dma"]:
                f.add(i)
            else:
                last[o["eng"]] = i
        f.update(last.values())
        if self.fence is not None:
            f = {i for i in f if i > self.fence_at or not self.ops[i]["dma"]}
        self.fence = f
        self.fence_at = len(self.ops)
        self.fenced = set()

    def open_scope(self):
        self.stack.append(ExitStack())

    def close_scope(self):
        self.barrier()
        self.stack.pop().close()

    def ps(self, name, shape, dt=F32):
        return self.es.enter_context(self.nc.psum_tensor(name, list(shape), dt))

    def dram(self, name, shape, dt=F32, kind="ExternalInput"):
        return self.nc.dram_tensor(name, list(shape), dt, kind=kind).ap()

    def op(self, eng, fn, reads, writes, dma=False):
        idx = len(self.ops)
        deps = set()
        rr = [_region(a) for a in reads if a is not None and hasattr(a, "tensor")]
        ww = [_region(a) for a in writes if a is not None and hasattr(a, "tensor")]
        for r in rr:
            for (g, oi, isw) in self.recs.get(r[0], ()):
                if isw and _overlap(r, g):
                    deps.add(oi)
        for w in ww:
            for (g, oi, isw) in self.recs.get(w[0], ()):
                if _overlap(w, g):
                    deps.add(oi)
        for w in ww:
            lst = self.recs.setdefault(w[0], [])
            lst[:] = [x for x in lst if not _covers(w, x[0])]
            lst.append((w, idx, True))
        for r in rr:
            lst = self.recs.setdefault(r[0], [])
            lst[:] = [x for x in lst if not ((not x[2]) and x[1] < idx and self.ops[x[1]]["eng"] == eng
                                             and not self.ops[x[1]]["dma"] and not dma and _covers(r, x[0]))]
            lst.append((r, idx, False))
        if self.fence is not None and eng not in self.fenced:
            deps.update(self.fence)
            self.fenced.add(eng)
        deps.discard(idx)
        self.ops.append(dict(eng=eng, fn=fn, deps=deps, dma=dma, rr=rr, ww=ww))
        return idx

    def dma(self, out, in_, q="sp"):
        return self.op(q, lambda e: e.dma_start(out=out, in_=in_), [in_], [out], dma=True)

    def matmul(self, out, lhsT, rhs, start=True, stop=True):
        return self.op("pe", lambda e: e.matmul(out, lhsT, rhs, start=start, stop=stop), [lhsT, rhs], [out])

    def transpose(self, out, in_, ident):
        return self.op("pe", lambda e: e.transpose(out, in_, ident), [in_, ident], [out])

    def act(self, out, in_, func, bias=None, scale=None, accum_out=None, eng="act"):
        kw = {}
        if bias is not None:
            kw["bias"] = bias
        if scale is not None:
            kw["scale"] = scale
        if accum_out is not None:
            kw["accum_out"] = accum_out
        return self.op(eng, lambda e: e.activation(out, in_, func, **kw), [in_, bias, scale], [out, accum_out])

    def tt(self, out, in0, in1, op, eng="dve"):
        return self.op(eng, lambda e: e.tensor_tensor(out, in0, in1, op), [in0, in1], [out])

    def ts(self, out, in0, s1, op0, s2=None, op1=None, accum_out=None, eng="dve"):
        def f(e):
            kw = {}
            if accum_out is not None:
                kw["accum_out"] = accum_out
            if op1 is None:
                return e.tensor_scalar(out, in0, s1, None, op0, **kw)
            return e.tensor_scalar(out, in0, s1, s2, op0, op1, **kw)
        return self.op(eng, f, [in0, s1, s2], [out, accum_out])

    def stt(self, out, in0, scalar, in1, op0, op1, eng="dve"):
        return self.op(eng, lambda e: e.scalar_tensor_tensor(out, in0, scalar, in1, op0, op1), [in0, scalar, in1], [out])

    def copy(self, out, in_, eng="dve"):
        if eng == "act":
            return self.op("act", lambda e: e.copy(out, in_), [in_], [out])
        return self.op(eng, lambda e: e.tensor_copy(out, in_), [in_], [out])

    def memset(self, ap, val, eng="dve"):
        return self.op(eng, lambda e: e.memset(ap, val), [], [ap])

    def recip(self, out, in_):
        return self.op("dve", lambda e: e.reciprocal(out, in_), [in_], [out])

    def reduce(self, out, in_, op=ALU.add, axis=AX.X, eng="dve"):
        return self.op(eng, lambda e: e.tensor_reduce(out, in_, axis, op), [in_], [out])

    def finish(self, out_aps):
        nc = self.nc
        ops = self.ops
        out_names = {a.tensor.name for a in out_aps}
        final_deps = set()
        for i, o in enumerate(ops):
            if o["dma"] and any(w[0] in out_names for w in o["ww"]):
                final_deps.add(i)
        ops.append(dict(eng="sp", fn=None, deps=final_deps, dma=False, rr=[], ww=[]))
        needs_sig = [False] * len(ops)
        for o in ops:
            for d in o["deps"]:
                if ops[d]["dma"]:
                    continue
                if ops[d]["eng"] == o["eng"] and not o["dma"] and o["eng"] == "pe":
                    continue
                needs_sig[d] = True
        sems = {e: self.es.enter_context(nc.semaphore("s_" + e)) for e in ENGS}
        dsems = {e: [self.es.enter_context(nc.semaphore("d_%s_%d" % (e, i))) for i in range(NDSEM)]
                 for e in ("sp", "act", "pool")}
        cnt = {e: 0 for e in ENGS}
        dcnt = {e: 0 for e in dsems}
        sig = [None] * len(ops)
        prevdma = [None] * len(ops)
        for i, o in enumerate(ops):
            if o["dma"]:
                q = o["eng"]
                n = dcnt[q]
                dcnt[q] += 1
                s = dsems[q][n % NDSEM]
                sig[i] = (s, 16 * (n // NDSEM + 1))
                if n >= NDSEM:
                    prevdma[i] = (s, 16 * (n // NDSEM))
            elif needs_sig[i]:
                cnt[o["eng"]] += 1
                sig[i] = (sems[o["eng"]], cnt[o["eng"]])
        per_eng = {e: [] for e in ENGS}
        for i, o in enumerate(ops):
            per_eng[o["eng"]].append(i)
        self.stats = {e: len(per_eng[e]) for e in ENGS}
        self.stats["sig"] = dict(cnt)

        def emit(ename):
            def body(eng):
                seen = {}
                for i in per_eng[ename]:
                    o = ops[i]
                    waits = {}
                    for d in o["deps"]:
                        po = ops[d]
                        if (not po["dma"]) and po["eng"] == ename and ename == "pe" and not o["dma"]:
                            continue
                        s, v = sig[d]
                        key = id(s)
                        if waits.get(key, (None, 0))[1] < v:
                            waits[key] = (s, v)
                    if prevdma[i] is not None:
                        s, v = prevdma[i]
                        key = id(s)
                        if waits.get(key, (None, 0))[1] < v:
                            waits[key] = (s, v)
                    for key, (s, v) in waits.items():
                        if seen.get(key, 0) >= v:
                            continue
                        eng.wait_ge(s, v)
                        seen[key] = v
                    if o["fn"] is None:
                        continue
                    ins = o["fn"](eng)
                    if sig[i] is not None:
                        s, v = sig[i]
                        ins.then_inc(s, 16 if o["dma"] else 1)
            return body

        with nc.Block() as block:
            block.tensor(emit("pe"))
            block.scalar(emit("act"))
            block.vector(emit("dve"))
            block.gpsimd(emit("pool"))
            block.sync(emit("sp"))
        self.es.close()
        return nc

D = 1024; DFF = 2816; NFF = 22; EPS = 1e-6; NT = 34


def col_tiles(cols):
    nt = (cols + 511) // 512
    base = cols // nt
    res = []; s = 0
    for i in range(nt):
        e = s + base + (1 if i < cols % nt else 0)
        res.append((s, e)); s = e
    return res


def load_cast_rows(k, dst, src, nk, width, q="pool", split=1):
    v = src.rearrange("(k p) c -> p k c", p=128)
    step = max(1, nk // split)
    for a in range(0, nk, step):
        b = min(nk, a + step)
        k.dma(dst[:, a:b, :], v[:, a:b, :], q=q)


def build_F(layer_has_ctx, Fdim, last, segs):
    k = KB()
    KF = Fdim // 128
    NMAX = max(n for n, _ in segs); CMAX = NMAX + 2
    d_x = [k.dram("xT_%d" % i, [D, n + 2]) for i, (n, _) in enumerate(segs)]
    d_o = [k.dram("oT_%d" % i, [Fdim, n + 2], BF16) for i, (n, _) in enumerate(segs)]
    d_hm = [k.dram("hm_%d" % i, [128, 2]) for i, (n, _) in enumerate(segs)]
    d_y = [k.dram("yT_%d" % i, [D, n], kind="ExternalOutput") for i, (n, _) in enumerate(segs)]
    d_cvec = k.dram("cvec", [128, 8, 2])
    d_wmod = k.dram("wmod", [D, 4096]); d_bmod = k.dram("bmod", [128, 32])
    d_n2g = k.dram("n2g", [128, 8]); d_fg = k.dram("fg", [128, 8])
    d_cw = k.dram("convw", [128, 3, 44]); d_cb = k.dram("convb", [128, 44])
    d_wout = k.dram("wout", [Fdim, D]); d_wup = k.dram("wup", [D, 2 * DFF]); d_wdn = k.dram("wdn", [DFF, D])
    ones_bf = k.sb("ones_bf", [128, 128], BF16)
    cvec = k.sb("cvec_s", [128, 8, 2]); scb = k.sb("scb", [128, 8, 2], BF16)
    bmod = k.sb("bmod_s", [128, 32]); modv = k.sb("modv", [128, 32, 2])
    n2g = k.sb("n2g_s", [128, 8]); fg = k.sb("fg_s", [128, 8]); gm2 = k.sb("gm2", [128, 8, 2])
    cw = k.sb("cw_s", [128, 3, 44]); cb = k.sb("cb_s", [128, 44])
    wmb = [k.sb("wmb%d" % i, [128, 8, 512], BF16) for i in range(2)]
    wout = k.sb("wout_s", [128, KF, D], BF16)
    wdn = k.sb("wdn_s", [128, NFF, D], BF16)
    oT = k.sb("oT_s", [128, KF, CMAX], BF16)
    x1 = k.sb("x1T", [128, 8, CMAX])
    sq = k.sb("sq", [128, 8, CMAX], BF16)
    rstd = k.sb("rstd", [128, CMAX]); tmp = k.sb("tmp", [128, CMAX])
    h2 = k.sb("h2T", [128, 8, CMAX], BF16)
    hm = k.sb("hm_s", [128, 2])
    wua = [k.sb("wua%d" % i, [128, 8, 256], BF16) for i in range(2)]
    wub = [k.sb("wub%d" % i, [128, 8, 256], BF16) for i in range(2)]
    u = [k.sb("u%d" % i, [128, 2, CMAX]) for i in range(2)]
    va = k.sb("va", [128, NMAX]); vb = k.sb("vb", [128, NMAX]); sa = k.sb("sa", [128, NMAX])
    tT = k.sb("tT", [128, NFF, NMAX], BF16)
    P = [k.ps("P%d" % i, [128, 512]) for i in range(8)]
    pctr = [0]

    def nextP():
        p = P[pctr[0] % 8]; pctr[0] += 1
        return p

    k.memset(ones_bf[:], 1.0)
    k.dma(cvec[:], d_cvec); k.dma(bmod[:], d_bmod); k.dma(n2g[:], d_n2g); k.dma(fg[:], d_fg)
    k.dma(cw[:], d_cw); k.dma(cb[:], d_cb)
    k.act(scb[:], cvec[:], AF.Silu)
    pm = nextP()
    for g in range(8):
        wb = wmb[g % 2]
        k.dma(wb[:], d_wmod.rearrange("(k p) c -> p k c", p=128)[:, :, g * 512:(g + 1) * 512], q="pool")
        for c4 in range(4):
            cc = g * 4 + c4
            for kk in range(8):
                k.matmul(pm[:, cc * 2:cc * 2 + 2], wb[:, kk, c4 * 128:(c4 + 1) * 128], scb[:, kk, :],
                         start=(kk == 0), stop=(kk == 7))
    pmv = pm[:, 0:64].rearrange("p (c j) -> p c j", j=2)
    for j in range(2):
        k.tt(modv[:, :, j], pmv[:, :, j], bmod[:], ALU.add)
    for j in range(2):
        k.ts(gm2[:, :, j], modv[:, 16:24, j], 1.0, ALU.add)
        k.tt(gm2[:, :, j], gm2[:, :, j], n2g[:], ALU.mult)
    load_cast_rows(k, wout, d_wout, KF, D, split=4)
    load_cast_rows(k, wdn, d_wdn, NFF, D, split=4)
    wupv = d_wup.rearrange("(k p) c -> p k c", p=128)

    for si, (n, j) in enumerate(segs):
        cols = n + 2
        tiles = col_tiles(cols)
        k.dma(x1[:, :, 0:cols], d_x[si].rearrange("(k p) t -> p k t", p=128))
        k.dma(oT[:, :, 0:cols], d_o[si].rearrange("(k p) t -> p k t", p=128))
        k.dma(hm[:], d_hm[si])
        for fc in range(8):
            for (a, b) in tiles:
                p = nextP()
                for kk in range(KF):
                    k.matmul(p[:, 0:b - a], wout[:, kk, fc * 128:(fc + 1) * 128], oT[:, kk, a:b],
                             start=(kk == 0), stop=(kk == KF - 1))
                k.stt(x1[:, fc, a:b], p[:, 0:b - a], modv[:, 0 + fc, j:j + 1], x1[:, fc, a:b], ALU.mult, ALU.add)
        for kk in range(8):
            k.act(sq[:, kk, 0:cols], x1[:, kk, 0:cols], AF.Square)
        for (a, b) in tiles:
            p = nextP()
            for kk in range(8):
                k.matmul(p[:, 0:b - a], ones_bf[:], sq[:, kk, a:b], start=(kk == 0), stop=(kk == 7))
            k.act(tmp[:, a:b], p[:, 0:b - a], AF.Sqrt, bias=EPSB[0], scale=1.0 / D)
        k.recip(rstd[:, 0:cols], tmp[:, 0:cols])
        for kk in range(8):
            k.stt(tmp[:, 0:cols], x1[:, kk, 0:cols], gm2[:, kk, j:j + 1], rstd[:, 0:cols], ALU.mult, ALU.mult)
            k.act(h2[:, kk, 0:cols], tmp[:, 0:cols], AF.Identity, bias=modv[:, 8 + kk, j:j + 1], scale=1.0)
        k.ts(h2[:, :, 0:1], h2[:, :, 0:1], hm[:, 0:1], ALU.mult)
        k.ts(h2[:, :, cols - 1:cols], h2[:, :, cols - 1:cols], hm[:, 1:2], ALU.mult)
        for g in range(11):
            wa = wua[g % 2]; wb_ = wub[g % 2]
            k.dma(wa[:], wupv[:, :, g * 256:(g + 1) * 256], q="pool")
            k.dma(wb_[:], wupv[:, :, DFF + g * 256:DFF + (g + 1) * 256], q="pool")
            for c2 in range(2):
                c = g * 2 + c2
                ub = u[c % 2]
                for half, w in ((0, wa), (1, wb_)):
                    for (a, b) in tiles:
                        p = nextP()
                        for kk in range(8):
                            k.matmul(p[:, 0:b - a], w[:, kk, c2 * 128:(c2 + 1) * 128], h2[:, kk, a:b],
                                     start=(kk == 0), stop=(kk == 7))
                        k.copy(ub[:, half, a:b], p[:, 0:b - a], eng="act")
                ca = c; cbi = NFF + c
                k.act(va[:, 0:n], ub[:, 0, 1:n + 1], AF.Identity, bias=cb[:, ca:ca + 1], scale=cw[:, 1, ca:ca + 1])
                k.stt(va[:, 0:n], ub[:, 0, 0:n], cw[:, 0, ca:ca + 1], va[:, 0:n], ALU.mult, ALU.add)
                k.stt(va[:, 0:n], ub[:, 0, 2:n + 2], cw[:, 2, ca:ca + 1], va[:, 0:n], ALU.mult, ALU.add)
                k.act(vb[:, 0:n], ub[:, 1, 1:n + 1], AF.Identity, bias=cb[:, cbi:cbi + 1], scale=cw[:, 1, cbi:cbi + 1])
                k.stt(vb[:, 0:n], ub[:, 1, 0:n], cw[:, 0, cbi:cbi + 1], vb[:, 0:n], ALU.mult, ALU.add)
                k.stt(vb[:, 0:n], ub[:, 1, 2:n + 2], cw[:, 2, cbi:cbi + 1], vb[:, 0:n], ALU.mult, ALU.add)
                k.act(sa[:, 0:n], va[:, 0:n], AF.Silu)
                k.tt(tT[:, c, 0:n], sa[:, 0:n], vb[:, 0:n], ALU.mult)
        for fc in range(8):
            p = nextP()
            for c in range(NFF):
                k.matmul(p[:, 0:n], wdn[:, c, fc * 128:(fc + 1) * 128], tT[:, c, 0:n], start=(c == 0), stop=(c == NFF - 1))
            k.stt(x1[:, fc, 1:n + 1], p[:, 0:n], modv[:, 24 + fc, j:j + 1], x1[:, fc, 1:n + 1], ALU.mult, ALU.add)
        if last:
            for kk in range(8):
                k.act(sq[:, kk, 0:n], x1[:, kk, 1:n + 1], AF.Square)
            p = nextP()
            for kk in range(8):
                k.matmul(p[:, 0:n], ones_bf[:], sq[:, kk, 0:n], start=(kk == 0), stop=(kk == 7))
            k.act(tmp[:, 0:n], p[:, 0:n], AF.Sqrt, bias=EPSB[0], scale=1.0 / D)
            k.recip(rstd[:, 0:n], tmp[:, 0:n])
            for kk in range(8):
                k.stt(x1[:, kk, 1:n + 1], x1[:, kk, 1:n + 1], fg[:, kk:kk + 1], rstd[:, 0:n], ALU.mult, ALU.mult)
        k.dma(d_y[si].rearrange("(k p) t -> p k t", p=128), x1[:, :, 1:n + 1])
    nc = k.finish(d_y)
    return nc, k

EPSB = [EPS]


NT = 34


def emit_mod(k, nextP, d_cvec, d_wmod, d_bmod, ncols, wmb, name="m"):
    ncc = ncols // 128
    cvec = k.sb(name + "cvec", [128, 8, 2]); scb = k.sb(name + "scb", [128, 8, 2], BF16)
    bmod = k.sb(name + "bmod", [128, ncc]); modv = k.sb(name + "modv", [128, ncc, 2])
    k.dma(cvec[:], d_cvec); k.dma(bmod[:], d_bmod)
    k.act(scb[:], cvec[:], AF.Silu)
    pm = nextP()
    for g in range(ncols // 512):
        wb = wmb[g % len(wmb)]
        k.dma(wb[:], d_wmod.rearrange("(k p) c -> p k c", p=128)[:, :, g * 512:(g + 1) * 512], q="pool")
        for c4 in range(4):
            cc = g * 4 + c4
            for kk in range(8):
                k.matmul(pm[:, cc * 2:cc * 2 + 2], wb[:, kk, c4 * 128:(c4 + 1) * 128], scb[:, kk, :],
                         start=(kk == 0), stop=(kk == 7))
    pmv = pm[:, 0:2 * ncc].rearrange("p (c j) -> p c j", j=2)
    for j in range(2):
        k.tt(modv[:, :, j], pmv[:, :, j], bmod[:], ALU.add)
    return modv


def emit_hT(k, nextP, hT, d_xT, d_xcT, modv, n1g, ones_bf, xt, sq, rstd, tmp, name=""):
    gm1 = k.sb(name + "gm1", [128, 8, 2]); tmp2 = k.sb(name + "tmp2", [128, 256])
    for j in range(2):
        k.ts(gm1[:, :, j], modv[:, 8:16, j], 1.0, ALU.add)
        k.tt(gm1[:, :, j], gm1[:, :, j], n1g[:], ALU.mult)
    W = 256
    jobs = [(d_xcT, 0, 0, 1)] + [(d_xT, i * W, 256 + i * W, 0) for i in range(4096 // W)]
    for ji, (src, c0, h0, j) in enumerate(jobs):
        x = xt[ji % len(xt)]
        k.dma(x[:], src.rearrange("(k p) t -> p k t", p=128)[:, :, c0:c0 + W])
        for kk in range(8):
            k.act(sq[:, kk, :], x[:, kk, :], AF.Square)
        p = nextP()
        for kk in range(8):
            k.matmul(p[:, 0:W], ones_bf[:], sq[:, kk, :], start=(kk == 0), stop=(kk == 7))
        k.ts(tmp[:, 0:W], p[:, 0:W], 1.0 / D, ALU.mult, EPS, ALU.add)
        k.act(tmp[:, 0:W], tmp[:, 0:W], AF.Sqrt)
        k.recip(rstd[:, 0:W], tmp[:, 0:W])
        for kk in range(8):
            tb = tmp if kk % 2 == 0 else sq[:, 0:4, :].bitcast(F32).rearrange("p a w -> p (a w)") if False else (tmp if kk % 2 == 0 else tmp2)
            k.stt(tb[:, 0:W], x[:, kk, :], gm1[:, kk, j:j + 1], rstd[:, 0:W], ALU.mult, ALU.mult)
            k.act(hT[:, kk, h0:h0 + W], tb[:, 0:W], AF.Identity, bias=modv[:, kk, j:j + 1], scale=1.0)


def rope(k, out_bf, x, cos_t, sin_t, tA, tB):
    xv = x.rearrange("p (a h f) -> p a h f", a=2, h=2)
    ov = out_bf.rearrange("p (a h f) -> p a h f", a=2, h=2)
    Av = tA.rearrange("p (a h f) -> p a h f", a=2, h=2)
    Bv = tB.rearrange("p (a h f) -> p a h f", a=2, h=2)
    cb = cos_t.rearrange("p (a f) -> p a f", a=2).unsqueeze(2).to_broadcast([128, 2, 2, 32])
    sv = sin_t.rearrange("p (a f) -> p a f", a=2)
    k.tt(Av, xv, cb, ALU.mult)
    k.tt(Bv[:, :, 0, :], xv[:, :, 1, :], sv, ALU.mult)
    k.tt(Bv[:, :, 1, :], xv[:, :, 0, :], sv, ALU.mult)
    k.tt(ov[:, :, 0, :], Av[:, :, 0, :], Bv[:, :, 0, :], ALU.subtract)
    k.tt(ov[:, :, 1, :], Av[:, :, 1, :], Bv[:, :, 1, :], ALU.add)


def make_consts():
    j = np.arange(128)[:, None].astype(np.float32); i = np.arange(128)[None, :].astype(np.float32)
    c = {}
    c["mask_f"] = (i >= j).astype(np.float32)
    c["mask_b"] = (j >= i).astype(np.float32)
    c["dm_f"] = np.maximum(i - j, 0.0); c["dm_b"] = np.maximum(j - i, 0.0)
    c["row_f"] = np.broadcast_to(i + 1.0, (128, 128)).copy()
    c["row_b"] = np.broadcast_to(128.0 - i, (128, 128)).copy()
    pc = np.zeros((128, 4), np.float32)
    pc[:, 0] = 127.0 - np.arange(128)
    pc[:, 1] = np.arange(128)
    pc[:, 2] = 128.0
    c["pcols"] = pc
    return {kk: np.ascontiguousarray(v.astype(np.float32)) for kk, v in c.items()}


def rope_tables():
    pos = np.arange(4096)
    row = (pos // 64).astype(np.float32); col = (pos % 64).astype(np.float32)
    inv = (10000.0 ** (-np.arange(0, 64, 2, dtype=np.float32) / 64.0)).astype(np.float32)
    ang = np.concatenate([row[:, None] * inv, col[:, None] * inv], axis=-1).astype(np.float32)
    cos = np.cos(ang).astype(np.float32); sin = np.sin(ang).astype(np.float32)
    cs = np.ascontiguousarray(cos.reshape(32, 128, 64).transpose(1, 0, 2))
    sn = np.ascontiguousarray(sin.reshape(32, 128, 64).transpose(1, 0, 2))
    return cs, sn


def build_M1():
    k = KB()
    DK = 128; DV = 256; NH = 4
    d_xT = k.dram("xT", [D, 4096]); d_xcT = k.dram("xcT", [D, 256])
    d_cvec = k.dram("cvec", [128, 8, 2]); d_wmod = k.dram("wmod", [D, 2048]); d_bmod = k.dram("bmod", [128, 16])
    d_n1g = k.dram("n1g", [128, 8])
    d_win = k.dram("win", [NH, D, 768])
    d_dec = k.dram("dec", [128, 8])
    d_ng = k.dram("ng", [128, NH, DV])
    d_cos = k.dram("cos", [128, 32, 64]); d_sin = k.dram("sin", [128, 32, 64])
    cn = {nm: k.dram(nm, [128, 128]) for nm in ("mask_f", "mask_b", "dm_f", "dm_b", "row_f", "row_b")}
    d_pcols = k.dram("pcols", [128, 4]); d_ident = k.dram("ident", [128, 128])
    d_oT = k.dram("oT", [NH * DV, 4096], BF16, kind="ExternalOutput")

    P = [k.ps("P%d" % i, [128, 512]) for i in range(6)]
    PT = [k.ps("PT%d" % i, [128, 1024], BF16) for i in range(2)]
    pc = [0, 0]

    def nextP():
        p = P[pc[0] % 6]; pc[0] += 1; return p

    def nextPT():
        p = PT[pc[1] % 2]; pc[1] += 1; return p

    ones_bf = k.sb("ones_bf", [128, 128], BF16); k.memset(ones_bf[:], 1.0)
    identf = k.sb("identf", [128, 128]); ident = k.sb("ident_s", [128, 128], BF16)
    k.dma(identf[:], d_ident); k.copy(ident[:], identf[:])
    n1g = k.sb("n1g_s", [128, 8]); k.dma(n1g[:], d_n1g)
    whd = [k.sb("whd0", [128, 8, 768], BF16)]
    wmb = [whd[0][:, :, 0:512]]
    import os
    if 'nomod' in os.environ.get('M1_SKIP', ''):
        modv = k.sb("mmodv", [128, 16, 2]); k.memset(modv[:], 0.1)
    else:
        modv = emit_mod(k, nextP, d_cvec, d_wmod, d_bmod, 2048, wmb)
    hT = k.sb("hT", [128, 8, NT * 128], BF16)
    xt = [k.sb("xt0", [128, 8, 256])]
    sq = k.sb("sq", [128, 8, 256], BF16); rstd = k.sb("rstd", [128, 256]); tmp = k.sb("tmp", [128, 256])
    import os
    if 'nohT' in os.environ.get('M1_SKIP', ''):
        k.memset(hT[:, :, 0:512], 0.01)
    else:
        emit_hT(k, nextP, hT, d_xT, d_xcT, modv, n1g, ones_bf, xt, sq, rstd, tmp)

    cs = {nm: k.sb(nm + "_s", [128, 128]) for nm in cn}
    for nm in cn:
        k.dma(cs[nm][:], cn[nm])
    pcols = k.sb("pcols_s", [128, 4]); k.dma(pcols[:], d_pcols)
    cos = k.sb("cos_s", [128, 32, 64]); sin = k.sb("sin_s", [128, 32, 64])
    ng = k.sb("ng_s", [128, NH, DV])
    if 'nocs' not in os.environ.get('M1_SKIP', ''):
        k.dma(cos[:], d_cos); k.dma(sin[:], d_sin)
        k.dma(ng[:], d_ng)
    dec = k.sb("dec_s", [128, 8]); k.dma(dec[:], d_dec)
    lg = k.sb("lg", [128, 8]); e1 = k.sb("e1", [128, 8])
    k.act(e1[:], dec[:], AF.Exp, scale=-1.0)
    k.act(e1[:], e1[:], AF.Ln, bias=1.0, scale=1.0)
    k.ts(lg[:], e1[:], -1.0, ALU.mult)

    kb = k.sb("kb", [128, NT, DK], BF16); qT = k.sb("qT", [128, NT, 128], BF16); kT = k.sb("kT", [128, NT, 128], BF16)
    vb = k.sb("vb", [128, NT, DV], BF16); gs = k.sb("gs", [128, 32, DV], BF16)
    oacc = k.sb("oacc", [128, 32, DV], BF16)
    oTh = [k.sb("oTh%d" % i, [128, 2, 512], BF16) for i in range(2)]
    qf = k.sb("qf", [128, 128]); kf = k.sb("kf", [128, 128]); ksc = k.sb("ksc", [128, 128])
    tA = k.sb("tA", [128, 128]); tB = k.sb("tB", [128, 128])
    DM = [k.sb("DM%d" % i, [128, 128]) for i in range(2)]
    EBr = [k.sb("EBr%d" % i, [128, 128]) for i in range(2)]
    ERc = k.sb("ERc", [128, 2]); dcol = k.sb("dcol", [128, 2])
    S = k.sb("S", [128, DV]); Sb = k.sb("Sb", [128, DV], BF16)
    AT = k.sb("AT", [128, 128], BF16); qin = k.sb("qin", [128, 128], BF16); kk_ = k.sb("kk", [128, 128], BF16)
    st6 = k.sb("st6", [128, 6]); mv = k.sb("mv", [128, 2]); rs = k.sb("rs", [128, 1]); on = k.sb("on", [128, DV])
    obf = k.sb("obf", [128, DV], BF16)

    import os
    STOP = float(os.environ.get('M1_STOP', '99')); NTL = int(os.environ.get('M1_NT', '34'))
    gf = k.sb("gf", [128, DV])
    for hl in range(NH):
        w = whd[0]
        k.dma(w[:], d_win[hl].rearrange("(k p) c -> p k c", p=128), q="pool")
        for di, (dmn, mkn, rown) in enumerate((("dm_f", "mask_f", "row_f"), ("dm_b", "mask_b", "row_b"))):
            lgc = lg[:, di * 4 + hl:di * 4 + hl + 1]
            k.act(DM[di][:], cs[dmn][:], AF.Exp, scale=lgc)
            k.tt(DM[di][:], DM[di][:], cs[mkn][:], ALU.mult)
            k.act(EBr[di][:], cs[rown][:], AF.Exp, scale=lgc)
            k.act(ERc[:, di:di + 1], pcols[:, di:di + 1], AF.Exp, scale=lgc)
            k.act(dcol[:, di:di + 1], pcols[:, 2:3], AF.Exp, scale=lgc)
        for t in range(NT):
            pa = nextP(); pb = nextP()
            for kk in range(8):
                k.matmul(pa[:, 0:512], hT[:, kk, t * 128:(t + 1) * 128], w[:, kk, 0:512], start=(kk == 0), stop=(kk == 7))
            for kk in range(8):
                k.matmul(pb[:, 0:256], hT[:, kk, t * 128:(t + 1) * 128], w[:, kk, 512:768], start=(kk == 0), stop=(kk == 7))
            k.ts(ksc[:], pa[:, 128:256], float(DK) ** -0.5, ALU.mult)
            if t >= 2:
                rope(k, qf[:], pa[:, 0:128], cos[:, t - 2, :], sin[:, t - 2, :], tA[:], tB[:])
                rope(k, kf[:], ksc[:], cos[:, t - 2, :], sin[:, t - 2, :], tA[:], tB[:])
                ksrc = kf
                k.copy(gf[:], pa[:, 256:512])
                k.act(gs[:, t - 2, :], gf[:], AF.Silu)
            else:
                k.copy(qf[:], pa[:, 0:128])
                ksrc = ksc
            k.copy(kb[:, t, :], ksrc[:], eng="pool")
            k.copy(vb[:, t, :], pb[:, 0:256])
            pt = nextP()
            k.transpose(pt[:, 0:128], qf[:], identf[:])
            k.transpose(pt[:, 128:256], ksrc[:], identf[:])
            k.copy(qT[:, t, :], pt[:, 0:128])
            k.copy(kT[:, t, :], pt[:, 128:256])
        for di in (1, 0):
            order = [1, 0] + list(range(33, 1, -1)) if di == 1 else list(range(NT))
            k.memset(S[:], 0.0); k.memset(Sb[:], 0.0)
            for t in order:
                if t >= 2:
                    pat = nextP()
                    k.matmul(pat[:, 0:128], kT[:, t, :], qT[:, t, :])
                    k.tt(AT[:], pat[:, 0:128], DM[di][:], ALU.mult)
                    k.tt(qin[:], qT[:, t, :], EBr[di][:], ALU.mult, eng="pool")
                    po = nextP()
                    k.matmul(po[:, 0:DV], AT[:], vb[:, t, :], start=True, stop=False)
                    k.matmul(po[:, 0:DV], qin[:], Sb[:], start=False, stop=True)
                k.ts(kk_[:], kb[:, t, :], ERc[:, di:di + 1], ALU.mult, eng="pool")
                pkv = nextP()
                k.matmul(pkv[:, 0:DV], kk_[:], vb[:, t, :])
                if t >= 2:
                    if di == 1:
                        k.copy(oacc[:, t - 2, :], po[:, 0:DV])
                    else:
                        k.tt(on[:], po[:, 0:DV], oacc[:, t - 2, :], ALU.add)
                        k.op("dve", lambda e, a=st6, b=on: e.bn_stats(a[:], b[:]), [on[:]], [st6[:]])
                        k.op("dve", lambda e, a=mv, b=st6: e.bn_aggr(a[:], b[:]), [st6[:]], [mv[:]])
                        k.ts(rs[:], mv[:, 1:2], EPS, ALU.add)
                        k.act(rs[:], rs[:], AF.Sqrt)
                        k.recip(rs[:], rs[:])
                        k.ts(on[:], on[:], mv[:, 0:1], ALU.subtract, rs[:, 0:1], ALU.mult)
                        k.tt(on[:], on[:], ng[:, hl, :], ALU.mult, eng="pool")
                        k.tt(obf[:], on[:], gs[:, t - 2, :], ALU.mult, eng="pool")
                        pt = nextPT()
                        k.transpose(pt[:, 0:128], obf[:, 0:128], ident[:])
                        k.transpose(pt[:, 128:256], obf[:, 128:256], ident[:])
                        lt = t - 2
                        ob = oTh[(lt // 4) % 2]
                        k.copy(ob[:, :, (lt % 4) * 128:(lt % 4 + 1) * 128], pt[:, 0:256].rearrange("p (c t) -> p c t", c=2))
                        if lt % 4 == 3:
                            k.dma(d_oT[hl * DV:(hl + 1) * DV, (lt // 4) * 512:(lt // 4 + 1) * 512].rearrange("(c p) t -> p c t", p=128), ob[:])
                k.stt(S[:], S[:], dcol[:, di:di + 1], pkv[:, 0:DV], ALU.mult, ALU.add)
                k.copy(Sb[:], S[:], eng="act")
    nc = k.finish([d_oT])
    return nc, k


NEG = -30000.0


def na_configs():
    cfgs = []; plan = {}
    for g in range(32):
        lst = []
        for u in range(32):
            key = []
            for kh in range(2):
                for qh in range(2):
                    r = 2 * g + qh; kr = 2 * u + kh
                    r0 = min(max(r - 4, 0), 56)
                    key.append(kr - r + 7 if r0 <= kr <= r0 + 7 else None)
            key = tuple(key)
            if all(x is None for x in key):
                continue
            if key not in cfgs:
                cfgs.append(key)
            lst.append((u, cfgs.index(key)))
        plan[g] = lst
    return cfgs, plan


def build_M0():
    k = KB()
    d_xT = k.dram("xT", [D, 4096]); d_xcT = k.dram("xcT", [D, 256])
    d_cvec = k.dram("cvec", [128, 8, 2]); d_wmod = k.dram("wmod", [D, 2048]); d_bmod = k.dram("bmod", [128, 16])
    d_n1g = k.dram("n1g", [128, 8])
    d_wna = k.dram("wna", [D, 768]); d_wgl = k.dram("wgl", [D, 768]); d_wa = k.dram("wa", [D, 32])
    d_waaug = k.dram("waaug", [17, 2, 128])
    d_rpbT = k.dram("rpbT", [128, 4, 15, 64]); d_colmask = k.dram("colmask", [128, 64])
    d_gng = k.dram("gng", [128, 128])
    d_maskf = k.dram("mask_f", [128, 128]); d_maskb = k.dram("mask_b", [128, 128])
    d_ident = k.dram("ident", [128, 128])
    d_oT = k.dram("oT", [512, NT * 128], BF16, kind="ExternalOutput")

    P = [k.ps("P%d" % i, [128, 512]) for i in range(6)]
    PT = [k.ps("PT%d" % i, [128, 1024], BF16) for i in range(2)]
    pc = [0, 0]

    NRR = [6]

    def nextP():
        p = P[pc[0] % NRR[0]]; pc[0] += 1; return p

    def nextPT():
        p = PT[pc[1] % 2]; pc[1] += 1; return p

    ones_bf = k.sb("ones_bf", [128, 128], BF16); k.memset(ones_bf[:], 1.0)
    identf = k.sb("identf", [128, 128]); ident = k.sb("ident_s", [128, 128], BF16)
    k.dma(identf[:], d_ident); k.copy(ident[:], identf[:])
    n1g = k.sb("n1g_s", [128, 8]); k.dma(n1g[:], d_n1g)
    hT = k.sb("hT", [128, 8, NT * 128], BF16)
    oTs = [k.sb("oTs%d" % i, [128, 2, 512], BF16) for i in range(2)]
    k.open_scope()
    wmb = [k.sb("wmb0", [128, 8, 512], BF16)]
    modv = emit_mod(k, nextP, d_cvec, d_wmod, d_bmod, 2048, wmb)
    xt = [k.sb("xt0", [128, 8, 256]), k.sb("xt1", [128, 8, 256])]
    sq = k.sb("sq", [128, 8, 256], BF16); rstd = k.sb("rstd", [128, 256]); tmp = k.sb("tmp", [128, 256])
    emit_hT(k, nextP, hT, d_xT, d_xcT, modv, n1g, ones_bf, xt, sq, rstd, tmp)
    k.close_scope()

    cfgs, plan = na_configs()
    k.open_scope()
    wna = k.sb("wna_s", [128, 8, 768], BF16)
    k.dma(wna[:], d_wna.rearrange("(k p) c -> p k c", p=128), q="pool")
    QT = k.sb("QT", [64, 4, NT * 128], BF16); KT = k.sb("KT", [64, 4, NT * 128], BF16)
    Vaug = k.sb("Vaug", [128, NT, 4, 65], BF16)
    k.memset(Vaug[:], 1.0)
    BT = k.sb("BT", [128, len(cfgs), 4, 128])
    k.open_scope()
    Btab = k.sb("Btab", [128, 4, 15, 64]); cmask = k.sb("cmask", [128, 64])
    k.dma(Btab[:], d_rpbT); k.dma(cmask[:], d_colmask)
    for h in range(4):
        k.tt(Btab[:, h, :, :], Btab[:, h, :, :], cmask[:].unsqueeze(1).to_broadcast([128, 15, 64]), ALU.add)
    for ci, key in enumerate(cfgs):
        bi = 0
        for kh in range(2):
            for qh in range(2):
                roff = key[bi]; bi += 1
                dst = BT[kh * 64:(kh + 1) * 64, ci, :, qh * 64:(qh + 1) * 64]
                if roff is None:
                    k.memset(dst, NEG, eng="pool")
                else:
                    k.copy(dst, Btab[kh * 64:(kh + 1) * 64, :, roff, :], eng="pool")
    k.close_scope()
    qs = k.sb("qs", [128, 256]); ks_ = k.sb("ks", [128, 256])
    for t in range(NT):
        pa = nextP(); pb = nextP()
        for kk in range(8):
            k.matmul(pa[:, 0:512], hT[:, kk, t * 128:(t + 1) * 128], wna[:, kk, 0:512], start=(kk == 0), stop=(kk == 7))
        for kk in range(8):
            k.matmul(pb[:, 0:256], hT[:, kk, t * 128:(t + 1) * 128], wna[:, kk, 512:768], start=(kk == 0), stop=(kk == 7))
        k.ts(qs[:], pa[:, 0:256], 0.125, ALU.mult)
        k.copy(ks_[:], pa[:, 256:512])
        k.copy(Vaug[:, t, :, 0:64], pb[:, 0:256].rearrange("p (h d) -> p h d", h=4))
        pt = nextP(); pt2 = nextP()
        for h in range(4):
            k.transpose(pt[0:64, h * 128:(h + 1) * 128], qs[:, h * 64:(h + 1) * 64], identf[:])
        for h in range(4):
            k.transpose(pt2[0:64, h * 128:(h + 1) * 128], ks_[:, h * 64:(h + 1) * 64], identf[:])
        k.copy(QT[:, :, t * 128:(t + 1) * 128], pt[0:64, 0:512].rearrange("p (c t) -> p c t", c=4))
        k.copy(KT[:, :, t * 128:(t + 1) * 128], pt2[0:64, 0:512].rearrange("p (c t) -> p c t", c=4))
    import os
    STOP = float(os.environ.get("M0_STOP", "99"))
    if STOP <= 2:
        k.close_scope(); return k.finish([d_oT]), k
    sc = [k.sb("sc%d" % i, [128, 512]) for i in range(2)]
    PTb = [k.sb("PTb%d" % i, [128, 512], BF16) for i in range(2)]
    rden = k.sb("rden", [128, 4, 1]); obf = k.sb("obf", [128, 256], BF16)
    it = [0]
    NRR[0] = 4
    for qt in range(NT if STOP > 2.5 else int(os.environ.get("M0_NQ", "1"))):
        if qt < 2:
            keys = [(0, None), (1, None)]
        else:
            keys = [(0, None), (1, None)] + [(u + 2, ci) for (u, ci) in plan[qt - 2]]
        po = P[4 + qt % 2]
        for ki, (kt, ci) in enumerate(keys):
            ps = nextP()
            for h in range(4):
                k.matmul(ps[:, h * 128:(h + 1) * 128], KT[:, h, kt * 128:(kt + 1) * 128],
                         QT[:, h, qt * 128:(qt + 1) * 128])
            s_ = sc[it[0] % 2]; p_ = PTb[it[0] % 2]; it[0] += 1
            if ci is None:
                k.copy(s_[:], ps[:, 0:512])
            else:
                k.tt(s_[:], ps[:, 0:512], BT[:, ci, :, :].rearrange("p h q -> p (h q)"), ALU.add)
            k.act(p_[:], s_[:], AF.Exp)
            for h in range(4):
                k.matmul(po[:, h * 65:(h + 1) * 65], p_[:, h * 128:(h + 1) * 128], Vaug[:, kt, h, :],
                         start=(ki == 0 and h == 0), stop=(ki == len(keys) - 1 and h == 3))
        pov = po[:, 0:260].rearrange("p (h e) -> p h e", e=65)
        k.recip(rden[:], pov[:, :, 64:65])
        k.tt(obf[:].rearrange("p (h d) -> p h d", h=4), pov[:, :, 0:64], rden[:].to_broadcast([128, 4, 64]), ALU.mult)
        ptt = nextPT()
        k.transpose(ptt[:, 0:128], obf[:, 0:128], ident[:])
        k.transpose(ptt[:, 128:256], obf[:, 128:256], ident[:])
        ob = oTs[(qt // 4) % 2]
        k.copy(ob[:, :, (qt % 4) * 128:(qt % 4 + 1) * 128], ptt[:, 0:256].rearrange("p (c t) -> p c t", c=2))
        if qt % 4 == 3 or qt == NT - 1:
            q0 = (qt // 4) * 4; n = qt - q0 + 1
            k.dma(d_oT[0:256, q0 * 128:(qt + 1) * 128].rearrange("(c p) t -> p c t", p=128), ob[:, :, 0:n * 128])
    NRR[0] = 6
    k.close_scope()
    if STOP <= 3:
        return k.finish([d_oT]), k

    k.open_scope()
    waaugf = k.sb("waaugf", [17, 2, 128]); waaug = k.sb("waaug_s", [17, 2, 128], BF16)
    k.dma(waaugf[:], d_waaug); k.copy(waaug[:], waaugf[:])
    gng = k.sb("gng_s", [128, 128]); k.dma(gng[:], d_gng)
    mk = [k.sb("mkf", [128, 128]), k.sb("mkb", [128, 128])]
    k.dma(mk[0][:], d_maskf); k.dma(mk[1][:], d_maskb)
    mks = [k.sb("mksf", [128, 128]), k.sb("mksb", [128, 128])]
    k.ts(mks[0][:], mk[0][:], -1.0, ALU.mult, 1.0, ALU.add)
    k.ts(mks[1][:], mk[1][:], -1.0, ALU.mult, 1.0, ALU.add)
    qTg = k.sb("qTg", [64, NT, 128], BF16); kTg = k.sb("kTg", [64, NT, 128], BF16)
    kbg = k.sb("kbg", [128, NT, 64], BF16); vg = k.sb("vg", [128, NT, 128], BF16)
    rsl = k.sb("rsl", [128, NT, 128], BF16); sp = k.sb("sp", [128, NT, 2, 64])
    oacc = k.sb("oaccg", [128, NT, 128], BF16)
    wgl = k.sb("wgl_s", [128, 8, 384], BF16); wa = k.sb("wa_s", [128, 8, 32], BF16)
    k.dma(wa[:], d_wa.rearrange("(k p) c -> p k c", p=128), q="pool")
    aT = k.sb("aT", [17, 2, 128], BF16); k.memset(aT[:], 1.0)
    qf = k.sb("qfg", [128, 64]); kf = k.sb("kfg", [128, 64]); rf = k.sb("rfg", [128, 128]); zf = k.sb("zfg", [128, 128])
    S = k.sb("Sg", [64, 128]); Sb = k.sb("Sbg", [64, 128], BF16)
    EBT = k.sb("EBT", [64, 128]); ENBT = k.sb("ENBT", [64, 128]); ER = k.sb("ERg", [128, 64])
    bcs = k.sb("bcs", [64, 128]); rsb = k.sb("rsb", [128, 64])
    qin = k.sb("qing", [64, 128], BF16); kin = k.sb("king", [64, 128], BF16); kkg = k.sb("kkg", [128, 64], BF16)
    AT = k.sb("ATg", [128, 128], BF16)
    on = k.sb("ong", [128, 128]); st6 = k.sb("st6g", [128, 6]); mv = k.sb("mvg", [128, 2]); ms = k.sb("msg", [128, 1])
    obg = k.sb("obg", [128, 128], BF16)
    oTg = [k.sb("oTg%d" % i, [128, 512], BF16) for i in range(2)]
    dwg = d_wgl.rearrange("(k p) c -> p k c", p=128)
    for gh in range(2):
        k.dma(wgl[:, :, 0:64], dwg[:, :, gh * 64:(gh + 1) * 64], q="pool")
        k.dma(wgl[:, :, 64:128], dwg[:, :, 128 + gh * 64:128 + (gh + 1) * 64], q="pool")
        k.dma(wgl[:, :, 128:256], dwg[:, :, 256 + gh * 128:256 + (gh + 1) * 128], q="pool")
        k.dma(wgl[:, :, 256:384], dwg[:, :, 512 + gh * 128:512 + (gh + 1) * 128], q="pool")
        for t in range(NT):
            pa = nextP(); pz = nextP()
            for kk in range(8):
                k.matmul(pa[:, 0:384], hT[:, kk, t * 128:(t + 1) * 128], wgl[:, kk, 0:384], start=(kk == 0), stop=(kk == 7))
            for di in range(2):
                for kk in range(8):
                    k.matmul(pz[0:16, di * 128:(di + 1) * 128], wa[:, kk, di * 16:(di + 1) * 16], hT[:, kk, t * 128:(t + 1) * 128],
                             start=(kk == 0), stop=(kk == 7))
            k.copy(aT[0:16, :, :], pz[0:16, 0:256].rearrange("p (a t) -> p a t", a=2))
            pz2 = nextP()
            for di in range(2):
                k.matmul(pz2[:, di * 64:(di + 1) * 64], aT[:, di, :], waaug[:, di, gh * 64:(gh + 1) * 64])
            k.copy(zf[:], pz2[:, 0:128])
            k.act(zf[:], zf[:], AF.Exp, scale=-1.0)
            k.act(sp[:, t, :, :].rearrange("p a d -> p (a d)"), zf[:], AF.Ln, bias=1.0, scale=1.0)
            k.ts(qf[:], pa[:, 0:64], 0.125, ALU.mult)
            k.copy(kf[:], pa[:, 64:128])
            k.copy(kbg[:, t, :], kf[:], eng="pool")
            k.copy(rf[:], pa[:, 128:256])
            k.act(rsl[:, t, :], rf[:], AF.Silu)
            k.copy(vg[:, t, :], pa[:, 256:384])
            pt = nextP()
            k.transpose(pt[0:64, 0:128], qf[:], identf[:])
            k.transpose(pt[0:64, 128:256], kf[:], identf[:])
            k.copy(qTg[:, t, :], pt[0:64, 0:128])
            k.copy(kTg[:, t, :], pt[0:64, 128:256])
        for di in (1, 0):
            order = ([1, 0] + list(range(33, 1, -1))) if di == 1 else list(range(NT))
            k.memset(S[:], 0.0); k.memset(Sb[:], 0.0)
            for t in order:
                spt = sp[:, t, di, :]
                pbc = nextP()
                k.matmul(pbc[0:64, 0:128], spt, mk[di][:])
                k.matmul(pbc[:, 128:192], mks[di][:], spt)
                k.copy(bcs[:], pbc[0:64, 0:128]); k.copy(rsb[:], pbc[:, 128:192])
                k.act(EBT[:], bcs[:], AF.Exp, scale=-1.0 / 16.0)
                k.act(ENBT[:], bcs[:], AF.Exp, scale=1.0 / 16.0)
                k.act(ER[:], rsb[:], AF.Exp, scale=-1.0 / 16.0)
                k.tt(qin[:], qTg[:, t, :], EBT[:], ALU.mult)
                k.tt(kin[:], kTg[:, t, :], ENBT[:], ALU.mult, eng="pool")
                k.tt(kkg[:], kbg[:, t, :], ER[:], ALU.mult, eng="pool")
                pat = nextP()
                k.matmul(pat[:, 0:128], kin[:], qin[:])
                k.tt(AT[:], pat[:, 0:128], mk[di][:], ALU.mult)
                po = nextP()
                k.matmul(po[:, 0:128], AT[:], vg[:, t, :], start=True, stop=False)
                k.matmul(po[:, 0:128], qin[:], Sb[:], start=False, stop=True)
                pkv = nextP()
                k.matmul(pkv[0:64, 0:128], kkg[:], vg[:, t, :])
                if di == 1:
                    k.copy(oacc[:, t, :], po[:, 0:128])
                else:
                    k.tt(on[:], po[:, 0:128], oacc[:, t, :], ALU.add)
                    k.op("dve", lambda e, a=st6, b=on: e.bn_stats(a[:], b[:]), [on[:]], [st6[:]])
                    k.op("dve", lambda e, a=mv, b=st6: e.bn_aggr(a[:], b[:]), [st6[:]], [mv[:]])
                    k.stt(ms[:], mv[:, 0:1], mv[:, 0:1], mv[:, 1:2], ALU.mult, ALU.add)
                    k.ts(ms[:], ms[:], EPS, ALU.add)
                    k.act(ms[:], ms[:], AF.Sqrt)
                    k.recip(ms[:], ms[:])
                    k.stt(on[:], on[:], ms[:, 0:1], gng[:], ALU.mult, ALU.mult)
                    k.tt(obg[:], on[:], rsl[:, t, :], ALU.mult, eng="pool")
                    ptt = nextPT()
                    k.transpose(ptt[:, 0:128], obg[:], ident[:])
                    ob = oTg[(t // 4) % 2]
                    k.copy(ob[:, (t % 4) * 128:(t % 4 + 1) * 128], ptt[:, 0:128])
                    if t % 4 == 3 or t == NT - 1:
                        q0 = (t // 4) * 4; n = t - q0 + 1
                        k.dma(d_oT[256 + gh * 128:256 + (gh + 1) * 128, q0 * 128:(t + 1) * 128], ob[:, 0:n * 128])
                dci = 127 if di == 0 else 0
                k.stt(S[:], S[:], EBT[:, dci:dci + 1], pkv[0:64, 0:128], ALU.mult, ALU.add)
                k.copy(Sb[:], S[:], eng="act")
    k.close_scope()
    nc = k.finish([d_oT])
    return nc, k

BF = ml_dtypes.bfloat16

def pk(v):
    return np.ascontiguousarray(v.reshape(-1, 128).T)

def seg_cols(arrT, t0, n, T):
    out = np.zeros((arrT.shape[0], n + 2), arrT.dtype)
    lo = max(t0 - 1, 0); hi = min(t0 + n + 1, T)
    out[:, lo - (t0 - 1): hi - (t0 - 1)] = arrT[:, lo:hi]
    hm = np.array([1.0 if t0 - 1 >= 0 else 0.0, 1.0 if t0 + n < T else 0.0], np.float32)
    return out, np.ascontiguousarray(np.broadcast_to(hm, (128, 2)))

def prep_F(inp, L, b, th, xT, oT, xcT=None, ocT=None, wout=None):
    m = {}
    segs = []
    for s in range(4):
        t0 = th * 2048 + s * 512
        m["xT_%d" % s], m["hm_%d" % s] = seg_cols(xT, t0, 512, 4096)
        m["oT_%d" % s], _ = seg_cols(oT, t0, 512, 4096)
        segs.append((512, 0))
    if xcT is not None:
        m["xT_4"], m["hm_4"] = seg_cols(xcT, 0, 256, 256)
        m["oT_4"], _ = seg_cols(ocT, 0, 256, 256)
        segs.append((256, 1))
    cv = np.stack([pk(inp["c"][b]), pk(inp["c_ctx"])], axis=-1)
    m["cvec"] = np.ascontiguousarray(cv.astype(np.float32))
    m["wmod"] = np.ascontiguousarray(inp["w_mod"][L][:, 2048:6144])
    m["bmod"] = pk(inp["b_mod"][L][2048:6144])
    m["n2g"] = pk(inp["norm2_g"][L]); m["fg"] = pk(inp["final_norm_g"])
    cw = inp["ffn_conv_w"][L]
    m["convw"] = np.ascontiguousarray(np.stack([pk(cw[i]) for i in range(3)], axis=1))
    m["convb"] = pk(inp["ffn_conv_b"][L])
    m["wout"] = wout
    m["wup"] = inp["ffn_w_up"][L]; m["wdn"] = inp["ffn_w_down"][L]
    return m, segs


_C = make_consts(); _COS, _SIN = rope_tables()

def prep_M_common(inp, L, b, xT, xcT):
    m = {"xT": np.ascontiguousarray(xT), "xcT": np.ascontiguousarray(xcT)}
    cv = np.stack([pk(inp["c"][b]), pk(inp["c_ctx"])], axis=-1)
    m["cvec"] = np.ascontiguousarray(cv.astype(np.float32))
    m["wmod"] = np.ascontiguousarray(inp["w_mod"][L][:, 0:2048])
    m["bmod"] = pk(inp["b_mod"][L][0:2048])
    m["n1g"] = pk(inp["norm1_g"][L])
    m["ident"] = np.eye(128, dtype=np.float32)
    return m

def prep_M1(inp, b, hh, xT, xcT):
    m = prep_M_common(inp, 1, b, xT, xcT)
    w = inp["ret_w_in"][0]
    heads = [hh * 4 + i for i in range(4)]
    m["win"] = np.ascontiguousarray(np.stack([np.concatenate([
        w[:, h * 128:(h + 1) * 128], w[:, 1024 + h * 128:1024 + (h + 1) * 128],
        w[:, 4096 + h * 256:4096 + (h + 1) * 256], w[:, 2048 + h * 256:2048 + (h + 1) * 256]], axis=1) for h in heads]))
    dec = np.concatenate([inp["ret_decay_fwd"][0][heads], inp["ret_decay_bwd"][0][heads]])
    m["dec"] = np.ascontiguousarray(np.broadcast_to(dec, (128, 8)).astype(np.float32))
    m["ng"] = np.ascontiguousarray(np.broadcast_to(inp["ret_norm_g"][0][heads], (128, 4, 256)).astype(np.float32))
    m["cos"] = _COS; m["sin"] = _SIN
    for kk in ("mask_f", "mask_b", "dm_f", "dm_b", "row_f", "row_b", "pcols"):
        m[kk] = _C[kk]
    return m

def _na_tables(rpb_heads):
    c = np.arange(64); c0 = np.clip(c - 8, 0, 48)
    kc = np.arange(64)
    allowed = (kc[:, None] >= c0[None, :]) & (kc[:, None] < c0[None, :] + 16)
    off = np.clip(kc[:, None] - c[None, :] + 15, 0, 30)
    g = rpb_heads[:, :, off]
    g = np.where(allowed[None, None], g, 0.0).astype(np.float32)
    g = np.transpose(g, (2, 0, 1, 3))
    rpbT = np.ascontiguousarray(np.concatenate([g, g], axis=0))
    cm = np.where(allowed, 0.0, -30000.0).astype(np.float32)
    return rpbT, np.ascontiguousarray(np.concatenate([cm, cm], axis=0))

def prep_M0(inp, b, hh, xT, xcT):
    m = prep_M_common(inp, 0, b, xT, xcT)
    w = inp["na_gla_w_in"][0]
    nh = [hh * 4 + i for i in range(4)]; gh = [hh * 2 + i for i in range(2)]
    m["wna"] = np.ascontiguousarray(np.concatenate(
        [w[:, h * 64:(h + 1) * 64] for h in nh] + [w[:, 512 + h * 64:512 + (h + 1) * 64] for h in nh] +
        [w[:, 1024 + h * 64:1024 + (h + 1) * 64] for h in nh], axis=1))
    m["wgl"] = np.ascontiguousarray(np.concatenate(
        [w[:, 1536 + h * 64:1536 + (h + 1) * 64] for h in gh] + [w[:, 1792 + h * 64:1792 + (h + 1) * 64] for h in gh] +
        [w[:, 2560 + h * 128:2560 + (h + 1) * 128] for h in gh] + [w[:, 2048 + h * 128:2048 + (h + 1) * 128] for h in gh], axis=1))
    m["wa"] = np.ascontiguousarray(w[:, 3072:3104])
    gc = slice(hh * 128, (hh + 1) * 128)
    wa = np.zeros((17, 2, 128), np.float32)
    wa[0:16, 0] = inp["gla_w_a_fwd"][0][:, gc]; wa[16, 0] = inp["gla_b_a_fwd"][0][gc]
    wa[0:16, 1] = inp["gla_w_a_bwd"][0][:, gc]; wa[16, 1] = inp["gla_b_a_bwd"][0][gc]
    m["waaug"] = wa
    m["rpbT"], m["colmask"] = _na_tables(inp["na_rpb"][0][nh])
    m["gng"] = np.ascontiguousarray(np.broadcast_to(inp["gla_norm_g"][0], (128, 128)).astype(np.float32))
    m["mask_f"] = _C["mask_f"]; m["mask_b"] = _C["mask_b"]
    return m


class Env:
    pass


def make_env(k):
    e = Env(); e.k = k
    e.P = [k.ps("P%d" % i, [128, 512]) for i in range(6)]
    e.PT = [k.ps("PT%d" % i, [128, 1024], BF16) for i in range(2)]
    e.pc = [0, 0]; e.NRR = [6]

    def nextP():
        p = e.P[e.pc[0] % e.NRR[0]]; e.pc[0] += 1; return p

    def nextPT():
        p = e.PT[e.pc[1] % 2]; e.pc[1] += 1; return p
    e.nextP = nextP; e.nextPT = nextPT
    e.ones_bf = k.sb("ones_bf", [128, 128], BF16); k.memset(e.ones_bf[:], 1.0)
    e.identf = k.sb("identf", [128, 128]); e.ident = k.sb("ident_s", [128, 128], BF16)
    d_ident = k.dram("ident", [128, 128])
    k.dma(e.identf[:], d_ident); k.copy(e.ident[:], e.identf[:])
    return e


def phase_hT(e, pfx, d_xT, d_xcT, d_cvec, d_wmod, d_bmod, d_n1g, hT):
    k = e.k
    k.open_scope()
    n1g = k.sb(pfx + "n1g_s", [128, 8]); k.dma(n1g[:], d_n1g)
    wmb = [k.sb(pfx + "wmb0", [128, 8, 512], BF16)]
    modv = emit_mod(k, e.nextP, d_cvec, d_wmod, d_bmod, 2048, wmb, name=pfx + "m")
    xt = [k.sb(pfx + "xt0", [128, 8, 256]), k.sb(pfx + "xt1", [128, 8, 256])]
    sq = k.sb(pfx + "sq", [128, 8, 256], BF16); rstd = k.sb(pfx + "rstd", [128, 256]); tmp = k.sb(pfx + "tmp", [128, 256])
    emit_hT(k, e.nextP, hT, d_xT, d_xcT, modv, n1g, e.ones_bf, xt, sq, rstd, tmp, name=pfx)
    k.close_scope()


def phase_NA(e, pfx, hT, d_wna, d_rpbT, d_colmask, d_oT, row0):
    k = e.k; nextP = e.nextP; nextPT = e.nextPT; identf = e.identf; ident = e.ident; P = e.P
    cfgs, plan = na_configs()
    k.open_scope()
    oTs = [k.sb(pfx + "oTs%d" % i, [128, 2, 512], BF16) for i in range(2)]
    wna = k.sb(pfx + "wna_s", [128, 8, 768], BF16)
    k.dma(wna[:], d_wna.rearrange("(k p) c -> p k c", p=128), q="pool")
    QT = k.sb(pfx + "QT", [64, 4, NT * 128], BF16); KT = k.sb(pfx + "KT", [64, 4, NT * 128], BF16)
    Vaug = k.sb(pfx + "Vaug", [128, NT, 4, 65], BF16)
    k.memset(Vaug[:], 1.0)
    BT = k.sb(pfx + "BT", [128, len(cfgs), 4, 128])
    k.open_scope()
    Btab = k.sb(pfx + "Btab", [128, 4, 15, 64]); cmask = k.sb(pfx + "cmask", [128, 64])
    k.dma(Btab[:], d_rpbT); k.dma(cmask[:], d_colmask)
    for h in range(4):
        k.tt(Btab[:, h, :, :], Btab[:, h, :, :], cmask[:].unsqueeze(1).to_broadcast([128, 15, 64]), ALU.add)
    for ci, key in enumerate(cfgs):
        bi = 0
        for kh in range(2):
            for qh in range(2):
                roff = key[bi]; bi += 1
                dst = BT[kh * 64:(kh + 1) * 64, ci, :, qh * 64:(qh + 1) * 64]
                if roff is None:
                    k.memset(dst, NEG, eng="pool")
                else:
                    k.copy(dst, Btab[kh * 64:(kh + 1) * 64, :, roff, :], eng="pool")
    k.close_scope()
    qs = k.sb(pfx + "qs", [128, 256]); ks_ = k.sb(pfx + "ks", [128, 256])
    for t in range(NT):
        pa = nextP(); pb = nextP()
        for kk in range(8):
            k.matmul(pa[:, 0:512], hT[:, kk, t * 128:(t + 1) * 128], wna[:, kk, 0:512], start=(kk == 0), stop=(kk == 7))
        for kk in range(8):
            k.matmul(pb[:, 0:256], hT[:, kk, t * 128:(t + 1) * 128], wna[:, kk, 512:768], start=(kk == 0), stop=(kk == 7))
        k.ts(qs[:], pa[:, 0:256], 0.125, ALU.mult)
        k.copy(ks_[:], pa[:, 256:512])
        k.copy(Vaug[:, t, :, 0:64], pb[:, 0:256].rearrange("p (h d) -> p h d", h=4))
        pt = nextP(); pt2 = nextP()
        for h in range(4):
            k.transpose(pt[0:64, h * 128:(h + 1) * 128], qs[:, h * 64:(h + 1) * 64], identf[:])
        for h in range(4):
            k.transpose(pt2[0:64, h * 128:(h + 1) * 128], ks_[:, h * 64:(h + 1) * 64], identf[:])
        k.copy(QT[:, :, t * 128:(t + 1) * 128], pt[0:64, 0:512].rearrange("p (c t) -> p c t", c=4))
        k.copy(KT[:, :, t * 128:(t + 1) * 128], pt2[0:64, 0:512].rearrange("p (c t) -> p c t", c=4))
    sc = [k.sb(pfx + "sc%d" % i, [128, 512]) for i in range(2)]
    PTb = [k.sb(pfx + "PTb%d" % i, [128, 512], BF16) for i in range(2)]
    rden = k.sb(pfx + "rden", [128, 4, 1]); obf = k.sb(pfx + "obf", [128, 256], BF16)
    it = [0]
    e.NRR[0] = 4
    for qt in range(NT):
        if qt < 2:
            keys = [(0, None), (1, None)]
        else:
            keys = [(0, None), (1, None)] + [(u + 2, ci) for (u, ci) in plan[qt - 2]]
        po = P[4 + qt % 2]
        for ki, (kt, ci) in enumerate(keys):
            ps = nextP()
            for h in range(4):
                k.matmul(ps[:, h * 128:(h + 1) * 128], KT[:, h, kt * 128:(kt + 1) * 128], QT[:, h, qt * 128:(qt + 1) * 128])
            s_ = sc[it[0] % 2]; p_ = PTb[it[0] % 2]; it[0] += 1
            if ci is None:
                k.copy(s_[:], ps[:, 0:512])
            else:
                k.tt(s_[:], ps[:, 0:512], BT[:, ci, :, :].rearrange("p h q -> p (h q)"), ALU.add)
            k.act(p_[:], s_[:], AF.Exp)
            for h in range(4):
                k.matmul(po[:, h * 65:(h + 1) * 65], p_[:, h * 128:(h + 1) * 128], Vaug[:, kt, h, :],
                         start=(ki == 0 and h == 0), stop=(ki == len(keys) - 1 and h == 3))
        pov = po[:, 0:260].rearrange("p (h e) -> p h e", e=65)
        k.recip(rden[:], pov[:, :, 64:65])
        k.tt(obf[:].rearrange("p (h d) -> p h d", h=4), pov[:, :, 0:64], rden[:].to_broadcast([128, 4, 64]), ALU.mult)
        ptt = nextPT()
        k.transpose(ptt[:, 0:128], obf[:, 0:128], ident[:])
        k.transpose(ptt[:, 128:256], obf[:, 128:256], ident[:])
        ob = oTs[(qt // 4) % 2]
        k.copy(ob[:, :, (qt % 4) * 128:(qt % 4 + 1) * 128], ptt[:, 0:256].rearrange("p (c t) -> p c t", c=2))
        if qt % 4 == 3 or qt == NT - 1:
            q0 = (qt // 4) * 4; n = qt - q0 + 1
            k.dma(d_oT[row0:row0 + 256, q0 * 128:(qt + 1) * 128].rearrange("(c p) t -> p c t", p=128), ob[:, :, 0:n * 128])
    e.NRR[0] = 6
    k.close_scope()


def phase_GLA(e, pfx, hT, d_wgl, d_wa, d_waaug, d_gng, d_maskf, d_maskb, d_oT, row0, ghs):
    k = e.k; nextP = e.nextP; nextPT = e.nextPT; identf = e.identf; ident = e.ident
    k.open_scope()
    waaugf = k.sb(pfx + "waaugf", [17, 2, 256]); waaug = k.sb(pfx + "waaug_s", [17, 2, 256], BF16)
    k.dma(waaugf[:], d_waaug); k.copy(waaug[:], waaugf[:])
    gng = k.sb(pfx + "gng_s", [128, 128]); k.dma(gng[:], d_gng)
    mk = [k.sb(pfx + "mkf", [128, 128]), k.sb(pfx + "mkb", [128, 128])]
    k.dma(mk[0][:], d_maskf); k.dma(mk[1][:], d_maskb)
    mks = [k.sb(pfx + "mksf", [128, 128]), k.sb(pfx + "mksb", [128, 128])]
    k.ts(mks[0][:], mk[0][:], -1.0, ALU.mult, 1.0, ALU.add)
    k.ts(mks[1][:], mk[1][:], -1.0, ALU.mult, 1.0, ALU.add)
    qTg = k.sb(pfx + "qTg", [64, NT, 128], BF16); kTg = k.sb(pfx + "kTg", [64, NT, 128], BF16)
    kbg = k.sb(pfx + "kbg", [128, NT, 64], BF16); vg = k.sb(pfx + "vg", [128, NT, 128], BF16)
    rsl = k.sb(pfx + "rsl", [128, NT, 128], BF16); sp = k.sb(pfx + "sp", [128, NT, 2, 64])
    oacc = k.sb(pfx + "oaccg", [128, NT, 128], BF16)
    wgl = k.sb(pfx + "wgl_s", [128, 8, 384], BF16); wa = k.sb(pfx + "wa_s", [128, 8, 32], BF16)
    k.dma(wa[:], d_wa.rearrange("(k p) c -> p k c", p=128), q="pool")
    aT = k.sb(pfx + "aT", [17, 2, 128], BF16); k.memset(aT[:], 1.0)
    qf = k.sb(pfx + "qfg", [128, 64]); kf = k.sb(pfx + "kfg", [128, 64]); rf = k.sb(pfx + "rfg", [128, 128]); zf = k.sb(pfx + "zfg", [128, 128])
    R2 = range(2)
    S2 = [k.sb(pfx + "Sg%d" % i, [64, 128]) for i in R2]; Sb2 = [k.sb(pfx + "Sbg%d" % i, [64, 128], BF16) for i in R2]
    EBT2 = [k.sb(pfx + "EBT%d" % i, [64, 128]) for i in R2]; ENBT2 = [k.sb(pfx + "ENBT%d" % i, [64, 128]) for i in R2]
    ER2 = [k.sb(pfx + "ERg%d" % i, [128, 64]) for i in R2]
    bcs2 = [k.sb(pfx + "bcs%d" % i, [64, 128]) for i in R2]; rsb2 = [k.sb(pfx + "rsb%d" % i, [128, 64]) for i in R2]
    qin2 = [k.sb(pfx + "qing%d" % i, [64, 128], BF16) for i in R2]; kin2 = [k.sb(pfx + "king%d" % i, [64, 128], BF16) for i in R2]
    kkg2 = [k.sb(pfx + "kkg%d" % i, [128, 64], BF16) for i in R2]
    AT2 = [k.sb(pfx + "ATg%d" % i, [128, 128], BF16) for i in R2]
    oT4 = [k.sb(pfx + "oT4g%d" % i, [128, 128], BF16) for i in range(4)]; oc = [0]
    on = k.sb(pfx + "ong", [128, 128]); st6 = k.sb(pfx + "st6g", [128, 6]); mv = k.sb(pfx + "mvg", [128, 2]); ms = k.sb(pfx + "msg", [128, 1])
    obg = k.sb(pfx + "obg", [128, 128], BF16)
    dwg = d_wgl.rearrange("(k p) c -> p k c", p=128)
    for (gl, gglob, rofs) in ghs:
        k.dma(wgl[:, :, 0:64], dwg[:, :, gl * 64:(gl + 1) * 64], q="pool")
        k.dma(wgl[:, :, 64:128], dwg[:, :, 128 + gl * 64:128 + (gl + 1) * 64], q="pool")
        k.dma(wgl[:, :, 128:256], dwg[:, :, 256 + gl * 128:256 + (gl + 1) * 128], q="pool")
        k.dma(wgl[:, :, 256:384], dwg[:, :, 512 + gl * 128:512 + (gl + 1) * 128], q="pool")
        for t in range(NT):
            pa = nextP(); pz = nextP()
            for kk in range(8):
                k.matmul(pa[:, 0:384], hT[:, kk, t * 128:(t + 1) * 128], wgl[:, kk, 0:384], start=(kk == 0), stop=(kk == 7))
            for di in range(2):
                for kk in range(8):
                    k.matmul(pz[0:16, di * 128:(di + 1) * 128], wa[:, kk, di * 16:(di + 1) * 16], hT[:, kk, t * 128:(t + 1) * 128],
                             start=(kk == 0), stop=(kk == 7))
            k.copy(aT[0:16, :, :], pz[0:16, 0:256].rearrange("p (a t) -> p a t", a=2))
            pz2 = nextP()
            for di in range(2):
                k.matmul(pz2[:, di * 64:(di + 1) * 64], aT[:, di, :], waaug[:, di, gglob * 64:(gglob + 1) * 64])
            k.copy(zf[:], pz2[:, 0:128])
            k.act(zf[:], zf[:], AF.Exp, scale=-1.0)
            k.act(sp[:, t, :, :].rearrange("p a d -> p (a d)"), zf[:], AF.Ln, bias=1.0, scale=1.0)
            k.ts(qf[:], pa[:, 0:64], 0.125, ALU.mult)
            k.copy(kf[:], pa[:, 64:128])
            k.copy(kbg[:, t, :], kf[:], eng="pool")
            k.copy(rsl[:, t, :], pa[:, 128:256])
            k.copy(vg[:, t, :], pa[:, 256:384])
            pt = nextP()
            k.transpose(pt[0:64, 0:128], qf[:], identf[:])
            k.transpose(pt[0:64, 128:256], kf[:], identf[:])
            k.copy(qTg[:, t, :], pt[0:64, 0:128])
            k.copy(kTg[:, t, :], pt[0:64, 128:256])
        for t in range(NT):
            k.act(rsl[:, t, :], rsl[:, t, :], AF.Silu)
        ordB = [1, 0] + list(range(33, 1, -1)); ordF = list(range(NT))
        for di in range(2):
            k.memset(S2[di][:], 0.0); k.memset(Sb2[di][:], 0.0)
        have = set()
        for i in range(NT):
            for di, t in ((1, ordB[i]), (0, ordF[i])):
                S = S2[di]; Sb = Sb2[di]; AT = AT2[di]; qin = qin2[di]; kin = kin2[di]; kkg = kkg2[di]
                EBT = EBT2[di]; ENBT = ENBT2[di]; ER = ER2[di]; bcs = bcs2[di]; rsb = rsb2[di]
                spt = sp[:, t, di, :]
                pbc = nextP()
                k.matmul(pbc[0:64, 0:128], spt, mk[di][:])
                k.matmul(pbc[:, 128:192], mks[di][:], spt)
                k.copy(bcs[:], pbc[0:64, 0:128]); k.copy(rsb[:], pbc[:, 128:192])
                k.act(EBT[:], bcs[:], AF.Exp, scale=-1.0 / 16.0)
                k.act(ENBT[:], bcs[:], AF.Exp, scale=1.0 / 16.0)
                k.act(ER[:], rsb[:], AF.Exp, scale=-1.0 / 16.0)
                k.tt(qin[:], qTg[:, t, :], EBT[:], ALU.mult)
                k.tt(kin[:], kTg[:, t, :], ENBT[:], ALU.mult)
                k.tt(kkg[:], kbg[:, t, :], ER[:], ALU.mult)
                pat = nextP()
                k.matmul(pat[:, 0:128], kin[:], qin[:])
                k.tt(AT[:], pat[:, 0:128], mk[di][:], ALU.mult)
                po = nextP()
                k.matmul(po[:, 0:128], AT[:], vg[:, t, :], start=True, stop=False)
                k.matmul(po[:, 0:128], qin[:], Sb[:], start=False, stop=True)
                pkv = nextP()
                k.matmul(pkv[0:64, 0:128], kkg[:], vg[:, t, :])
                if t not in have:
                    k.copy(oacc[:, t, :], po[:, 0:128]); have.add(t)
                else:
                    k.tt(on[:], po[:, 0:128], oacc[:, t, :], ALU.add)
                    k.op("dve", lambda e_, a=st6, b=on: e_.bn_stats(a[:], b[:]), [on[:]], [st6[:]])
                    k.op("dve", lambda e_, a=mv, b=st6: e_.bn_aggr(a[:], b[:]), [st6[:]], [mv[:]])
                    k.stt(ms[:], mv[:, 0:1], mv[:, 0:1], mv[:, 1:2], ALU.mult, ALU.add)
                    k.ts(ms[:], ms[:], EPS, ALU.add)
                    k.act(ms[:], ms[:], AF.Ln)
                    k.act(ms[:], ms[:], AF.Exp, scale=-0.5)
                    k.stt(on[:], on[:], ms[:, 0:1], gng[:], ALU.mult, ALU.mult)
                    k.tt(obg[:], on[:], rsl[:, t, :], ALU.mult, eng="pool")
                    ptt = nextPT()
                    k.transpose(ptt[:, 0:128], obg[:], ident[:])
                    ob = oT4[oc[0] % 4]; oc[0] += 1
                    k.copy(ob[:], ptt[:, 0:128])
                    k.dma(d_oT[row0 + rofs:row0 + rofs + 128, t * 128:(t + 1) * 128], ob[:])
                dci = 127 if di == 0 else 0
                k.stt(S[:], S[:], EBT[:, dci:dci + 1], pkv[0:64, 0:128], ALU.mult, ALU.add)
                k.copy(Sb[:], S[:], eng="act")
    k.close_scope()


def phase_RET(e, pfx, hT, d_win, d_dec, d_ng, d_cos, d_sin, cn, d_pcols, d_oT, NH):
    k = e.k; nextP = e.nextP; nextPT = e.nextPT; identf = e.identf; ident = e.ident
    DK = 128; DV = 256
    k.open_scope()
    cs = {nm: k.sb(pfx + nm + "_s", [128, 128]) for nm in cn}
    for nm in cn:
        k.dma(cs[nm][:], cn[nm])
    pcols = k.sb(pfx + "pcols_s", [128, 4]); k.dma(pcols[:], d_pcols)
    cos = k.sb(pfx + "cos_s", [128, 32, 128]); sin = k.sb(pfx + "sin_s", [128, 32, 128])
    k.dma(cos[:], d_cos); k.dma(sin[:], d_sin)
    ng = k.sb(pfx + "ng_s", [128, 2, DV])
    dec = k.sb(pfx + "dec_s", [128, 2 * NH]); k.dma(dec[:], d_dec)
    lg = k.sb(pfx + "lg", [128, 2 * NH]); e1 = k.sb(pfx + "e1", [128, 2 * NH])
    k.act(e1[:], dec[:], AF.Exp, scale=-1.0)
    k.act(e1[:], e1[:], AF.Ln, bias=1.0, scale=1.0)
    k.ts(lg[:], e1[:], -1.0, ALU.mult)
    w = k.sb(pfx + "whd0", [128, 8, 768], BF16)
    kb = k.sb(pfx + "kb", [128, NT, DK], BF16); qT = k.sb(pfx + "qT", [128, NT, 128], BF16); kT = k.sb(pfx + "kT", [128, NT, 128], BF16)
    vb = k.sb(pfx + "vb", [128, NT, DV], BF16); gs = k.sb(pfx + "gs", [128, 32, DV], BF16)
    oacc = k.sb(pfx + "oacc", [128, 32, DV], BF16)
    qkL = [k.sb(pfx + "qk%d" % i, [128, 256]) for i in range(2)]
    tAL = [k.sb(pfx + "tA%d" % i, [128, 256]) for i in range(1)] * 2; tBL = [k.sb(pfx + "tB%d" % i, [128, 256]) for i in range(1)] * 2
    DM = [k.sb(pfx + "DM%d" % i, [128, 128]) for i in range(2)]
    EBr = [k.sb(pfx + "EBr%d" % i, [128, 128]) for i in range(2)]
    ERc = k.sb(pfx + "ERc", [128, 2]); dcol = k.sb(pfx + "dcol", [128, 2])
    S2 = [k.sb(pfx + "S%d" % i, [128, DV]) for i in range(2)]; Sb2 = [k.sb(pfx + "Sb%d" % i, [128, DV], BF16) for i in range(2)]
    AT2 = [k.sb(pfx + "AT%d" % i, [128, 128], BF16) for i in range(2)]; qin2 = [k.sb(pfx + "qin%d" % i, [128, 128], BF16) for i in range(2)]
    kk2 = [k.sb(pfx + "kk%d" % i, [128, 128], BF16) for i in range(2)]
    oT4 = [k.sb(pfx + "oT4%d" % i, [128, 2, 128], BF16) for i in range(2)] * 2; oc = [0]
    R2_ = range(2)
    st6L = [k.sb(pfx + "st6%d" % i, [128, 6]) for i in R2_]; mvL = [k.sb(pfx + "mv%d" % i, [128, 2]) for i in R2_]
    rsL = [k.sb(pfx + "rs%d" % i, [128, 1]) for i in R2_]; nbL = [k.sb(pfx + "nb%d" % i, [128, 1]) for i in R2_]
    onL = [k.sb(pfx + "on%d" % i, [128, DV]) for i in R2_]; obfL = [k.sb(pfx + "obf%d" % i, [128, DV], BF16) for i in R2_]
    pend = [None]; pendB = [None]; rc = [0]
    gfL = tBL
    for hl in range(NH):
        k.dma(w[:], d_win[hl].rearrange("(k p) c -> p k c", p=128), q="pool")
        k.dma(ng[:, hl % 2, :], d_ng[:, hl, :])
        for di, (dmn, mkn, rown) in enumerate((("dm_f", "mask_f", "row_f"), ("dm_b", "mask_b", "row_b"))):
            lgc = lg[:, di * NH + hl:di * NH + hl + 1]
            k.act(DM[di][:], cs[dmn][:], AF.Exp, scale=lgc)
            k.tt(DM[di][:], DM[di][:], cs[mkn][:], ALU.mult)
            k.act(EBr[di][:], cs[rown][:], AF.Exp, scale=lgc)
            k.act(ERc[:, di:di + 1], pcols[:, di:di + 1], AF.Exp, scale=lgc)
            k.act(dcol[:, di:di + 1], pcols[:, 2:3], AF.Exp, scale=lgc)
        for t in range(NT):
            pa = nextP(); pb = nextP()
            for kk in range(8):
                k.matmul(pa[:, 0:512], hT[:, kk, t * 128:(t + 1) * 128], w[:, kk, 0:512], start=(kk == 0), stop=(kk == 7))
            for kk in range(8):
                k.matmul(pb[:, 0:256], hT[:, kk, t * 128:(t + 1) * 128], w[:, kk, 512:768], start=(kk == 0), stop=(kk == 7))
            qk = qkL[t % 2]; gf = gfL[t % 2]
            SC = float(DK) ** -0.5
            if t >= 2:
                xv = pa[:, 0:256].rearrange("p (g h f) -> p g h f", g=4, h=2)
                ov = qk[:].rearrange("p (g h f) -> p g h f", g=4, h=2)
                Av = tAL[t % 2][:].rearrange("p (g h f) -> p g h f", g=4, h=2)
                Bv = tBL[t % 2][:].rearrange("p (g h f) -> p g h f", g=4, h=2)
                cb_ = cos[:, t - 2, :].rearrange("p (g f) -> p g f", g=4).unsqueeze(2).to_broadcast([128, 4, 2, 32])
                sv = sin[:, t - 2, :].rearrange("p (g f) -> p g f", g=4)
                k.tt(Av, xv, cb_, ALU.mult)
                k.tt(Bv[:, :, 0, :], xv[:, :, 1, :], sv, ALU.mult)
                k.tt(Bv[:, :, 1, :], xv[:, :, 0, :], sv, ALU.mult)
                k.tt(ov[:, :, 0, :], Av[:, :, 0, :], Bv[:, :, 0, :], ALU.subtract)
                k.tt(ov[:, :, 1, :], Av[:, :, 1, :], Bv[:, :, 1, :], ALU.add)
                k.copy(gf[:], pa[:, 256:512])
                k.act(gs[:, t - 2, :], gf[:], AF.Silu)
            else:
                k.copy(qk[:], pa[:, 0:256])
            k.ts(kb[:, t, :], qk[:, 128:256], SC, ALU.mult, eng="pool")
            k.copy(vb[:, t, :], pb[:, 0:256])
            pt = nextP()
            k.transpose(pt[:, 0:128], qk[:, 0:128], identf[:])
            k.transpose(pt[:, 128:256], qk[:, 128:256], identf[:])
            k.copy(qT[:, t, :], pt[:, 0:128])
            k.ts(kT[:, t, :], pt[:, 128:256], SC, ALU.mult)
        ordB = [1, 0] + list(range(33, 1, -1)); ordF = list(range(NT))
        for di in range(2):
            k.memset(S2[di][:], 0.0); k.memset(Sb2[di][:], 0.0)
        have = set()
        for i in range(NT):
            for di, t in ((1, ordB[i]), (0, ordF[i])):
                S = S2[di]; Sb = Sb2[di]; AT = AT2[di]; qin = qin2[di]; kk_ = kk2[di]
                if t >= 2:
                    pat = nextP()
                    k.matmul(pat[:, 0:128], kT[:, t, :], qT[:, t, :])
                    k.tt(AT[:], pat[:, 0:128], DM[di][:], ALU.mult)
                    k.tt(qin[:], qT[:, t, :], EBr[di][:], ALU.mult)
                    po = nextP()
                    k.matmul(po[:, 0:DV], AT[:], vb[:, t, :], start=True, stop=False)
                    k.matmul(po[:, 0:DV], qin[:], Sb[:], start=False, stop=True)
                k.ts(kk_[:], kb[:, t, :], ERc[:, di:di + 1], ALU.mult)
                pkv = nextP()
                k.matmul(pkv[:, 0:DV], kk_[:], vb[:, t, :])
                k.stt(S[:], S[:], dcol[:, di:di + 1], pkv[:, 0:DV], ALU.mult, ALU.add)
                k.copy(Sb[:], S[:], eng="act")
                if pendB[0] is not None:
                    pendB[0](); pendB[0] = None
                if pend[0] is not None:
                    pendB[0] = pend[0](); pend[0] = None
                if t >= 2:
                    if t not in have:
                        k.copy(oacc[:, t - 2, :], po[:, 0:DV]); have.add(t)
                    else:
                        par = rc[0] % 2; rc[0] += 1
                        onb = onL[par]
                        k.tt(onb[:], po[:, 0:DV], oacc[:, t - 2, :], ALU.add)

                        def readout(onb=onb, par=par, t=t, hl=hl):
                            st6_ = st6L[par]; mv_ = mvL[par]; rs_ = rsL[par]; nb_ = nbL[par]; obf_ = obfL[par]
                            k.op("dve", lambda e_, a=st6_, b_=onb: e_.bn_stats(a[:], b_[:]), [onb[:]], [st6_[:]])
                            k.op("dve", lambda e_, a=mv_, b_=st6_: e_.bn_aggr(a[:], b_[:]), [st6_[:]], [mv_[:]])
                            k.ts(rs_[:], mv_[:, 1:2], EPS, ALU.add)
                            k.act(rs_[:], rs_[:], AF.Sqrt)
                            k.recip(rs_[:], rs_[:])
                            k.stt(nb_[:], mv_[:, 0:1], -1.0, rs_[:], ALU.mult, ALU.mult)
                            k.act(onb[:], onb[:], AF.Identity, bias=nb_[:, 0:1], scale=rs_[:, 0:1])
                            k.tt(onb[:], onb[:], ng[:, hl % 2, :], ALU.mult, eng="pool")
                            k.tt(obf_[:], onb[:], gs[:, t - 2, :], ALU.mult, eng="pool")
                            return lambda: partB(obf_, t, hl)

                        def partB(obf_, t, hl):
                            ptt = nextPT()
                            k.transpose(ptt[:, 0:128], obf_[:, 0:128], ident[:])
                            k.transpose(ptt[:, 128:256], obf_[:, 128:256], ident[:])
                            lt = t - 2
                            ob = oT4[oc[0] % 4]; oc[0] += 1
                            k.copy(ob[:], ptt[:, 0:256].rearrange("p (c t) -> p c t", c=2))
                            k.dma(d_oT[hl * DV:(hl + 1) * DV, lt * 128:(lt + 1) * 128].rearrange("(c p) t -> p c t", p=128), ob[:])
                        pend[0] = readout
        if pendB[0] is not None:
            pendB[0](); pendB[0] = None
        if pend[0] is not None:
            pend[0]()(); pend[0] = None
    k.close_scope()


def phase_F(e, pfx, Fdim, last, segs, xsrc, osrc, ydst, d_cvec, d_wmod, d_bmod, d_n2g, d_fg, d_cw, d_cb, d_wout, d_wup, d_wdn):
    k = e.k; nextP = e.nextP; ones_bf = e.ones_bf
    KF = Fdim // 128
    NMAX = max(s[0] for s in segs); CMAX = NMAX + 2
    k.open_scope()
    cvec = k.sb(pfx + "cvec_s", [128, 8, 2]); scb = k.sb(pfx + "scb", [128, 8, 2], BF16)
    bmod = k.sb(pfx + "bmod_s", [128, 32]); modv = k.sb(pfx + "modv", [128, 32, 2])
    n2g = k.sb(pfx + "n2g_s", [128, 8]); fg = k.sb(pfx + "fg_s", [128, 8]); gm2 = k.sb(pfx + "gm2", [128, 8, 2])
    cw = k.sb(pfx + "cw_s", [128, 3, 44]); cb = k.sb(pfx + "cb_s", [128, 44])
    wout = k.sb(pfx + "wout_s", [128, KF, D], BF16)
    wdn = k.sb(pfx + "wdn_s", [128, NFF, D], BF16)
    oT = k.sb(pfx + "oT_s", [128, KF, CMAX], BF16)
    x1 = k.sb(pfx + "x1T", [128, 8, CMAX])
    sq = k.sb(pfx + "sq", [128, 8, CMAX], BF16)
    rstd = k.sb(pfx + "rstd", [128, CMAX]); tmp = k.sb(pfx + "tmp", [128, CMAX]); tmpb = k.sb(pfx + "tmpb", [128, CMAX])
    h2 = k.sb(pfx + "h2T", [128, 8, CMAX], BF16)
    wua = [k.sb(pfx + "wua%d" % i, [128, 8, 256], BF16) for i in range(2)]
    wub = [k.sb(pfx + "wub%d" % i, [128, 8, 256], BF16) for i in range(2)]
    u = [k.sb(pfx + "u%d" % i, [128, 2, CMAX]) for i in range(2)]
    vaL = [k.sb(pfx + "va%d" % i, [128, NMAX]) for i in range(2)]; vbL = [k.sb(pfx + "vb%d" % i, [128, NMAX]) for i in range(2)]
    saL = [k.sb(pfx + "sa%d" % i, [128, NMAX]) for i in range(2)]
    tT = k.sb(pfx + "tT", [128, NFF, NMAX], BF16)
    k.dma(cvec[:], d_cvec); k.dma(bmod[:], d_bmod); k.dma(n2g[:], d_n2g); k.dma(fg[:], d_fg)
    k.dma(cw[:], d_cw); k.dma(cb[:], d_cb)
    k.act(scb[:], cvec[:], AF.Silu)
    pm = nextP()
    for g in range(8):
        wb = wua[g % 2][:, :, :]
        wb2 = wub[g % 2][:, :, :]
        k.dma(wb, d_wmod.rearrange("(k p) c -> p k c", p=128)[:, :, g * 512:g * 512 + 256], q="pool")
        k.dma(wb2, d_wmod.rearrange("(k p) c -> p k c", p=128)[:, :, g * 512 + 256:(g + 1) * 512], q="pool")
        for c4 in range(4):
            cc = g * 4 + c4
            src = wb if c4 < 2 else wb2
            for kk in range(8):
                k.matmul(pm[:, cc * 2:cc * 2 + 2], src[:, kk, (c4 % 2) * 128:(c4 % 2 + 1) * 128], scb[:, kk, :],
                         start=(kk == 0), stop=(kk == 7))
    pmv = pm[:, 0:64].rearrange("p (c j) -> p c j", j=2)
    for j in range(2):
        k.tt(modv[:, :, j], pmv[:, :, j], bmod[:], ALU.add)
    for j in range(2):
        k.ts(gm2[:, :, j], modv[:, 16:24, j], 1.0, ALU.add)
        k.tt(gm2[:, :, j], gm2[:, :, j], n2g[:], ALU.mult)
    load_cast_rows(k, wout, d_wout, KF, D, split=4)
    load_cast_rows(k, wdn, d_wdn, NFF, D, split=4)
    wupv = d_wup.rearrange("(k p) c -> p k c", p=128)
    wctr = [0]
    k.dma(wua[0][:], wupv[:, :, 0:256], q="pool")
    k.dma(wub[0][:], wupv[:, :, DFF:DFF + 256], q="pool")

    for si, (n, j, kind, t0) in enumerate(segs):
        cols = n + 2
        tiles = col_tiles(cols)
        xs, T = xsrc[kind]; os_, oc0, _ = osrc[kind]
        lo = max(t0 - 1, 0); hi = min(t0 + n + 1, T)
        c_lo = lo - (t0 - 1); c_hi = hi - (t0 - 1)
        if c_lo > 0:
            k.memset(x1[:, :, 0:1], 0.0); k.memset(oT[:, :, 0:1], 0.0)
        if c_hi < cols:
            k.memset(x1[:, :, cols - 1:cols], 0.0); k.memset(oT[:, :, cols - 1:cols], 0.0)
        k.dma(x1[:, :, c_lo:c_hi], xs.rearrange("(k p) t -> p k t", p=128)[:, :, lo:hi])
        k.dma(oT[:, :, c_lo:c_hi], os_.rearrange("(k p) t -> p k t", p=128)[:, :, oc0 + lo:oc0 + hi])
        for fc in range(8):
            for (a, b) in tiles:
                p = nextP()
                for kk in range(KF):
                    k.matmul(p[:, 0:b - a], wout[:, kk, fc * 128:(fc + 1) * 128], oT[:, kk, a:b],
                             start=(kk == 0), stop=(kk == KF - 1))
                k.stt(x1[:, fc, a:b], p[:, 0:b - a], modv[:, 0 + fc, j:j + 1], x1[:, fc, a:b], ALU.mult, ALU.add)
        for kk in range(8):
            k.act(sq[:, kk, 0:cols], x1[:, kk, 0:cols], AF.Square)
        for (a, b) in tiles:
            p = nextP()
            for kk in range(8):
                k.matmul(p[:, 0:b - a], ones_bf[:], sq[:, kk, a:b], start=(kk == 0), stop=(kk == 7))
            k.ts(tmp[:, a:b], p[:, 0:b - a], 1.0 / D, ALU.mult, EPS, ALU.add)
        k.act(tmp[:, 0:cols], tmp[:, 0:cols], AF.Sqrt)
        k.recip(rstd[:, 0:cols], tmp[:, 0:cols])
        for kk in range(8):
            tb = tmp if kk % 2 == 0 else tmpb
            k.stt(tb[:, 0:cols], x1[:, kk, 0:cols], gm2[:, kk, j:j + 1], rstd[:, 0:cols], ALU.mult, ALU.mult)
            k.act(h2[:, kk, 0:cols], tb[:, 0:cols], AF.Identity, bias=modv[:, 8 + kk, j:j + 1], scale=1.0)
        if c_lo > 0:
            k.memset(h2[:, :, 0:1], 0.0)
        if c_hi < cols:
            k.memset(h2[:, :, cols - 1:cols], 0.0)
        for g in range(11):
            wa = wua[wctr[0] % 2]; wb_ = wub[wctr[0] % 2]
            wctr[0] += 1
            gn = g + 1 if g < 10 else (0 if si + 1 < len(segs) else None)
            if gn is not None:
                k.dma(wua[wctr[0] % 2][:], wupv[:, :, gn * 256:(gn + 1) * 256], q="pool")
                k.dma(wub[wctr[0] % 2][:], wupv[:, :, DFF + gn * 256:DFF + (gn + 1) * 256], q="pool")
            for c2 in range(2):
                c = g * 2 + c2
                ub = u[c % 2]
                for half, w in ((0, wa), (1, wb_)):
                    for (a, b) in tiles:
                        p = nextP()
                        for kk in range(8):
                            k.matmul(p[:, 0:b - a], w[:, kk, c2 * 128:(c2 + 1) * 128], h2[:, kk, a:b],
                                     start=(kk == 0), stop=(kk == 7))
                        k.copy(ub[:, half, a:b], p[:, 0:b - a])
                ca = c; cbi = NFF + c
                va = vaL[c % 2]; vb = vbL[c % 2]; sa = saL[c % 2]
                k.act(va[:, 0:n], ub[:, 0, 1:n + 1], AF.Identity, bias=cb[:, ca:ca + 1], scale=cw[:, 1, ca:ca + 1])
                k.stt(va[:, 0:n], ub[:, 0, 0:n], cw[:, 0, ca:ca + 1], va[:, 0:n], ALU.mult, ALU.add)
                k.stt(va[:, 0:n], ub[:, 0, 2:n + 2], cw[:, 2, ca:ca + 1], va[:, 0:n], ALU.mult, ALU.add)
                k.act(vb[:, 0:n], ub[:, 1, 1:n + 1], AF.Identity, bias=cb[:, cbi:cbi + 1], scale=cw[:, 1, cbi:cbi + 1])
                k.stt(vb[:, 0:n], ub[:, 1, 0:n], cw[:, 0, cbi:cbi + 1], vb[:, 0:n], ALU.mult, ALU.add)
                k.stt(vb[:, 0:n], ub[:, 1, 2:n + 2], cw[:, 2, cbi:cbi + 1], vb[:, 0:n], ALU.mult, ALU.add)
                k.act(sa[:, 0:n], va[:, 0:n], AF.Silu)
                k.tt(tT[:, c, 0:n], sa[:, 0:n], vb[:, 0:n], ALU.mult, eng="pool")
        for fc in range(8):
            p = nextP()
            for c in range(NFF):
                k.matmul(p[:, 0:n], wdn[:, c, fc * 128:(fc + 1) * 128], tT[:, c, 0:n], start=(c == 0), stop=(c == NFF - 1))
            k.stt(x1[:, fc, 1:n + 1], p[:, 0:n], modv[:, 24 + fc, j:j + 1], x1[:, fc, 1:n + 1], ALU.mult, ALU.add)
        if last:
            for kk in range(8):
                k.act(sq[:, kk, 0:n], x1[:, kk, 1:n + 1], AF.Square)
            p = nextP()
            for kk in range(8):
                k.matmul(p[:, 0:n], ones_bf[:], sq[:, kk, 0:n], start=(kk == 0), stop=(kk == 7))
            k.ts(tmp[:, 0:n], p[:, 0:n], 1.0 / D, ALU.mult, EPS, ALU.add)
            k.act(tmp[:, 0:n], tmp[:, 0:n], AF.Sqrt)
            k.recip(rstd[:, 0:n], tmp[:, 0:n])
            for kk in range(8):
                k.stt(x1[:, kk, 1:n + 1], x1[:, kk, 1:n + 1], fg[:, kk:kk + 1], rstd[:, 0:n], ALU.mult, ALU.mult)
        k.dma(ydst[kind].rearrange("(k p) t -> p k t", p=128)[:, :, t0:t0 + n], x1[:, :, 1:n + 1])
    k.close_scope()


def build_fused():
    k = KB(); nc = k.nc
    e = make_env(k)
    d_xT = k.dram("xT", [D, 4096]); d_xcT = k.dram("xcT", [D, 256]); d_cvec = k.dram("cvec", [128, 8, 2])
    cn = {nm: k.dram(nm, [128, 128]) for nm in ("mask_f", "mask_b", "dm_f", "dm_b", "row_f", "row_b")}
    d_pcols = k.dram("pcols", [128, 4]); d_cos = k.dram("cos", [128, 32, 128]); d_sin = k.dram("sin", [128, 32, 128])
    L = []
    for l in range(2):
        L.append(dict(wmodA=k.dram("wmodA%d" % l, [D, 2048]), bmodA=k.dram("bmodA%d" % l, [128, 16]), n1g=k.dram("n1g%d" % l, [128, 8]),
                      wmodB=k.dram("wmodB%d" % l, [D, 4096]), bmodB=k.dram("bmodB%d" % l, [128, 32]), n2g=k.dram("n2g%d" % l, [128, 8]),
                      cw=k.dram("convw%d" % l, [128, 3, 44]), cb=k.dram("convb%d" % l, [128, 44]),
                      wout=k.dram("wout%d" % l, [1024 * (l + 1), D]), wup=k.dram("wup%d" % l, [D, 2 * DFF]), wdn=k.dram("wdn%d" % l, [DFF, D])))
    d_fg = k.dram("fg", [128, 8])
    d_wna = k.dram("wna", [2, D, 768]); d_wgl = k.dram("wgl", [2, D, 768]); d_wa = k.dram("wa", [D, 32])
    d_waaug = k.dram("waaug", [17, 2, 256]); d_rpbT = k.dram("rpbT", [2, 128, 4, 15, 64]); d_colmask = k.dram("colmask", [128, 64])
    d_gng = k.dram("gng", [128, 128])
    d_win = k.dram("win", [8, D, 768]); d_dec = k.dram("dec", [128, 16]); d_ng = k.dram("ng", [128, 8, 256])
    d_y = k.dram("yT", [D, 4096], kind="ExternalOutput")
    o0T = nc.dram_tensor("o0T", [1024, NT * 128], BF16, kind="Internal").ap()
    x2T = nc.dram_tensor("x2T", [D, 4096], F32, kind="Internal").ap()
    xc2T = nc.dram_tensor("xc2T", [D, 256], F32, kind="Internal").ap()
    o1T = nc.dram_tensor("o1T", [2048, 4096], BF16, kind="Internal").ap()
    xcdump = nc.dram_tensor("xcdump", [D, 256], F32, kind="Internal").ap()

    k.open_scope()
    hT = k.sb("hT0", [128, 8, NT * 128], BF16)
    phase_hT(e, "a0", d_xT, d_xcT, d_cvec, L[0]["wmodA"], L[0]["bmodA"], L[0]["n1g"], hT)
    for hh in range(2):
        phase_NA(e, "na%d" % hh, hT, d_wna[hh], d_rpbT[hh], d_colmask, o0T, hh * 512)
        phase_GLA(e, "gl%d" % hh, hT, d_wgl[hh], d_wa, d_waaug, d_gng, cn["mask_f"], cn["mask_b"], o0T, hh * 512 + 256,
                  [(0, hh * 2, 0), (1, hh * 2 + 1, 128)])
    k.close_scope()
    segs0 = [(512, 0, "lat", s * 512) for s in range(8)] + [(256, 1, "ctx", 0)]
    phase_F(e, "f0", 1024, False, segs0,
            {"lat": (d_xT, 4096), "ctx": (d_xcT, 256)}, {"lat": (o0T, 256, 4096), "ctx": (o0T, 0, 256)},
            {"lat": x2T, "ctx": xc2T}, d_cvec, L[0]["wmodB"], L[0]["bmodB"], L[0]["n2g"], d_fg, L[0]["cw"], L[0]["cb"],
            L[0]["wout"], L[0]["wup"], L[0]["wdn"])
    k.open_scope()
    hT = k.sb("hT1", [128, 8, NT * 128], BF16)
    phase_hT(e, "a1", x2T, xc2T, d_cvec, L[1]["wmodA"], L[1]["bmodA"], L[1]["n1g"], hT)
    phase_RET(e, "rt", hT, d_win, d_dec, d_ng, d_cos, d_sin, cn, d_pcols, o1T, 8)
    k.close_scope()
    segs1 = [(512, 0, "lat", s * 512) for s in range(8)]
    phase_F(e, "f1", 2048, True, segs1,
            {"lat": (x2T, 4096)}, {"lat": (o1T, 0, 4096)}, {"lat": d_y},
            d_cvec, L[1]["wmodB"], L[1]["bmodB"], L[1]["n2g"], d_fg, L[1]["cw"], L[1]["cb"],
            L[1]["wout"], L[1]["wup"], L[1]["wdn"])
    ncc = k.finish([d_y])
    return ncc, k


_PROG = []


def _maps(inp):
    B = 4
    shared = {}
    shared["ident"] = np.eye(128, dtype=np.float32)
    for kk in ("mask_f", "mask_b", "dm_f", "dm_b", "row_f", "row_b", "pcols"):
        shared[kk] = _C[kk]
    shared["cos"] = np.ascontiguousarray(np.concatenate([_COS, _COS], axis=2)); shared["sin"] = np.ascontiguousarray(np.concatenate([_SIN, _SIN], axis=2))
    for l in range(2):
        shared["wmodA%d" % l] = np.ascontiguousarray(inp["w_mod"][l][:, 0:2048])
        shared["bmodA%d" % l] = pk(inp["b_mod"][l][0:2048])
        shared["n1g%d" % l] = pk(inp["norm1_g"][l])
        shared["wmodB%d" % l] = np.ascontiguousarray(inp["w_mod"][l][:, 2048:6144])
        shared["bmodB%d" % l] = pk(inp["b_mod"][l][2048:6144])
        shared["n2g%d" % l] = pk(inp["norm2_g"][l])
        cw = inp["ffn_conv_w"][l]
        shared["convw%d" % l] = np.ascontiguousarray(np.stack([pk(cw[i]) for i in range(3)], axis=1))
        shared["convb%d" % l] = pk(inp["ffn_conv_b"][l])
        shared["wup%d" % l] = inp["ffn_w_up"][l]; shared["wdn%d" % l] = inp["ffn_w_down"][l]
    perm = np.concatenate([np.arange(0, 256), np.arange(512, 768), np.arange(256, 512), np.arange(768, 1024)])
    shared["wout0"] = np.ascontiguousarray(inp["na_gla_w_out"][0][perm])
    shared["wout1"] = np.ascontiguousarray(inp["ret_w_out"][0])
    shared["fg"] = pk(inp["final_norm_g"])
    w = inp["na_gla_w_in"][0]
    wna = []; wgl = []; rp = []
    for hh in range(2):
        nh = [hh * 4 + i for i in range(4)]; gh = [hh * 2 + i for i in range(2)]
        wna.append(np.concatenate([w[:, h * 64:(h + 1) * 64] for h in nh] + [w[:, 512 + h * 64:512 + (h + 1) * 64] for h in nh] +
                                  [w[:, 1024 + h * 64:1024 + (h + 1) * 64] for h in nh], axis=1))
        wgl.append(np.concatenate([w[:, 1536 + h * 64:1536 + (h + 1) * 64] for h in gh] + [w[:, 1792 + h * 64:1792 + (h + 1) * 64] for h in gh] +
                                  [w[:, 2560 + h * 128:2560 + (h + 1) * 128] for h in gh] + [w[:, 2048 + h * 128:2048 + (h + 1) * 128] for h in gh], axis=1))
        r_, cm = _na_tables(inp["na_rpb"][0][nh]); rp.append(r_)
    shared["wna"] = np.ascontiguousarray(np.stack(wna)); shared["wgl"] = np.ascontiguousarray(np.stack(wgl))
    shared["rpbT"] = np.ascontiguousarray(np.stack(rp)); shared["colmask"] = cm
    shared["wa"] = np.ascontiguousarray(w[:, 3072:3104])
    wa = np.zeros((17, 2, 256), np.float32)
    wa[0:16, 0] = inp["gla_w_a_fwd"][0]; wa[16, 0] = inp["gla_b_a_fwd"][0]
    wa[0:16, 1] = inp["gla_w_a_bwd"][0]; wa[16, 1] = inp["gla_b_a_bwd"][0]
    shared["waaug"] = wa
    shared["gng"] = np.ascontiguousarray(np.broadcast_to(inp["gla_norm_g"][0], (128, 128)).astype(np.float32))
    wr = inp["ret_w_in"][0]
    shared["win"] = np.ascontiguousarray(np.stack([np.concatenate([
        wr[:, h * 128:(h + 1) * 128], wr[:, 1024 + h * 128:1024 + (h + 1) * 128],
        wr[:, 4096 + h * 256:4096 + (h + 1) * 256], wr[:, 2048 + h * 256:2048 + (h + 1) * 256]], axis=1) for h in range(8)]))
    dec = np.concatenate([inp["ret_decay_fwd"][0], inp["ret_decay_bwd"][0]])
    shared["dec"] = np.ascontiguousarray(np.broadcast_to(dec, (128, 16)).astype(np.float32))
    shared["ng"] = np.ascontiguousarray(np.broadcast_to(inp["ret_norm_g"][0], (128, 8, 256)).astype(np.float32))
    maps = []
    for core in range(8):
        b = core // 2
        m = dict(shared)
        m["xT"] = np.ascontiguousarray(inp["x"][b].T); m["xcT"] = np.ascontiguousarray(inp["ctx"][b].T)
        m["cvec"] = np.ascontiguousarray(np.stack([pk(inp["c"][b]), pk(inp["c_ctx"])], axis=-1).astype(np.float32))
        maps.append(m)
    return maps


def kernel(**inp):
    inp = {k_: np.asarray(v) for k_, v in inp.items()}
    if not _PROG:
        _PROG.append(build_fused()[0])
    res = run_bass_kernel_spmd(_PROG[0], _maps(inp), core_ids=list(range(8))).results
    out = np.empty((4, 4096, 1024), np.float32)
    for b in range(4):
        out[b, 0:2048] = res[2 * b]["yT"][:, 0:2048].T
        out[b, 2048:4096] = res[2 * b + 1]["yT"][:, 2048:4096].T
    return out
```

```python
import os
import ml_dtypes
from concourse.bass_utils import run_bass_kernel_spmd

from contextlib import ExitStack
import numpy as np
import concourse.bass as bass
import concourse.mybir as mybir

F32 = mybir.dt.float32
BF16 = mybir.dt.bfloat16
AF = mybir.ActivationFunctionType
ALU = mybir.AluOpType
AX = mybir.AxisListType

ENGS = ("pe", "act", "dve", "pool", "sp")
NDSEM = 12


def _region(ap):
    t = ap.tensor
    name = t.name
    dims = list(ap.ap)
    off = int(ap.offset)
    sp = str(ap.space) if hasattr(ap, "space") else ""
    if "DRAM" in sp.upper() or type(t).__name__.startswith("DRam"):
        ext = sum((int(c) - 1) * abs(int(s)) for s, c in dims)
        return (name, 0, 1, off, off + ext + 1)
    if type(t).__name__.startswith("PSum"):
        return (name, 0, 128, 0, 1 << 40)
    pstep, pcnt = int(dims[0][0]), int(dims[0][1])
    if pstep == 0:
        pstep = 1 << 40
    p0 = off // pstep
    f0 = off % pstep
    ext = sum((int(c) - 1) * abs(int(s)) for s, c in dims[1:])
    return (name, p0, p0 + pcnt, f0, f0 + ext + 1)


def _overlap(a, b):
    return a[1] < b[2] and b[1] < a[2] and a[3] < b[4] and b[3] < a[4]


def _covers(a, b):
    return a[1] <= b[1] and a[2] >= b[2] and a[3] <= b[3] and a[4] >= b[4]


class KB:
    def __init__(self):
        self.nc = bass.Bass("TRN2", target_bir_lowering=False)
        self.es = ExitStack()
        self.ops = []
        self.recs = {}
        self.n_alloc = 0
        self.fence = None
        self.fenced = set()
        self.stack = [self.es]

    def sb(self, name, shape, dt=F32):
        return self.stack[-1].enter_context(self.nc.sbuf_tensor(name, list(shape), dt))

    def barrier(self):
        last = {}
        f = set()
        for i, o in enumerate(self.ops):
            if o["dma"]:
                f.add(i)
            else:
                last[o["eng"]] = i
        f.update(last.values())
        if self.fence is not None:
            f = {i for i in f if i > self.fence_at or not self.ops[i]["dma"]}
        self.fence = f
        self.fence_at = len(self.ops)
        self.fenced = set()

    def open_scope(self):
        self.stack.append(ExitStack())

    def close_scope(self):
        self.barrier()
        self.stack.pop().close()

    def ps(self, name, shape, dt=F32):
        return self.es.enter_context(self.nc.psum_tensor(name, list(shape), dt))

    def dram(self, name, shape, dt=F32, kind="ExternalInput"):
        return self.nc.dram_tensor(name, list(shape), dt, kind=kind).ap()

    def op(self, eng, fn, reads, writes, dma=False):
        idx = len(self.ops)
        deps = set()
        rr = [_region(a) for a in reads if a is not None and hasattr(a, "tensor")]
        ww = [_region(a) for a in writes if a is not None and hasattr(a, "tensor")]
        for r in rr:
            for (g, oi, isw) in self.recs.get(r[0], ()):
                if isw and _overlap(r, g):
                    deps.add(oi)
        for w in ww:
            for (g, oi, isw) in self.recs.get(w[0], ()):
                if _overlap(w, g):
                    deps.add(oi)
        for w in ww:
            lst = self.recs.setdefault(w[0], [])
            lst[:] = [x for x in lst if not _covers(w, x[0])]
            lst.append((w, idx, True))
        for r in rr:
            lst = self.recs.setdefault(r[0], [])
            lst[:] = [x for x in lst if not ((not x[2]) and x[1] < idx and self.ops[x[1]]["eng"] == eng
                                             and not self.ops[x[1]]["dma"] and not dma and _covers(r, x[0]))]
            lst.append((r, idx, False))
        if self.fence is not None and eng not in self.fenced:
            deps.update(self.fence)
            self.fenced.add(eng)
        deps.discard(idx)
        self.ops.append(dict(eng=eng, fn=fn, deps=deps, dma=dma, rr=rr, ww=ww))
        return idx

    def dma(self, out, in_, q="sp"):
        return self.op(q, lambda e: e.dma_start(out=out, in_=in_), [in_], [out], dma=True)

    def matmul(self, out, lhsT, rhs, start=True, stop=True):
        return self.op("pe", lambda e: e.matmul(out, lhsT, rhs, start=start, stop=stop), [lhsT, rhs], [out])

    def transpose(self, out, in_, ident):
        return self.op("pe", lambda e: e.transpose(out, in_, ident), [in_, ident], [out])

    def act(self, out, in_, func, bias=None, scale=None, accum_out=None, eng="act"):
        kw = {}
        if bias is not None:
            kw["bias"] = bias
        if scale is not None:
            kw["scale"] = scale
        if accum_out is not None:
            kw["accum_out"] = accum_out
        return self.op(eng, lambda e: e.activation(out, in_, func, **kw), [in_, bias, scale], [out, accum_out])

    def tt(self, out, in0, in1, op, eng="dve"):
        return self.op(eng, lambda e: e.tensor_tensor(out, in0, in1, op), [in0, in1], [out])

    def ts(self, out, in0, s1, op0, s2=None, op1=None, accum_out=None, eng="dve"):
        def f(e):
            kw = {}
            if accum_out is not None:
                kw["accum_out"] = accum_out
            if op1 is None:
                return e.tensor_scalar(out, in0, s1, None, op0, **kw)
            return e.tensor_scalar(out, in0, s1, s2, op0, op1, **kw)
        return self.op(eng, f, [in0, s1, s2], [out, accum_out])

    def stt(self, out, in0, scalar, in1, op0, op1, eng="dve"):
        return self.op(eng, lambda e: e.scalar_tensor_tensor(out, in0, scalar, in1, op0, op1), [in0, scalar, in1], [out])

    def copy(self, out, in_, eng="dve"):
        if eng == "act":
            return self.op("act", lambda e: e.copy(out, in_), [in_], [out])
        return self.op(eng, lambda e: e.tensor_copy(out, in_), [in_], [out])

    def memset(self, ap, val, eng="dve"):
        return self.op(eng, lambda e: e.memset(ap, val), [], [ap])

    def recip(self, out, in_):
        return self.op("dve", lambda e: e.reciprocal(out, in_), [in_], [out])

    def reduce(self, out, in_, op=ALU.add, axis=AX.X, eng="dve"):
        return self.op(eng, lambda e: e.tensor_reduce(out, in_, axis, op), [in_], [out])

    def finish(self, out_aps):
        nc = self.nc
        ops = self.ops
        out_names = {a.tensor.name for a in out_aps}
        final_deps = set()
        for i, o in enumerate(ops):
            if o["dma"] and any(w[0] in out_names for w in o["ww"]):
                final_deps.add(i)
        ops.append(dict(eng="sp", fn=None, deps=final_deps, dma=False, rr=[], ww=[]))
        needs_sig = [False] * len(ops)
        for o in ops:
            for d in o["deps"]:
                if ops[d]["dma"]:
                    continue
                if ops[d]["eng"] == o["eng"] and not o["dma"] and o["eng"] == "pe":
                    continue
                needs_sig[d] = True
        sems = {e: self.es.enter_context(nc.semaphore("s_" + e)) for e in ENGS}
        dsems = {e: [self.es.enter_context(nc.semaphore("d_%s_%d" % (e, i))) for i in range(NDSEM)]
                 for e in ("sp", "act", "pool")}
        cnt = {e: 0 for e in ENGS}
        dcnt = {e: 0 for e in dsems}
        sig = [None] * len(ops)
        prevdma = [None] * len(ops)
        for i, o in enumerate(ops):
            if o["dma"]:
                q = o["eng"]
                n = dcnt[q]
                dcnt[q] += 1
                s = dsems[q][n % NDSEM]
                sig[i] = (s, 16 * (n // NDSEM + 1))
                if n >= NDSEM:
                    prevdma[i] = (s, 16 * (n // NDSEM))
            elif needs_sig[i]:
                cnt[o["eng"]] += 1
                sig[i] = (sems[o["eng"]], cnt[o["eng"]])
        per_eng = {e: [] for e in ENGS}
        for i, o in enumerate(ops):
            per_eng[o["eng"]].append(i)
        self.stats = {e: len(per_eng[e]) for e in ENGS}
        self.stats["sig"] = dict(cnt)

        def emit(ename):
            def body(eng):
                seen = {}
                for i in per_eng[ename]:
                    o = ops[i]
                    waits = {}
                    for d in o["deps"]:
                        po = ops[d]
                        if (not po["dma"]) and po["eng"] == ename and ename == "pe" and not o["dma"]:
                            continue
                        s, v = sig[d]
                        key = id(s)
                        if waits.get(key, (None, 0))[1] < v:
                            waits[key] = (s, v)
                    if prevdma[i] is not None:
                        s, v = prevdma[i]
                        key = id(s)
                        if waits.get(key, (None, 0))[1] < v:
                            waits[key] = (s, v)
                    for key, (s, v) in waits.items():
                        if seen.get(key, 0) >= v:
                            continue
                        eng.wait_ge(s, v)
                        seen[key] = v
                    if o["fn"] is None:
                        continue
                    ins = o["fn"](eng)
                    if sig[i] is not None:
                        s, v = sig[i]
                        ins.then_inc(s, 16 if o["dma"] else 1)
            return body

        with nc.Block() as block:
            block.tensor(emit("pe"))
            block.scalar(emit("act"))
            block.vector(emit("dve"))
            block.gpsimd(emit("pool"))
            block.sync(emit("sp"))
        self.es.close()
        return nc

D = 1024; DFF = 2816; NFF = 22; EPS = 1e-6; NT = 34


def col_tiles(cols):
    nt = (cols + 511) // 512
    base = cols // nt
    res = []; s = 0
    for i in range(nt):
        e = s + base + (1 if i < cols % nt else 0)
        res.append((s, e)); s = e
    return res


def load_cast_rows(k, dst, src, nk, width, q="pool", split=1):
    v = src.rearrange("(k p) c -> p k c", p=128)
    step = max(1, nk // split)
    for a in range(0, nk, step):
        b = min(nk, a + step)
        k.dma(dst[:, a:b, :], v[:, a:b, :], q=q)


def build_F(layer_has_ctx, Fdim, last, segs):
    k = KB()
    KF = Fdim // 128
    NMAX = max(n for n, _ in segs); CMAX = NMAX + 2
    d_x = [k.dram("xT_%d" % i, [D, n + 2]) for i, (n, _) in enumerate(segs)]
    d_o = [k.dram("oT_%d" % i, [Fdim, n + 2], BF16) for i, (n, _) in enumerate(segs)]
    d_hm = [k.dram("hm_%d" % i, [128, 2]) for i, (n, _) in enumerate(segs)]
    d_y = [k.dram("yT_%d" % i, [D, n], kind="ExternalOutput") for i, (n, _) in enumerate(segs)]
    d_cvec = k.dram("cvec", [128, 8, 2])
    d_wmod = k.dram("wmod", [D, 4096]); d_bmod = k.dram("bmod", [128, 32])
    d_n2g = k.dram("n2g", [128, 8]); d_fg = k.dram("fg", [128, 8])
    d_cw = k.dram("convw", [128, 3, 44]); d_cb = k.dram("convb", [128, 44])
    d_wout = k.dram("wout", [Fdim, D]); d_wup = k.dram("wup", [D, 2 * DFF]); d_wdn = k.dram("wdn", [DFF, D])
    ones_bf = k.sb("ones_bf", [128, 128], BF16)
    cvec = k.sb("cvec_s", [128, 8, 2]); scb = k.sb("scb", [128, 8, 2], BF16)
    bmod = k.sb("bmod_s", [128, 32]); modv = k.sb("modv", [128, 32, 2])
    n2g = k.sb("n2g_s", [128, 8]); fg = k.sb("fg_s", [128, 8]); gm2 = k.sb("gm2", [128, 8, 2])
    cw = k.sb("cw_s", [128, 3, 44]); cb = k.sb("cb_s", [128, 44])
    wmb = [k.sb("wmb%d" % i, [128, 8, 512], BF16) for i in range(2)]
    wout = k.sb("wout_s", [128, KF, D], BF16)
    wdn = k.sb("wdn_s", [128, NFF, D], BF16)
    oT = k.sb("oT_s", [128, KF, CMAX], BF16)
    x1 = k.sb("x1T", [128, 8, CMAX])
    sq = k.sb("sq", [128, 8, CMAX], BF16)
    rstd = k.sb("rstd", [128, CMAX]); tmp = k.sb("tmp", [128, CMAX])
    h2 = k.sb("h2T", [128, 8, CMAX], BF16)
    hm = k.sb("hm_s", [128, 2])
    wua = [k.sb("wua%d" % i, [128, 8, 256], BF16) for i in range(2)]
    wub = [k.sb("wub%d" % i, [128, 8, 256], BF16) for i in range(2)]
    u = [k.sb("u%d" % i, [128, 2, CMAX]) for i in range(2)]
    va = k.sb("va", [128, NMAX]); vb = k.sb("vb", [128, NMAX]); sa = k.sb("sa", [128, NMAX])
    tT = k.sb("tT", [128, NFF, NMAX], BF16)
    P = [k.ps("P%d" % i, [128, 512]) for i in range(8)]
    pctr = [0]

    def nextP():
        p = P[pctr[0] % 8]; pctr[0] += 1
        return p

    k.memset(ones_bf[:], 1.0)
    k.dma(cvec[:], d_cvec); k.dma(bmod[:], d_bmod); k.dma(n2g[:], d_n2g); k.dma(fg[:], d_fg)
    k.dma(cw[:], d_cw); k.dma(cb[:], d_cb)
    k.act(scb[:], cvec[:], AF.Silu)
    pm = nextP()
    for g in range(8):
        wb = wmb[g % 2]
        k.dma(wb[:], d_wmod.rearrange("(k p) c -> p k c", p=128)[:, :, g * 512:(g + 1) * 512], q="pool")
        for c4 in range(4):
            cc = g * 4 + c4
            for kk in range(8):
                k.matmul(pm[:, cc * 2:cc * 2 + 2], wb[:, kk, c4 * 128:(c4 + 1) * 128], scb[:, kk, :],
                         start=(kk == 0), stop=(kk == 7))
    pmv = pm[:, 0:64].rearrange("p (c j) -> p c j", j=2)
    for j in range(2):
        k.tt(modv[:, :, j], pmv[:, :, j], bmod[:], ALU.add)
    for j in range(2):
        k.ts(gm2[:, :, j], modv[:, 16:24, j], 1.0, ALU.add)
        k.tt(gm2[:, :, j], gm2[:, :, j], n2g[:], ALU.mult)
    load_cast_rows(k, wout, d_wout, KF, D, split=4)
    load_cast_rows(k, wdn, d_wdn, NFF, D, split=4)
    wupv = d_wup.rearrange("(k p) c -> p k c", p=128)

    for si, (n, j) in enumerate(segs):
        cols = n + 2
        tiles = col_tiles(cols)
        k.dma(x1[:, :, 0:cols], d_x[si].rearrange("(k p) t -> p k t", p=128))
        k.dma(oT[:, :, 0:cols], d_o[si].rearrange("(k p) t -> p k t", p=128))
        k.dma(hm[:], d_hm[si])
        for fc in range(8):
            for (a, b) in tiles:
                p = nextP()
                for kk in range(KF):
                    k.matmul(p[:, 0:b - a], wout[:, kk, fc * 128:(fc + 1) * 128], oT[:, kk, a:b],
                             start=(kk == 0), stop=(kk == KF - 1))
                k.stt(x1[:, fc, a:b], p[:, 0:b - a], modv[:, 0 + fc, j:j + 1], x1[:, fc, a:b], ALU.mult, ALU.add)
        for kk in range(8):
            k.act(sq[:, kk, 0:cols], x1[:, kk, 0:cols], AF.Square)
        for (a, b) in tiles:
            p = nextP()
            for kk in range(8):
                k.matmul(p[:, 0:b - a], ones_bf[:], sq[:, kk, a:b], start=(kk == 0), stop=(kk == 7))
            k.act(tmp[:, a:b], p[:, 0:b - a], AF.Sqrt, bias=EPSB[0], scale=1.0 / D)
        k.recip(rstd[:, 0:cols], tmp[:, 0:cols])
        for kk in range(8):
            k.stt(tmp[:, 0:cols], x1[:, kk, 0:cols], gm2[:, kk, j:j + 1], rstd[:, 0:cols], ALU.mult, ALU.mult)
            k.act(h2[:, kk, 0:cols], tmp[:, 0:cols], AF.Identity, bias=modv[:, 8 + kk, j:j + 1], scale=1.0)
        k.ts(h2[:, :, 0:1], h2[:, :, 0:1], hm[:, 0:1], ALU.mult)
        k.ts(h2[:, :, cols - 1:cols], h2[:, :, cols - 1:cols], hm[:, 1:2], ALU.mult)
        for g in range(11):
            wa = wua[g % 2]; wb_ = wub[g % 2]
            k.dma(wa[:], wupv[:, :, g * 256:(g + 1) * 256], q="pool")
            k.dma(wb_[:], wupv[:, :, DFF + g * 256:DFF + (g + 1) * 256], q="pool")
            for c2 in range(2):
                c = g * 2 + c2
                ub = u[c % 2]
                for half, w in ((0, wa), (1, wb_)):
                    for (a, b) in tiles:
                        p = nextP()
                        for kk in range(8):
                            k.matmul(p[:, 0:b - a], w[:, kk, c2 * 128:(c2 + 1) * 128], h2[:, kk, a:b],
                                     start=(kk == 0), stop=(kk == 7))
                        k.copy(ub[:, half, a:b], p[:, 0:b - a], eng="act")
                ca = c; cbi = NFF + c
                k.act(va[:, 0:n], ub[:, 0, 1:n + 1], AF.Identity, bias=cb[:, ca:ca + 1], scale=cw[:, 1, ca:ca + 1])
                k.stt(va[:, 0:n], ub[:, 0, 0:n], cw[:, 0, ca:ca + 1], va[:, 0:n], ALU.mult, ALU.add)
                k.stt(va[:, 0:n], ub[:, 0, 2:n + 2], cw[:, 2, ca:ca + 1], va[:, 0:n], ALU.mult, ALU.add)
                k.act(vb[:, 0:n], ub[:, 1, 1:n + 1], AF.Identity, bias=cb[:, cbi:cbi + 1], scale=cw[:, 1, cbi:cbi + 1])
                k.stt(vb[:, 0:n], ub[:, 1, 0:n], cw[:, 0, cbi:cbi + 1], vb[:, 0:n], ALU.mult, ALU.add)
                k.stt(vb[:, 0:n], ub[:, 1, 2:n + 2], cw[:, 2, cbi:cbi + 1], vb[:, 0:n], ALU.mult, ALU.add)
                k.act(sa[:, 0:n], va[:, 0:n], AF.Silu)
                k.tt(tT[:, c, 0:n], sa[:, 0:n], vb[:, 0:n], ALU.mult)
        for fc in range(8):
            p = nextP()
            for c in range(NFF):
                k.matmul(p[:, 0:n], wdn[:, c, fc * 128:(fc + 1) * 128], tT[:, c, 0:n], start=(c == 0), stop=(c == NFF - 1))
            k.stt(x1[:, fc, 1:n + 1], p[:, 0:n], modv[:, 24 + fc, j:j + 1], x1[:, fc, 1:n + 1], ALU.mult, ALU.add)
        if last:
            for kk in range(8):
                k.act(sq[:, kk, 0:n], x1[:, kk, 1:n + 1], AF.Square)
            p = nextP()
            for kk in range(8):
                k.matmul(p[:, 0:n], ones_bf[:], sq[:, kk, 0:n], start=(kk == 0), stop=(kk == 7))
            k.act(tmp[:, 0:n], p[:, 0:n], AF.Sqrt, bias=EPSB[0], scale=1.0 / D)
            k.recip(rstd[:, 0:n], tmp[:, 0:n])
            for kk in range(8):
                k.stt(x1[:, kk, 1:n + 1], x1[:, kk, 1:n + 1], fg[:, kk:kk + 1], rstd[:, 0:n], ALU.mult, ALU.mult)
        k.dma(d_y[si].rearrange("(k p) t -> p k t", p=128), x1[:, :, 1:n + 1])
    nc = k.finish(d_y)
    return nc, k

EPSB = [EPS]


NT = 34


def emit_mod(k, nextP, d_cvec, d_wmod, d_bmod, ncols, wmb, name="m"):
    ncc = ncols // 128
    cvec = k.sb(name + "cvec", [128, 8, 2]); scb = k.sb(name + "scb", [128, 8, 2], BF16)
    bmod = k.sb(name + "bmod", [128, ncc]); modv = k.sb(name + "modv", [128, ncc, 2])
    k.dma(cvec[:], d_cvec); k.dma(bmod[:], d_bmod)
    k.act(scb[:], cvec[:], AF.Silu)
    pm = nextP()
    for g in range(ncols // 512):
        wb = wmb[g % len(wmb)]
        k.dma(wb[:], d_wmod.rearrange("(k p) c -> p k c", p=128)[:, :, g * 512:(g + 1) * 512], q="pool")
        for c4 in range(4):
            cc = g * 4 + c4
            for kk in range(8):
                k.matmul(pm[:, cc * 2:cc * 2 + 2], wb[:, kk, c4 * 128:(c4 + 1) * 128], scb[:, kk, :],
                         start=(kk == 0), stop=(kk == 7))
    pmv = pm[:, 0:2 * ncc].rearrange("p (c j) -> p c j", j=2)
    for j in range(2):
        k.tt(modv[:, :, j], pmv[:, :, j], bmod[:], ALU.add)
    return modv


def emit_hT(k, nextP, hT, d_xT, d_xcT, modv, n1g, ones_bf, xt, sq, rstd, tmp, name=""):
    gm1 = k.sb(name + "gm1", [128, 8, 2]); tmp2 = k.sb(name + "tmp2", [128, 256])
    for j in range(2):
        k.ts(gm1[:, :, j], modv[:, 8:16, j], 1.0, ALU.add)
        k.tt(gm1[:, :, j], gm1[:, :, j], n1g[:], ALU.mult)
    W = 256
    jobs = [(d_xcT, 0, 0, 1)] + [(d_xT, i * W, 256 + i * W, 0) for i in range(4096 // W)]
    for ji, (src, c0, h0, j) in enumerate(jobs):
        x = xt[ji % len(xt)]
        k.dma(x[:], src.rearrange("(k p) t -> p k t", p=128)[:, :, c0:c0 + W])
        for kk in range(8):
            k.act(sq[:, kk, :], x[:, kk, :], AF.Square)
        p = nextP()
        for kk in range(8):
            k.matmul(p[:, 0:W], ones_bf[:], sq[:, kk, :], start=(kk == 0), stop=(kk == 7))
        k.ts(tmp[:, 0:W], p[:, 0:W], 1.0 / D, ALU.mult, EPS, ALU.add)
        k.act(tmp[:, 0:W], tmp[:, 0:W], AF.Sqrt)
        k.recip(rstd[:, 0:W], tmp[:, 0:W])
        for kk in range(8):
            tb = tmp if kk % 2 == 0 else sq[:, 0:4, :].bitcast(F32).rearrange("p a w -> p (a w)") if False else (tmp if kk % 2 == 0 else tmp2)
            k.stt(tb[:, 0:W], x[:, kk, :], gm1[:, kk, j:j + 1], rstd[:, 0:W], ALU.mult, ALU.mult)
            k.act(hT[:, kk, h0:h0 + W], tb[:, 0:W], AF.Identity, bias=modv[:, kk, j:j + 1], scale=1.0)


def rope(k, out_bf, x, cos_t, sin_t, tA, tB):
    xv = x.rearrange("p (a h f) -> p a h f", a=2, h=2)
    ov = out_bf.rearrange("p (a h f) -> p a h f", a=2, h=2)
    Av = tA.rearrange("p (a h f) -> p a h f", a=2, h=2)
    Bv = tB.rearrange("p (a h f) -> p a h f", a=2, h=2)
    cb = cos_t.rearrange("p (a f) -> p a f", a=2).unsqueeze(2).to_broadcast([128, 2, 2, 32])
    sv = sin_t.rearrange("p (a f) -> p a f", a=2)
    k.tt(Av, xv, cb, ALU.mult)
    k.tt(Bv[:, :, 0, :], xv[:, :, 1, :], sv, ALU.mult)
    k.tt(Bv[:, :, 1, :], xv[:, :, 0, :], sv, ALU.mult)
    k.tt(ov[:, :, 0, :], Av[:, :, 0, :], Bv[:, :, 0, :], ALU.subtract)
    k.tt(ov[:, :, 1, :], Av[:, :, 1, :], Bv[:, :, 1, :], ALU.add)


def make_consts():
    j = np.arange(128)[:, None].astype(np.float32); i = np.arange(128)[None, :].astype(np.float32)
    c = {}
    c["mask_f"] = (i >= j).astype(np.float32)
    c["mask_b"] = (j >= i).astype(np.float32)
    c["dm_f"] = np.maximum(i - j, 0.0); c["dm_b"] = np.maximum(j - i, 0.0)
    c["row_f"] = np.broadcast_to(i + 1.0, (128, 128)).copy()
    c["row_b"] = np.broadcast_to(128.0 - i, (128, 128)).copy()
    pc = np.zeros((128, 4), np.float32)
    pc[:, 0] = 127.0 - np.arange(128)
    pc[:, 1] = np.arange(128)
    pc[:, 2] = 128.0
    c["pcols"] = pc
    return {kk: np.ascontiguousarray(v.astype(np.float32)) for kk, v in c.items()}


def rope_tables():
    pos = np.arange(4096)
    row = (pos // 64).astype(np.float32); col = (pos % 64).astype(np.float32)
    inv = (10000.0 ** (-np.arange(0, 64, 2, dtype=np.float32) / 64.0)).astype(np.float32)
    ang = np.concatenate([row[:, None] * inv, col[:, None] * inv], axis=-1).astype(np.float32)
    cos = np.cos(ang).astype(np.float32); sin = np.sin(ang).astype(np.float32)
    cs = np.ascontiguousarray(cos.reshape(32, 128, 64).transpose(1, 0, 2))
    sn = np.ascontiguousarray(sin.reshape(32, 128, 64).transpose(1, 0, 2))
    return cs, sn


def build_M1():
    k = KB()
    DK = 128; DV = 256; NH = 4
    d_xT = k.dram("xT", [D, 4096]); d_xcT = k.dram("xcT", [D, 256])
    d_cvec = k.dram("cvec", [128, 8, 2]); d_wmod = k.dram("wmod", [D, 2048]); d_bmod = k.dram("bmod", [128, 16])
    d_n1g = k.dram("n1g", [128, 8])
    d_win = k.dram("win", [NH, D, 768])
    d_dec = k.dram("dec", [128, 8])
    d_ng = k.dram("ng", [128, NH, DV])
    d_cos = k.dram("cos", [128, 32, 64]); d_sin = k.dram("sin", [128, 32, 64])
    cn = {nm: k.dram(nm, [128, 128]) for nm in ("mask_f", "mask_b", "dm_f", "dm_b", "row_f", "row_b")}
    d_pcols = k.dram("pcols", [128, 4]); d_ident = k.dram("ident", [128, 128])
    d_oT = k.dram("oT", [NH * DV, 4096], BF16, kind="ExternalOutput")

    P = [k.ps("P%d" % i, [128, 512]) for i in range(6)]
    PT = [k.ps("PT%d" % i, [128, 1024], BF16) for i in range(2)]
    pc = [0, 0]

    def nextP():
        p = P[pc[0] % 6]; pc[0] += 1; return p

    def nextPT():
        p = PT[pc[1] % 2]; pc[1] += 1; return p

    ones_bf = k.sb("ones_bf", [128, 128], BF16); k.memset(ones_bf[:], 1.0)
    identf = k.sb("identf", [128, 128]); ident = k.sb("ident_s", [128, 128], BF16)
    k.dma(identf[:], d_ident); k.copy(ident[:], identf[:])
    n1g = k.sb("n1g_s", [128, 8]); k.dma(n1g[:], d_n1g)
    whd = [k.sb("whd0", [128, 8, 768], BF16)]
    wmb = [whd[0][:, :, 0:512]]
    import os
    if 'nomod' in os.environ.get('M1_SKIP', ''):
        modv = k.sb("mmodv", [128, 16, 2]); k.memset(modv[:], 0.1)
    else:
        modv = emit_mod(k, nextP, d_cvec, d_wmod, d_bmod, 2048, wmb)
    hT = k.sb("hT", [128, 8, NT * 128], BF16)
    xt = [k.sb("xt0", [128, 8, 256])]
    sq = k.sb("sq", [128, 8, 256], BF16); rstd = k.sb("rstd", [128, 256]); tmp = k.sb("tmp", [128, 256])
    import os
    if 'nohT' in os.environ.get('M1_SKIP', ''):
        k.memset(hT[:, :, 0:512], 0.01)
    else:
        emit_hT(k, nextP, hT, d_xT, d_xcT, modv, n1g, ones_bf, xt, sq, rstd, tmp)

    cs = {nm: k.sb(nm + "_s", [128, 128]) for nm in cn}
    for nm in cn:
        k.dma(cs[nm][:], cn[nm])
    pcols = k.sb("pcols_s", [128, 4]); k.dma(pcols[:], d_pcols)
    cos = k.sb("cos_s", [128, 32, 64]); sin = k.sb("sin_s", [128, 32, 64])
    ng = k.sb("ng_s", [128, NH, DV])
    if 'nocs' not in os.environ.get('M1_SKIP', ''):
        k.dma(cos[:], d_cos); k.dma(sin[:], d_sin)
        k.dma(ng[:], d_ng)
    dec = k.sb("dec_s", [128, 8]); k.dma(dec[:], d_dec)
    lg = k.sb("lg", [128, 8]); e1 = k.sb("e1", [128, 8])
    k.act(e1[:], dec[:], AF.Exp, scale=-1.0)
    k.act(e1[:], e1[:], AF.Ln, bias=1.0, scale=1.0)
    k.ts(lg[:], e1[:], -1.0, ALU.mult)

    kb = k.sb("kb", [128, NT, DK], BF16); qT = k.sb("qT", [128, NT, 128], BF16); kT = k.sb("kT", [128, NT, 128], BF16)
    vb = k.sb("vb", [128, NT, DV], BF16); gs = k.sb("gs", [128, 32, DV], BF16)
    oacc = k.sb("oacc", [128, 32, DV], BF16)
    oTh = [k.sb("oTh%d" % i, [128, 2, 512], BF16) for i in range(2)]
    qf = k.sb("qf", [128, 128]); kf = k.sb("kf", [128, 128]); ksc = k.sb("ksc", [128, 128])
    tA = k.sb("tA", [128, 128]); tB = k.sb("tB", [128, 128])
    DM = [k.sb("DM%d" % i, [128, 128]) for i in range(2)]
    EBr = [k.sb("EBr%d" % i, [128, 128]) for i in range(2)]
    ERc = k.sb("ERc", [128, 2]); dcol = k.sb("dcol", [128, 2])
    S = k.sb("S", [128, DV]); Sb = k.sb("Sb", [128, DV], BF16)
    AT = k.sb("AT", [128, 128], BF16); qin = k.sb("qin", [128, 128], BF16); kk_ = k.sb("kk", [128, 128], BF16)
    st6 = k.sb("st6", [128, 6]); mv = k.sb("mv", [128, 2]); rs = k.sb("rs", [128, 1]); on = k.sb("on", [128, DV])
    obf = k.sb("obf", [128, DV], BF16)

    import os
    STOP = float(os.environ.get('M1_STOP', '99')); NTL = int(os.environ.get('M1_NT', '34'))
    gf = k.sb("gf", [128, DV])
    for hl in range(NH):
        w = whd[0]
        k.dma(w[:], d_win[hl].rearrange("(k p) c -> p k c", p=128), q="pool")
        for di, (dmn, mkn, rown) in enumerate((("dm_f", "mask_f", "row_f"), ("dm_b", "mask_b", "row_b"))):
            lgc = lg[:, di * 4 + hl:di * 4 + hl + 1]
            k.act(DM[di][:], cs[dmn][:], AF.Exp, scale=lgc)
            k.tt(DM[di][:], DM[di][:], cs[mkn][:], ALU.mult)
            k.act(EBr[di][:], cs[rown][:], AF.Exp, scale=lgc)
            k.act(ERc[:, di:di + 1], pcols[:, di:di + 1], AF.Exp, scale=lgc)
            k.act(dcol[:, di:di + 1], pcols[:, 2:3], AF.Exp, scale=lgc)
        for t in range(NT):
            pa = nextP(); pb = nextP()
            for kk in range(8):
                k.matmul(pa[:, 0:512], hT[:, kk, t * 128:(t + 1) * 128], w[:, kk, 0:512], start=(kk == 0), stop=(kk == 7))
            for kk in range(8):
                k.matmul(pb[:, 0:256], hT[:, kk, t * 128:(t + 1) * 128], w[:, kk, 512:768], start=(kk == 0), stop=(kk == 7))
            k.ts(ksc[:], pa[:, 128:256], float(DK) ** -0.5, ALU.mult)
            if t >= 2:
                rope(k, qf[:], pa[:, 0:128], cos[:, t - 2, :], sin[:, t - 2, :], tA[:], tB[:])
                rope(k, kf[:], ksc[:], cos[:, t - 2, :], sin[:, t - 2, :], tA[:], tB[:])
                ksrc = kf
                k.copy(gf[:], pa[:, 256:512])
                k.act(gs[:, t - 2, :], gf[:], AF.Silu)
            else:
                k.copy(qf[:], pa[:, 0:128])
                ksrc = ksc
            k.copy(kb[:, t, :], ksrc[:], eng="pool")
            k.copy(vb[:, t, :], pb[:, 0:256])
            pt = nextP()
            k.transpose(pt[:, 0:128], qf[:], identf[:])
            k.transpose(pt[:, 128:256], ksrc[:], identf[:])
            k.copy(qT[:, t, :], pt[:, 0:128])
            k.copy(kT[:, t, :], pt[:, 128:256])
        for di in (1, 0):
            order = [1, 0] + list(range(33, 1, -1)) if di == 1 else list(range(NT))
            k.memset(S[:], 0.0); k.memset(Sb[:], 0.0)
            for t in order:
                if t >= 2:
                    pat = nextP()
                    k.matmul(pat[:, 0:128], kT[:, t, :], qT[:, t, :])
                    k.tt(AT[:], pat[:, 0:128], DM[di][:], ALU.mult)
                    k.tt(qin[:], qT[:, t, :], EBr[di][:], ALU.mult, eng="pool")
                    po = nextP()
                    k.matmul(po[:, 0:DV], AT[:], vb[:, t, :], start=True, stop=False)
                    k.matmul(po[:, 0:DV], qin[:], Sb[:], start=False, stop=True)
                k.ts(kk_[:], kb[:, t, :], ERc[:, di:di + 1], ALU.mult, eng="pool")
                pkv = nextP()
                k.matmul(pkv[:, 0:DV], kk_[:], vb[:, t, :])
                if t >= 2:
                    if di == 1:
                        k.copy(oacc[:, t - 2, :], po[:, 0:DV])
                    else:
                        k.tt(on[:], po[:, 0:DV], oacc[:, t - 2, :], ALU.add)
                        k.op("dve", lambda e, a=st6, b=on: e.bn_stats(a[:], b[:]), [on[:]], [st6[:]])
                        k.op("dve", lambda e, a=mv, b=st6: e.bn_aggr(a[:], b[:]), [st6[:]], [mv[:]])
                        k.ts(rs[:], mv[:, 1:2], EPS, ALU.add)
                        k.act(rs[:], rs[:], AF.Sqrt)
                        k.recip(rs[:], rs[:])
                        k.ts(on[:], on[:], mv[:, 0:1], ALU.subtract, rs[:, 0:1], ALU.mult)
                        k.tt(on[:], on[:], ng[:, hl, :], ALU.mult, eng="pool")
                        k.tt(obf[:], on[:], gs[:, t - 2, :], ALU.mult, eng="pool")
                        pt = nextPT()
                        k.transpose(pt[:, 0:128], obf[:, 0:128], ident[:])
                        k.transpose(pt[:, 128:256], obf[:, 128:256], ident[:])
                        lt = t - 2
                        ob = oTh[(lt // 4) % 2]
                        k.copy(ob[:, :, (lt % 4) * 128:(lt % 4 + 1) * 128], pt[:, 0:256].rearrange("p (c t) -> p c t", c=2))
                        if lt % 4 == 3:
                            k.dma(d_oT[hl * DV:(hl + 1) * DV, (lt // 4) * 512:(lt // 4 + 1) * 512].rearrange("(c p) t -> p c t", p=128), ob[:])
                k.stt(S[:], S[:], dcol[:, di:di + 1], pkv[:, 0:DV], ALU.mult, ALU.add)
                k.copy(Sb[:], S[:], eng="act")
    nc = k.finish([d_oT])
    return nc, k


NEG = -30000.0


def na_configs():
    cfgs = []; plan = {}
    for g in range(32):
        lst = []
        for u in range(32):
            key = []
            for kh in range(2):
                for qh in range(2):
                    r = 2 * g + qh; kr = 2 * u + kh
                    r0 = min(max(r - 4, 0), 56)
                    key.append(kr - r + 7 if r0 <= kr <= r0 + 7 else None)
            key = tuple(key)
            if all(x is None for x in key):
                continue
            if key not in cfgs:
                cfgs.append(key)
            lst.append((u, cfgs.index(key)))
        plan[g] = lst
    return cfgs, plan


def build_M0():
    k = KB()
    d_xT = k.dram("xT", [D, 4096]); d_xcT = k.dram("xcT", [D, 256])
    d_cvec = k.dram("cvec", [128, 8, 2]); d_wmod = k.dram("wmod", [D, 2048]); d_bmod = k.dram("bmod", [128, 16])
    d_n1g = k.dram("n1g", [128, 8])
    d_wna = k.dram("wna", [D, 768]); d_wgl = k.dram("wgl", [D, 768]); d_wa = k.dram("wa", [D, 32])
    d_waaug = k.dram("waaug", [17, 2, 128])
    d_rpbT = k.dram("rpbT", [128, 4, 15, 64]); d_colmask = k.dram("colmask", [128, 64])
    d_gng = k.dram("gng", [128, 128])
    d_maskf = k.dram("mask_f", [128, 128]); d_maskb = k.dram("mask_b", [128, 128])
    d_ident = k.dram("ident", [128, 128])
    d_oT = k.dram("oT", [512, NT * 128], BF16, kind="ExternalOutput")

    P = [k.ps("P%d" % i, [128, 512]) for i in range(6)]
    PT = [k.ps("PT%d" % i, [128, 1024], BF16) for i in range(2)]
    pc = [0, 0]

    NRR = [6]

    def nextP():
        p = P[pc[0] % NRR[0]]; pc[0] += 1; return p

    def nextPT():
        p = PT[pc[1] % 2]; pc[1] += 1; return p

    ones_bf = k.sb("ones_bf", [128, 128], BF16); k.memset(ones_bf[:], 1.0)
    identf = k.sb("identf", [128, 128]); ident = k.sb("ident_s", [128, 128], BF16)
    k.dma(identf[:], d_ident); k.copy(ident[:], identf[:])
    n1g = k.sb("n1g_s", [128, 8]); k.dma(n1g[:], d_n1g)
    hT = k.sb("hT", [128, 8, NT * 128], BF16)
    oTs = [k.sb("oTs%d" % i, [128, 2, 512], BF16) for i in range(2)]
    k.open_scope()
    wmb = [k.sb("wmb0", [128, 8, 512], BF16)]
    modv = emit_mod(k, nextP, d_cvec, d_wmod, d_bmod, 2048, wmb)
    xt = [k.sb("xt0", [128, 8, 256]), k.sb("xt1", [128, 8, 256])]
    sq = k.sb("sq", [128, 8, 256], BF16); rstd = k.sb("rstd", [128, 256]); tmp = k.sb("tmp", [128, 256])
    emit_hT(k, nextP, hT, d_xT, d_xcT, modv, n1g, ones_bf, xt, sq, rstd, tmp)
    k.close_scope()

    cfgs, plan = na_configs()
    k.open_scope()
    wna = k.sb("wna_s", [128, 8, 768], BF16)
    k.dma(wna[:], d_wna.rearrange("(k p) c -> p k c", p=128), q="pool")
    QT = k.sb("QT", [64, 4, NT * 128], BF16); KT = k.sb("KT", [64, 4, NT * 128], BF16)
    Vaug = k.sb("Vaug", [128, NT, 4, 65], BF16)
    k.memset(Vaug[:], 1.0)
    BT = k.sb("BT", [128, len(cfgs), 4, 128])
    k.open_scope()
    Btab = k.sb("Btab", [128, 4, 15, 64]); cmask = k.sb("cmask", [128, 64])
    k.dma(Btab[:], d_rpbT); k.dma(cmask[:], d_colmask)
    for h in range(4):
        k.tt(Btab[:, h, :, :], Btab[:, h, :, :], cmask[:].unsqueeze(1).to_broadcast([128, 15, 64]), ALU.add)
    for ci, key in enumerate(cfgs):
        bi = 0
        for kh in range(2):
            for qh in range(2):
                roff = key[bi]; bi += 1
                dst = BT[kh * 64:(kh + 1) * 64, ci, :, qh * 64:(qh + 1) * 64]
                if roff is None:
                    k.memset(dst, NEG, eng="pool")
                else:
                    k.copy(dst, Btab[kh * 64:(kh + 1) * 64, :, roff, :], eng="pool")
    k.close_scope()
    qs = k.sb("qs", [128, 256]); ks_ = k.sb("ks", [128, 256])
    for t in range(NT):
        pa = nextP(); pb = nextP()
        for kk in range(8):
            k.matmul(pa[:, 0:512], hT[:, kk, t * 128:(t + 1) * 128], wna[:, kk, 0:512], start=(kk == 0), stop=(kk == 7))
        for kk in range(8):
            k.matmul(pb[:, 0:256], hT[:, kk, t * 128:(t + 1) * 128], wna[:, kk, 512:768], start=(kk == 0), stop=(kk == 7))
        k.ts(qs[:], pa[:, 0:256], 0.125, ALU.mult)
        k.copy(ks_[:], pa[:, 256:512])
        k.copy(Vaug[:, t, :, 0:64], pb[:, 0:256].rearrange("p (h d) -> p h d", h=4))
        pt = nextP(); pt2 = nextP()
        for h in range(4):
            k.transpose(pt[0:64, h * 128:(h + 1) * 128], qs[:, h * 64:(h + 1) * 64], identf[:])
        for h in range(4):
            k.transpose(pt2[0:64, h * 128:(h + 1) * 128], ks_[:, h * 64:(h + 1) * 64], identf[:])
        k.copy(QT[:, :, t * 128:(t + 1) * 128], pt[0:64, 0:512].rearrange("p (c t) -> p c t", c=4))
        k.copy(KT[:, :, t * 128:(t + 1) * 128], pt2[0:64, 0:512].rearrange("p (c t) -> p c t", c=4))
    import os
    STOP = float(os.environ.get("M0_STOP", "99"))
    if STOP <= 2:
        k.close_scope(); return k.finish([d_oT]), k
    sc = [k.sb("sc%d" % i, [128, 512]) for i in range(2)]
    PTb = [k.sb("PTb%d" % i, [128, 512], BF16) for i in range(2)]
    rden = k.sb("rden", [128, 4, 1]); obf = k.sb("obf", [128, 256], BF16)
    it = [0]
    NRR[0] = 4
    for qt in range(NT if STOP > 2.5 else int(os.environ.get("M0_NQ", "1"))):
        if qt < 2:
            keys = [(0, None), (1, None)]
        else:
            keys = [(0, None), (1, None)] + [(u + 2, ci) for (u, ci) in plan[qt - 2]]
        po = P[4 + qt % 2]
        for ki, (kt, ci) in enumerate(keys):
            ps = nextP()
            for h in range(4):
                k.matmul(ps[:, h * 128:(h + 1) * 128], KT[:, h, kt * 128:(kt + 1) * 128],
                         QT[:, h, qt * 128:(qt + 1) * 128])
            s_ = sc[it[0] % 2]; p_ = PTb[it[0] % 2]; it[0] += 1
            if ci is None:
                k.copy(s_[:], ps[:, 0:512])
            else:
                k.tt(s_[:], ps[:, 0:512], BT[:, ci, :, :].rearrange("p h q -> p (h q)"), ALU.add)
            k.act(p_[:], s_[:], AF.Exp)
            for h in range(4):
                k.matmul(po[:, h * 65:(h + 1) * 65], p_[:, h * 128:(h + 1) * 128], Vaug[:, kt, h, :],
                         start=(ki == 0 and h == 0), stop=(ki == len(keys) - 1 and h == 3))
        pov = po[:, 0:260].rearrange("p (h e) -> p h e", e=65)
        k.recip(rden[:], pov[:, :, 64:65])
        k.tt(obf[:].rearrange("p (h d) -> p h d", h=4), pov[:, :, 0:64], rden[:].to_broadcast([128, 4, 64]), ALU.mult)
        ptt = nextPT()
        k.transpose(ptt[:, 0:128], obf[:, 0:128], ident[:])
        k.transpose(ptt[:, 128:256], obf[:, 128:256], ident[:])
        ob = oTs[(qt // 4) % 2]
        k.copy(ob[:, :, (qt % 4) * 128:(qt % 4 + 1) * 128], ptt[:, 0:256].rearrange("p (c t) -> p c t", c=2))
        if qt % 4 == 3 or qt == NT - 1:
            q0 = (qt // 4) * 4; n = qt - q0 + 1
            k.dma(d_oT[0:256, q0 * 128:(qt + 1) * 128].rearrange("(c p) t -> p c t", p=128), ob[:, :, 0:n * 128])
    NRR[0] = 6
    k.close_scope()
    if STOP <= 3:
        return k.finish([d_oT]), k

    k.open_scope()
    waaugf = k.sb("waaugf", [17, 2, 128]); waaug = k.sb("waaug_s", [17, 2, 128], BF16)
    k.dma(waaugf[:], d_waaug); k.copy(waaug[:], waaugf[:])
    gng = k.sb("gng_s", [128, 128]); k.dma(gng[:], d_gng)
    mk = [k.sb("mkf", [128, 128]), k.sb("mkb", [128, 128])]
    k.dma(mk[0][:], d_maskf); k.dma(mk[1][:], d_maskb)
    mks = [k.sb("mksf", [128, 128]), k.sb("mksb", [128, 128])]
    k.ts(mks[0][:], mk[0][:], -1.0, ALU.mult, 1.0, ALU.add)
    k.ts(mks[1][:], mk[1][:], -1.0, ALU.mult, 1.0, ALU.add)
    qTg = k.sb("qTg", [64, NT, 128], BF16); kTg = k.sb("kTg", [64, NT, 128], BF16)
    kbg = k.sb("kbg", [128, NT, 64], BF16); vg = k.sb("vg", [128, NT, 128], BF16)
    rsl = k.sb("rsl", [128, NT, 128], BF16); sp = k.sb("sp", [128, NT, 2, 64])
    oacc = k.sb("oaccg", [128, NT, 128], BF16)
    wgl = k.sb("wgl_s", [128, 8, 384], BF16); wa = k.sb("wa_s", [128, 8, 32], BF16)
    k.dma(wa[:], d_wa.rearrange("(k p) c -> p k c", p=128), q="pool")
    aT = k.sb("aT", [17, 2, 128], BF16); k.memset(aT[:], 1.0)
    qf = k.sb("qfg", [128, 64]); kf = k.sb("kfg", [128, 64]); rf = k.sb("rfg", [128, 128]); zf = k.sb("zfg", [128, 128])
    S = k.sb("Sg", [64, 128]); Sb = k.sb("Sbg", [64, 128], BF16)
    EBT = k.sb("EBT", [64, 128]); ENBT = k.sb("ENBT", [64, 128]); ER = k.sb("ERg", [128, 64])
    bcs = k.sb("bcs", [64, 128]); rsb = k.sb("rsb", [128, 64])
    qin = k.sb("qing", [64, 128], BF16); kin = k.sb("king", [64, 128], BF16); kkg = k.sb("kkg", [128, 64], BF16)
    AT = k.sb("ATg", [128, 128], BF16)
    on = k.sb("ong", [128, 128]); st6 = k.sb("st6g", [128, 6]); mv = k.sb("mvg", [128, 2]); ms = k.sb("msg", [128, 1])
    obg = k.sb("obg", [128, 128], BF16)
    oTg = [k.sb("oTg%d" % i, [128, 512], BF16) for i in range(2)]
    dwg = d_wgl.rearrange("(k p) c -> p k c", p=128)
    for gh in range(2):
        k.dma(wgl[:, :, 0:64], dwg[:, :, gh * 64:(gh + 1) * 64], q="pool")
        k.dma(wgl[:, :, 64:128], dwg[:, :, 128 + gh * 64:128 + (gh + 1) * 64], q="pool")
        k.dma(wgl[:, :, 128:256], dwg[:, :, 256 + gh * 128:256 + (gh + 1) * 128], q="pool")
        k.dma(wgl[:, :, 256:384], dwg[:, :, 512 + gh * 128:512 + (gh + 1) * 128], q="pool")
        for t in range(NT):
            pa = nextP(); pz = nextP()
            for kk in range(8):
                k.matmul(pa[:, 0:384], hT[:, kk, t * 128:(t + 1) * 128], wgl[:, kk, 0:384], start=(kk == 0), stop=(kk == 7))
            for di in range(2):
                for kk in range(8):
                    k.matmul(pz[0:16, di * 128:(di + 1) * 128], wa[:, kk, di * 16:(di + 1) * 16], hT[:, kk, t * 128:(t + 1) * 128],
                             start=(kk == 0), stop=(kk == 7))
            k.copy(aT[0:16, :, :], pz[0:16, 0:256].rearrange("p (a t) -> p a t", a=2))
            pz2 = nextP()
            for di in range(2):
                k.matmul(pz2[:, di * 64:(di + 1) * 64], aT[:, di, :], waaug[:, di, gh * 64:(gh + 1) * 64])
            k.copy(zf[:], pz2[:, 0:128])
            k.act(zf[:], zf[:], AF.Exp, scale=-1.0)
            k.act(sp[:, t, :, :].rearrange("p a d -> p (a d)"), zf[:], AF.Ln, bias=1.0, scale=1.0)
            k.ts(qf[:], pa[:, 0:64], 0.125, ALU.mult)
            k.copy(kf[:], pa[:, 64:128])
            k.copy(kbg[:, t, :], kf[:], eng="pool")
            k.copy(rf[:], pa[:, 128:256])
            k.act(rsl[:, t, :], rf[:], AF.Silu)
            k.copy(vg[:, t, :], pa[:, 256:384])
            pt = nextP()
            k.transpose(pt[0:64, 0:128], qf[:], identf[:])
            k.transpose(pt[0:64, 128:256], kf[:], identf[:])
            k.copy(qTg[:, t, :], pt[0:64, 0:128])
            k.copy(kTg[:, t, :], pt[0:64, 128:256])
        for di in (1, 0):
            order = ([1, 0] + list(range(33, 1, -1))) if di == 1 else list(range(NT))
            k.memset(S[:], 0.0); k.memset(Sb[:], 0.0)
            for t in order:
                spt = sp[:, t, di, :]
                pbc = nextP()
                k.matmul(pbc[0:64, 0:128], spt, mk[di][:])
                k.matmul(pbc[:, 128:192], mks[di][:], spt)
                k.copy(bcs[:], pbc[0:64, 0:128]); k.copy(rsb[:], pbc[:, 128:192])
                k.act(EBT[:], bcs[:], AF.Exp, scale=-1.0 / 16.0)
                k.act(ENBT[:], bcs[:], AF.Exp, scale=1.0 / 16.0)
                k.act(ER[:], rsb[:], AF.Exp, scale=-1.0 / 16.0)
                k.tt(qin[:], qTg[:, t, :], EBT[:], ALU.mult)
                k.tt(kin[:], kTg[:, t, :], ENBT[:], ALU.mult, eng="pool")
                k.tt(kkg[:], kbg[:, t, :], ER[:], ALU.mult, eng="pool")
                pat = nextP()
                k.matmul(pat[:, 0:128], kin[:], qin[:])
                k.tt(AT[:], pat[:, 0:128], mk[di][:], ALU.mult)
                po = nextP()
                k.matmul(po[:, 0:128], AT[:], vg[:, t, :], start=True, stop=False)
                k.matmul(po[:, 0:128], qin[:], Sb[:], start=False, stop=True)
                pkv = nextP()
                k.matmul(pkv[0:64, 0:128], kkg[:], vg[:, t, :])
                if di == 1:
                    k.copy(oacc[:, t, :], po[:, 0:128])
                else:
                    k.tt(on[:], po[:, 0:128], oacc[:, t, :], ALU.add)
                    k.op("dve", lambda e, a=st6, b=on: e.bn_stats(a[:], b[:]), [on[:]], [st6[:]])
                    k.op("dve", lambda e, a=mv, b=st6: e.bn_aggr(a[:], b[:]), [st6[:]], [mv[:]])
                    k.stt(ms[:], mv[:, 0:1], mv[:, 0:1], mv[:, 1:2], ALU.mult, ALU.add)
                    k.ts(ms[:], ms[:], EPS, ALU.add)
                    k.act(ms[:], ms[:], AF.Sqrt)
                    k.recip(ms[:], ms[:])
                    k.stt(on[:], on[:], ms[:, 0:1], gng[:], ALU.mult, ALU.mult)
                    k.tt(obg[:], on[:], rsl[:, t, :], ALU.mult, eng="pool")
                    ptt = nextPT()
                    k.transpose(ptt[:, 0:128], obg[:], ident[:])
                    ob = oTg[(t // 4) % 2]
                    k.copy(ob[:, (t % 4) * 128:(t % 4 + 1) * 128], ptt[:, 0:128])
                    if t % 4 == 3 or t == NT - 1:
                        q0 = (t // 4) * 4; n = t - q0 + 1
                        k.dma(d_oT[256 + gh * 128:256 + (gh + 1) * 128, q0 * 128:(t + 1) * 128], ob[:, 0:n * 128])
                dci = 127 if di == 0 else 0
                k.stt(S[:], S[:], EBT[:, dci:dci + 1], pkv[0:64, 0:128], ALU.mult, ALU.add)
                k.copy(Sb[:], S[:], eng="act")
    k.close_scope()
    nc = k.finish([d_oT])
    return nc, k

BF = ml_dtypes.bfloat16

def pk(v):
    return np.ascontiguousarray(v.reshape(-1, 128).T)

def seg_cols(arrT, t0, n, T):
    out = np.zeros((arrT.shape[0], n + 2), arrT.dtype)
    lo = max(t0 - 1, 0); hi = min(t0 + n + 1, T)
    out[:, lo - (t0 - 1): hi - (t0 - 1)] = arrT[:, lo:hi]
    hm = np.array([1.0 if t0 - 1 >= 0 else 0.0, 1.0 if t0 + n < T else 0.0], np.float32)
    return out, np.ascontiguousarray(np.broadcast_to(hm, (128, 2)))

def prep_F(inp, L, b, th, xT, oT, xcT=None, ocT=None, wout=None):
    m = {}
    segs = []
    for s in range(4):
        t0 = th * 2048 + s * 512
        m["xT_%d" % s], m["hm_%d" % s] = seg_cols(xT, t0, 512, 4096)
        m["oT_%d" % s], _ = seg_cols(oT, t0, 512, 4096)
        segs.append((512, 0))
    if xcT is not None:
        m["xT_4"], m["hm_4"] = seg_cols(xcT, 0, 256, 256)
        m["oT_4"], _ = seg_cols(ocT, 0, 256, 256)
        segs.append((256, 1))
    cv = np.stack([pk(inp["c"][b]), pk(inp["c_ctx"])], axis=-1)
    m["cvec"] = np.ascontiguousarray(cv.astype(np.float32))
    m["wmod"] = np.ascontiguousarray(inp["w_mod"][L][:, 2048:6144])
    m["bmod"] = pk(inp["b_mod"][L][2048:6144])
    m["n2g"] = pk(inp["norm2_g"][L]); m["fg"] = pk(inp["final_norm_g"])
    cw = inp["ffn_conv_w"][L]
    m["convw"] = np.ascontiguousarray(np.stack([pk(cw[i]) for i in range(3)], axis=1))
    m["convb"] = pk(inp["ffn_conv_b"][L])
    m["wout"] = wout
    m["wup"] = inp["ffn_w_up"][L]; m["wdn"] = inp["ffn_w_down"][L]
    return m, segs


_C = make_consts(); _COS, _SIN = rope_tables()

def prep_M_common(inp, L, b, xT, xcT):
    m = {"xT": np.ascontiguousarray(xT), "xcT": np.ascontiguousarray(xcT)}
    cv = np.stack([pk(inp["c"][b]), pk(inp["c_ctx"])], axis=-1)
    m["cvec"] = np.ascontiguousarray(cv.astype(np.float32))
    m["wmod"] = np.ascontiguousarray(inp["w_mod"][L][:, 0:2048])
    m["bmod"] = pk(inp["b_mod"][L][0:2048])
    m["n1g"] = pk(inp["norm1_g"][L])
    m["ident"] = np.eye(128, dtype=np.float32)
    return m

def prep_M1(inp, b, hh, xT, xcT):
    m = prep_M_common(inp, 1, b, xT, xcT)
    w = inp["ret_w_in"][0]
    heads = [hh * 4 + i for i in range(4)]
    m["win"] = np.ascontiguousarray(np.stack([np.concatenate([
        w[:, h * 128:(h + 1) * 128], w[:, 1024 + h * 128:1024 + (h + 1) * 128],
        w[:, 4096 + h * 256:4096 + (h + 1) * 256], w[:, 2048 + h * 256:2048 + (h + 1) * 256]], axis=1) for h in heads]))
    dec = np.concatenate([inp["ret_decay_fwd"][0][heads], inp["ret_decay_bwd"][0][heads]])
    m["dec"] = np.ascontiguousarray(np.broadcast_to(dec, (128, 8)).astype(np.float32))
    m["ng"] = np.ascontiguousarray(np.broadcast_to(inp["ret_norm_g"][0][heads], (128, 4, 256)).astype(np.float32))
    m["cos"] = _COS; m["sin"] = _SIN
    for kk in ("mask_f", "mask_b", "dm_f", "dm_b", "row_f", "row_b", "pcols"):
        m[kk] = _C[kk]
    return m

def _na_tables(rpb_heads):
    c = np.arange(64); c0 = np.clip(c - 8, 0, 48)
    kc = np.arange(64)
    allowed = (kc[:, None] >= c0[None, :]) & (kc[:, None] < c0[None, :] + 16)
    off = np.clip(kc[:, None] - c[None, :] + 15, 0, 30)
    g = rpb_heads[:, :, off]
    g = np.where(allowed[None, None], g, 0.0).astype(np.float32)
    g = np.transpose(g, (2, 0, 1, 3))
    rpbT = np.ascontiguousarray(np.concatenate([g, g], axis=0))
    cm = np.where(allowed, 0.0, -30000.0).astype(np.float32)
    return rpbT, np.ascontiguousarray(np.concatenate([cm, cm], axis=0))

def prep_M0(inp, b, hh, xT, xcT):
    m = prep_M_common(inp, 0, b, xT, xcT)
    w = inp["na_gla_w_in"][0]
    nh = [hh * 4 + i for i in range(4)]; gh = [hh * 2 + i for i in range(2)]
    m["wna"] = np.ascontiguousarray(np.concatenate(
        [w[:, h * 64:(h + 1) * 64] for h in nh] + [w[:, 512 + h * 64:512 + (h + 1) * 64] for h in nh] +
        [w[:, 1024 + h * 64:1024 + (h + 1) * 64] for h in nh], axis=1))
    m["wgl"] = np.ascontiguousarray(np.concatenate(
        [w[:, 1536 + h * 64:1536 + (h + 1) * 64] for h in gh] + [w[:, 1792 + h * 64:1792 + (h + 1) * 64] for h in gh] +
        [w[:, 2560 + h * 128:2560 + (h + 1) * 128] for h in gh] + [w[:, 2048 + h * 128:2048 + (h + 1) * 128] for h in gh], axis=1))
    m["wa"] = np.ascontiguousarray(w[:, 3072:3104])
    gc = slice(hh * 128, (hh + 1) * 128)
    wa = np.zeros((17, 2, 128), np.float32)
    wa[0:16, 0] = inp["gla_w_a_fwd"][0][:, gc]; wa[16, 0] = inp["gla_b_a_fwd"][0][gc]
    wa[0:16, 1] = inp["gla_w_a_bwd"][0][:, gc]; wa[16, 1] = inp["gla_b_a_bwd"][0][gc]
    m["waaug"] = wa
    m["rpbT"], m["colmask"] = _na_tables(inp["na_rpb"][0][nh])
    m["gng"] = np.ascontiguousarray(np.broadcast_to(inp["gla_norm_g"][0], (128, 128)).astype(np.float32))
    m["mask_f"] = _C["mask_f"]; m["mask_b"] = _C["mask_b"]
    return m


class Env:
    pass


def make_env(k):
    e = Env(); e.k = k
    e.P = [k.ps("P%d" % i, [128, 512]) for i in range(6)]
    e.PT = [k.ps("PT%d" % i, [128, 1024], BF16) for i in range(2)]
    e.pc = [0, 0]; e.NRR = [6]

    def nextP():
        p = e.P[e.pc[0] % e.NRR[0]]; e.pc[0] += 1; return p

    def nextPT():
        p = e.PT[e.pc[1] % 2]; e.pc[1] += 1; return p
    e.nextP = nextP; e.nextPT = nextPT
    e.ones_bf = k.sb("ones_bf", [128, 128], BF16); k.memset(e.ones_bf[:], 1.0)
    e.identf = k.sb("identf", [128, 128]); e.ident = k.sb("ident_s", [128, 128], BF16)
    d_ident = k.dram("ident", [128, 128])
    k.dma(e.identf[:], d_ident); k.copy(e.ident[:], e.identf[:])
    return e


def phase_hT(e, pfx, d_xT, d_xcT, d_cvec, d_wmod, d_bmod, d_n1g, hT):
    k = e.k
    k.open_scope()
    n1g = k.sb(pfx + "n1g_s", [128, 8]); k.dma(n1g[:], d_n1g)
    wmb = [k.sb(pfx + "wmb0", [128, 8, 512], BF16)]
    modv = emit_mod(k, e.nextP, d_cvec, d_wmod, d_bmod, 2048, wmb, name=pfx + "m")
    xt = [k.sb(pfx + "xt0", [128, 8, 256]), k.sb(pfx + "xt1", [128, 8, 256])]
    sq = k.sb(pfx + "sq", [128, 8, 256], BF16); rstd = k.sb(pfx + "rstd", [128, 256]); tmp = k.sb(pfx + "tmp", [128, 256])
    emit_hT(k, e.nextP, hT, d_xT, d_xcT, modv, n1g, e.ones_bf, xt, sq, rstd, tmp, name=pfx)
    k.close_scope()


def phase_NA(e, pfx, hT, d_wna, d_rpbT, d_colmask, d_oT, row0):
    k = e.k; nextP = e.nextP; nextPT = e.nextPT; identf = e.identf; ident = e.ident; P = e.P
    cfgs, plan = na_configs()
    k.open_scope()
    oTs = [k.sb(pfx + "oTs%d" % i, [128, 2, 512], BF16) for i in range(2)]
    wna = k.sb(pfx + "wna_s", [128, 8, 768], BF16)
    k.dma(wna[:], d_wna.rearrange("(k p) c -> p k c", p=128), q="pool")
    QT = k.sb(pfx + "QT", [64, 4, NT * 128], BF16); KT = k.sb(pfx + "KT", [64, 4, NT * 128], BF16)
    Vaug = k.sb(pfx + "Vaug", [128, NT, 4, 65], BF16)
    k.memset(Vaug[:], 1.0)
    BT = k.sb(pfx + "BT", [128, len(cfgs), 4, 128])
    k.open_scope()
    Btab = k.sb(pfx + "Btab", [128, 4, 15, 64]); cmask = k.sb(pfx + "cmask", [128, 64])
    k.dma(Btab[:], d_rpbT); k.dma(cmask[:], d_colmask)
    for h in range(4):
        k.tt(Btab[:, h, :, :], Btab[:, h, :, :], cmask[:].unsqueeze(1).to_broadcast([128, 15, 64]), ALU.add)
    for ci, key in enumerate(cfgs):
        bi = 0
        for kh in range(2):
            for qh in range(2):
                roff = key[bi]; bi += 1
                dst = BT[kh * 64:(kh + 1) * 64, ci, :, qh * 64:(qh + 1) * 64]
                if roff is None:
                    k.memset(dst, NEG, eng="pool")
                else:
                    k.copy(dst, Btab[kh * 64:(kh + 1) * 64, :, roff, :], eng="pool")
    k.close_scope()
    qs = k.sb(pfx + "qs", [128, 256]); ks_ = k.sb(pfx + "ks", [128, 256])
    for t in range(NT):
        pa = nextP(); pb = nextP()
        for kk in range(8):
            k.matmul(pa[:, 0:512], hT[:, kk, t * 128:(t + 1) * 128], wna[:, kk, 0:512], start=(kk == 0), stop=(kk == 7))
        for kk in range(8):
            k.matmul(pb[:, 0:256], hT[:, kk, t * 128:(t + 1) * 128], wna[:, kk, 512:768], start=(kk == 0), stop=(kk == 7))
        k.ts(qs[:], pa[:, 0:256], 0.125, ALU.mult)
        k.copy(ks_[:], pa[:, 256:512])
        k.copy(Vaug[:, t, :, 0:64], pb[:, 0:256].rearrange("p (h d) -> p h d", h=4))
        pt = nextP(); pt2 = nextP()
        for h in range(4):
            k.transpose(pt[0:64, h * 128:(h + 1) * 128], qs[:, h * 64:(h + 1) * 64], identf[:])
        for h in range(4):
            k.transpose(pt2[0:64, h * 128:(h + 1) * 128], ks_[:, h * 64:(h + 1) * 64], identf[:])
        k.copy(QT[:, :, t * 128:(t + 1) * 128], pt[0:64, 0:512].rearrange("p (c t) -> p c t", c=4))
        k.copy(KT[:, :, t * 128:(t + 1) * 128], pt2[0:64, 0:512].rearrange("p (c t) -> p c t", c=4))
    sc = [k.sb(pfx + "sc%d" % i, [128, 512]) for i in range(2)]
    PTb = [k.sb(pfx + "PTb%d" % i, [128, 512], BF16) for i in range(2)]
    rden = k.sb(pfx + "rden", [128, 4, 1]); obf = k.sb(pfx + "obf", [128, 256], BF16)
    it = [0]
    e.NRR[0] = 4
    obfL = [obf, k.sb(pfx + "obf2", [128, 256], BF16)]
    items = []
    for qt in range(NT):
        if qt < 2:
            keys = [(0, None), (1, None)]
        else:
            keys = [(0, None), (1, None)] + [(u + 2, ci) for (u, ci) in plan[qt - 2]]
        for ki, (kt, ci) in enumerate(keys):
            items.append((qt, ki, kt, ci, len(keys)))

    def stage1(qt, ki, kt, ci, nk):
        ps = nextP()
        for h in range(4):
            k.matmul(ps[:, h * 128:(h + 1) * 128], KT[:, h, kt * 128:(kt + 1) * 128], QT[:, h, qt * 128:(qt + 1) * 128])
        s_ = sc[it[0] % 2]; p_ = PTb[it[0] % 2]; it[0] += 1
        if ci is None:
            k.copy(s_[:], ps[:, 0:512])
        else:
            k.tt(s_[:], ps[:, 0:512], BT[:, ci, :, :].rearrange("p h q -> p (h q)"), ALU.add)
        k.act(p_[:], s_[:], AF.Exp)
        return p_

    def stage2(qt, ki, kt, ci, nk, p_):
        po = P[4 + qt % 2]
        for h in range(4):
            k.matmul(po[:, h * 65:(h + 1) * 65], p_[:, h * 128:(h + 1) * 128], Vaug[:, kt, h, :],
                     start=(ki == 0 and h == 0), stop=(ki == nk - 1 and h == 3))
        if ki == nk - 1:
            ob_ = obfL[qt % 2]
            pov = po[:, 0:260].rearrange("p (h e) -> p h e", e=65)
            k.recip(rden[:], pov[:, :, 64:65])
            k.tt(ob_[:].rearrange("p (h d) -> p h d", h=4), pov[:, :, 0:64], rden[:].to_broadcast([128, 4, 64]), ALU.mult)
            return (qt, ob_)
        return None

    def stage3(qt, ob_):
        ptt = nextPT()
        k.transpose(ptt[:, 0:128], ob_[:, 0:128], ident[:])
        k.transpose(ptt[:, 128:256], ob_[:, 128:256], ident[:])
        ob = oTs[(qt // 4) % 2]
        k.copy(ob[:, :, (qt % 4) * 128:(qt % 4 + 1) * 128], ptt[:, 0:256].rearrange("p (c t) -> p c t", c=2))
        if qt % 4 == 3 or qt == NT - 1:
            q0 = (qt // 4) * 4; n = qt - q0 + 1
            k.dma(d_oT[row0:row0 + 256, q0 * 128:(qt + 1) * 128].rearrange("(c p) t -> p c t", p=128), ob[:, :, 0:n * 128])

    prev = None; fin = []
    for idx in range(len(items) + 1):
        cur = None
        if idx < len(items):
            cur = (items[idx], stage1(*items[idx]))
        while fin and fin[0][0] <= idx - 1:
            _, f3 = fin.pop(0); stage3(*f3)
        if prev is not None:
            r = stage2(*prev[0], prev[1])
            if r is not None:
                fin.append((idx, r))
        prev = cur
    for _, f3 in fin:
        stage3(*f3)
    e.NRR[0] = 6
    k.close_scope()


def phase_GLA(e, pfx, hT, d_wgl, d_wa, d_waaug, d_gng, d_maskf, d_maskb, d_oT, row0, ghs):
    k = e.k; nextP = e.nextP; nextPT = e.nextPT; identf = e.identf; ident = e.ident
    k.open_scope()
    waaugf = k.sb(pfx + "waaugf", [17, 2, 256]); waaug = k.sb(pfx + "waaug_s", [17, 2, 256], BF16)
    k.dma(waaugf[:], d_waaug); k.copy(waaug[:], waaugf[:])
    gng = k.sb(pfx + "gng_s", [128, 128]); k.dma(gng[:], d_gng)
    mk = [k.sb(pfx + "mkf", [128, 128]), k.sb(pfx + "mkb", [128, 128])]
    k.dma(mk[0][:], d_maskf); k.dma(mk[1][:], d_maskb)
    mks = [k.sb(pfx + "mksf", [128, 128]), k.sb(pfx + "mksb", [128, 128])]
    k.ts(mks[0][:], mk[0][:], -1.0, ALU.mult, 1.0, ALU.add)
    k.ts(mks[1][:], mk[1][:], -1.0, ALU.mult, 1.0, ALU.add)
    qTg = k.sb(pfx + "qTg", [64, NT, 128], BF16); kTg = k.sb(pfx + "kTg", [64, NT, 128], BF16)
    kbg = k.sb(pfx + "kbg", [128, NT, 64], BF16); vg = k.sb(pfx + "vg", [128, NT, 128], BF16)
    rsl = k.sb(pfx + "rsl", [128, NT, 128], BF16); sp = k.sb(pfx + "sp", [128, NT, 2, 64])
    oacc = k.sb(pfx + "oaccg", [128, NT, 128], BF16)
    wgl = k.sb(pfx + "wgl_s", [128, 8, 384], BF16); wa = k.sb(pfx + "wa_s", [128, 8, 32], BF16)
    k.dma(wa[:], d_wa.rearrange("(k p) c -> p k c", p=128), q="pool")
    aT = k.sb(pfx + "aT", [17, 2, 128], BF16); k.memset(aT[:], 1.0)
    qf = k.sb(pfx + "qfg", [128, 64]); kf = k.sb(pfx + "kfg", [128, 64]); rf = k.sb(pfx + "rfg", [128, 128]); zf = k.sb(pfx + "zfg", [128, 128])
    R2 = range(2)
    S2 = [k.sb(pfx + "Sg%d" % i, [64, 128]) for i in R2]; Sb2 = [k.sb(pfx + "Sbg%d" % i, [64, 128], BF16) for i in R2]
    EBT2 = [k.sb(pfx + "EBT%d" % i, [64, 128]) for i in R2]; ENBT2 = [k.sb(pfx + "ENBT%d" % i, [64, 128]) for i in R2]
    ER2 = [k.sb(pfx + "ERg%d" % i, [128, 64]) for i in R2]
    bcs2 = [k.sb(pfx + "bcs%d" % i, [64, 128]) for i in R2]; rsb2 = [k.sb(pfx + "rsb%d" % i, [128, 64]) for i in R2]
    qin2 = [k.sb(pfx + "qing%d" % i, [64, 128], BF16) for i in R2]; kin2 = [k.sb(pfx + "king%d" % i, [64, 128], BF16) for i in R2]
    kkg2 = [k.sb(pfx + "kkg%d" % i, [128, 64], BF16) for i in R2]
    AT2 = [k.sb(pfx + "ATg%d" % i, [128, 128], BF16) for i in R2]
    oT4 = [k.sb(pfx + "oT4g%d" % i, [128, 128], BF16) for i in range(4)]; oc = [0]
    Q2 = range(2)
    onL = [k.sb(pfx + "ong%d" % i, [128, 128]) for i in Q2]; st6L = [k.sb(pfx + "st6g%d" % i, [128, 6]) for i in Q2]
    mvL = [k.sb(pfx + "mvg%d" % i, [128, 2]) for i in Q2]; msL = [k.sb(pfx + "msg%d" % i, [128, 1]) for i in Q2]
    obgL = [k.sb(pfx + "obg%d" % i, [128, 128], BF16) for i in Q2]
    pend = [None]; pendB = [None]; rc = [0]
    dwg = d_wgl.rearrange("(k p) c -> p k c", p=128)
    for (gl, gglob, rofs) in ghs:
        k.dma(wgl[:, :, 0:64], dwg[:, :, gl * 64:(gl + 1) * 64], q="pool")
        k.dma(wgl[:, :, 64:128], dwg[:, :, 128 + gl * 64:128 + (gl + 1) * 64], q="pool")
        k.dma(wgl[:, :, 128:256], dwg[:, :, 256 + gl * 128:256 + (gl + 1) * 128], q="pool")
        k.dma(wgl[:, :, 256:384], dwg[:, :, 512 + gl * 128:512 + (gl + 1) * 128], q="pool")
        for t in range(NT):
            pa = nextP(); pz = nextP()
            for kk in range(8):
                k.matmul(pa[:, 0:384], hT[:, kk, t * 128:(t + 1) * 128], wgl[:, kk, 0:384], start=(kk == 0), stop=(kk == 7))
            for di in range(2):
                for kk in range(8):
                    k.matmul(pz[0:16, di * 128:(di + 1) * 128], wa[:, kk, di * 16:(di + 1) * 16], hT[:, kk, t * 128:(t + 1) * 128],
                             start=(kk == 0), stop=(kk == 7))
            k.copy(aT[0:16, :, :], pz[0:16, 0:256].rearrange("p (a t) -> p a t", a=2))
            pz2 = nextP()
            for di in range(2):
                k.matmul(pz2[:, di * 64:(di + 1) * 64], aT[:, di, :], waaug[:, di, gglob * 64:(gglob + 1) * 64])
            k.copy(zf[:], pz2[:, 0:128])
            k.act(zf[:], zf[:], AF.Exp, scale=-1.0)
            k.act(sp[:, t, :, :].rearrange("p a d -> p (a d)"), zf[:], AF.Ln, bias=1.0, scale=1.0)
            k.ts(qf[:], pa[:, 0:64], 0.125, ALU.mult)
            k.copy(kf[:], pa[:, 64:128])
            k.copy(kbg[:, t, :], kf[:], eng="pool")
            k.copy(rsl[:, t, :], pa[:, 128:256])
            k.copy(vg[:, t, :], pa[:, 256:384])
            pt = nextP()
            k.transpose(pt[0:64, 0:128], qf[:], identf[:])
            k.transpose(pt[0:64, 128:256], kf[:], identf[:])
            k.copy(qTg[:, t, :], pt[0:64, 0:128])
            k.copy(kTg[:, t, :], pt[0:64, 128:256])
        for t in range(NT):
            k.act(rsl[:, t, :], rsl[:, t, :], AF.Silu)
        ordB = [1, 0] + list(range(33, 1, -1)); ordF = list(range(NT))
        for di in range(2):
            k.memset(S2[di][:], 0.0); k.memset(Sb2[di][:], 0.0)
        have = set()
        for i in range(NT):
            for di, t in ((1, ordB[i]), (0, ordF[i])):
                S = S2[di]; Sb = Sb2[di]; AT = AT2[di]; qin = qin2[di]; kin = kin2[di]; kkg = kkg2[di]
                EBT = EBT2[di]; ENBT = ENBT2[di]; ER = ER2[di]; bcs = bcs2[di]; rsb = rsb2[di]
                spt = sp[:, t, di, :]
                pbc = nextP()
                k.matmul(pbc[0:64, 0:128], spt, mk[di][:])
                k.matmul(pbc[:, 128:192], mks[di][:], spt)
                k.copy(bcs[:], pbc[0:64, 0:128]); k.copy(rsb[:], pbc[:, 128:192])
                k.act(EBT[:], bcs[:], AF.Exp, scale=-1.0 / 16.0)
                k.act(ENBT[:], bcs[:], AF.Exp, scale=1.0 / 16.0)
                k.act(ER[:], rsb[:], AF.Exp, scale=-1.0 / 16.0)
                k.tt(qin[:], qTg[:, t, :], EBT[:], ALU.mult)
                k.tt(kin[:], kTg[:, t, :], ENBT[:], ALU.mult)
                k.tt(kkg[:], kbg[:, t, :], ER[:], ALU.mult)
                pat = nextP()
                k.matmul(pat[:, 0:128], kin[:], qin[:])
                k.tt(AT[:], pat[:, 0:128], mk[di][:], ALU.mult)
                po = nextP()
                k.matmul(po[:, 0:128], AT[:], vg[:, t, :], start=True, stop=False)
                k.matmul(po[:, 0:128], qin[:], Sb[:], start=False, stop=True)
                pkv = nextP()
                k.matmul(pkv[0:64, 0:128], kkg[:], vg[:, t, :])
                dci = 127 if di == 0 else 0
                k.stt(S[:], S[:], EBT[:, dci:dci + 1], pkv[0:64, 0:128], ALU.mult, ALU.add)
                k.copy(Sb[:], S[:], eng="act")
                if pendB[0] is not None:
                    pendB[0](); pendB[0] = None
                if pend[0] is not None:
                    pendB[0] = pend[0](); pend[0] = None
                if t not in have:
                    k.copy(oacc[:, t, :], po[:, 0:128]); have.add(t)
                else:
                    par = rc[0] % 2; rc[0] += 1
                    onb = onL[par]
                    k.tt(onb[:], po[:, 0:128], oacc[:, t, :], ALU.add)

                    def readout(onb=onb, par=par, t=t, rofs=rofs):
                        st6_ = st6L[par]; mv_ = mvL[par]; ms_ = msL[par]; obg_ = obgL[par]
                        k.op("dve", lambda e_, a=st6_, b_=onb: e_.bn_stats(a[:], b_[:]), [onb[:]], [st6_[:]])
                        k.op("dve", lambda e_, a=mv_, b_=st6_: e_.bn_aggr(a[:], b_[:]), [st6_[:]], [mv_[:]])
                        k.stt(ms_[:], mv_[:, 0:1], mv_[:, 0:1], mv_[:, 1:2], ALU.mult, ALU.add)
                        k.ts(ms_[:], ms_[:], EPS, ALU.add)
                        k.act(ms_[:], ms_[:], AF.Ln)
                        k.act(ms_[:], ms_[:], AF.Exp, scale=-0.5)
                        k.stt(onb[:], onb[:], ms_[:, 0:1], gng[:], ALU.mult, ALU.mult)
                        k.tt(obg_[:], onb[:], rsl[:, t, :], ALU.mult, eng="pool")
                        return lambda: partB(obg_, t, rofs)

                    def partB(obg_, t, rofs):
                        ptt = nextPT()
                        k.transpose(ptt[:, 0:128], obg_[:], ident[:])
                        ob = oT4[oc[0] % 4]; oc[0] += 1
                        k.copy(ob[:], ptt[:, 0:128])
                        k.dma(d_oT[row0 + rofs:row0 + rofs + 128, t * 128:(t + 1) * 128], ob[:])
                    pend[0] = readout
        if pendB[0] is not None:
            pendB[0](); pendB[0] = None
        if pend[0] is not None:
            pend[0]()(); pend[0] = None
    k.close_scope()


def phase_RET(e, pfx, hT, d_win, d_dec, d_ng, d_cos, d_sin, cn, d_pcols, d_oT, NH):
    k = e.k; nextP = e.nextP; nextPT = e.nextPT; identf = e.identf; ident = e.ident
    DK = 128; DV = 256
    k.open_scope()
    cs = {nm: k.sb(pfx + nm + "_s", [128, 128]) for nm in cn}
    for nm in cn:
        k.dma(cs[nm][:], cn[nm])
    pcols = k.sb(pfx + "pcols_s", [128, 4]); k.dma(pcols[:], d_pcols)
    cos = k.sb(pfx + "cos_s", [128, 32, 128]); sin = k.sb(pfx + "sin_s", [128, 32, 128])
    k.dma(cos[:], d_cos); k.dma(sin[:], d_sin)
    ng = k.sb(pfx + "ng_s", [128, 2, DV])
    dec = k.sb(pfx + "dec_s", [128, 2 * NH]); k.dma(dec[:], d_dec)
    lg = k.sb(pfx + "lg", [128, 2 * NH]); e1 = k.sb(pfx + "e1", [128, 2 * NH])
    k.act(e1[:], dec[:], AF.Exp, scale=-1.0)
    k.act(e1[:], e1[:], AF.Ln, bias=1.0, scale=1.0)
    k.ts(lg[:], e1[:], -1.0, ALU.mult)
    w = k.sb(pfx + "whd0", [128, 8, 768], BF16)
    kb = k.sb(pfx + "kb", [128, NT, DK], BF16); qT = k.sb(pfx + "qT", [128, NT, 128], BF16); kT = k.sb(pfx + "kT", [128, NT, 128], BF16)
    vb = k.sb(pfx + "vb", [128, NT, DV], BF16); gs = k.sb(pfx + "gs", [128, 32, DV], BF16)
    oacc = k.sb(pfx + "oacc", [128, 32, DV], BF16)
    qkL = [k.sb(pfx + "qk%d" % i, [128, 256]) for i in range(2)]
    tAL = [k.sb(pfx + "tA%d" % i, [128, 256]) for i in range(1)] * 2; tBL = [k.sb(pfx + "tB%d" % i, [128, 256]) for i in range(1)] * 2
    DM = [k.sb(pfx + "DM%d" % i, [128, 128]) for i in range(2)]
    EBr = [k.sb(pfx + "EBr%d" % i, [128, 128]) for i in range(2)]
    ERc = k.sb(pfx + "ERc", [128, 2]); dcol = k.sb(pfx + "dcol", [128, 2])
    S2 = [k.sb(pfx + "S%d" % i, [128, DV]) for i in range(2)]; Sb2 = [k.sb(pfx + "Sb%d" % i, [128, DV], BF16) for i in range(2)]
    AT2 = [k.sb(pfx + "AT%d" % i, [128, 128], BF16) for i in range(2)]; qin2 = [k.sb(pfx + "qin%d" % i, [128, 128], BF16) for i in range(2)]
    kk2 = [k.sb(pfx + "kk%d" % i, [128, 128], BF16) for i in range(2)]
    oT4 = [k.sb(pfx + "oT4%d" % i, [128, 2, 128], BF16) for i in range(2)] * 2; oc = [0]
    R2_ = range(2)
    st6L = [k.sb(pfx + "st6%d" % i, [128, 6]) for i in R2_]; mvL = [k.sb(pfx + "mv%d" % i, [128, 2]) for i in R2_]
    rsL = [k.sb(pfx + "rs%d" % i, [128, 1]) for i in R2_]; nbL = [k.sb(pfx + "nb%d" % i, [128, 1]) for i in R2_]
    onL = [k.sb(pfx + "on%d" % i, [128, DV]) for i in R2_]; obfL = [k.sb(pfx + "obf%d" % i, [128, DV], BF16) for i in R2_]
    pend = [None]; pendB = [None]; rc = [0]
    gfL = tBL
    for hl in range(NH):
        k.dma(w[:], d_win[hl].rearrange("(k p) c -> p k c", p=128), q="pool")
        k.dma(ng[:, hl % 2, :], d_ng[:, hl, :])
        for di, (dmn, mkn, rown) in enumerate((("dm_f", "mask_f", "row_f"), ("dm_b", "mask_b", "row_b"))):
            lgc = lg[:, di * NH + hl:di * NH + hl + 1]
            k.act(DM[di][:], cs[dmn][:], AF.Exp, scale=lgc)
            k.tt(DM[di][:], DM[di][:], cs[mkn][:], ALU.mult)
            k.act(EBr[di][:], cs[rown][:], AF.Exp, scale=lgc)
            k.act(ERc[:, di:di + 1], pcols[:, di:di + 1], AF.Exp, scale=lgc)
            k.act(dcol[:, di:di + 1], pcols[:, 2:3], AF.Exp, scale=lgc)
        for t in range(NT):
            pa = nextP(); pb = nextP()
            for kk in range(8):
                k.matmul(pa[:, 0:512], hT[:, kk, t * 128:(t + 1) * 128], w[:, kk, 0:512], start=(kk == 0), stop=(kk == 7))
            for kk in range(8):
                k.matmul(pb[:, 0:256], hT[:, kk, t * 128:(t + 1) * 128], w[:, kk, 512:768], start=(kk == 0), stop=(kk == 7))
            qk = qkL[t % 2]; gf = gfL[t % 2]
            SC = float(DK) ** -0.5
            if t >= 2:
                xv = pa[:, 0:256].rearrange("p (g h f) -> p g h f", g=4, h=2)
                ov = qk[:].rearrange("p (g h f) -> p g h f", g=4, h=2)
                Av = tAL[t % 2][:].rearrange("p (g h f) -> p g h f", g=4, h=2)
                Bv = tBL[t % 2][:].rearrange("p (g h f) -> p g h f", g=4, h=2)
                cb_ = cos[:, t - 2, :].rearrange("p (g f) -> p g f", g=4).unsqueeze(2).to_broadcast([128, 4, 2, 32])
                sv = sin[:, t - 2, :].rearrange("p (g f) -> p g f", g=4)
                k.tt(Av, xv, cb_, ALU.mult)
                k.tt(Bv[:, :, 0, :], xv[:, :, 1, :], sv, ALU.mult)
                k.tt(Bv[:, :, 1, :], xv[:, :, 0, :], sv, ALU.mult)
                k.tt(ov[:, :, 0, :], Av[:, :, 0, :], Bv[:, :, 0, :], ALU.subtract)
                k.tt(ov[:, :, 1, :], Av[:, :, 1, :], Bv[:, :, 1, :], ALU.add)
                k.copy(gf[:], pa[:, 256:512])
                k.act(gs[:, t - 2, :], gf[:], AF.Silu)
            else:
                k.copy(qk[:], pa[:, 0:256])
            k.ts(kb[:, t, :], qk[:, 128:256], SC, ALU.mult, eng="pool")
            k.copy(vb[:, t, :], pb[:, 0:256])
            pt = nextP()
            k.transpose(pt[:, 0:128], qk[:, 0:128], identf[:])
            k.transpose(pt[:, 128:256], qk[:, 128:256], identf[:])
            k.copy(qT[:, t, :], pt[:, 0:128])
            k.ts(kT[:, t, :], pt[:, 128:256], SC, ALU.mult)
        ordB = [1, 0] + list(range(33, 1, -1)); ordF = list(range(NT))
        for di in range(2):
            k.memset(S2[di][:], 0.0); k.memset(Sb2[di][:], 0.0)
        have = set()
        for i in range(NT):
            for di, t in ((1, ordB[i]), (0, ordF[i])):
                S = S2[di]; Sb = Sb2[di]; AT = AT2[di]; qin = qin2[di]; kk_ = kk2[di]
                if t >= 2:
                    pat = nextP()
                    k.matmul(pat[:, 0:128], kT[:, t, :], qT[:, t, :])
                    k.tt(AT[:], pat[:, 0:128], DM[di][:], ALU.mult)
                    k.tt(qin[:], qT[:, t, :], EBr[di][:], ALU.mult)
                    po = nextP()
                    k.matmul(po[:, 0:DV], AT[:], vb[:, t, :], start=True, stop=False)
                    k.matmul(po[:, 0:DV], qin[:], Sb[:], start=False, stop=True)
                k.ts(kk_[:], kb[:, t, :], ERc[:, di:di + 1], ALU.mult)
                pkv = nextP()
                k.matmul(pkv[:, 0:DV], kk_[:], vb[:, t, :])
                k.stt(S[:], S[:], dcol[:, di:di + 1], pkv[:, 0:DV], ALU.mult, ALU.add)
                k.copy(Sb[:], S[:], eng="act")
                if pendB[0] is not None:
                    pendB[0](); pendB[0] = None
                if pend[0] is not None:
                    pendB[0] = pend[0](); pend[0] = None
                if t >= 2:
                    if t not in have:
                        k.copy(oacc[:, t - 2, :], po[:, 0:DV]); have.add(t)
                    else:
                        par = rc[0] % 2; rc[0] += 1
                        onb = onL[par]
                        k.tt(onb[:], po[:, 0:DV], oacc[:, t - 2, :], ALU.add)

                        def readout(onb=onb, par=par, t=t, hl=hl):
                            st6_ = st6L[par]; mv_ = mvL[par]; rs_ = rsL[par]; nb_ = nbL[par]; obf_ = obfL[par]
                            k.op("dve", lambda e_, a=st6_, b_=onb: e_.bn_stats(a[:], b_[:]), [onb[:]], [st6_[:]])
                            k.op("dve", lambda e_, a=mv_, b_=st6_: e_.bn_aggr(a[:], b_[:]), [st6_[:]], [mv_[:]])
                            k.ts(rs_[:], mv_[:, 1:2], EPS, ALU.add)
                            k.act(rs_[:], rs_[:], AF.Sqrt)
                            k.recip(rs_[:], rs_[:])
                            k.stt(nb_[:], mv_[:, 0:1], -1.0, rs_[:], ALU.mult, ALU.mult)
                            k.act(onb[:], onb[:], AF.Identity, bias=nb_[:, 0:1], scale=rs_[:, 0:1])
                            k.tt(onb[:], onb[:], ng[:, hl % 2, :], ALU.mult, eng="pool")
                            k.tt(obf_[:], onb[:], gs[:, t - 2, :], ALU.mult, eng="pool")
                            return lambda: partB(obf_, t, hl)

                        def partB(obf_, t, hl):
                            ptt = nextPT()
                            k.transpose(ptt[:, 0:128], obf_[:, 0:128], ident[:])
                            k.transpose(ptt[:, 128:256], obf_[:, 128:256], ident[:])
                            lt = t - 2
                            ob = oT4[oc[0] % 4]; oc[0] += 1
                            k.copy(ob[:], ptt[:, 0:256].rearrange("p (c t) -> p c t", c=2))
                            k.dma(d_oT[hl * DV:(hl + 1) * DV, lt * 128:(lt + 1) * 128].rearrange("(c p) t -> p c t", p=128), ob[:])
                        pend[0] = readout
        if pendB[0] is not None:
            pendB[0](); pendB[0] = None
        if pend[0] is not None:
            pend[0]()(); pend[0] = None
    k.close_scope()


def phase_F(e, pfx, Fdim, last, segs, xsrc, osrc, ydst, d_cvec, d_wmod, d_bmod, d_n2g, d_fg, d_cw, d_cb, d_wout, d_wup, d_wdn):
    k = e.k; nextP = e.nextP; ones_bf = e.ones_bf
    KF = Fdim // 128
    NMAX = max(s[0] for s in segs); CMAX = NMAX + 2
    k.open_scope()
    cvec = k.sb(pfx + "cvec_s", [128, 8, 2]); scb = k.sb(pfx + "scb", [128, 8, 2], BF16)
    bmod = k.sb(pfx + "bmod_s", [128, 32]); modv = k.sb(pfx + "modv", [128, 32, 2])
    n2g = k.sb(pfx + "n2g_s", [128, 8]); fg = k.sb(pfx + "fg_s", [128, 8]); gm2 = k.sb(pfx + "gm2", [128, 8, 2])
    cw = k.sb(pfx + "cw_s", [128, 3, 44]); cb = k.sb(pfx + "cb_s", [128, 44])
    wout = k.sb(pfx + "wout_s", [128, KF, D], BF16)
    wdn = k.sb(pfx + "wdn_s", [128, NFF, D], BF16)
    oT = k.sb(pfx + "oT_s", [128, KF, CMAX], BF16)
    x1 = k.sb(pfx + "x1T", [128, 8, CMAX])
    sq = k.sb(pfx + "sq", [128, 8, CMAX], BF16)
    rstd = k.sb(pfx + "rstd", [128, CMAX]); tmp = k.sb(pfx + "tmp", [128, CMAX]); tmpb = k.sb(pfx + "tmpb", [128, CMAX])
    h2 = k.sb(pfx + "h2T", [128, 8, CMAX], BF16)
    wua = [k.sb(pfx + "wua%d" % i, [128, 8, 256], BF16) for i in range(2)]
    wub = [k.sb(pfx + "wub%d" % i, [128, 8, 256], BF16) for i in range(2)]
    u = [k.sb(pfx + "u%d" % i, [128, 2, CMAX]) for i in range(2)]
    vaL = [k.sb(pfx + "va%d" % i, [128, NMAX]) for i in range(2)]; vbL = [k.sb(pfx + "vb%d" % i, [128, NMAX]) for i in range(2)]
    saL = [k.sb(pfx + "sa%d" % i, [128, NMAX]) for i in range(2)]
    tT = k.sb(pfx + "tT", [128, NFF, NMAX], BF16)
    k.dma(cvec[:], d_cvec); k.dma(bmod[:], d_bmod); k.dma(n2g[:], d_n2g); k.dma(fg[:], d_fg)
    k.dma(cw[:], d_cw); k.dma(cb[:], d_cb)
    k.act(scb[:], cvec[:], AF.Silu)
    pm = nextP()
    for g in range(8):
        wb = wua[g % 2][:, :, :]
        wb2 = wub[g % 2][:, :, :]
        k.dma(wb, d_wmod.rearrange("(k p) c -> p k c", p=128)[:, :, g * 512:g * 512 + 256], q="pool")
        k.dma(wb2, d_wmod.rearrange("(k p) c -> p k c", p=128)[:, :, g * 512 + 256:(g + 1) * 512], q="pool")
        for c4 in range(4):
            cc = g * 4 + c4
            src = wb if c4 < 2 else wb2
            for kk in range(8):
                k.matmul(pm[:, cc * 2:cc * 2 + 2], src[:, kk, (c4 % 2) * 128:(c4 % 2 + 1) * 128], scb[:, kk, :],
                         start=(kk == 0), stop=(kk == 7))
    pmv = pm[:, 0:64].rearrange("p (c j) -> p c j", j=2)
    for j in range(2):
        k.tt(modv[:, :, j], pmv[:, :, j], bmod[:], ALU.add)
    for j in range(2):
        k.ts(gm2[:, :, j], modv[:, 16:24, j], 1.0, ALU.add)
        k.tt(gm2[:, :, j], gm2[:, :, j], n2g[:], ALU.mult)
    load_cast_rows(k, wout, d_wout, KF, D, split=4)
    load_cast_rows(k, wdn, d_wdn, NFF, D, split=4)
    wupv = d_wup.rearrange("(k p) c -> p k c", p=128)
    wctr = [0]
    k.dma(wua[0][:], wupv[:, :, 0:256], q="pool")
    k.dma(wub[0][:], wupv[:, :, DFF:DFF + 256], q="pool")

    for si, (n, j, kind, t0) in enumerate(segs):
        cols = n + 2
        tiles = col_tiles(cols)
        xs, T = xsrc[kind]; os_, oc0, _ = osrc[kind]
        lo = max(t0 - 1, 0); hi = min(t0 + n + 1, T)
        c_lo = lo - (t0 - 1); c_hi = hi - (t0 - 1)
        if c_lo > 0:
            k.memset(x1[:, :, 0:1], 0.0); k.memset(oT[:, :, 0:1], 0.0)
        if c_hi < cols:
            k.memset(x1[:, :, cols - 1:cols], 0.0); k.memset(oT[:, :, cols - 1:cols], 0.0)
        k.dma(x1[:, :, c_lo:c_hi], xs.rearrange("(k p) t -> p k t", p=128)[:, :, lo:hi])
        k.dma(oT[:, :, c_lo:c_hi], os_.rearrange("(k p) t -> p k t", p=128)[:, :, oc0 + lo:oc0 + hi])
        for fc in range(8):
            for (a, b) in tiles:
                p = nextP()
                for kk in range(KF):
                    k.matmul(p[:, 0:b - a], wout[:, kk, fc * 128:(fc + 1) * 128], oT[:, kk, a:b],
                             start=(kk == 0), stop=(kk == KF - 1))
                k.stt(x1[:, fc, a:b], p[:, 0:b - a], modv[:, 0 + fc, j:j + 1], x1[:, fc, a:b], ALU.mult, ALU.add)
        for kk in range(8):
            k.act(sq[:, kk, 0:cols], x1[:, kk, 0:cols], AF.Square)
        for (a, b) in tiles:
            p = nextP()
            for kk in range(8):
                k.matmul(p[:, 0:b - a], ones_bf[:], sq[:, kk, a:b], start=(kk == 0), stop=(kk == 7))
            k.ts(tmp[:, a:b], p[:, 0:b - a], 1.0 / D, ALU.mult, EPS, ALU.add)
        k.act(tmp[:, 0:cols], tmp[:, 0:cols], AF.Sqrt)
        k.recip(rstd[:, 0:cols], tmp[:, 0:cols])
        for kk in range(8):
            tb = tmp if kk % 2 == 0 else tmpb
            k.stt(tb[:, 0:cols], x1[:, kk, 0:cols], gm2[:, kk, j:j + 1], rstd[:, 0:cols], ALU.mult, ALU.mult)
            k.act(h2[:, kk, 0:cols], tb[:, 0:cols], AF.Identity, bias=modv[:, 8 + kk, j:j + 1], scale=1.0)
        if c_lo > 0:
            k.memset(h2[:, :, 0:1], 0.0)
        if c_hi < cols:
            k.memset(h2[:, :, cols - 1:cols], 0.0)
        for g in range(11):
            wa = wua[wctr[0] % 2]; wb_ = wub[wctr[0] % 2]
            wctr[0] += 1
            gn = g + 1 if g < 10 else (0 if si + 1 < len(segs) else None)
            if gn is not None:
                k.dma(wua[wctr[0] % 2][:], wupv[:, :, gn * 256:(gn + 1) * 256], q="pool")
                k.dma(wub[wctr[0] % 2][:], wupv[:, :, DFF + gn * 256:DFF + (gn + 1) * 256], q="pool")
            for c2 in range(2):
                c = g * 2 + c2
                ub = u[c % 2]
                for half, w in ((0, wa), (1, wb_)):
                    for (a, b) in tiles:
                        p = nextP()
                        for kk in range(8):
                            k.matmul(p[:, 0:b - a], w[:, kk, c2 * 128:(c2 + 1) * 128], h2[:, kk, a:b],
                                     start=(kk == 0), stop=(kk == 7))
                        k.copy(ub[:, half, a:b], p[:, 0:b - a])
                ca = c; cbi = NFF + c
                va = vaL[c % 2]; vb = vbL[c % 2]; sa = saL[c % 2]
                k.act(va[:, 0:n], ub[:, 0, 1:n + 1], AF.Identity, bias=cb[:, ca:ca + 1], scale=cw[:, 1, ca:ca + 1])
                k.stt(va[:, 0:n], ub[:, 0, 0:n], cw[:, 0, ca:ca + 1], va[:, 0:n], ALU.mult, ALU.add)
                k.stt(va[:, 0:n], ub[:, 0, 2:n + 2], cw[:, 2, ca:ca + 1], va[:, 0:n], ALU.mult, ALU.add)
                k.act(vb[:, 0:n], ub[:, 1, 1:n + 1], AF.Identity, bias=cb[:, cbi:cbi + 1], scale=cw[:, 1, cbi:cbi + 1])
                k.stt(vb[:, 0:n], ub[:, 1, 0:n], cw[:, 0, cbi:cbi + 1], vb[:, 0:n], ALU.mult, ALU.add)
                k.stt(vb[:, 0:n], ub[:, 1, 2:n + 2], cw[:, 2, cbi:cbi + 1], vb[:, 0:n], ALU.mult, ALU.add)
                k.act(sa[:, 0:n], va[:, 0:n], AF.Silu)
                k.tt(tT[:, c, 0:n], sa[:, 0:n], vb[:, 0:n], ALU.mult, eng="pool")
        for fc in range(8):
            p = nextP()
            for c in range(NFF):
                k.matmul(p[:, 0:n], wdn[:, c, fc * 128:(fc + 1) * 128], tT[:, c, 0:n], start=(c == 0), stop=(c == NFF - 1))
            k.stt(x1[:, fc, 1:n + 1], p[:, 0:n], modv[:, 24 + fc, j:j + 1], x1[:, fc, 1:n + 1], ALU.mult, ALU.add)
        if last:
            for kk in range(8):
                k.act(sq[:, kk, 0:n], x1[:, kk, 1:n + 1], AF.Square)
            p = nextP()
            for kk in range(8):
                k.matmul(p[:, 0:n], ones_bf[:], sq[:, kk, 0:n], start=(kk == 0), stop=(kk == 7))
            k.ts(tmp[:, 0:n], p[:, 0:n], 1.0 / D, ALU.mult, EPS, ALU.add)
            k.act(tmp[:, 0:n], tmp[:, 0:n], AF.Sqrt)
            k.recip(rstd[:, 0:n], tmp[:, 0:n])
            for kk in range(8):
                k.stt(x1[:, kk, 1:n + 1], x1[:, kk, 1:n + 1], fg[:, kk:kk + 1], rstd[:, 0:n], ALU.mult, ALU.mult)
        k.dma(ydst[kind].rearrange("(k p) t -> p k t", p=128)[:, :, t0:t0 + n], x1[:, :, 1:n + 1])
    k.close_scope()


def build_fused():
    k = KB(); nc = k.nc
    e = make_env(k)
    d_xT = k.dram("xT", [D, 4096]); d_xcT = k.dram("xcT", [D, 256]); d_cvec = k.dram("cvec", [128, 8, 2])
    cn = {nm: k.dram(nm, [128, 128]) for nm in ("mask_f", "mask_b", "dm_f", "dm_b", "row_f", "row_b")}
    d_pcols = k.dram("pcols", [128, 4]); d_cos = k.dram("cos", [128, 32, 128]); d_sin = k.dram("sin", [128, 32, 128])
    L = []
    for l in range(2):
        L.append(dict(wmodA=k.dram("wmodA%d" % l, [D, 2048]), bmodA=k.dram("bmodA%d" % l, [128, 16]), n1g=k.dram("n1g%d" % l, [128, 8]),
                      wmodB=k.dram("wmodB%d" % l, [D, 4096]), bmodB=k.dram("bmodB%d" % l, [128, 32]), n2g=k.dram("n2g%d" % l, [128, 8]),
                      cw=k.dram("convw%d" % l, [128, 3, 44]), cb=k.dram("convb%d" % l, [128, 44]),
                      wout=k.dram("wout%d" % l, [1024 * (l + 1), D]), wup=k.dram("wup%d" % l, [D, 2 * DFF]), wdn=k.dram("wdn%d" % l, [DFF, D])))
    d_fg = k.dram("fg", [128, 8])
    d_wna = k.dram("wna", [2, D, 768]); d_wgl = k.dram("wgl", [2, D, 768]); d_wa = k.dram("wa", [D, 32])
    d_waaug = k.dram("waaug", [17, 2, 256]); d_rpbT = k.dram("rpbT", [2, 128, 4, 15, 64]); d_colmask = k.dram("colmask", [128, 64])
    d_gng = k.dram("gng", [128, 128])
    d_win = k.dram("win", [8, D, 768]); d_dec = k.dram("dec", [128, 16]); d_ng = k.dram("ng", [128, 8, 256])
    d_y = k.dram("yT", [D, 4096], kind="ExternalOutput")
    o0T = nc.dram_tensor("o0T", [1024, NT * 128], BF16, kind="Internal").ap()
    x2T = nc.dram_tensor("x2T", [D, 4096], F32, kind="Internal").ap()
    xc2T = nc.dram_tensor("xc2T", [D, 256], F32, kind="Internal").ap()
    o1T = nc.dram_tensor("o1T", [2048, 4096], BF16, kind="Internal").ap()
    xcdump = nc.dram_tensor("xcdump", [D, 256], F32, kind="Internal").ap()

    k.open_scope()
    hT = k.sb("hT0", [128, 8, NT * 128], BF16)
    phase_hT(e, "a0", d_xT, d_xcT, d_cvec, L[0]["wmodA"], L[0]["bmodA"], L[0]["n1g"], hT)
    for hh in range(2):
        phase_NA(e, "na%d" % hh, hT, d_wna[hh], d_rpbT[hh], d_colmask, o0T, hh * 512)
        phase_GLA(e, "gl%d" % hh, hT, d_wgl[hh], d_wa, d_waaug, d_gng, cn["mask_f"], cn["mask_b"], o0T, hh * 512 + 256,
                  [(0, hh * 2, 0), (1, hh * 2 + 1, 128)])
    k.close_scope()
    segs0 = [(512, 0, "lat", s * 512) for s in range(8)] + [(256, 1, "ctx", 0)]
    phase_F(e, "f0", 1024, False, segs0,
            {"lat": (d_xT, 4096), "ctx": (d_xcT, 256)}, {"lat": (o0T, 256, 4096), "ctx": (o0T, 0, 256)},
            {"lat": x2T, "ctx": xc2T}, d_cvec, L[0]["wmodB"], L[0]["bmodB"], L[0]["n2g"], d_fg, L[0]["cw"], L[0]["cb"],
            L[0]["wout"], L[0]["wup"], L[0]["wdn"])
    k.open_scope()
    hT = k.sb("hT1", [128, 8, NT * 128], BF16)
    phase_hT(e, "a1", x2T, xc2T, d_cvec, L[1]["wmodA"], L[1]["bmodA"], L[1]["n1g"], hT)
    phase_RET(e, "rt", hT, d_win, d_dec, d_ng, d_cos, d_sin, cn, d_pcols, o1T, 8)
    k.close_scope()
    segs1 = [(512, 0, "lat", s * 512) for s in range(8)]
    phase_F(e, "f1", 2048, True, segs1,
            {"lat": (x2T, 4096)}, {"lat": (o1T, 0, 4096)}, {"lat": d_y},
            d_cvec, L[1]["wmodB"], L[1]["bmodB"], L[1]["n2g"], d_fg, L[1]["cw"], L[1]["cb"],
            L[1]["wout"], L[1]["wup"], L[1]["wdn"])
    ncc = k.finish([d_y])
    return ncc, k


_PROG = []


def _maps(inp):
    B = 4
    shared = {}
    shared["ident"] = np.eye(128, dtype=np.float32)
    for kk in ("mask_f", "mask_b", "dm_f", "dm_b", "row_f", "row_b", "pcols"):
        shared[kk] = _C[kk]
    shared["cos"] = np.ascontiguousarray(np.concatenate([_COS, _COS], axis=2)); shared["sin"] = np.ascontiguousarray(np.concatenate([_SIN, _SIN], axis=2))
    for l in range(2):
        shared["wmodA%d" % l] = np.ascontiguousarray(inp["w_mod"][l][:, 0:2048])
        shared["bmodA%d" % l] = pk(inp["b_mod"][l][0:2048])
        shared["n1g%d" % l] = pk(inp["norm1_g"][l])
        shared["wmodB%d" % l] = np.ascontiguousarray(inp["w_mod"][l][:, 2048:6144])
        shared["bmodB%d" % l] = pk(inp["b_mod"][l][2048:6144])
        shared["n2g%d" % l] = pk(inp["norm2_g"][l])
        cw = inp["ffn_conv_w"][l]
        shared["convw%d" % l] = np.ascontiguousarray(np.stack([pk(cw[i]) for i in range(3)], axis=1))
        shared["convb%d" % l] = pk(inp["ffn_conv_b"][l])
        shared["wup%d" % l] = inp["ffn_w_up"][l]; shared["wdn%d" % l] = inp["ffn_w_down"][l]
    perm = np.concatenate([np.arange(0, 256), np.arange(512, 768), np.arange(256, 512), np.arange(768, 1024)])
    shared["wout0"] = np.ascontiguousarray(inp["na_gla_w_out"][0][perm])
    shared["wout1"] = np.ascontiguousarray(inp["ret_w_out"][0])
    shared["fg"] = pk(inp["final_norm_g"])
    w = inp["na_gla_w_in"][0]
    wna = []; wgl = []; rp = []
    for hh in range(2):
        nh = [hh * 4 + i for i in range(4)]; gh = [hh * 2 + i for i in range(2)]
        wna.append(np.concatenate([w[:, h * 64:(h + 1) * 64] for h in nh] + [w[:, 512 + h * 64:512 + (h + 1) * 64] for h in nh] +
                                  [w[:, 1024 + h * 64:1024 + (h + 1) * 64] for h in nh], axis=1))
        wgl.append(np.concatenate([w[:, 1536 + h * 64:1536 + (h + 1) * 64] for h in gh] + [w[:, 1792 + h * 64:1792 + (h + 1) * 64] for h in gh] +
                                  [w[:, 2560 + h * 128:2560 + (h + 1) * 128] for h in gh] + [w[:, 2048 + h * 128:2048 + (h + 1) * 128] for h in gh], axis=1))
        r_, cm = _na_tables(inp["na_rpb"][0][nh]); rp.append(r_)
    shared["wna"] = np.ascontiguousarray(np.stack(wna)); shared["wgl"] = np.ascontiguousarray(np.stack(wgl))
    shared["rpbT"] = np.ascontiguousarray(np.stack(rp)); shared["colmask"] = cm
    shared["wa"] = np.ascontiguousarray(w[:, 3072:3104])
    wa = np.zeros((17, 2, 256), np.float32)
    wa[0:16, 0] = inp["gla_w_a_fwd"][0]; wa[16, 0] = inp["gla_b_a_fwd"][0]
    wa[0:16, 1] = inp["gla_w_a_bwd"][0]; wa[16, 1] = inp["gla_b_a_bwd"][0]
    shared["waaug"] = wa
    shared["gng"] = np.ascontiguousarray(np.broadcast_to(inp["gla_norm_g"][0], (128, 128)).astype(np.float32))
    wr = inp["ret_w_in"][0]
    shared["win"] = np.ascontiguousarray(np.stack([np.concatenate([
        wr[:, h * 128:(h + 1) * 128], wr[:, 1024 + h * 128:1024 + (h + 1) * 128],
        wr[:, 4096 + h * 256:4096 + (h + 1) * 256], wr[:, 2048 + h * 256:2048 + (h + 1) * 256]], axis=1) for h in range(8)]))
    dec = np.concatenate([inp["ret_decay_fwd"][0], inp["ret_decay_bwd"][0]])
    shared["dec"] = np.ascontiguousarray(np.broadcast_to(dec, (128, 16)).astype(np.float32))
    shared["ng"] = np.ascontiguousarray(np.broadcast_to(inp["ret_norm_g"][0], (128, 8, 256)).astype(np.float32))
    maps = []
    for core in range(8):
        b = core // 2
        m = dict(shared)
        m["xT"] = np.ascontiguousarray(inp["x"][b].T); m["xcT"] = np.ascontiguousarray(inp["ctx"][b].T)
        m["cvec"] = np.ascontiguousarray(np.stack([pk(inp["c"][b]), pk(inp["c_ctx"])], axis=-1).astype(np.float32))
        maps.append(m)
    return maps


def kernel(**inp):
    inp = {k_: np.asarray(v) for k_, v in inp.items()}
    if not _PROG:
        _PROG.append(build_fused()[0])
    res = run_bass_kernel_spmd(_PROG[0], _maps(inp), core_ids=list(range(8))).results
    out = np.empty((4, 4096, 1024), np.float32)
    for b in range(4):
        out[b, 0:2048] = res[2 * b]["yT"][:, 0:2048].T
        out[b, 2048:4096] = res[2 * b + 1]["yT"][:, 2048:4096].T
    return out
```

```python
import os
import ml_dtypes
from concourse.bass_utils import run_bass_kernel_spmd

from contextlib import ExitStack
import numpy as np
import concourse.bass as bass
import concourse.mybir as mybir

F32 = mybir.dt.float32
BF16 = mybir.dt.bfloat16
AF = mybir.ActivationFunctionType
ALU = mybir.AluOpType
AX = mybir.AxisListType

ENGS = ("pe", "act", "dve", "pool", "sp")
NDSEM = 12


def _region(ap):
    t = ap.tensor
    name = t.name
    dims = list(ap.ap)
    off = int(ap.offset)
    sp = str(ap.space) if hasattr(ap, "space") else ""
    if "DRAM" in sp.upper() or type(t).__name__.startswith("DRam"):
        ext = sum((int(c) - 1) * abs(int(s)) for s, c in dims)
        return (name, 0, 1, off, off + ext + 1)
    if type(t).__name__.startswith("PSum"):
        return (name, 0, 128, 0, 1 << 40)
    pstep, pcnt = int(dims[0][0]), int(dims[0][1])
    if pstep == 0:
        pstep = 1 << 40
    p0 = off // pstep
    f0 = off % pstep
    ext = sum((int(c) - 1) * abs(int(s)) for s, c in dims[1:])
    return (name, p0, p0 + pcnt, f0, f0 + ext + 1)


def _overlap(a, b):
    return a[1] < b[2] and b[1] < a[2] and a[3] < b[4] and b[3] < a[4]


def _covers(a, b):
    return a[1] <= b[1] and a[2] >= b[2] and a[3] <= b[3] and a[4] >= b[4]


class KB:
    def __init__(self):
        self.nc = bass.Bass("TRN2", target_bir_lowering=False)
        self.es = ExitStack()
        self.ops = []
        self.recs = {}
        self.n_alloc = 0
        self.fence = None
        self.fenced = set()
        self.stack = [self.es]

    def sb(self, name, shape, dt=F32):
        return self.stack[-1].enter_context(self.nc.sbuf_tensor(name, list(shape), dt))

    def barrier(self):
        last = {}
        f = set()
        for i, o in enumerate(self.ops):
            if o["dma"]:
                f.add(i)
            else:
                last[o["eng"]] = i
        f.update(last.values())
        if self.fence is not None:
            f = {i for i in f if i > self.fence_at or not self.ops[i]["dma"]}
        self.fence = f
        self.fence_at = len(self.ops)
        self.fenced = set()

    def open_scope(self):
        self.stack.append(ExitStack())

    def close_scope(self):
        self.barrier()
        self.stack.pop().close()

    def ps(self, name, shape, dt=F32):
        return self.es.enter_context(self.nc.psum_tensor(name, list(shape), dt))

    def dram(self, name, shape, dt=F32, kind="ExternalInput"):
        return self.nc.dram_tensor(name, list(shape), dt, kind=kind).ap()

    def op(self, eng, fn, reads, writes, dma=False):
        idx = len(self.ops)
        deps = set()
        rr = [_region(a) for a in reads if a is not None and hasattr(a, "tensor")]
        ww = [_region(a) for a in writes if a is not None and hasattr(a, "tensor")]
        for r in rr:
            for (g, oi, isw) in self.recs.get(r[0], ()):
                if isw and _overlap(r, g):
                    deps.add(oi)
        for w in ww:
            for (g, oi, isw) in self.recs.get(w[0], ()):
                if _overlap(w, g):
                    deps.add(oi)
        for w in ww:
            lst = self.recs.setdefault(w[0], [])
            lst[:] = [x for x in lst if not _covers(w, x[0])]
            lst.append((w, idx, True))
        for r in rr:
            lst = self.recs.setdefault(r[0], [])
            lst[:] = [x for x in lst if not ((not x[2]) and x[1] < idx and self.ops[x[1]]["eng"] == eng
                                             and not self.ops[x[1]]["dma"] and not dma and _covers(r, x[0]))]
            lst.append((r, idx, False))
        if self.fence is not None and eng not in self.fenced:
            deps.update(self.fence)
            self.fenced.add(eng)
        deps.discard(idx)
        self.ops.append(dict(eng=eng, fn=fn, deps=deps, dma=dma, rr=rr, ww=ww))
        return idx

    def dma(self, out, in_, q="sp"):
        return self.op(q, lambda e: e.dma_start(out=out, in_=in_), [in_], [out], dma=True)

    def matmul(self, out, lhsT, rhs, start=True, stop=True):
        return self.op("pe", lambda e: e.matmul(out, lhsT, rhs, start=start, stop=stop), [lhsT, rhs], [out])

    def transpose(self, out, in_, ident):
        return self.op("pe", lambda e: e.transpose(out, in_, ident), [in_, ident], [out])

    def act(self, out, in_, func, bias=None, scale=None, accum_out=None, eng="act"):
        kw = {}
        if bias is not None:
            kw["bias"] = bias
        if scale is not None:
            kw["scale"] = scale
        if accum_out is not None:
            kw["accum_out"] = accum_out
        return self.op(eng, lambda e: e.activation(out, in_, func, **kw), [in_, bias, scale], [out, accum_out])

    def tt(self, out, in0, in1, op, eng="dve"):
        return self.op(eng, lambda e: e.tensor_tensor(out, in0, in1, op), [in0, in1], [out])

    def ts(self, out, in0, s1, op0, s2=None, op1=None, accum_out=None, eng="dve"):
        def f(e):
            kw = {}
            if accum_out is not None:
                kw["accum_out"] = accum_out
            if op1 is None:
                return e.tensor_scalar(out, in0, s1, None, op0, **kw)
            return e.tensor_scalar(out, in0, s1, s2, op0, op1, **kw)
        return self.op(eng, f, [in0, s1, s2], [out, accum_out])

    def stt(self, out, in0, scalar, in1, op0, op1, eng="dve"):
        return self.op(eng, lambda e: e.scalar_tensor_tensor(out, in0, scalar, in1, op0, op1), [in0, scalar, in1], [out])

    def copy(self, out, in_, eng="dve"):
        if eng == "act":
            return self.op("act", lambda e: e.copy(out, in_), [in_], [out])
        return self.op(eng, lambda e: e.tensor_copy(out, in_), [in_], [out])

    def memset(self, ap, val, eng="dve"):
        return self.op(eng, lambda e: e.memset(ap, val), [], [ap])

    def recip(self, out, in_):
        return self.op("dve", lambda e: e.reciprocal(out, in_), [in_], [out])

    def reduce(self, out, in_, op=ALU.add, axis=AX.X, eng="dve"):
        return self.op(eng, lambda e: e.tensor_reduce(out, in_, axis, op), [in_], [out])

    def finish(self, out_aps):
        nc = self.nc
        ops = self.ops
        out_names = {a.tensor.name for a in out_aps}
        final_deps = set()
        for i, o in enumerate(ops):
            if o["dma"] and any(w[0] in out_names for w in o["ww"]):
                final_deps.add(i)
        ops.append(dict(eng="sp", fn=None, deps=final_deps, dma=False, rr=[], ww=[]))
        needs_sig = [False] * len(ops)
        for o in ops:
            for d in o["deps"]:
                if ops[d]["dma"]:
                    continue
                if ops[d]["eng"] == o["eng"] and not o["dma"] and o["eng"] == "pe":
                    continue
                needs_sig[d] = True
        sems = {e: self.es.enter_context(nc.semaphore("s_" + e)) for e in ENGS}
        dsems = {e: [self.es.enter_context(nc.semaphore("d_%s_%d" % (e, i))) for i in range(NDSEM)]
                 for e in ("sp", "act", "pool")}
        cnt = {e: 0 for e in ENGS}
        dcnt = {e: 0 for e in dsems}
        sig = [None] * len(ops)
        prevdma = [None] * len(ops)
        for i, o in enumerate(ops):
            if o["dma"]:
                q = o["eng"]
                n = dcnt[q]
                dcnt[q] += 1
                s = dsems[q][n % NDSEM]
                sig[i] = (s, 16 * (n // NDSEM + 1))
                if n >= NDSEM:
                    prevdma[i] = (s, 16 * (n // NDSEM))
            elif needs_sig[i]:
                cnt[o["eng"]] += 1
                sig[i] = (sems[o["eng"]], cnt[o["eng"]])
        per_eng = {e: [] for e in ENGS}
        for i, o in enumerate(ops):
            per_eng[o["eng"]].append(i)
        self.stats = {e: len(per_eng[e]) for e in ENGS}
        self.stats["sig"] = dict(cnt)

        def emit(ename):
            def body(eng):
                seen = {}
                for i in per_eng[ename]:
                    o = ops[i]
                    waits = {}
                    for d in o["deps"]:
                        po = ops[d]
                        if (not po["dma"]) and po["eng"] == ename and ename == "pe" and not o["dma"]:
                            continue
                        s, v = sig[d]
                        key = id(s)
                        if waits.get(key, (None, 0))[1] < v:
                            waits[key] = (s, v)
                    if prevdma[i] is not None:
                        s, v = prevdma[i]
                        key = id(s)
                        if waits.get(key, (None, 0))[1] < v:
                            waits[key] = (s, v)
                    for key, (s, v) in waits.items():
                        if seen.get(key, 0) >= v:
                            continue
                        eng.wait_ge(s, v)
                        seen[key] = v
                    if o["fn"] is None:
                        continue
                    ins = o["fn"](eng)
                    if sig[i] is not None:
                        s, v = sig[i]
                        ins.then_inc(s, 16 if o["dma"] else 1)
            return body

        with nc.Block() as block:
            block.tensor(emit("pe"))
            block.scalar(emit("act"))
            block.vector(emit("dve"))
            block.gpsimd(emit("pool"))
            block.sync(emit("sp"))
        self.es.close()
        return nc

D = 1024; DFF = 2816; NFF = 22; EPS = 1e-6; NT = 34


def col_tiles(cols):
    nt = (cols + 511) // 512
    base = cols // nt
    res = []; s = 0
    for i in range(nt):
        e = s + base + (1 if i < cols % nt else 0)
        res.append((s, e)); s = e
    return res


def load_cast_rows(k, dst, src, nk, width, q="pool", split=1):
    v = src.rearrange("(k p) c -> p k c", p=128)
    step = max(1, nk // split)
    for a in range(0, nk, step):
        b = min(nk, a + step)
        k.dma(dst[:, a:b, :], v[:, a:b, :], q=q)


def build_F(layer_has_ctx, Fdim, last, segs):
    k = KB()
    KF = Fdim // 128
    NMAX = max(n for n, _ in segs); CMAX = NMAX + 2
    d_x = [k.dram("xT_%d" % i, [D, n + 2]) for i, (n, _) in enumerate(segs)]
    d_o = [k.dram("oT_%d" % i, [Fdim, n + 2], BF16) for i, (n, _) in enumerate(segs)]
    d_hm = [k.dram("hm_%d" % i, [128, 2]) for i, (n, _) in enumerate(segs)]
    d_y = [k.dram("yT_%d" % i, [D, n], kind="ExternalOutput") for i, (n, _) in enumerate(segs)]
    d_cvec = k.dram("cvec", [128, 8, 2])
    d_wmod = k.dram("wmod", [D, 4096]); d_bmod = k.dram("bmod", [128, 32])
    d_n2g = k.dram("n2g", [128, 8]); d_fg = k.dram("fg", [128, 8])
    d_cw = k.dram("convw", [128, 3, 44]); d_cb = k.dram("convb", [128, 44])
    d_wout = k.dram("wout", [Fdim, D]); d_wup = k.dram("wup", [D, 2 * DFF]); d_wdn = k.dram("wdn", [DFF, D])
    ones_bf = k.sb("ones_bf", [128, 128], BF16)
    cvec = k.sb("cvec_s", [128, 8, 2]); scb = k.sb("scb", [128, 8, 2], BF16)
    bmod = k.sb("bmod_s", [128, 32]); modv = k.sb("modv", [128, 32, 2])
    n2g = k.sb("n2g_s", [128, 8]); fg = k.sb("fg_s", [128, 8]); gm2 = k.sb("gm2", [128, 8, 2])
    cw = k.sb("cw_s", [128, 3, 44]); cb = k.sb("cb_s", [128, 44])
    wmb = [k.sb("wmb%d" % i, [128, 8, 512], BF16) for i in range(2)]
    wout = k.sb("wout_s", [128, KF, D], BF16)
    wdn = k.sb("wdn_s", [128, NFF, D], BF16)
    oT = k.sb("oT_s", [128, KF, CMAX], BF16)
    x1 = k.sb("x1T", [128, 8, CMAX])
    sq = k.sb("sq", [128, 8, CMAX], BF16)
    rstd = k.sb("rstd", [128, CMAX]); tmp = k.sb("tmp", [128, CMAX])
    h2 = k.sb("h2T", [128, 8, CMAX], BF16)
    hm = k.sb("hm_s", [128, 2])
    wua = [k.sb("wua%d" % i, [128, 8, 256], BF16) for i in range(2)]
    wub = [k.sb("wub%d" % i, [128, 8, 256], BF16) for i in range(2)]
    u = [k.sb("u%d" % i, [128, 2, CMAX]) for i in range(2)]
    va = k.sb("va", [128, NMAX]); vb = k.sb("vb", [128, NMAX]); sa = k.sb("sa", [128, NMAX])
    tT = k.sb("tT", [128, NFF, NMAX], BF16)
    P = [k.ps("P%d" % i, [128, 512]) for i in range(8)]
    pctr = [0]

    def nextP():
        p = P[pctr[0] % 8]; pctr[0] += 1
        return p

    k.memset(ones_bf[:], 1.0)
    k.dma(cvec[:], d_cvec); k.dma(bmod[:], d_bmod); k.dma(n2g[:], d_n2g); k.dma(fg[:], d_fg)
    k.dma(cw[:], d_cw); k.dma(cb[:], d_cb)
    k.act(scb[:], cvec[:], AF.Silu)
    pm = nextP()
    for g in range(8):
        wb = wmb[g % 2]
        k.dma(wb[:], d_wmod.rearrange("(k p) c -> p k c", p=128)[:, :, g * 512:(g + 1) * 512], q="pool")
        for c4 in range(4):
            cc = g * 4 + c4
            for kk in range(8):
                k.matmul(pm[:, cc * 2:cc * 2 + 2], wb[:, kk, c4 * 128:(c4 + 1) * 128], scb[:, kk, :],
                         start=(kk == 0), stop=(kk == 7))
    pmv = pm[:, 0:64].rearrange("p (c j) -> p c j", j=2)
    for j in range(2):
        k.tt(modv[:, :, j], pmv[:, :, j], bmod[:], ALU.add)
    for j in range(2):
        k.ts(gm2[:, :, j], modv[:, 16:24, j], 1.0, ALU.add)
        k.tt(gm2[:, :, j], gm2[:, :, j], n2g[:], ALU.mult)
    load_cast_rows(k, wout, d_wout, KF, D, split=4)
    load_cast_rows(k, wdn, d_wdn, NFF, D, split=4)
    wupv = d_wup.rearrange("(k p) c -> p k c", p=128)

    for si, (n, j) in enumerate(segs):
        cols = n + 2
        tiles = col_tiles(cols)
        k.dma(x1[:, :, 0:cols], d_x[si].rearrange("(k p) t -> p k t", p=128))
        k.dma(oT[:, :, 0:cols], d_o[si].rearrange("(k p) t -> p k t", p=128))
        k.dma(hm[:], d_hm[si])
        for fc in range(8):
            for (a, b) in tiles:
                p = nextP()
                for kk in range(KF):
                    k.matmul(p[:, 0:b - a], wout[:, kk, fc * 128:(fc + 1) * 128], oT[:, kk, a:b],
                             start=(kk == 0), stop=(kk == KF - 1))
                k.stt(x1[:, fc, a:b], p[:, 0:b - a], modv[:, 0 + fc, j:j + 1], x1[:, fc, a:b], ALU.mult, ALU.add)
        for kk in range(8):
            k.act(sq[:, kk, 0:cols], x1[:, kk, 0:cols], AF.Square)
        for (a, b) in tiles:
            p = nextP()
            for kk in range(8):
                k.matmul(p[:, 0:b - a], ones_bf[:], sq[:, kk, a:b], start=(kk == 0), stop=(kk == 7))
            k.act(tmp[:, a:b], p[:, 0:b - a], AF.Sqrt, bias=EPSB[0], scale=1.0 / D)
        k.recip(rstd[:, 0:cols], tmp[:, 0:cols])
        for kk in range(8):
            k.stt(tmp[:, 0:cols], x1[:, kk, 0:cols], gm2[:, kk, j:j + 1], rstd[:, 0:cols], ALU.mult, ALU.mult)
            k.act(h2[:, kk, 0:cols], tmp[:, 0:cols], AF.Identity, bias=modv[:, 8 + kk, j:j + 1], scale=1.0)
        k.ts(h2[:, :, 0:1], h2[:, :, 0:1], hm[:, 0:1], ALU.mult)
        k.ts(h2[:, :, cols - 1:cols], h2[:, :, cols - 1:cols], hm[:, 1:2], ALU.mult)
        for g in range(11):
            wa = wua[g % 2]; wb_ = wub[g % 2]
            k.dma(wa[:], wupv[:, :, g * 256:(g + 1) * 256], q="pool")
            k.dma(wb_[:], wupv[:, :, DFF + g * 256:DFF + (g + 1) * 256], q="pool")
            for c2 in range(2):
                c = g * 2 + c2
                ub = u[c % 2]
                for half, w in ((0, wa), (1, wb_)):
                    for (a, b) in tiles:
                        p = nextP()
                        for kk in range(8):
                            k.matmul(p[:, 0:b - a], w[:, kk, c2 * 128:(c2 + 1) * 128], h2[:, kk, a:b],
                                     start=(kk == 0), stop=(kk == 7))
                        k.copy(ub[:, half, a:b], p[:, 0:b - a], eng="act")
                ca = c; cbi = NFF + c
                k.act(va[:, 0:n], ub[:, 0, 1:n + 1], AF.Identity, bias=cb[:, ca:ca + 1], scale=cw[:, 1, ca:ca + 1])
                k.stt(va[:, 0:n], ub[:, 0, 0:n], cw[:, 0, ca:ca + 1], va[:, 0:n], ALU.mult, ALU.add)
                k.stt(va[:, 0:n], ub[:, 0, 2:n + 2], cw[:, 2, ca:ca + 1], va[:, 0:n], ALU.mult, ALU.add)
                k.act(vb[:, 0:n], ub[:, 1, 1:n + 1], AF.Identity, bias=cb[:, cbi:cbi + 1], scale=cw[:, 1, cbi:cbi + 1])
                k.stt(vb[:, 0:n], ub[:, 1, 0:n], cw[:, 0, cbi:cbi + 1], vb[:, 0:n], ALU.mult, ALU.add)
                k.stt(vb[:, 0:n], ub[:, 1, 2:n + 2], cw[:, 2, cbi:cbi + 1], vb[:, 0:n], ALU.mult, ALU.add)
                k.act(sa[:, 0:n], va[:, 0:n], AF.Silu)
                k.tt(tT[:, c, 0:n], sa[:, 0:n], vb[:, 0:n], ALU.mult)
        for fc in range(8):
            p = nextP()
            for c in range(NFF):
                k.matmul(p[:, 0:n], wdn[:, c, fc * 128:(fc + 1) * 128], tT[:, c, 0:n], start=(c == 0), stop=(c == NFF - 1))
            k.stt(x1[:, fc, 1:n + 1], p[:, 0:n], modv[:, 24 + fc, j:j + 1], x1[:, fc, 1:n + 1], ALU.mult, ALU.add)
        if last:
            for kk in range(8):
                k.act(sq[:, kk, 0:n], x1[:, kk, 1:n + 1], AF.Square)
            p = nextP()
            for kk in range(8):
                k.matmul(p[:, 0:n], ones_bf[:], sq[:, kk, 0:n], start=(kk == 0), stop=(kk == 7))
            k.act(tmp[:, 0:n], p[:, 0:n], AF.Sqrt, bias=EPSB[0], scale=1.0 / D)
            k.recip(rstd[:, 0:n], tmp[:, 0:n])
            for kk in range(8):
                k.stt(x1[:, kk, 1:n + 1], x1[:, kk, 1:n + 1], fg[:, kk:kk + 1], rstd[:, 0:n], ALU.mult, ALU.mult)
        k.dma(d_y[si].rearrange("(k p) t -> p k t", p=128), x1[:, :, 1:n + 1])
    nc = k.finish(d_y)
    return nc, k

EPSB = [EPS]


NT = 34


def emit_mod(k, nextP, d_cvec, d_wmod, d_bmod, ncols, wmb, name="m"):
    ncc = ncols // 128
    cvec = k.sb(name + "cvec", [128, 8, 2]); scb = k.sb(name + "scb", [128, 8, 2], BF16)
    bmod = k.sb(name + "bmod", [128, ncc]); modv = k.sb(name + "modv", [128, ncc, 2])
    k.dma(cvec[:], d_cvec); k.dma(bmod[:], d_bmod)
    k.act(scb[:], cvec[:], AF.Silu)
    pm = nextP()
    for g in range(ncols // 512):
        wb = wmb[g % len(wmb)]
        k.dma(wb[:], d_wmod.rearrange("(k p) c -> p k c", p=128)[:, :, g * 512:(g + 1) * 512], q="pool")
        for c4 in range(4):
            cc = g * 4 + c4
            for kk in range(8):
                k.matmul(pm[:, cc * 2:cc * 2 + 2], wb[:, kk, c4 * 128:(c4 + 1) * 128], scb[:, kk, :],
                         start=(kk == 0), stop=(kk == 7))
    pmv = pm[:, 0:2 * ncc].rearrange("p (c j) -> p c j", j=2)
    for j in range(2):
        k.tt(modv[:, :, j], pmv[:, :, j], bmod[:], ALU.add)
    return modv


def emit_hT(k, nextP, hT, d_xT, d_xcT, modv, n1g, ones_bf, xt, sq, rstd, tmp, name=""):
    gm1 = k.sb(name + "gm1", [128, 8, 2]); tmp2 = k.sb(name + "tmp2", [128, 256])
    for j in range(2):
        k.ts(gm1[:, :, j], modv[:, 8:16, j], 1.0, ALU.add)
        k.tt(gm1[:, :, j], gm1[:, :, j], n1g[:], ALU.mult)
    W = 256
    jobs = [(d_xcT, 0, 0, 1)] + [(d_xT, i * W, 256 + i * W, 0) for i in range(4096 // W)]
    for ji, (src, c0, h0, j) in enumerate(jobs):
        x = xt[ji % len(xt)]
        k.dma(x[:], src.rearrange("(k p) t -> p k t", p=128)[:, :, c0:c0 + W])
        for kk in range(8):
            k.act(sq[:, kk, :], x[:, kk, :], AF.Square)
        p = nextP()
        for kk in range(8):
            k.matmul(p[:, 0:W], ones_bf[:], sq[:, kk, :], start=(kk == 0), stop=(kk == 7))
        k.ts(tmp[:, 0:W], p[:, 0:W], 1.0 / D, ALU.mult, EPS, ALU.add)
        k.act(tmp[:, 0:W], tmp[:, 0:W], AF.Sqrt)
        k.recip(rstd[:, 0:W], tmp[:, 0:W])
        for kk in range(8):
            tb = tmp if kk % 2 == 0 else sq[:, 0:4, :].bitcast(F32).rearrange("p a w -> p (a w)") if False else (tmp if kk % 2 == 0 else tmp2)
            k.stt(tb[:, 0:W], x[:, kk, :], gm1[:, kk, j:j + 1], rstd[:, 0:W], ALU.mult, ALU.mult)
            k.act(hT[:, kk, h0:h0 + W], tb[:, 0:W], AF.Identity, bias=modv[:, kk, j:j + 1], scale=1.0)


def rope(k, out_bf, x, cos_t, sin_t, tA, tB):
    xv = x.rearrange("p (a h f) -> p a h f", a=2, h=2)
    ov = out_bf.rearrange("p (a h f) -> p a h f", a=2, h=2)
    Av = tA.rearrange("p (a h f) -> p a h f", a=2, h=2)
    Bv = tB.rearrange("p (a h f) -> p a h f", a=2, h=2)
    cb = cos_t.rearrange("p (a f) -> p a f", a=2).unsqueeze(2).to_broadcast([128, 2, 2, 32])
    sv = sin_t.rearrange("p (a f) -> p a f", a=2)
    k.tt(Av, xv, cb, ALU.mult)
    k.tt(Bv[:, :, 0, :], xv[:, :, 1, :], sv, ALU.mult)
    k.tt(Bv[:, :, 1, :], xv[:, :, 0, :], sv, ALU.mult)
    k.tt(ov[:, :, 0, :], Av[:, :, 0, :], Bv[:, :, 0, :], ALU.subtract)
    k.tt(ov[:, :, 1, :], Av[:, :, 1, :], Bv[:, :, 1, :], ALU.add)


def make_consts():
    j = np.arange(128)[:, None].astype(np.float32); i = np.arange(128)[None, :].astype(np.float32)
    c = {}
    c["mask_f"] = (i >= j).astype(np.float32)
    c["mask_b"] = (j >= i).astype(np.float32)
    c["dm_f"] = np.maximum(i - j, 0.0); c["dm_b"] = np.maximum(j - i, 0.0)
    c["row_f"] = np.broadcast_to(i + 1.0, (128, 128)).copy()
    c["row_b"] = np.broadcast_to(128.0 - i, (128, 128)).copy()
    pc = np.zeros((128, 4), np.float32)
    pc[:, 0] = 127.0 - np.arange(128)
    pc[:, 1] = np.arange(128)
    pc[:, 2] = 128.0
    c["pcols"] = pc
    return {kk: np.ascontiguousarray(v.astype(np.float32)) for kk, v in c.items()}


def rope_tables():
    pos = np.arange(4096)
    row = (pos // 64).astype(np.float32); col = (pos % 64).astype(np.float32)
    inv = (10000.0 ** (-np.arange(0, 64, 2, dtype=np.float32) / 64.0)).astype(np.float32)
    ang = np.concatenate([row[:, None] * inv, col[:, None] * inv], axis=-1).astype(np.float32)
    cos = np.cos(ang).astype(np.float32); sin = np.sin(ang).astype(np.float32)
    cs = np.ascontiguousarray(cos.reshape(32, 128, 64).transpose(1, 0, 2))
    sn = np.ascontiguousarray(sin.reshape(32, 128, 64).transpose(1, 0, 2))
    return cs, sn


def build_M1():
    k = KB()
    DK = 128; DV = 256; NH = 4
    d_xT = k.dram("xT", [D, 4096]); d_xcT = k.dram("xcT", [D, 256])
    d_cvec = k.dram("cvec", [128, 8, 2]); d_wmod = k.dram("wmod", [D, 2048]); d_bmod = k.dram("bmod", [128, 16])
    d_n1g = k.dram("n1g", [128, 8])
    d_win = k.dram("win", [NH, D, 768])
    d_dec = k.dram("dec", [128, 8])
    d_ng = k.dram("ng", [128, NH, DV])
    d_cos = k.dram("cos", [128, 32, 64]); d_sin = k.dram("sin", [128, 32, 64])
    cn = {nm: k.dram(nm, [128, 128]) for nm in ("mask_f", "mask_b", "dm_f", "dm_b", "row_f", "row_b")}
    d_pcols = k.dram("pcols", [128, 4]); d_ident = k.dram("ident", [128, 128])
    d_oT = k.dram("oT", [NH * DV, 4096], BF16, kind="ExternalOutput")

    P = [k.ps("P%d" % i, [128, 512]) for i in range(6)]
    PT = [k.ps("PT%d" % i, [128, 1024], BF16) for i in range(2)]
    pc = [0, 0]

    def nextP():
        p = P[pc[0] % 6]; pc[0] += 1; return p

    def nextPT():
        p = PT[pc[1] % 2]; pc[1] += 1; return p

    ones_bf = k.sb("ones_bf", [128, 128], BF16); k.memset(ones_bf[:], 1.0)
    identf = k.sb("identf", [128, 128]); ident = k.sb("ident_s", [128, 128], BF16)
    k.dma(identf[:], d_ident); k.copy(ident[:], identf[:])
    n1g = k.sb("n1g_s", [128, 8]); k.dma(n1g[:], d_n1g)
    whd = [k.sb("whd0", [128, 8, 768], BF16)]
    wmb = [whd[0][:, :, 0:512]]
    import os
    if 'nomod' in os.environ.get('M1_SKIP', ''):
        modv = k.sb("mmodv", [128, 16, 2]); k.memset(modv[:], 0.1)
    else:
        modv = emit_mod(k, nextP, d_cvec, d_wmod, d_bmod, 2048, wmb)
    hT = k.sb("hT", [128, 8, NT * 128], BF16)
    xt = [k.sb("xt0", [128, 8, 256])]
    sq = k.sb("sq", [128, 8, 256], BF16); rstd = k.sb("rstd", [128, 256]); tmp = k.sb("tmp", [128, 256])
    import os
    if 'nohT' in os.environ.get('M1_SKIP', ''):
        k.memset(hT[:, :, 0:512], 0.01)
    else:
        emit_hT(k, nextP, hT, d_xT, d_xcT, modv, n1g, ones_bf, xt, sq, rstd, tmp)

    cs = {nm: k.sb(nm + "_s", [128, 128]) for nm in cn}
    for nm in cn:
        k.dma(cs[nm][:], cn[nm])
    pcols = k.sb("pcols_s", [128, 4]); k.dma(pcols[:], d_pcols)
    cos = k.sb("cos_s", [128, 32, 64]); sin = k.sb("sin_s", [128, 32, 64])
    ng = k.sb("ng_s", [128, NH, DV])
    if 'nocs' not in os.environ.get('M1_SKIP', ''):
        k.dma(cos[:], d_cos); k.dma(sin[:], d_sin)
        k.dma(ng[:], d_ng)
    dec = k.sb("dec_s", [128, 8]); k.dma(dec[:], d_dec)
    lg = k.sb("lg", [128, 8]); e1 = k.sb("e1", [128, 8])
    k.act(e1[:], dec[:], AF.Exp, scale=-1.0)
    k.act(e1[:], e1[:], AF.Ln, bias=1.0, scale=1.0)
    k.ts(lg[:], e1[:], -1.0, ALU.mult)

    kb = k.sb("kb", [128, NT, DK], BF16); qT = k.sb("qT", [128, NT, 128], BF16); kT = k.sb("kT", [128, NT, 128], BF16)
    vb = k.sb("vb", [128, NT, DV], BF16); gs = k.sb("gs", [128, 32, DV], BF16)
    oacc = k.sb("oacc", [128, 32, DV], BF16)
    oTh = [k.sb("oTh%d" % i, [128, 2, 512], BF16) for i in range(2)]
    qf = k.sb("qf", [128, 128]); kf = k.sb("kf", [128, 128]); ksc = k.sb("ksc", [128, 128])
    tA = k.sb("tA", [128, 128]); tB = k.sb("tB", [128, 128])
    DM = [k.sb("DM%d" % i, [128, 128]) for i in range(2)]
    EBr = [k.sb("EBr%d" % i, [128, 128]) for i in range(2)]
    ERc = k.sb("ERc", [128, 2]); dcol = k.sb("dcol", [128, 2])
    S = k.sb("S", [128, DV]); Sb = k.sb("Sb", [128, DV], BF16)
    AT = k.sb("AT", [128, 128], BF16); qin = k.sb("qin", [128, 128], BF16); kk_ = k.sb("kk", [128, 128], BF16)
    st6 = k.sb("st6", [128, 6]); mv = k.sb("mv", [128, 2]); rs = k.sb("rs", [128, 1]); on = k.sb("on", [128, DV])
    obf = k.sb("obf", [128, DV], BF16)

    import os
    STOP = float(os.environ.get('M1_STOP', '99')); NTL = int(os.environ.get('M1_NT', '34'))
    gf = k.sb("gf", [128, DV])
    for hl in range(NH):
        w = whd[0]
        k.dma(w[:], d_win[hl].rearrange("(k p) c -> p k c", p=128), q="pool")
        for di, (dmn, mkn, rown) in enumerate((("dm_f", "mask_f", "row_f"), ("dm_b", "mask_b", "row_b"))):
            lgc = lg[:, di * 4 + hl:di * 4 + hl + 1]
            k.act(DM[di][:], cs[dmn][:], AF.Exp, scale=lgc)
            k.tt(DM[di][:], DM[di][:], cs[mkn][:], ALU.mult)
            k.act(EBr[di][:], cs[rown][:], AF.Exp, scale=lgc)
            k.act(ERc[:, di:di + 1], pcols[:, di:di + 1], AF.Exp, scale=lgc)
            k.act(dcol[:, di:di + 1], pcols[:, 2:3], AF.Exp, scale=lgc)
        for t in range(NT):
            pa = nextP(); pb = nextP()
            for kk in range(8):
                k.matmul(pa[:, 0:512], hT[:, kk, t * 128:(t + 1) * 128], w[:, kk, 0:512], start=(kk == 0), stop=(kk == 7))
            for kk in range(8):
                k.matmul(pb[:, 0:256], hT[:, kk, t * 128:(t + 1) * 128], w[:, kk, 512:768], start=(kk == 0), stop=(kk == 7))
            k.ts(ksc[:], pa[:, 128:256], float(DK) ** -0.5, ALU.mult)
            if t >= 2:
                rope(k, qf[:], pa[:, 0:128], cos[:, t - 2, :], sin[:, t - 2, :], tA[:], tB[:])
                rope(k, kf[:], ksc[:], cos[:, t - 2, :], sin[:, t - 2, :], tA[:], tB[:])
                ksrc = kf
                k.copy(gf[:], pa[:, 256:512])
                k.act(gs[:, t - 2, :], gf[:], AF.Silu)
            else:
                k.copy(qf[:], pa[:, 0:128])
                ksrc = ksc
            k.copy(kb[:, t, :], ksrc[:], eng="pool")
            k.copy(vb[:, t, :], pb[:, 0:256])
            pt = nextP()
            k.transpose(pt[:, 0:128], qf[:], identf[:])
            k.transpose(pt[:, 128:256], ksrc[:], identf[:])
            k.copy(qT[:, t, :], pt[:, 0:128])
            k.copy(kT[:, t, :], pt[:, 128:256])
        for di in (1, 0):
            order = [1, 0] + list(range(33, 1, -1)) if di == 1 else list(range(NT))
            k.memset(S[:], 0.0); k.memset(Sb[:], 0.0)
            for t in order:
                if t >= 2:
                    pat = nextP()
                    k.matmul(pat[:, 0:128], kT[:, t, :], qT[:, t, :])
                    k.tt(AT[:], pat[:, 0:128], DM[di][:], ALU.mult)
                    k.tt(qin[:], qT[:, t, :], EBr[di][:], ALU.mult, eng="pool")
                    po = nextP()
                    k.matmul(po[:, 0:DV], AT[:], vb[:, t, :], start=True, stop=False)
                    k.matmul(po[:, 0:DV], qin[:], Sb[:], start=False, stop=True)
                k.ts(kk_[:], kb[:, t, :], ERc[:, di:di + 1], ALU.mult, eng="pool")
                pkv = nextP()
                k.matmul(pkv[:, 0:DV], kk_[:], vb[:, t, :])
                if t >= 2:
                    if di == 1:
                        k.copy(oacc[:, t - 2, :], po[:, 0:DV])
                    else:
                        k.tt(on[:], po[:, 0:DV], oacc[:, t - 2, :], ALU.add)
                        k.op("dve", lambda e, a=st6, b=on: e.bn_stats(a[:], b[:]), [on[:]], [st6[:]])
                        k.op("dve", lambda e, a=mv, b=st6: e.bn_aggr(a[:], b[:]), [st6[:]], [mv[:]])
                        k.ts(rs[:], mv[:, 1:2], EPS, ALU.add)
                        k.act(rs[:], rs[:], AF.Sqrt)
                        k.recip(rs[:], rs[:])
                        k.ts(on[:], on[:], mv[:, 0:1], ALU.subtract, rs[:, 0:1], ALU.mult)
                        k.tt(on[:], on[:], ng[:, hl, :], ALU.mult, eng="pool")
                        k.tt(obf[:], on[:], gs[:, t - 2, :], ALU.mult, eng="pool")
                        pt = nextPT()
                        k.transpose(pt[:, 0:128], obf[:, 0:128], ident[:])
                        k.transpose(pt[:, 128:256], obf[:, 128:256], ident[:])
                        lt = t - 2
                        ob = oTh[(lt // 4) % 2]
                        k.copy(ob[:, :, (lt % 4) * 128:(lt % 4 + 1) * 128], pt[:, 0:256].rearrange("p (c t) -> p c t", c=2))
                        if lt % 4 == 3:
                            k.dma(d_oT[hl * DV:(hl + 1) * DV, (lt // 4) * 512:(lt // 4 + 1) * 512].rearrange("(c p) t -> p c t", p=128), ob[:])
                k.stt(S[:], S[:], dcol[:, di:di + 1], pkv[:, 0:DV], ALU.mult, ALU.add)
                k.copy(Sb[:], S[:], eng="act")
    nc = k.finish([d_oT])
    return nc, k


NEG = -30000.0


def na_configs():
    cfgs = []; plan = {}
    for g in range(32):
        lst = []
        for u in range(32):
            key = []
            for kh in range(2):
                for qh in range(2):
                    r = 2 * g + qh; kr = 2 * u + kh
                    r0 = min(max(r - 4, 0), 56)
                    key.append(kr - r + 7 if r0 <= kr <= r0 + 7 else None)
            key = tuple(key)
            if all(x is None for x in key):
                continue
            if key not in cfgs:
                cfgs.append(key)
            lst.append((u, cfgs.index(key)))
        plan[g] = lst
    return cfgs, plan


def build_M0():
    k = KB()
    d_xT = k.dram("xT", [D, 4096]); d_xcT = k.dram("xcT", [D, 256])
    d_cvec = k.dram("cvec", [128, 8, 2]); d_wmod = k.dram("wmod", [D, 2048]); d_bmod = k.dram("bmod", [128, 16])
    d_n1g = k.dram("n1g", [128, 8])
    d_wna = k.dram("wna", [D, 768]); d_wgl = k.dram("wgl", [D, 768]); d_wa = k.dram("wa", [D, 32])
    d_waaug = k.dram("waaug", [17, 2, 128])
    d_rpbT = k.dram("rpbT", [128, 4, 15, 64]); d_colmask = k.dram("colmask", [128, 64])
    d_gng = k.dram("gng", [128, 128])
    d_maskf = k.dram("mask_f", [128, 128]); d_maskb = k.dram("mask_b", [128, 128])
    d_ident = k.dram("ident", [128, 128])
    d_oT = k.dram("oT", [512, NT * 128], BF16, kind="ExternalOutput")

    P = [k.ps("P%d" % i, [128, 512]) for i in range(6)]
    PT = [k.ps("PT%d" % i, [128, 1024], BF16) for i in range(2)]
    pc = [0, 0]

    NRR = [6]

    def nextP():
        p = P[pc[0] % NRR[0]]; pc[0] += 1; return p

    def nextPT():
        p = PT[pc[1] % 2]; pc[1] += 1; return p

    ones_bf = k.sb("ones_bf", [128, 128], BF16); k.memset(ones_bf[:], 1.0)
    identf = k.sb("identf", [128, 128]); ident = k.sb("ident_s", [128, 128], BF16)
    k.dma(identf[:], d_ident); k.copy(ident[:], identf[:])
    n1g = k.sb("n1g_s", [128, 8]); k.dma(n1g[:], d_n1g)
    hT = k.sb("hT", [128, 8, NT * 128], BF16)
    oTs = [k.sb("oTs%d" % i, [128, 2, 512], BF16) for i in range(2)]
    k.open_scope()
    wmb = [k.sb("wmb0", [128, 8, 512], BF16)]
    modv = emit_mod(k, nextP, d_cvec, d_wmod, d_bmod, 2048, wmb)
    xt = [k.sb("xt0", [128, 8, 256]), k.sb("xt1", [128, 8, 256])]
    sq = k.sb("sq", [128, 8, 256], BF16); rstd = k.sb("rstd", [128, 256]); tmp = k.sb("tmp", [128, 256])
    emit_hT(k, nextP, hT, d_xT, d_xcT, modv, n1g, ones_bf, xt, sq, rstd, tmp)
    k.close_scope()

    cfgs, plan = na_configs()
    k.open_scope()
    wna = k.sb("wna_s", [128, 8, 768], BF16)
    k.dma(wna[:], d_wna.rearrange("(k p) c -> p k c", p=128), q="pool")
    QT = k.sb("QT", [64, 4, NT * 128], BF16); KT = k.sb("KT", [64, 4, NT * 128], BF16)
    Vaug = k.sb("Vaug", [128, NT, 4, 65], BF16)
    k.memset(Vaug[:], 1.0)
    BT = k.sb("BT", [128, len(cfgs), 4, 128])
    k.open_scope()
    Btab = k.sb("Btab", [128, 4, 15, 64]); cmask = k.sb("cmask", [128, 64])
    k.dma(Btab[:], d_rpbT); k.dma(cmask[:], d_colmask)
    for h in range(4):
        k.tt(Btab[:, h, :, :], Btab[:, h, :, :], cmask[:].unsqueeze(1).to_broadcast([128, 15, 64]), ALU.add)
    for ci, key in enumerate(cfgs):
        bi = 0
        for kh in range(2):
            for qh in range(2):
                roff = key[bi]; bi += 1
                dst = BT[kh * 64:(kh + 1) * 64, ci, :, qh * 64:(qh + 1) * 64]
                if roff is None:
                    k.memset(dst, NEG, eng="pool")
                else:
                    k.copy(dst, Btab[kh * 64:(kh + 1) * 64, :, roff, :], eng="pool")
    k.close_scope()
    qs = k.sb("qs", [128, 256]); ks_ = k.sb("ks", [128, 256])
    for t in range(NT):
        pa = nextP(); pb = nextP()
        for kk in range(8):
            k.matmul(pa[:, 0:512], hT[:, kk, t * 128:(t + 1) * 128], wna[:, kk, 0:512], start=(kk == 0), stop=(kk == 7))
        for kk in range(8):
            k.matmul(pb[:, 0:256], hT[:, kk, t * 128:(t + 1) * 128], wna[:, kk, 512:768], start=(kk == 0), stop=(kk == 7))
        k.ts(qs[:], pa[:, 0:256], 0.125, ALU.mult)
        k.copy(ks_[:], pa[:, 256:512])
        k.copy(Vaug[:, t, :, 0:64], pb[:, 0:256].rearrange("p (h d) -> p h d", h=4))
        pt = nextP(); pt2 = nextP()
        for h in range(4):
            k.transpose(pt[0:64, h * 128:(h + 1) * 128], qs[:, h * 64:(h + 1) * 64], identf[:])
        for h in range(4):
            k.transpose(pt2[0:64, h * 128:(h + 1) * 128], ks_[:, h * 64:(h + 1) * 64], identf[:])
        k.copy(QT[:, :, t * 128:(t + 1) * 128], pt[0:64, 0:512].rearrange("p (c t) -> p c t", c=4))
        k.copy(KT[:, :, t * 128:(t + 1) * 128], pt2[0:64, 0:512].rearrange("p (c t) -> p c t", c=4))
    import os
    STOP = float(os.environ.get("M0_STOP", "99"))
    if STOP <= 2:
        k.close_scope(); return k.finish([d_oT]), k
    sc = [k.sb("sc%d" % i, [128, 512]) for i in range(2)]
    PTb = [k.sb("PTb%d" % i, [128, 512], BF16) for i in range(2)]
    rden = k.sb("rden", [128, 4, 1]); obf = k.sb("obf", [128, 256], BF16)
    it = [0]
    NRR[0] = 4
    for qt in range(NT if STOP > 2.5 else int(os.environ.get("M0_NQ", "1"))):
        if qt < 2:
            keys = [(0, None), (1, None)]
        else:
            keys = [(0, None), (1, None)] + [(u + 2, ci) for (u, ci) in plan[qt - 2]]
        po = P[4 + qt % 2]
        for ki, (kt, ci) in enumerate(keys):
            ps = nextP()
            for h in range(4):
                k.matmul(ps[:, h * 128:(h + 1) * 128], KT[:, h, kt * 128:(kt + 1) * 128],
                         QT[:, h, qt * 128:(qt + 1) * 128])
            s_ = sc[it[0] % 2]; p_ = PTb[it[0] % 2]; it[0] += 1
            if ci is None:
                k.copy(s_[:], ps[:, 0:512])
            else:
                k.tt(s_[:], ps[:, 0:512], BT[:, ci, :, :].rearrange("p h q -> p (h q)"), ALU.add)
            k.act(p_[:], s_[:], AF.Exp)
            for h in range(4):
                k.matmul(po[:, h * 65:(h + 1) * 65], p_[:, h * 128:(h + 1) * 128], Vaug[:, kt, h, :],
                         start=(ki == 0 and h == 0), stop=(ki == len(keys) - 1 and h == 3))
        pov = po[:, 0:260].rearrange("p (h e) -> p h e", e=65)
        k.recip(rden[:], pov[:, :, 64:65])
        k.tt(obf[:].rearrange("p (h d) -> p h d", h=4), pov[:, :, 0:64], rden[:].to_broadcast([128, 4, 64]), ALU.mult)
        ptt = nextPT()
        k.transpose(ptt[:, 0:128], obf[:, 0:128], ident[:])
        k.transpose(ptt[:, 128:256], obf[:, 128:256], ident[:])
        ob = oTs[(qt // 4) % 2]
        k.copy(ob[:, :, (qt % 4) * 128:(qt % 4 + 1) * 128], ptt[:, 0:256].rearrange("p (c t) -> p c t", c=2))
        if qt % 4 == 3 or qt == NT - 1:
            q0 = (qt // 4) * 4; n = qt - q0 + 1
            k.dma(d_oT[0:256, q0 * 128:(qt + 1) * 128].rearrange("(c p) t -> p c t", p=128), ob[:, :, 0:n * 128])
    NRR[0] = 6
    k.close_scope()
    if STOP <= 3:
        return k.finish([d_oT]), k

    k.open_scope()
    waaugf = k.sb("waaugf", [17, 2, 128]); waaug = k.sb("waaug_s", [17, 2, 128], BF16)
    k.dma(waaugf[:], d_waaug); k.copy(waaug[:], waaugf[:])
    gng = k.sb("gng_s", [128, 128]); k.dma(gng[:], d_gng)
    mk = [k.sb("mkf", [128, 128]), k.sb("mkb", [128, 128])]
    k.dma(mk[0][:], d_maskf); k.dma(mk[1][:], d_maskb)
    mks = [k.sb("mksf", [128, 128]), k.sb("mksb", [128, 128])]
    k.ts(mks[0][:], mk[0][:], -1.0, ALU.mult, 1.0, ALU.add)
    k.ts(mks[1][:], mk[1][:], -1.0, ALU.mult, 1.0, ALU.add)
    qTg = k.sb("qTg", [64, NT, 128], BF16); kTg = k.sb("kTg", [64, NT, 128], BF16)
    kbg = k.sb("kbg", [128, NT, 64], BF16); vg = k.sb("vg", [128, NT, 128], BF16)
    rsl = k.sb("rsl", [128, NT, 128], BF16); sp = k.sb("sp", [128, NT, 2, 64])
    oacc = k.sb("oaccg", [128, NT, 128], BF16)
    wgl = k.sb("wgl_s", [128, 8, 384], BF16); wa = k.sb("wa_s", [128, 8, 32], BF16)
    k.dma(wa[:], d_wa.rearrange("(k p) c -> p k c", p=128), q="pool")
    aT = k.sb("aT", [17, 2, 128], BF16); k.memset(aT[:], 1.0)
    qf = k.sb("qfg", [128, 64]); kf = k.sb("kfg", [128, 64]); rf = k.sb("rfg", [128, 128]); zf = k.sb("zfg", [128, 128])
    S = k.sb("Sg", [64, 128]); Sb = k.sb("Sbg", [64, 128], BF16)
    EBT = k.sb("EBT", [64, 128]); ENBT = k.sb("ENBT", [64, 128]); ER = k.sb("ERg", [128, 64])
    bcs = k.sb("bcs", [64, 128]); rsb = k.sb("rsb", [128, 64])
    qin = k.sb("qing", [64, 128], BF16); kin = k.sb("king", [64, 128], BF16); kkg = k.sb("kkg", [128, 64], BF16)
    AT = k.sb("ATg", [128, 128], BF16)
    on = k.sb("ong", [128, 128]); st6 = k.sb("st6g", [128, 6]); mv = k.sb("mvg", [128, 2]); ms = k.sb("msg", [128, 1])
    obg = k.sb("obg", [128, 128], BF16)
    oTg = [k.sb("oTg%d" % i, [128, 512], BF16) for i in range(2)]
    dwg = d_wgl.rearrange("(k p) c -> p k c", p=128)
    for gh in range(2):
        k.dma(wgl[:, :, 0:64], dwg[:, :, gh * 64:(gh + 1) * 64], q="pool")
        k.dma(wgl[:, :, 64:128], dwg[:, :, 128 + gh * 64:128 + (gh + 1) * 64], q="pool")
        k.dma(wgl[:, :, 128:256], dwg[:, :, 256 + gh * 128:256 + (gh + 1) * 128], q="pool")
        k.dma(wgl[:, :, 256:384], dwg[:, :, 512 + gh * 128:512 + (gh + 1) * 128], q="pool")
        for t in range(NT):
            pa = nextP(); pz = nextP()
            for kk in range(8):
                k.matmul(pa[:, 0:384], hT[:, kk, t * 128:(t + 1) * 128], wgl[:, kk, 0:384], start=(kk == 0), stop=(kk == 7))
            for di in range(2):
                for kk in range(8):
                    k.matmul(pz[0:16, di * 128:(di + 1) * 128], wa[:, kk, di * 16:(di + 1) * 16], hT[:, kk, t * 128:(t + 1) * 128],
                             start=(kk == 0), stop=(kk == 7))
            k.copy(aT[0:16, :, :], pz[0:16, 0:256].rearrange("p (a t) -> p a t", a=2))
            pz2 = nextP()
            for di in range(2):
                k.matmul(pz2[:, di * 64:(di + 1) * 64], aT[:, di, :], waaug[:, di, gh * 64:(gh + 1) * 64])
            k.copy(zf[:], pz2[:, 0:128])
            k.act(zf[:], zf[:], AF.Exp, scale=-1.0)
            k.act(sp[:, t, :, :].rearrange("p a d -> p (a d)"), zf[:], AF.Ln, bias=1.0, scale=1.0)
            k.ts(qf[:], pa[:, 0:64], 0.125, ALU.mult)
            k.copy(kf[:], pa[:, 64:128])
            k.copy(kbg[:, t, :], kf[:], eng="pool")
            k.copy(rf[:], pa[:, 128:256])
            k.act(rsl[:, t, :], rf[:], AF.Silu)
            k.copy(vg[:, t, :], pa[:, 256:384])
            pt = nextP()
            k.transpose(pt[0:64, 0:128], qf[:], identf[:])
            k.transpose(pt[0:64, 128:256], kf[:], identf[:])
            k.copy(qTg[:, t, :], pt[0:64, 0:128])
            k.copy(kTg[:, t, :], pt[0:64, 128:256])
        for di in (1, 0):
            order = ([1, 0] + list(range(33, 1, -1))) if di == 1 else list(range(NT))
            k.memset(S[:], 0.0); k.memset(Sb[:], 0.0)
            for t in order:
                spt = sp[:, t, di, :]
                pbc = nextP()
                k.matmul(pbc[0:64, 0:128], spt, mk[di][:])
                k.matmul(pbc[:, 128:192], mks[di][:], spt)
                k.copy(bcs[:], pbc[0:64, 0:128]); k.copy(rsb[:], pbc[:, 128:192])
                k.act(EBT[:], bcs[:], AF.Exp, scale=-1.0 / 16.0)
                k.act(ENBT[:], bcs[:], AF.Exp, scale=1.0 / 16.0)
                k.act(ER[:], rsb[:], AF.Exp, scale=-1.0 / 16.0)
                k.tt(qin[:], qTg[:, t, :], EBT[:], ALU.mult)
                k.tt(kin[:], kTg[:, t, :], ENBT[:], ALU.mult, eng="pool")
                k.tt(kkg[:], kbg[:, t, :], ER[:], ALU.mult, eng="pool")
                pat = nextP()
                k.matmul(pat[:, 0:128], kin[:], qin[:])
                k.tt(AT[:], pat[:, 0:128], mk[di][:], ALU.mult)
                po = nextP()
                k.matmul(po[:, 0:128], AT[:], vg[:, t, :], start=True, stop=False)
                k.matmul(po[:, 0:128], qin[:], Sb[:], start=False, stop=True)
                pkv = nextP()
                k.matmul(pkv[0:64, 0:128], kkg[:], vg[:, t, :])
                if di == 1:
                    k.copy(oacc[:, t, :], po[:, 0:128])
                else:
                    k.tt(on[:], po[:, 0:128], oacc[:, t, :], ALU.add)
                    k.op("dve", lambda e, a=st6, b=on: e.bn_stats(a[:], b[:]), [on[:]], [st6[:]])
                    k.op("dve", lambda e, a=mv, b=st6: e.bn_aggr(a[:], b[:]), [st6[:]], [mv[:]])
                    k.stt(ms[:], mv[:, 0:1], mv[:, 0:1], mv[:, 1:2], ALU.mult, ALU.add)
                    k.ts(ms[:], ms[:], EPS, ALU.add)
                    k.act(ms[:], ms[:], AF.Sqrt)
                    k.recip(ms[:], ms[:])
                    k.stt(on[:], on[:], ms[:, 0:1], gng[:], ALU.mult, ALU.mult)
                    k.tt(obg[:], on[:], rsl[:, t, :], ALU.mult, eng="pool")
                    ptt = nextPT()
                    k.transpose(ptt[:, 0:128], obg[:], ident[:])
                    ob = oTg[(t // 4) % 2]
                    k.copy(ob[:, (t % 4) * 128:(t % 4 + 1) * 128], ptt[:, 0:128])
                    if t % 4 == 3 or t == NT - 1:
                        q0 = (t // 4) * 4; n = t - q0 + 1
                        k.dma(d_oT[256 + gh * 128:256 + (gh + 1) * 128, q0 * 128:(t + 1) * 128], ob[:, 0:n * 128])
                dci = 127 if di == 0 else 0
                k.stt(S[:], S[:], EBT[:, dci:dci + 1], pkv[0:64, 0:128], ALU.mult, ALU.add)
                k.copy(Sb[:], S[:], eng="act")
    k.close_scope()
    nc = k.finish([d_oT])
    return nc, k

BF = ml_dtypes.bfloat16

def pk(v):
    return np.ascontiguousarray(v.reshape(-1, 128).T)

def seg_cols(arrT, t0, n, T):
    out = np.zeros((arrT.shape[0], n + 2), arrT.dtype)
    lo = max(t0 - 1, 0); hi = min(t0 + n + 1, T)
    out[:, lo - (t0 - 1): hi - (t0 - 1)] = arrT[:, lo:hi]
    hm = np.array([1.0 if t0 - 1 >= 0 else 0.0, 1.0 if t0 + n < T else 0.0], np.float32)
    return out, np.ascontiguousarray(np.broadcast_to(hm, (128, 2)))

def prep_F(inp, L, b, th, xT, oT, xcT=None, ocT=None, wout=None):
    m = {}
    segs = []
    for s in range(4):
        t0 = th * 2048 + s * 512
        m["xT_%d" % s], m["hm_%d" % s] = seg_cols(xT, t0, 512, 4096)
        m["oT_%d" % s], _ = seg_cols(oT, t0, 512, 4096)
        segs.append((512, 0))
    if xcT is not None:
        m["xT_4"], m["hm_4"] = seg_cols(xcT, 0, 256, 256)
        m["oT_4"], _ = seg_cols(ocT, 0, 256, 256)
        segs.append((256, 1))
    cv = np.stack([pk(inp["c"][b]), pk(inp["c_ctx"])], axis=-1)
    m["cvec"] = np.ascontiguousarray(cv.astype(np.float32))
    m["wmod"] = np.ascontiguousarray(inp["w_mod"][L][:, 2048:6144])
    m["bmod"] = pk(inp["b_mod"][L][2048:6144])
    m["n2g"] = pk(inp["norm2_g"][L]); m["fg"] = pk(inp["final_norm_g"])
    cw = inp["ffn_conv_w"][L]
    m["convw"] = np.ascontiguousarray(np.stack([pk(cw[i]) for i in range(3)], axis=1))
    m["convb"] = pk(inp["ffn_conv_b"][L])
    m["wout"] = wout
    m["wup"] = inp["ffn_w_up"][L]; m["wdn"] = inp["ffn_w_down"][L]
    return m, segs


_C = make_consts(); _COS, _SIN = rope_tables()

def prep_M_common(inp, L, b, xT, xcT):
    m = {"xT": np.ascontiguousarray(xT), "xcT": np.ascontiguousarray(xcT)}
    cv = np.stack([pk(inp["c"][b]), pk(inp["c_ctx"])], axis=-1)
    m["cvec"] = np.ascontiguousarray(cv.astype(np.float32))
    m["wmod"] = np.ascontiguousarray(inp["w_mod"][L][:, 0:2048])
    m["bmod"] = pk(inp["b_mod"][L][0:2048])
    m["n1g"] = pk(inp["norm1_g"][L])
    m["ident"] = np.eye(128, dtype=np.float32)
    return m

def prep_M1(inp, b, hh, xT, xcT):
    m = prep_M_common(inp, 1, b, xT, xcT)
    w = inp["ret_w_in"][0]
    heads = [hh * 4 + i for i in range(4)]
    m["win"] = np.ascontiguousarray(np.stack([np.concatenate([
        w[:, h * 128:(h + 1) * 128], w[:, 1024 + h * 128:1024 + (h + 1) * 128],
        w[:, 4096 + h * 256:4096 + (h + 1) * 256], w[:, 2048 + h * 256:2048 + (h + 1) * 256]], axis=1) for h in heads]))
    dec = np.concatenate([inp["ret_decay_fwd"][0][heads], inp["ret_decay_bwd"][0][heads]])
    m["dec"] = np.ascontiguousarray(np.broadcast_to(dec, (128, 8)).astype(np.float32))
    m["ng"] = np.ascontiguousarray(np.broadcast_to(inp["ret_norm_g"][0][heads], (128, 4, 256)).astype(np.float32))
    m["cos"] = _COS; m["sin"] = _SIN
    for kk in ("mask_f", "mask_b", "dm_f", "dm_b", "row_f", "row_b", "pcols"):
        m[kk] = _C[kk]
    return m

def _na_tables(rpb_heads):
    c = np.arange(64); c0 = np.clip(c - 8, 0, 48)
    kc = np.arange(64)
    allowed = (kc[:, None] >= c0[None, :]) & (kc[:, None] < c0[None, :] + 16)
    off = np.clip(kc[:, None] - c[None, :] + 15, 0, 30)
    g = rpb_heads[:, :, off]
    g = np.where(allowed[None, None], g, 0.0).astype(np.float32)
    g = np.transpose(g, (2, 0, 1, 3))
    rpbT = np.ascontiguousarray(np.concatenate([g, g], axis=0))
    cm = np.where(allowed, 0.0, -30000.0).astype(np.float32)
    return rpbT, np.ascontiguousarray(np.concatenate([cm, cm], axis=0))

def prep_M0(inp, b, hh, xT, xcT):
    m = prep_M_common(inp, 0, b, xT, xcT)
    w = inp["na_gla_w_in"][0]
    nh = [hh * 4 + i for i in range(4)]; gh = [hh * 2 + i for i in range(2)]
    m["wna"] = np.ascontiguousarray(np.concatenate(
        [w[:, h * 64:(h + 1) * 64] for h in nh] + [w[:, 512 + h * 64:512 + (h + 1) * 64] for h in nh] +
        [w[:, 1024 + h * 64:1024 + (h + 1) * 64] for h in nh], axis=1))
    m["wgl"] = np.ascontiguousarray(np.concatenate(
        [w[:, 1536 + h * 64:1536 + (h + 1) * 64] for h in gh] + [w[:, 1792 + h * 64:1792 + (h + 1) * 64] for h in gh] +
        [w[:, 2560 + h * 128:2560 + (h + 1) * 128] for h in gh] + [w[:, 2048 + h * 128:2048 + (h + 1) * 128] for h in gh], axis=1))
    m["wa"] = np.ascontiguousarray(w[:, 3072:3104])
    gc = slice(hh * 128, (hh + 1) * 128)
    wa = np.zeros((17, 2, 128), np.float32)
    wa[0:16, 0] = inp["gla_w_a_fwd"][0][:, gc]; wa[16, 0] = inp["gla_b_a_fwd"][0][gc]
    wa[0:16, 1] = inp["gla_w_a_bwd"][0][:, gc]; wa[16, 1] = inp["gla_b_a_bwd"][0][gc]
    m["waaug"] = wa
    m["rpbT"], m["colmask"] = _na_tables(inp["na_rpb"][0][nh])
    m["gng"] = np.ascontiguousarray(np.broadcast_to(inp["gla_norm_g"][0], (128, 128)).astype(np.float32))
    m["mask_f"] = _C["mask_f"]; m["mask_b"] = _C["mask_b"]
    return m


class Env:
    pass


def make_env(k):
    e = Env(); e.k = k
    e.P = [k.ps("P%d" % i, [128, 512]) for i in range(6)]
    e.PT = [k.ps("PT%d" % i, [128, 1024], BF16) for i in range(2)]
    e.pc = [0, 0]; e.NRR = [6]

    def nextP():
        p = e.P[e.pc[0] % e.NRR[0]]; e.pc[0] += 1; return p

    def nextPT():
        p = e.PT[e.pc[1] % 2]; e.pc[1] += 1; return p
    e.nextP = nextP; e.nextPT = nextPT
    e.ones_bf = k.sb("ones_bf", [128, 128], BF16); k.memset(e.ones_bf[:], 1.0)
    e.identf = k.sb("identf", [128, 128]); e.ident = k.sb("ident_s", [128, 128], BF16)
    d_ident = k.dram("ident", [128, 128])
    k.dma(e.identf[:], d_ident); k.copy(e.ident[:], e.identf[:])
    return e


def phase_hT(e, pfx, d_xT, d_xcT, d_cvec, d_wmod, d_bmod, d_n1g, hT):
    k = e.k
    k.open_scope()
    n1g = k.sb(pfx + "n1g_s", [128, 8]); k.dma(n1g[:], d_n1g)
    wmb = [k.sb(pfx + "wmb0", [128, 8, 512], BF16)]
    modv = emit_mod(k, e.nextP, d_cvec, d_wmod, d_bmod, 2048, wmb, name=pfx + "m")
    xt = [k.sb(pfx + "xt0", [128, 8, 256]), k.sb(pfx + "xt1", [128, 8, 256])]
    sq = k.sb(pfx + "sq", [128, 8, 256], BF16); rstd = k.sb(pfx + "rstd", [128, 256]); tmp = k.sb(pfx + "tmp", [128, 256])
    emit_hT(k, e.nextP, hT, d_xT, d_xcT, modv, n1g, e.ones_bf, xt, sq, rstd, tmp, name=pfx)
    k.close_scope()


def phase_NA(e, pfx, hT, d_wna, d_rpbT, d_colmask, d_oT, row0):
    k = e.k; nextP = e.nextP; nextPT = e.nextPT; identf = e.identf; ident = e.ident; P = e.P
    cfgs, plan = na_configs()
    k.open_scope()
    oTs = [k.sb(pfx + "oTs%d" % i, [128, 2, 512], BF16) for i in range(2)]
    wna = k.sb(pfx + "wna_s", [128, 8, 768], BF16)
    k.dma(wna[:], d_wna.rearrange("(k p) c -> p k c", p=128), q="pool")
    QT = k.sb(pfx + "QT", [64, 4, NT * 128], BF16); KT = k.sb(pfx + "KT", [64, 4, NT * 128], BF16)
    Vaug = k.sb(pfx + "Vaug", [128, NT, 4, 65], BF16)
    k.memset(Vaug[:], 1.0)
    BT = k.sb(pfx + "BT", [128, len(cfgs), 4, 128])
    k.open_scope()
    Btab = k.sb(pfx + "Btab", [128, 4, 15, 64]); cmask = k.sb(pfx + "cmask", [128, 64])
    k.dma(Btab[:], d_rpbT); k.dma(cmask[:], d_colmask)
    for h in range(4):
        k.tt(Btab[:, h, :, :], Btab[:, h, :, :], cmask[:].unsqueeze(1).to_broadcast([128, 15, 64]), ALU.add)
    for ci, key in enumerate(cfgs):
        bi = 0
        for kh in range(2):
            for qh in range(2):
                roff = key[bi]; bi += 1
                dst = BT[kh * 64:(kh + 1) * 64, ci, :, qh * 64:(qh + 1) * 64]
                if roff is None:
                    k.memset(dst, NEG, eng="pool")
                else:
                    k.copy(dst, Btab[kh * 64:(kh + 1) * 64, :, roff, :], eng="pool")
    k.close_scope()
    qs = k.sb(pfx + "qs", [128, 256]); ks_ = k.sb(pfx + "ks", [128, 256])
    for t in range(NT):
        pa = nextP(); pb = nextP()
        for kk in range(8):
            k.matmul(pa[:, 0:512], hT[:, kk, t * 128:(t + 1) * 128], wna[:, kk, 0:512], start=(kk == 0), stop=(kk == 7))
        for kk in range(8):
            k.matmul(pb[:, 0:256], hT[:, kk, t * 128:(t + 1) * 128], wna[:, kk, 512:768], start=(kk == 0), stop=(kk == 7))
        k.ts(qs[:], pa[:, 0:256], 0.125, ALU.mult)
        k.copy(ks_[:], pa[:, 256:512])
        k.copy(Vaug[:, t, :, 0:64], pb[:, 0:256].rearrange("p (h d) -> p h d", h=4))
        pt = nextP(); pt2 = nextP()
        for h in range(4):
            k.transpose(pt[0:64, h * 128:(h + 1) * 128], qs[:, h * 64:(h + 1) * 64], identf[:])
        for h in range(4):
            k.transpose(pt2[0:64, h * 128:(h + 1) * 128], ks_[:, h * 64:(h + 1) * 64], identf[:])
        k.copy(QT[:, :, t * 128:(t + 1) * 128], pt[0:64, 0:512].rearrange("p (c t) -> p c t", c=4))
        k.copy(KT[:, :, t * 128:(t + 1) * 128], pt2[0:64, 0:512].rearrange("p (c t) -> p c t", c=4))
    sc = [k.sb(pfx + "sc%d" % i, [128, 512]) for i in range(2)]
    PTb = [k.sb(pfx + "PTb%d" % i, [128, 512], BF16) for i in range(2)]
    rden = k.sb(pfx + "rden", [128, 4, 1]); obf = k.sb(pfx + "obf", [128, 256], BF16)
    it = [0]
    e.NRR[0] = 4
    obfL = [obf, k.sb(pfx + "obf2", [128, 256], BF16)]
    items = []
    for qt in range(NT):
        if qt < 2:
            keys = [(0, None), (1, None)]
        else:
            keys = [(0, None), (1, None)] + [(u + 2, ci) for (u, ci) in plan[qt - 2]]
        for ki, (kt, ci) in enumerate(keys):
            items.append((qt, ki, kt, ci, len(keys)))

    def stage1(qt, ki, kt, ci, nk):
        ps = nextP()
        for h in range(4):
            k.matmul(ps[:, h * 128:(h + 1) * 128], KT[:, h, kt * 128:(kt + 1) * 128], QT[:, h, qt * 128:(qt + 1) * 128])
        s_ = sc[it[0] % 2]; p_ = PTb[it[0] % 2]; it[0] += 1
        if ci is None:
            k.copy(s_[:], ps[:, 0:512])
        else:
            k.tt(s_[:], ps[:, 0:512], BT[:, ci, :, :].rearrange("p h q -> p (h q)"), ALU.add)
        k.act(p_[:], s_[:], AF.Exp)
        return p_

    def stage2(qt, ki, kt, ci, nk, p_):
        po = P[4 + qt % 2]
        for h in range(4):
            k.matmul(po[:, h * 65:(h + 1) * 65], p_[:, h * 128:(h + 1) * 128], Vaug[:, kt, h, :],
                     start=(ki == 0 and h == 0), stop=(ki == nk - 1 and h == 3))
        if ki == nk - 1:
            ob_ = obfL[qt % 2]
            pov = po[:, 0:260].rearrange("p (h e) -> p h e", e=65)
            k.recip(rden[:], pov[:, :, 64:65])
            k.tt(ob_[:].rearrange("p (h d) -> p h d", h=4), pov[:, :, 0:64], rden[:].to_broadcast([128, 4, 64]), ALU.mult)
            return (qt, ob_)
        return None

    def stage3(qt, ob_):
        ptt = nextPT()
        k.transpose(ptt[:, 0:128], ob_[:, 0:128], ident[:])
        k.transpose(ptt[:, 128:256], ob_[:, 128:256], ident[:])
        ob = oTs[(qt // 4) % 2]
        k.copy(ob[:, :, (qt % 4) * 128:(qt % 4 + 1) * 128], ptt[:, 0:256].rearrange("p (c t) -> p c t", c=2))
        if qt % 4 == 3 or qt == NT - 1:
            q0 = (qt // 4) * 4; n = qt - q0 + 1
            k.dma(d_oT[row0:row0 + 256, q0 * 128:(qt + 1) * 128].rearrange("(c p) t -> p c t", p=128), ob[:, :, 0:n * 128])

    prev = None; fin = []
    for idx in range(len(items) + 1):
        cur = None
        if idx < len(items):
            cur = (items[idx], stage1(*items[idx]))
        while fin and fin[0][0] <= idx - 1:
            _, f3 = fin.pop(0); stage3(*f3)
        if prev is not None:
            r = stage2(*prev[0], prev[1])
            if r is not None:
                fin.append((idx, r))
        prev = cur
    for _, f3 in fin:
        stage3(*f3)
    e.NRR[0] = 6
    k.close_scope()


def phase_GLA(e, pfx, hT, d_wgl, d_wa, d_waaug, d_gng, d_maskf, d_maskb, d_oT, row0, ghs):
    k = e.k; nextP = e.nextP; nextPT = e.nextPT; identf = e.identf; ident = e.ident
    k.open_scope()
    waaugf = k.sb(pfx + "waaugf", [17, 2, 256]); waaug = k.sb(pfx + "waaug_s", [17, 2, 256], BF16)
    k.dma(waaugf[:], d_waaug); k.copy(waaug[:], waaugf[:])
    gng = k.sb(pfx + "gng_s", [128, 128]); k.dma(gng[:], d_gng)
    mk = [k.sb(pfx + "mkf", [128, 128]), k.sb(pfx + "mkb", [128, 128])]
    k.dma(mk[0][:], d_maskf); k.dma(mk[1][:], d_maskb)
    mks = [k.sb(pfx + "mksf", [128, 128]), k.sb(pfx + "mksb", [128, 128])]
    k.ts(mks[0][:], mk[0][:], -1.0, ALU.mult, 1.0, ALU.add)
    k.ts(mks[1][:], mk[1][:], -1.0, ALU.mult, 1.0, ALU.add)
    qTg = k.sb(pfx + "qTg", [64, NT, 128], BF16); kTg = k.sb(pfx + "kTg", [64, NT, 128], BF16)
    kbg = k.sb(pfx + "kbg", [128, NT, 64], BF16); vg = k.sb(pfx + "vg", [128, NT, 128], BF16)
    rsl = k.sb(pfx + "rsl", [128, NT, 128], BF16); sp = k.sb(pfx + "sp", [128, NT, 2, 64])
    oacc = k.sb(pfx + "oaccg", [128, NT, 128], BF16)
    wgl = k.sb(pfx + "wgl_s", [128, 8, 384], BF16); wa = k.sb(pfx + "wa_s", [128, 8, 32], BF16)
    k.dma(wa[:], d_wa.rearrange("(k p) c -> p k c", p=128), q="pool")
    aT = k.sb(pfx + "aT", [17, 2, 128], BF16); k.memset(aT[:], 1.0)
    qf = k.sb(pfx + "qfg", [128, 64]); kf = k.sb(pfx + "kfg", [128, 64]); rf = k.sb(pfx + "rfg", [128, 128]); zf = k.sb(pfx + "zfg", [128, 128])
    R2 = range(2)
    S2 = [k.sb(pfx + "Sg%d" % i, [64, 128]) for i in R2]; Sb2 = [k.sb(pfx + "Sbg%d" % i, [64, 128], BF16) for i in R2]
    EBT2 = [k.sb(pfx + "EBT%d" % i, [64, 128]) for i in R2]; ENBT2 = [k.sb(pfx + "ENBT%d" % i, [64, 128]) for i in R2]
    ER2 = [k.sb(pfx + "ERg%d" % i, [128, 64]) for i in R2]
    bcs2 = [k.sb(pfx + "bcs%d" % i, [64, 128]) for i in R2]; rsb2 = [k.sb(pfx + "rsb%d" % i, [128, 64]) for i in R2]
    qin2 = [k.sb(pfx + "qing%d" % i, [64, 128], BF16) for i in R2]; kin2 = [k.sb(pfx + "king%d" % i, [64, 128], BF16) for i in R2]
    kkg2 = [k.sb(pfx + "kkg%d" % i, [128, 64], BF16) for i in R2]
    AT2 = [k.sb(pfx + "ATg%d" % i, [128, 128], BF16) for i in R2]
    oT4 = [k.sb(pfx + "oT4g%d" % i, [128, 128], BF16) for i in range(4)]; oc = [0]
    Q2 = range(2)
    onL = [k.sb(pfx + "ong%d" % i, [128, 128]) for i in Q2]; st6L = [k.sb(pfx + "st6g%d" % i, [128, 6]) for i in Q2]
    mvL = [k.sb(pfx + "mvg%d" % i, [128, 2]) for i in Q2]; msL = [k.sb(pfx + "msg%d" % i, [128, 1]) for i in Q2]
    obgL = [k.sb(pfx + "obg%d" % i, [128, 128], BF16) for i in Q2]
    pend = [None]; pendB = [None]; rc = [0]
    dcl2 = [k.sb(pfx + "dcl%d" % i, [64, 1]) for i in range(2)]
    dwg = d_wgl.rearrange("(k p) c -> p k c", p=128)
    for (gl, gglob, rofs) in ghs:
        k.dma(wgl[:, :, 0:64], dwg[:, :, gl * 64:(gl + 1) * 64], q="pool")
        k.dma(wgl[:, :, 64:128], dwg[:, :, 128 + gl * 64:128 + (gl + 1) * 64], q="pool")
        k.dma(wgl[:, :, 128:256], dwg[:, :, 256 + gl * 128:256 + (gl + 1) * 128], q="pool")
        k.dma(wgl[:, :, 256:384], dwg[:, :, 512 + gl * 128:512 + (gl + 1) * 128], q="pool")
        for t in range(NT):
            pa = nextP(); pz = nextP()
            for kk in range(8):
                k.matmul(pa[:, 0:384], hT[:, kk, t * 128:(t + 1) * 128], wgl[:, kk, 0:384], start=(kk == 0), stop=(kk == 7))
            for di in range(2):
                for kk in range(8):
                    k.matmul(pz[0:16, di * 128:(di + 1) * 128], wa[:, kk, di * 16:(di + 1) * 16], hT[:, kk, t * 128:(t + 1) * 128],
                             start=(kk == 0), stop=(kk == 7))
            k.copy(aT[0:16, :, :], pz[0:16, 0:256].rearrange("p (a t) -> p a t", a=2))
            pz2 = nextP()
            for di in range(2):
                k.matmul(pz2[:, di * 64:(di + 1) * 64], aT[:, di, :], waaug[:, di, gglob * 64:(gglob + 1) * 64])
            k.copy(zf[:], pz2[:, 0:128])
            k.act(zf[:], zf[:], AF.Exp, scale=-1.0)
            k.act(sp[:, t, :, :].rearrange("p a d -> p (a d)"), zf[:], AF.Ln, bias=1.0, scale=1.0)
            k.ts(qf[:], pa[:, 0:64], 0.125, ALU.mult)
            k.copy(kf[:], pa[:, 64:128])
            k.copy(kbg[:, t, :], kf[:], eng="pool")
            k.copy(rsl[:, t, :], pa[:, 128:256])
            k.copy(vg[:, t, :], pa[:, 256:384])
            pt = nextP()
            k.transpose(pt[0:64, 0:128], qf[:], identf[:])
            k.transpose(pt[0:64, 128:256], kf[:], identf[:])
            k.copy(qTg[:, t, :], pt[0:64, 0:128])
            k.copy(kTg[:, t, :], pt[0:64, 128:256])
        for t in range(NT):
            k.act(rsl[:, t, :], rsl[:, t, :], AF.Silu)
        ordB = [1, 0] + list(range(33, 1, -1)); ordF = list(range(NT))
        for di in range(2):
            k.memset(S2[di][:], 0.0); k.memset(Sb2[di][:], 0.0)
        have = set()
        steps = [(di, t) for i in range(NT) for (di, t) in ((1, ordB[i]), (0, ordF[i]))]

        def stage1(di, t):
            qin = qin2[di]; kin = kin2[di]; kkg = kkg2[di]
            EBT = EBT2[di]; ENBT = ENBT2[di]; ER = ER2[di]; bcs = bcs2[di]; rsb = rsb2[di]
            spt = sp[:, t, di, :]
            pbc = nextP()
            k.matmul(pbc[0:64, 0:128], spt, mk[di][:])
            k.matmul(pbc[:, 128:192], mks[di][:], spt)
            k.copy(bcs[:], pbc[0:64, 0:128]); k.copy(rsb[:], pbc[:, 128:192])
            k.act(EBT[:], bcs[:], AF.Exp, scale=-1.0 / 16.0)
            k.act(ENBT[:], bcs[:], AF.Exp, scale=1.0 / 16.0)
            k.act(ER[:], rsb[:], AF.Exp, scale=-1.0 / 16.0)
            k.tt(qin[:], qTg[:, t, :], EBT[:], ALU.mult)
            k.tt(kin[:], kTg[:, t, :], ENBT[:], ALU.mult)
            k.tt(kkg[:], kbg[:, t, :], ER[:], ALU.mult)
            k.copy(dcl2[di][:], EBT[:, (127 if di == 0 else 0):(128 if di == 0 else 1)], eng="pool")

        stage1(*steps[0])
        for si_ in range(len(steps)):
            if True:
                di, t = steps[si_]
                S = S2[di]; Sb = Sb2[di]; AT = AT2[di]; qin = qin2[di]; kin = kin2[di]; kkg = kkg2[di]
                EBT = EBT2[di]
                if si_ + 1 < len(steps) and steps[si_ + 1][0] != di:
                    stage1(*steps[si_ + 1])
                pat = nextP()
                k.matmul(pat[:, 0:128], kin[:], qin[:])
                k.tt(AT[:], pat[:, 0:128], mk[di][:], ALU.mult)
                po = nextP()
                k.matmul(po[:, 0:128], AT[:], vg[:, t, :], start=True, stop=False)
                k.matmul(po[:, 0:128], qin[:], Sb[:], start=False, stop=True)
                pkv = nextP()
                k.matmul(pkv[0:64, 0:128], kkg[:], vg[:, t, :])
                k.stt(S[:], S[:], dcl2[di][:, 0:1], pkv[0:64, 0:128], ALU.mult, ALU.add)
                k.copy(Sb[:], S[:], eng="act")
                if si_ + 1 < len(steps) and steps[si_ + 1][0] == di:
                    stage1(*steps[si_ + 1])
                if pendB[0] is not None:
                    pendB[0](); pendB[0] = None
                if pend[0] is not None:
                    pendB[0] = pend[0](); pend[0] = None
                if t not in have:
                    k.copy(oacc[:, t, :], po[:, 0:128]); have.add(t)
                else:
                    par = rc[0] % 2; rc[0] += 1
                    onb = onL[par]
                    k.tt(onb[:], po[:, 0:128], oacc[:, t, :], ALU.add)

                    def readout(onb=onb, par=par, t=t, rofs=rofs):
                        st6_ = st6L[par]; mv_ = mvL[par]; ms_ = msL[par]; obg_ = obgL[par]
                        k.op("dve", lambda e_, a=st6_, b_=onb: e_.bn_stats(a[:], b_[:]), [onb[:]], [st6_[:]])
                        k.op("dve", lambda e_, a=mv_, b_=st6_: e_.bn_aggr(a[:], b_[:]), [st6_[:]], [mv_[:]])
                        k.stt(ms_[:], mv_[:, 0:1], mv_[:, 0:1], mv_[:, 1:2], ALU.mult, ALU.add)
                        k.ts(ms_[:], ms_[:], EPS, ALU.add)
                        k.act(ms_[:], ms_[:], AF.Ln)
                        k.act(ms_[:], ms_[:], AF.Exp, scale=-0.5)
                        k.stt(onb[:], onb[:], ms_[:, 0:1], gng[:], ALU.mult, ALU.mult)
                        k.tt(obg_[:], onb[:], rsl[:, t, :], ALU.mult, eng="pool")
                        return lambda: partB(obg_, t, rofs)

                    def partB(obg_, t, rofs):
                        ptt = nextPT()
                        k.transpose(ptt[:, 0:128], obg_[:], ident[:])
                        ob = oT4[oc[0] % 4]; oc[0] += 1
                        k.copy(ob[:], ptt[:, 0:128])
                        k.dma(d_oT[row0 + rofs:row0 + rofs + 128, t * 128:(t + 1) * 128], ob[:])
                    pend[0] = readout
        if pendB[0] is not None:
            pendB[0](); pendB[0] = None
        if pend[0] is not None:
            pend[0]()(); pend[0] = None
    k.close_scope()


def phase_RET(e, pfx, hT, d_win, d_dec, d_ng, d_cos, d_sin, cn, d_pcols, d_oT, NH):
    k = e.k; nextP = e.nextP; nextPT = e.nextPT; identf = e.identf; ident = e.ident
    DK = 128; DV = 256
    k.open_scope()
    cs = {nm: k.sb(pfx + nm + "_s", [128, 128]) for nm in cn}
    for nm in cn:
        k.dma(cs[nm][:], cn[nm])
    pcols = k.sb(pfx + "pcols_s", [128, 4]); k.dma(pcols[:], d_pcols)
    cos = k.sb(pfx + "cos_s", [128, 32, 128]); sin = k.sb(pfx + "sin_s", [128, 32, 128])
    k.dma(cos[:], d_cos); k.dma(sin[:], d_sin)
    ng = k.sb(pfx + "ng_s", [128, 2, DV])
    dec = k.sb(pfx + "dec_s", [128, 2 * NH]); k.dma(dec[:], d_dec)
    lg = k.sb(pfx + "lg", [128, 2 * NH]); e1 = k.sb(pfx + "e1", [128, 2 * NH])
    k.act(e1[:], dec[:], AF.Exp, scale=-1.0)
    k.act(e1[:], e1[:], AF.Ln, bias=1.0, scale=1.0)
    k.ts(lg[:], e1[:], -1.0, ALU.mult)
    w = k.sb(pfx + "whd0", [128, 8, 768], BF16)
    kb = k.sb(pfx + "kb", [128, NT, DK], BF16); qT = k.sb(pfx + "qT", [128, NT, 128], BF16); kT = k.sb(pfx + "kT", [128, NT, 128], BF16)
    vb = k.sb(pfx + "vb", [128, NT, DV], BF16); gs = k.sb(pfx + "gs", [128, 32, DV], BF16)
    oacc = k.sb(pfx + "oacc", [128, 32, DV], BF16)
    qkL = [k.sb(pfx + "qk%d" % i, [128, 256]) for i in range(2)]
    tAL = [k.sb(pfx + "tA%d" % i, [128, 256]) for i in range(1)] * 2; tBL = [k.sb(pfx + "tB%d" % i, [128, 256]) for i in range(1)] * 2
    DM = [k.sb(pfx + "DM%d" % i, [128, 128]) for i in range(2)]
    EBr = [k.sb(pfx + "EBr%d" % i, [128, 128]) for i in range(2)]
    ERc = k.sb(pfx + "ERc", [128, 2]); dcol = k.sb(pfx + "dcol", [128, 2])
    S2 = [k.sb(pfx + "S%d" % i, [128, DV]) for i in range(2)]; Sb2 = [k.sb(pfx + "Sb%d" % i, [128, DV], BF16) for i in range(2)]
    AT2 = [k.sb(pfx + "AT%d" % i, [128, 128], BF16) for i in range(2)]; qin2 = [k.sb(pfx + "qin%d" % i, [128, 128], BF16) for i in range(2)]
    kk2 = [k.sb(pfx + "kk%d" % i, [128, 128], BF16) for i in range(2)]
    oT4 = [k.sb(pfx + "oT4%d" % i, [128, 2, 128], BF16) for i in range(2)] * 2; oc = [0]
    R2_ = range(2)
    st6L = [k.sb(pfx + "st6%d" % i, [128, 6]) for i in R2_]; mvL = [k.sb(pfx + "mv%d" % i, [128, 2]) for i in R2_]
    rsL = [k.sb(pfx + "rs%d" % i, [128, 1]) for i in R2_]; nbL = [k.sb(pfx + "nb%d" % i, [128, 1]) for i in R2_]
    onL = [k.sb(pfx + "on%d" % i, [128, DV]) for i in R2_]; obfL = [k.sb(pfx + "obf%d" % i, [128, DV], BF16) for i in R2_]
    pend = [None]; pendB = [None]; rc = [0]
    gfL = tBL
    for hl in range(NH):
        k.dma(w[:], d_win[hl].rearrange("(k p) c -> p k c", p=128), q="pool")
        k.dma(ng[:, hl % 2, :], d_ng[:, hl, :])
        for di, (dmn, mkn, rown) in enumerate((("dm_f", "mask_f", "row_f"), ("dm_b", "mask_b", "row_b"))):
            lgc = lg[:, di * NH + hl:di * NH + hl + 1]
            k.act(DM[di][:], cs[dmn][:], AF.Exp, scale=lgc)
            k.tt(DM[di][:], DM[di][:], cs[mkn][:], ALU.mult)
            k.act(EBr[di][:], cs[rown][:], AF.Exp, scale=lgc)
            k.act(ERc[:, di:di + 1], pcols[:, di:di + 1], AF.Exp, scale=lgc)
            k.act(dcol[:, di:di + 1], pcols[:, 2:3], AF.Exp, scale=lgc)
        for t in range(NT):
            pa = nextP(); pb = nextP()
            for kk in range(8):
                k.matmul(pa[:, 0:512], hT[:, kk, t * 128:(t + 1) * 128], w[:, kk, 0:512], start=(kk == 0), stop=(kk == 7))
            for kk in range(8):
                k.matmul(pb[:, 0:256], hT[:, kk, t * 128:(t + 1) * 128], w[:, kk, 512:768], start=(kk == 0), stop=(kk == 7))
            qk = qkL[t % 2]; gf = gfL[t % 2]
            SC = float(DK) ** -0.5
            if t >= 2:
                xv = pa[:, 0:256].rearrange("p (g h f) -> p g h f", g=4, h=2)
                ov = qk[:].rearrange("p (g h f) -> p g h f", g=4, h=2)
                Av = tAL[t % 2][:].rearrange("p (g h f) -> p g h f", g=4, h=2)
                Bv = tBL[t % 2][:].rearrange("p (g h f) -> p g h f", g=4, h=2)
                cb_ = cos[:, t - 2, :].rearrange("p (g f) -> p g f", g=4).unsqueeze(2).to_broadcast([128, 4, 2, 32])
                sv = sin[:, t - 2, :].rearrange("p (g f) -> p g f", g=4)
                k.tt(Av, xv, cb_, ALU.mult)
                k.tt(Bv[:, :, 0, :], xv[:, :, 1, :], sv, ALU.mult)
                k.tt(Bv[:, :, 1, :], xv[:, :, 0, :], sv, ALU.mult)
                k.tt(ov[:, :, 0, :], Av[:, :, 0, :], Bv[:, :, 0, :], ALU.subtract)
                k.tt(ov[:, :, 1, :], Av[:, :, 1, :], Bv[:, :, 1, :], ALU.add)
                k.copy(gf[:], pa[:, 256:512])
                k.act(gs[:, t - 2, :], gf[:], AF.Silu)
            else:
                k.copy(qk[:], pa[:, 0:256])
            k.ts(kb[:, t, :], qk[:, 128:256], SC, ALU.mult, eng="pool")
            k.copy(vb[:, t, :], pb[:, 0:256])
            pt = nextP()
            k.transpose(pt[:, 0:128], qk[:, 0:128], identf[:])
            k.transpose(pt[:, 128:256], qk[:, 128:256], identf[:])
            k.copy(qT[:, t, :], pt[:, 0:128])
            k.ts(kT[:, t, :], pt[:, 128:256], SC, ALU.mult)
        ordB = [1, 0] + list(range(33, 1, -1)); ordF = list(range(NT))
        for di in range(2):
            k.memset(S2[di][:], 0.0); k.memset(Sb2[di][:], 0.0)
        have = set()
        for i in range(NT):
            for di, t in ((1, ordB[i]), (0, ordF[i])):
                S = S2[di]; Sb = Sb2[di]; AT = AT2[di]; qin = qin2[di]; kk_ = kk2[di]
                if t >= 2:
                    pat = nextP()
                    k.matmul(pat[:, 0:128], kT[:, t, :], qT[:, t, :])
                    k.tt(AT[:], pat[:, 0:128], DM[di][:], ALU.mult)
                    k.tt(qin[:], qT[:, t, :], EBr[di][:], ALU.mult)
                    po = nextP()
                    k.matmul(po[:, 0:DV], AT[:], vb[:, t, :], start=True, stop=False)
                    k.matmul(po[:, 0:DV], qin[:], Sb[:], start=False, stop=True)
                k.ts(kk_[:], kb[:, t, :], ERc[:, di:di + 1], ALU.mult)
                pkv = nextP()
                k.matmul(pkv[:, 0:DV], kk_[:], vb[:, t, :])
                k.stt(S[:], S[:], dcol[:, di:di + 1], pkv[:, 0:DV], ALU.mult, ALU.add)
                k.copy(Sb[:], S[:], eng="act")
                if pendB[0] is not None:
                    pendB[0](); pendB[0] = None
                if pend[0] is not None:
                    pendB[0] = pend[0](); pend[0] = None
                if t >= 2:
                    if t not in have:
                        k.copy(oacc[:, t - 2, :], po[:, 0:DV]); have.add(t)
                    else:
                        par = rc[0] % 2; rc[0] += 1
                        onb = onL[par]
                        k.tt(onb[:], po[:, 0:DV], oacc[:, t - 2, :], ALU.add)

                        def readout(onb=onb, par=par, t=t, hl=hl):
                            st6_ = st6L[par]; mv_ = mvL[par]; rs_ = rsL[par]; nb_ = nbL[par]; obf_ = obfL[par]
                            k.op("dve", lambda e_, a=st6_, b_=onb: e_.bn_stats(a[:], b_[:]), [onb[:]], [st6_[:]])
                            k.op("dve", lambda e_, a=mv_, b_=st6_: e_.bn_aggr(a[:], b_[:]), [st6_[:]], [mv_[:]])
                            k.ts(rs_[:], mv_[:, 1:2], EPS, ALU.add)
                            k.act(rs_[:], rs_[:], AF.Sqrt)
                            k.recip(rs_[:], rs_[:])
                            k.stt(nb_[:], mv_[:, 0:1], -1.0, rs_[:], ALU.mult, ALU.mult)
                            k.act(onb[:], onb[:], AF.Identity, bias=nb_[:, 0:1], scale=rs_[:, 0:1])
                            k.tt(onb[:], onb[:], ng[:, hl % 2, :], ALU.mult, eng="pool")
                            k.tt(obf_[:], onb[:], gs[:, t - 2, :], ALU.mult, eng="pool")
                            return lambda: partB(obf_, t, hl)

                        def partB(obf_, t, hl):
                            ptt = nextPT()
                            k.transpose(ptt[:, 0:128], obf_[:, 0:128], ident[:])
                            k.transpose(ptt[:, 128:256], obf_[:, 128:256], ident[:])
                            lt = t - 2
                            ob = oT4[oc[0] % 4]; oc[0] += 1
                            k.copy(ob[:], ptt[:, 0:256].rearrange("p (c t) -> p c t", c=2))
                            k.dma(d_oT[hl * DV:(hl + 1) * DV, lt * 128:(lt + 1) * 128].rearrange("(c p) t -> p c t", p=128), ob[:])
                        pend[0] = readout
        if pendB[0] is not None:
            pendB[0](); pendB[0] = None
        if pend[0] is not None:
            pend[0]()(); pend[0] = None
    k.close_scope()


def phase_F(e, pfx, Fdim, last, segs, xsrc, osrc, ydst, d_cvec, d_wmod, d_bmod, d_n2g, d_fg, d_cw, d_cb, d_wout, d_wup, d_wdn):
    k = e.k; nextP = e.nextP; ones_bf = e.ones_bf
    KF = Fdim // 128
    NMAX = max(s[0] for s in segs); CMAX = NMAX + 2
    k.open_scope()
    cvec = k.sb(pfx + "cvec_s", [128, 8, 2]); scb = k.sb(pfx + "scb", [128, 8, 2], BF16)
    bmod = k.sb(pfx + "bmod_s", [128, 32]); modv = k.sb(pfx + "modv", [128, 32, 2])
    n2g = k.sb(pfx + "n2g_s", [128, 8]); fg = k.sb(pfx + "fg_s", [128, 8]); gm2 = k.sb(pfx + "gm2", [128, 8, 2])
    cw = k.sb(pfx + "cw_s", [128, 3, 44]); cb = k.sb(pfx + "cb_s", [128, 44])
    wout = k.sb(pfx + "wout_s", [128, KF, D], BF16)
    wdn = k.sb(pfx + "wdn_s", [128, NFF, D], BF16)
    oT = k.sb(pfx + "oT_s", [128, KF, CMAX], BF16)
    x1 = k.sb(pfx + "x1T", [128, 8, CMAX])
    sq = k.sb(pfx + "sq", [128, 8, CMAX], BF16)
    rstd = k.sb(pfx + "rstd", [128, CMAX]); tmp = k.sb(pfx + "tmp", [128, CMAX]); tmpb = k.sb(pfx + "tmpb", [128, CMAX])
    h2 = k.sb(pfx + "h2T", [128, 8, CMAX], BF16)
    wua = [k.sb(pfx + "wua%d" % i, [128, 8, 256], BF16) for i in range(2)]
    wub = [k.sb(pfx + "wub%d" % i, [128, 8, 256], BF16) for i in range(2)]
    u = [k.sb(pfx + "u%d" % i, [128, 2, CMAX]) for i in range(2)]
    vaL = [k.sb(pfx + "va%d" % i, [128, NMAX]) for i in range(2)]; vbL = [k.sb(pfx + "vb%d" % i, [128, NMAX]) for i in range(2)]
    saL = [k.sb(pfx + "sa%d" % i, [128, NMAX]) for i in range(2)]
    tT = k.sb(pfx + "tT", [128, NFF, NMAX], BF16)
    k.dma(cvec[:], d_cvec); k.dma(bmod[:], d_bmod); k.dma(n2g[:], d_n2g); k.dma(fg[:], d_fg)
    k.dma(cw[:], d_cw); k.dma(cb[:], d_cb)
    k.act(scb[:], cvec[:], AF.Silu)
    pm = nextP()
    for g in range(8):
        wb = wua[g % 2][:, :, :]
        wb2 = wub[g % 2][:, :, :]
        k.dma(wb, d_wmod.rearrange("(k p) c -> p k c", p=128)[:, :, g * 512:g * 512 + 256], q="pool")
        k.dma(wb2, d_wmod.rearrange("(k p) c -> p k c", p=128)[:, :, g * 512 + 256:(g + 1) * 512], q="pool")
        for c4 in range(4):
            cc = g * 4 + c4
            src = wb if c4 < 2 else wb2
            for kk in range(8):
                k.matmul(pm[:, cc * 2:cc * 2 + 2], src[:, kk, (c4 % 2) * 128:(c4 % 2 + 1) * 128], scb[:, kk, :],
                         start=(kk == 0), stop=(kk == 7))
    pmv = pm[:, 0:64].rearrange("p (c j) -> p c j", j=2)
    for j in range(2):
        k.tt(modv[:, :, j], pmv[:, :, j], bmod[:], ALU.add)
    for j in range(2):
        k.ts(gm2[:, :, j], modv[:, 16:24, j], 1.0, ALU.add)
        k.tt(gm2[:, :, j], gm2[:, :, j], n2g[:], ALU.mult)
    load_cast_rows(k, wout, d_wout, KF, D, split=4)
    load_cast_rows(k, wdn, d_wdn, NFF, D, split=4)
    wupv = d_wup.rearrange("(k p) c -> p k c", p=128)
    wctr = [0]
    k.dma(wua[0][:], wupv[:, :, 0:256], q="pool")
    k.dma(wub[0][:], wupv[:, :, DFF:DFF + 256], q="pool")

    for si, (n, j, kind, t0) in enumerate(segs):
        cols = n + 2
        tiles = col_tiles(cols)
        xs, T = xsrc[kind]; os_, oc0, _ = osrc[kind]
        lo = max(t0 - 1, 0); hi = min(t0 + n + 1, T)
        c_lo = lo - (t0 - 1); c_hi = hi - (t0 - 1)
        if c_lo > 0:
            k.memset(x1[:, :, 0:1], 0.0); k.memset(oT[:, :, 0:1], 0.0)
        if c_hi < cols:
            k.memset(x1[:, :, cols - 1:cols], 0.0); k.memset(oT[:, :, cols - 1:cols], 0.0)
        k.dma(x1[:, :, c_lo:c_hi], xs.rearrange("(k p) t -> p k t", p=128)[:, :, lo:hi])
        k.dma(oT[:, :, c_lo:c_hi], os_.rearrange("(k p) t -> p k t", p=128)[:, :, oc0 + lo:oc0 + hi])
        for fc in range(8):
            for (a, b) in tiles:
                p = nextP()
                for kk in range(KF):
                    k.matmul(p[:, 0:b - a], wout[:, kk, fc * 128:(fc + 1) * 128], oT[:, kk, a:b],
                             start=(kk == 0), stop=(kk == KF - 1))
                k.stt(x1[:, fc, a:b], p[:, 0:b - a], modv[:, 0 + fc, j:j + 1], x1[:, fc, a:b], ALU.mult, ALU.add)
        for kk in range(8):
            k.act(sq[:, kk, 0:cols], x1[:, kk, 0:cols], AF.Square)
        for (a, b) in tiles:
            p = nextP()
            for kk in range(8):
                k.matmul(p[:, 0:b - a], ones_bf[:], sq[:, kk, a:b], start=(kk == 0), stop=(kk == 7))
            k.ts(tmp[:, a:b], p[:, 0:b - a], 1.0 / D, ALU.mult, EPS, ALU.add)
        k.act(tmp[:, 0:cols], tmp[:, 0:cols], AF.Sqrt)
        k.recip(rstd[:, 0:cols], tmp[:, 0:cols])
        for kk in range(8):
            tb = tmp if kk % 2 == 0 else tmpb
            k.stt(tb[:, 0:cols], x1[:, kk, 0:cols], gm2[:, kk, j:j + 1], rstd[:, 0:cols], ALU.mult, ALU.mult)
            k.act(h2[:, kk, 0:cols], tb[:, 0:cols], AF.Identity, bias=modv[:, 8 + kk, j:j + 1], scale=1.0)
        if c_lo > 0:
            k.memset(h2[:, :, 0:1], 0.0)
        if c_hi < cols:
            k.memset(h2[:, :, cols - 1:cols], 0.0)
        for g in range(11):
            wa = wua[wctr[0] % 2]; wb_ = wub[wctr[0] % 2]
            wctr[0] += 1
            gn = g + 1 if g < 10 else (0 if si + 1 < len(segs) else None)
            if gn is not None:
                k.dma(wua[wctr[0] % 2][:], wupv[:, :, gn * 256:(gn + 1) * 256], q="pool")
                k.dma(wub[wctr[0] % 2][:], wupv[:, :, DFF + gn * 256:DFF + (gn + 1) * 256], q="pool")
            for c2 in range(2):
                c = g * 2 + c2
                ub = u[c % 2]
                for half, w in ((0, wa), (1, wb_)):
                    for (a, b) in tiles:
                        p = nextP()
                        for kk in range(8):
                            k.matmul(p[:, 0:b - a], w[:, kk, c2 * 128:(c2 + 1) * 128], h2[:, kk, a:b],
                                     start=(kk == 0), stop=(kk == 7))
                        k.copy(ub[:, half, a:b], p[:, 0:b - a])
                ca = c; cbi = NFF + c
                va = vaL[c % 2]; vb = vbL[c % 2]; sa = saL[c % 2]
                k.act(va[:, 0:n], ub[:, 0, 1:n + 1], AF.Identity, bias=cb[:, ca:ca + 1], scale=cw[:, 1, ca:ca + 1])
                k.stt(va[:, 0:n], ub[:, 0, 0:n], cw[:, 0, ca:ca + 1], va[:, 0:n], ALU.mult, ALU.add)
                k.stt(va[:, 0:n], ub[:, 0, 2:n + 2], cw[:, 2, ca:ca + 1], va[:, 0:n], ALU.mult, ALU.add)
                k.act(vb[:, 0:n], ub[:, 1, 1:n + 1], AF.Identity, bias=cb[:, cbi:cbi + 1], scale=cw[:, 1, cbi:cbi + 1])
                k.stt(vb[:, 0:n], ub[:, 1, 0:n], cw[:, 0, cbi:cbi + 1], vb[:, 0:n], ALU.mult, ALU.add)
                k.stt(vb[:, 0:n], ub[:, 1, 2:n + 2], cw[:, 2, cbi:cbi + 1], vb[:, 0:n], ALU.mult, ALU.add)
                k.act(sa[:, 0:n], va[:, 0:n], AF.Silu)
                k.tt(tT[:, c, 0:n], sa[:, 0:n], vb[:, 0:n], ALU.mult, eng="pool")
        for fc in range(8):
            p = nextP()
            for c in range(NFF):
                k.matmul(p[:, 0:n], wdn[:, c, fc * 128:(fc + 1) * 128], tT[:, c, 0:n], start=(c == 0), stop=(c == NFF - 1))
            k.stt(x1[:, fc, 1:n + 1], p[:, 0:n], modv[:, 24 + fc, j:j + 1], x1[:, fc, 1:n + 1], ALU.mult, ALU.add)
        if last:
            for kk in range(8):
                k.act(sq[:, kk, 0:n], x1[:, kk, 1:n + 1], AF.Square)
            p = nextP()
            for kk in range(8):
                k.matmul(p[:, 0:n], ones_bf[:], sq[:, kk, 0:n], start=(kk == 0), stop=(kk == 7))
            k.ts(tmp[:, 0:n], p[:, 0:n], 1.0 / D, ALU.mult, EPS, ALU.add)
            k.act(tmp[:, 0:n], tmp[:, 0:n], AF.Sqrt)
            k.recip(rstd[:, 0:n], tmp[:, 0:n])
            for kk in range(8):
                k.stt(x1[:, kk, 1:n + 1], x1[:, kk, 1:n + 1], fg[:, kk:kk + 1], rstd[:, 0:n], ALU.mult, ALU.mult)
        k.dma(ydst[kind].rearrange("(k p) t -> p k t", p=128)[:, :, t0:t0 + n], x1[:, :, 1:n + 1])
    k.close_scope()


def build_fused():
    k = KB(); nc = k.nc
    e = make_env(k)
    d_xT = k.dram("xT", [D, 4096]); d_xcT = k.dram("xcT", [D, 256]); d_cvec = k.dram("cvec", [128, 8, 2])
    cn = {nm: k.dram(nm, [128, 128]) for nm in ("mask_f", "mask_b", "dm_f", "dm_b", "row_f", "row_b")}
    d_pcols = k.dram("pcols", [128, 4]); d_cos = k.dram("cos", [128, 32, 128]); d_sin = k.dram("sin", [128, 32, 128])
    L = []
    for l in range(2):
        L.append(dict(wmodA=k.dram("wmodA%d" % l, [D, 2048]), bmodA=k.dram("bmodA%d" % l, [128, 16]), n1g=k.dram("n1g%d" % l, [128, 8]),
                      wmodB=k.dram("wmodB%d" % l, [D, 4096]), bmodB=k.dram("bmodB%d" % l, [128, 32]), n2g=k.dram("n2g%d" % l, [128, 8]),
                      cw=k.dram("convw%d" % l, [128, 3, 44]), cb=k.dram("convb%d" % l, [128, 44]),
                      wout=k.dram("wout%d" % l, [1024 * (l + 1), D]), wup=k.dram("wup%d" % l, [D, 2 * DFF]), wdn=k.dram("wdn%d" % l, [DFF, D])))
    d_fg = k.dram("fg", [128, 8])
    d_wna = k.dram("wna", [2, D, 768]); d_wgl = k.dram("wgl", [2, D, 768]); d_wa = k.dram("wa", [D, 32])
    d_waaug = k.dram("waaug", [17, 2, 256]); d_rpbT = k.dram("rpbT", [2, 128, 4, 15, 64]); d_colmask = k.dram("colmask", [128, 64])
    d_gng = k.dram("gng", [128, 128])
    d_win = k.dram("win", [8, D, 768]); d_dec = k.dram("dec", [128, 16]); d_ng = k.dram("ng", [128, 8, 256])
    d_y = k.dram("yT", [D, 4096], kind="ExternalOutput")
    o0T = nc.dram_tensor("o0T", [1024, NT * 128], BF16, kind="Internal").ap()
    x2T = nc.dram_tensor("x2T", [D, 4096], F32, kind="Internal").ap()
    xc2T = nc.dram_tensor("xc2T", [D, 256], F32, kind="Internal").ap()
    o1T = nc.dram_tensor("o1T", [2048, 4096], BF16, kind="Internal").ap()
    xcdump = nc.dram_tensor("xcdump", [D, 256], F32, kind="Internal").ap()

    k.open_scope()
    hT = k.sb("hT0", [128, 8, NT * 128], BF16)
    phase_hT(e, "a0", d_xT, d_xcT, d_cvec, L[0]["wmodA"], L[0]["bmodA"], L[0]["n1g"], hT)
    for hh in range(2):
        phase_NA(e, "na%d" % hh, hT, d_wna[hh], d_rpbT[hh], d_colmask, o0T, hh * 512)
        phase_GLA(e, "gl%d" % hh, hT, d_wgl[hh], d_wa, d_waaug, d_gng, cn["mask_f"], cn["mask_b"], o0T, hh * 512 + 256,
                  [(0, hh * 2, 0), (1, hh * 2 + 1, 128)])
    k.close_scope()
    segs0 = [(512, 0, "lat", s * 512) for s in range(8)] + [(256, 1, "ctx", 0)]
    phase_F(e, "f0", 1024, False, segs0,
            {"lat": (d_xT, 4096), "ctx": (d_xcT, 256)}, {"lat": (o0T, 256, 4096), "ctx": (o0T, 0, 256)},
            {"lat": x2T, "ctx": xc2T}, d_cvec, L[0]["wmodB"], L[0]["bmodB"], L[0]["n2g"], d_fg, L[0]["cw"], L[0]["cb"],
            L[0]["wout"], L[0]["wup"], L[0]["wdn"])
    k.open_scope()
    hT = k.sb("hT1", [128, 8, NT * 128], BF16)
    phase_hT(e, "a1", x2T, xc2T, d_cvec, L[1]["wmodA"], L[1]["bmodA"], L[1]["n1g"], hT)
    phase_RET(e, "rt", hT, d_win, d_dec, d_ng, d_cos, d_sin, cn, d_pcols, o1T, 8)
    k.close_scope()
    segs1 = [(512, 0, "lat", s * 512) for s in range(8)]
    phase_F(e, "f1", 2048, True, segs1,
            {"lat": (x2T, 4096)}, {"lat": (o1T, 0, 4096)}, {"lat": d_y},
            d_cvec, L[1]["wmodB"], L[1]["bmodB"], L[1]["n2g"], d_fg, L[1]["cw"], L[1]["cb"],
            L[1]["wout"], L[1]["wup"], L[1]["wdn"])
    ncc = k.finish([d_y])
    return ncc, k


_PROG = []


def _maps(inp):
    B = 4
    shared = {}
    shared["ident"] = np.eye(128, dtype=np.float32)
    for kk in ("mask_f", "mask_b", "dm_f", "dm_b", "row_f", "row_b", "pcols"):
        shared[kk] = _C[kk]
    shared["cos"] = np.ascontiguousarray(np.concatenate([_COS, _COS], axis=2)); shared["sin"] = np.ascontiguousarray(np.concatenate([_SIN, _SIN], axis=2))
    for l in range(2):
        shared["wmodA%d" % l] = np.ascontiguousarray(inp["w_mod"][l][:, 0:2048])
        shared["bmodA%d" % l] = pk(inp["b_mod"][l][0:2048])
        shared["n1g%d" % l] = pk(inp["norm1_g"][l])
        shared["wmodB%d" % l] = np.ascontiguousarray(inp["w_mod"][l][:, 2048:6144])
        shared["bmodB%d" % l] = pk(inp["b_mod"][l][2048:6144])
        shared["n2g%d" % l] = pk(inp["norm2_g"][l])
        cw = inp["ffn_conv_w"][l]
        shared["convw%d" % l] = np.ascontiguousarray(np.stack([pk(cw[i]) for i in range(3)], axis=1))
        shared["convb%d" % l] = pk(inp["ffn_conv_b"][l])
        shared["wup%d" % l] = inp["ffn_w_up"][l]; shared["wdn%d" % l] = inp["ffn_w_down"][l]
    perm = np.concatenate([np.arange(0, 256), np.arange(512, 768), np.arange(256, 512), np.arange(768, 1024)])
    shared["wout0"] = np.ascontiguousarray(inp["na_gla_w_out"][0][perm])
    shared["wout1"] = np.ascontiguousarray(inp["ret_w_out"][0])
    shared["fg"] = pk(inp["final_norm_g"])
    w = inp["na_gla_w_in"][0]
    wna = []; wgl = []; rp = []
    for hh in range(2):
        nh = [hh * 4 + i for i in range(4)]; gh = [hh * 2 + i for i in range(2)]
        wna.append(np.concatenate([w[:, h * 64:(h + 1) * 64] for h in nh] + [w[:, 512 + h * 64:512 + (h + 1) * 64] for h in nh] +
                                  [w[:, 1024 + h * 64:1024 + (h + 1) * 64] for h in nh], axis=1))
        wgl.append(np.concatenate([w[:, 1536 + h * 64:1536 + (h + 1) * 64] for h in gh] + [w[:, 1792 + h * 64:1792 + (h + 1) * 64] for h in gh] +
                                  [w[:, 2560 + h * 128:2560 + (h + 1) * 128] for h in gh] + [w[:, 2048 + h * 128:2048 + (h + 1) * 128] for h in gh], axis=1))
        r_, cm = _na_tables(inp["na_rpb"][0][nh]); rp.append(r_)
    shared["wna"] = np.ascontiguousarray(np.stack(wna)); shared["wgl"] = np.ascontiguousarray(np.stack(wgl))
    shared["rpbT"] = np.ascontiguousarray(np.stack(rp)); shared["colmask"] = cm
    shared["wa"] = np.ascontiguousarray(w[:, 3072:3104])
    wa = np.zeros((17, 2, 256), np.float32)
    wa[0:16, 0] = inp["gla_w_a_fwd"][0]; wa[16, 0] = inp["gla_b_a_fwd"][0]
    wa[0:16, 1] = inp["gla_w_a_bwd"][0]; wa[16, 1] = inp["gla_b_a_bwd"][0]
    shared["waaug"] = wa
    shared["gng"] = np.ascontiguousarray(np.broadcast_to(inp["gla_norm_g"][0], (128, 128)).astype(np.float32))
    wr = inp["ret_w_in"][0]
    shared["win"] = np.ascontiguousarray(np.stack([np.concatenate([
        wr[:, h * 128:(h + 1) * 128], wr[:, 1024 + h * 128:1024 + (h + 1) * 128],
        wr[:, 4096 + h * 256:4096 + (h + 1) * 256], wr[:, 2048 + h * 256:2048 + (h + 1) * 256]], axis=1) for h in range(8)]))
    dec = np.concatenate([inp["ret_decay_fwd"][0], inp["ret_decay_bwd"][0]])
    shared["dec"] = np.ascontiguousarray(np.broadcast_to(dec, (128, 16)).astype(np.float32))
    shared["ng"] = np.ascontiguousarray(np.broadcast_to(inp["ret_norm_g"][0], (128, 8, 256)).astype(np.float32))
    maps = []
    for core in range(8):
        b = core // 2
        m = dict(shared)
        m["xT"] = np.ascontiguousarray(inp["x"][b].T); m["xcT"] = np.ascontiguousarray(inp["ctx"][b].T)
        m["cvec"] = np.ascontiguousarray(np.stack([pk(inp["c"][b]), pk(inp["c_ctx"])], axis=-1).astype(np.float32))
        maps.append(m)
    return maps


def kernel(**inp):
    inp = {k_: np.asarray(v) for k_, v in inp.items()}
    if not _PROG:
        _PROG.append(build_fused()[0])
    res = run_bass_kernel_spmd(_PROG[0], _maps(inp), core_ids=list(range(8))).results
    out = np.empty((4, 4096, 1024), np.float32)
    for b in range(4):
        out[b, 0:2048] = res[2 * b]["yT"][:, 0:2048].T
        out[b, 2048:4096] = res[2 * b + 1]["yT"][:, 2048:4096].T
    return out
```

```python
import os
import ml_dtypes
from concourse.bass_utils import run_bass_kernel_spmd

from contextlib import ExitStack
import numpy as np
import concourse.bass as bass
import concourse.mybir as mybir

F32 = mybir.dt.float32
BF16 = mybir.dt.bfloat16
AF = mybir.ActivationFunctionType
ALU = mybir.AluOpType
AX = mybir.AxisListType

ENGS = ("pe", "act", "dve", "pool", "sp")
NDSEM = 12


def _region(ap):
    t = ap.tensor
    name = t.name
    dims = list(ap.ap)
    off = int(ap.offset)
    sp = str(ap.space) if hasattr(ap, "space") else ""
    if "DRAM" in sp.upper() or type(t).__name__.startswith("DRam"):
        ext = sum((int(c) - 1) * abs(int(s)) for s, c in dims)
        return (name, 0, 1, off, off + ext + 1)
    if type(t).__name__.startswith("PSum"):
        return (name, 0, 128, 0, 1 << 40)
    pstep, pcnt = int(dims[0][0]), int(dims[0][1])
    if pstep == 0:
        pstep = 1 << 40
    p0 = off // pstep
    f0 = off % pstep
    ext = sum((int(c) - 1) * abs(int(s)) for s, c in dims[1:])
    return (name, p0, p0 + pcnt, f0, f0 + ext + 1)


def _overlap(a, b):
    return a[1] < b[2] and b[1] < a[2] and a[3] < b[4] and b[3] < a[4]


def _covers(a, b):
    return a[1] <= b[1] and a[2] >= b[2] and a[3] <= b[3] and a[4] >= b[4]


class KB:
    def __init__(self):
        self.nc = bass.Bass("TRN2", target_bir_lowering=False)
        self.es = ExitStack()
        self.ops = []
        self.recs = {}
        self.n_alloc = 0
        self.fence = None
        self.fenced = set()
        self.stack = [self.es]

    def sb(self, name, shape, dt=F32):
        return self.stack[-1].enter_context(self.nc.sbuf_tensor(name, list(shape), dt))

    def barrier(self):
        last = {}
        f = set()
        for i, o in enumerate(self.ops):
            if o["dma"]:
                f.add(i)
            else:
                last[o["eng"]] = i
        f.update(last.values())
        if self.fence is not None:
            f = {i for i in f if i > self.fence_at or not self.ops[i]["dma"]}
        self.fence = f
        self.fence_at = len(self.ops)
        self.fenced = set()

    def open_scope(self):
        self.stack.append(ExitStack())

    def close_scope(self):
        self.barrier()
        self.stack.pop().close()

    def ps(self, name, shape, dt=F32):
        return self.es.enter_context(self.nc.psum_tensor(name, list(shape), dt))

    def dram(self, name, shape, dt=F32, kind="ExternalInput"):
        return self.nc.dram_tensor(name, list(shape), dt, kind=kind).ap()

    def op(self, eng, fn, reads, writes, dma=False):
        idx = len(self.ops)
        deps = set()
        rr = [_region(a) for a in reads if a is not None and hasattr(a, "tensor")]
        ww = [_region(a) for a in writes if a is not None and hasattr(a, "tensor")]
        for r in rr:
            for (g, oi, isw) in self.recs.get(r[0], ()):
                if isw and _overlap(r, g):
                    deps.add(oi)
        for w in ww:
            for (g, oi, isw) in self.recs.get(w[0], ()):
                if _overlap(w, g):
                    deps.add(oi)
        for w in ww:
            lst = self.recs.setdefault(w[0], [])
            lst[:] = [x for x in lst if not _covers(w, x[0])]
            lst.append((w, idx, True))
        for r in rr:
            lst = self.recs.setdefault(r[0], [])
            lst[:] = [x for x in lst if not ((not x[2]) and x[1] < idx and self.ops[x[1]]["eng"] == eng
                                             and not self.ops[x[1]]["dma"] and not dma and _covers(r, x[0]))]
            lst.append((r, idx, False))
        if self.fence is not None and eng not in self.fenced:
            deps.update(self.fence)
            self.fenced.add(eng)
        deps.discard(idx)
        self.ops.append(dict(eng=eng, fn=fn, deps=deps, dma=dma, rr=rr, ww=ww))
        return idx

    def dma(self, out, in_, q="sp"):
        return self.op(q, lambda e: e.dma_start(out=out, in_=in_), [in_], [out], dma=True)

    def matmul(self, out, lhsT, rhs, start=True, stop=True):
        return self.op("pe", lambda e: e.matmul(out, lhsT, rhs, start=start, stop=stop), [lhsT, rhs], [out])

    def transpose(self, out, in_, ident):
        return self.op("pe", lambda e: e.transpose(out, in_, ident), [in_, ident], [out])

    def act(self, out, in_, func, bias=None, scale=None, accum_out=None, eng="act"):
        kw = {}
        if bias is not None:
            kw["bias"] = bias
        if scale is not None:
            kw["scale"] = scale
        if accum_out is not None:
            kw["accum_out"] = accum_out
        return self.op(eng, lambda e: e.activation(out, in_, func, **kw), [in_, bias, scale], [out, accum_out])

    def tt(self, out, in0, in1, op, eng="dve"):
        return self.op(eng, lambda e: e.tensor_tensor(out, in0, in1, op), [in0, in1], [out])

    def ts(self, out, in0, s1, op0, s2=None, op1=None, accum_out=None, eng="dve"):
        def f(e):
            kw = {}
            if accum_out is not None:
                kw["accum_out"] = accum_out
            if op1 is None:
                return e.tensor_scalar(out, in0, s1, None, op0, **kw)
            return e.tensor_scalar(out, in0, s1, s2, op0, op1, **kw)
        return self.op(eng, f, [in0, s1, s2], [out, accum_out])

    def stt(self, out, in0, scalar, in1, op0, op1, eng="dve"):
        return self.op(eng, lambda e: e.scalar_tensor_tensor(out, in0, scalar, in1, op0, op1), [in0, scalar, in1], [out])

    def copy(self, out, in_, eng="dve"):
        if eng == "act":
            return self.op("act", lambda e: e.copy(out, in_), [in_], [out])
        return self.op(eng, lambda e: e.tensor_copy(out, in_), [in_], [out])

    def memset(self, ap, val, eng="dve"):
        return self.op(eng, lambda e: e.memset(ap, val), [], [ap])

    def recip(self, out, in_):
        return self.op("dve", lambda e: e.reciprocal(out, in_), [in_], [out])

    def reduce(self, out, in_, op=ALU.add, axis=AX.X, eng="dve"):
        return self.op(eng, lambda e: e.tensor_reduce(out, in_, axis, op), [in_], [out])

    def finish(self, out_aps):
        nc = self.nc
        ops = self.ops
        out_names = {a.tensor.name for a in out_aps}
        final_deps = set()
        for i, o in enumerate(ops):
            if o["dma"] and any(w[0] in out_names for w in o["ww"]):
                final_deps.add(i)
        ops.append(dict(eng="sp", fn=None, deps=final_deps, dma=False, rr=[], ww=[]))
        needs_sig = [False] * len(ops)
        for o in ops:
            for d in o["deps"]:
                if ops[d]["dma"]:
                    continue
                if ops[d]["eng"] == o["eng"] and not o["dma"] and o["eng"] == "pe":
                    continue
                needs_sig[d] = True
        sems = {e: self.es.enter_context(nc.semaphore("s_" + e)) for e in ENGS}
        dsems = {e: [self.es.enter_context(nc.semaphore("d_%s_%d" % (e, i))) for i in range(NDSEM)]
                 for e in ("sp", "act", "pool")}
        cnt = {e: 0 for e in ENGS}
        dcnt = {e: 0 for e in dsems}
        sig = [None] * len(ops)
        prevdma = [None] * len(ops)
        for i, o in enumerate(ops):
            if o["dma"]:
                q = o["eng"]
                n = dcnt[q]
                dcnt[q] += 1
                s = dsems[q][n % NDSEM]
                sig[i] = (s, 16 * (n // NDSEM + 1))
                if n >= NDSEM:
                    prevdma[i] = (s, 16 * (n // NDSEM))
            elif needs_sig[i]:
                cnt[o["eng"]] += 1
                sig[i] = (sems[o["eng"]], cnt[o["eng"]])
        per_eng = {e: [] for e in ENGS}
        for i, o in enumerate(ops):
            per_eng[o["eng"]].append(i)
        self.stats = {e: len(per_eng[e]) for e in ENGS}
        self.stats["sig"] = dict(cnt)

        def emit(ename):
            def body(eng):
                seen = {}
                for i in per_eng[ename]:
                    o = ops[i]
                    waits = {}
                    for d in o["deps"]:
                        po = ops[d]
                        if (not po["dma"]) and po["eng"] == ename and ename == "pe" and not o["dma"]:
                            continue
                        s, v = sig[d]
                        key = id(s)
                        if waits.get(key, (None, 0))[1] < v:
                            waits[key] = (s, v)
                    if prevdma[i] is not None:
                        s, v = prevdma[i]
                        key = id(s)
                        if waits.get(key, (None, 0))[1] < v:
                            waits[key] = (s, v)
                    for key, (s, v) in waits.items():
                        if seen.get(key, 0) >= v:
                            continue
                        eng.wait_ge(s, v)
                        seen[key] = v
                    if o["fn"] is None:
                        continue
                    ins = o["fn"](eng)
                    if sig[i] is not None:
                        s, v = sig[i]
                        ins.then_inc(s, 16 if o["dma"] else 1)
            return body

        with nc.Block() as block:
            block.tensor(emit("pe"))
            block.scalar(emit("act"))
            block.vector(emit("dve"))
            block.gpsimd(emit("pool"))
            block.sync(emit("sp"))
        self.es.close()
        return nc

D = 1024; DFF = 2816; NFF = 22; EPS = 1e-6; NT = 34


def col_tiles(cols):
    nt = (cols + 511) // 512
    base = cols // nt
    res = []; s = 0
    for i in range(nt):
        e = s + base + (1 if i < cols % nt else 0)
        res.append((s, e)); s = e
    return res


def load_cast_rows(k, dst, src, nk, width, q="pool", split=1):
    v = src.rearrange("(k p) c -> p k c", p=128)
    step = max(1, nk // split)
    for a in range(0, nk, step):
        b = min(nk, a + step)
        k.dma(dst[:, a:b, :], v[:, a:b, :], q=q)


def build_F(layer_has_ctx, Fdim, last, segs):
    k = KB()
    KF = Fdim // 128
    NMAX = max(n for n, _ in segs); CMAX = NMAX + 2
    d_x = [k.dram("xT_%d" % i, [D, n + 2]) for i, (n, _) in enumerate(segs)]
    d_o = [k.dram("oT_%d" % i, [Fdim, n + 2], BF16) for i, (n, _) in enumerate(segs)]
    d_hm = [k.dram("hm_%d" % i, [128, 2]) for i, (n, _) in enumerate(segs)]
    d_y = [k.dram("yT_%d" % i, [D, n], kind="ExternalOutput") for i, (n, _) in enumerate(segs)]
    d_cvec = k.dram("cvec", [128, 8, 2])
    d_wmod = k.dram("wmod", [D, 4096]); d_bmod = k.dram("bmod", [128, 32])
    d_n2g = k.dram("n2g", [128, 8]); d_fg = k.dram("fg", [128, 8])
    d_cw = k.dram("convw", [128, 3, 44]); d_cb = k.dram("convb", [128, 44])
    d_wout = k.dram("wout", [Fdim, D]); d_wup = k.dram("wup", [D, 2 * DFF]); d_wdn = k.dram("wdn", [DFF, D])
    ones_bf = k.sb("ones_bf", [128, 128], BF16)
    cvec = k.sb("cvec_s", [128, 8, 2]); scb = k.sb("scb", [128, 8, 2], BF16)
    bmod = k.sb("bmod_s", [128, 32]); modv = k.sb("modv", [128, 32, 2])
    n2g = k.sb("n2g_s", [128, 8]); fg = k.sb("fg_s", [128, 8]); gm2 = k.sb("gm2", [128, 8, 2])
    cw = k.sb("cw_s", [128, 3, 44]); cb = k.sb("cb_s", [128, 44])
    wmb = [k.sb("wmb%d" % i, [128, 8, 512], BF16) for i in range(2)]
    wout = k.sb("wout_s", [128, KF, D], BF16)
    wdn = k.sb("wdn_s", [128, NFF, D], BF16)
    oT = k.sb("oT_s", [128, KF, CMAX], BF16)
    x1 = k.sb("x1T", [128, 8, CMAX])
    sq = k.sb("sq", [128, 8, CMAX], BF16)
    rstd = k.sb("rstd", [128, CMAX]); tmp = k.sb("tmp", [128, CMAX])
    h2 = k.sb("h2T", [128, 8, CMAX], BF16)
    hm = k.sb("hm_s", [128, 2])
    wua = [k.sb("wua%d" % i, [128, 8, 256], BF16) for i in range(2)]
    wub = [k.sb("wub%d" % i, [128, 8, 256], BF16) for i in range(2)]
    u = [k.sb("u%d" % i, [128, 2, CMAX]) for i in range(2)]
    va = k.sb("va", [128, NMAX]); vb = k.sb("vb", [128, NMAX]); sa = k.sb("sa", [128, NMAX])
    tT = k.sb("tT", [128, NFF, NMAX], BF16)
    P = [k.ps("P%d" % i, [128, 512]) for i in range(8)]
    pctr = [0]

    def nextP():
        p = P[pctr[0] % 8]; pctr[0] += 1
        return p

    k.memset(ones_bf[:], 1.0)
    k.dma(cvec[:], d_cvec); k.dma(bmod[:], d_bmod); k.dma(n2g[:], d_n2g); k.dma(fg[:], d_fg)
    k.dma(cw[:], d_cw); k.dma(cb[:], d_cb)
    k.act(scb[:], cvec[:], AF.Silu)
    pm = nextP()
    for g in range(8):
        wb = wmb[g % 2]
        k.dma(wb[:], d_wmod.rearrange("(k p) c -> p k c", p=128)[:, :, g * 512:(g + 1) * 512], q="pool")
        for c4 in range(4):
            cc = g * 4 + c4
            for kk in range(8):
                k.matmul(pm[:, cc * 2:cc * 2 + 2], wb[:, kk, c4 * 128:(c4 + 1) * 128], scb[:, kk, :],
                         start=(kk == 0), stop=(kk == 7))
    pmv = pm[:, 0:64].rearrange("p (c j) -> p c j", j=2)
    for j in range(2):
        k.tt(modv[:, :, j], pmv[:, :, j], bmod[:], ALU.add)
    for j in range(2):
        k.ts(gm2[:, :, j], modv[:, 16:24, j], 1.0, ALU.add)
        k.tt(gm2[:, :, j], gm2[:, :, j], n2g[:], ALU.mult)
    load_cast_rows(k, wout, d_wout, KF, D, split=4)
    load_cast_rows(k, wdn, d_wdn, NFF, D, split=4)
    wupv = d_wup.rearrange("(k p) c -> p k c", p=128)

    for si, (n, j) in enumerate(segs):
        cols = n + 2
        tiles = col_tiles(cols)
        k.dma(x1[:, :, 0:cols], d_x[si].rearrange("(k p) t -> p k t", p=128))
        k.dma(oT[:, :, 0:cols], d_o[si].rearrange("(k p) t -> p k t", p=128))
        k.dma(hm[:], d_hm[si])
        for fc in range(8):
            for (a, b) in tiles:
                p = nextP()
                for kk in range(KF):
                    k.matmul(p[:, 0:b - a], wout[:, kk, fc * 128:(fc + 1) * 128], oT[:, kk, a:b],
                             start=(kk == 0), stop=(kk == KF - 1))
                k.stt(x1[:, fc, a:b], p[:, 0:b - a], modv[:, 0 + fc, j:j + 1], x1[:, fc, a:b], ALU.mult, ALU.add)
        for kk in range(8):
            k.act(sq[:, kk, 0:cols], x1[:, kk, 0:cols], AF.Square)
        for (a, b) in tiles:
            p = nextP()
            for kk in range(8):
                k.matmul(p[:, 0:b - a], ones_bf[:], sq[:, kk, a:b], start=(kk == 0), stop=(kk == 7))
            k.act(tmp[:, a:b], p[:, 0:b - a], AF.Sqrt, bias=EPSB[0], scale=1.0 / D)
        k.recip(rstd[:, 0:cols], tmp[:, 0:cols])
        for kk in range(8):
            k.stt(tmp[:, 0:cols], x1[:, kk, 0:cols], gm2[:, kk, j:j + 1], rstd[:, 0:cols], ALU.mult, ALU.mult)
            k.act(h2[:, kk, 0:cols], tmp[:, 0:cols], AF.Identity, bias=modv[:, 8 + kk, j:j + 1], scale=1.0)
        k.ts(h2[:, :, 0:1], h2[:, :, 0:1], hm[:, 0:1], ALU.mult)
        k.ts(h2[:, :, cols - 1:cols], h2[:, :, cols - 1:cols], hm[:, 1:2], ALU.mult)
        for g in range(11):
            wa = wua[g % 2]; wb_ = wub[g % 2]
            k.dma(wa[:], wupv[:, :, g * 256:(g + 1) * 256], q="pool")
            k.dma(wb_[:], wupv[:, :, DFF + g * 256:DFF + (g + 1) * 256], q="pool")
            for c2 in range(2):
                c = g * 2 + c2
                ub = u[c % 2]
                for half, w in ((0, wa), (1, wb_)):
                    for (a, b) in tiles:
                        p = nextP()
                        for kk in range(8):
                            k.matmul(p[:, 0:b - a], w[:, kk, c2 * 128:(c2 + 1) * 128], h2[:, kk, a:b],
                                     start=(kk == 0), stop=(kk == 7))
                        k.copy(ub[:, half, a:b], p[:, 0:b - a], eng="act")
                ca = c; cbi = NFF + c
                k.act(va[:, 0:n], ub[:, 0, 1:n + 1], AF.Identity, bias=cb[:, ca:ca + 1], scale=cw[:, 1, ca:ca + 1])
                k.stt(va[:, 0:n], ub[:, 0, 0:n], cw[:, 0, ca:ca + 1], va[:, 0:n], ALU.mult, ALU.add)
                k.stt(va[:, 0:n], ub[:, 0, 2:n + 2], cw[:, 2, ca:ca + 1], va[:, 0:n], ALU.mult, ALU.add)
                k.act(vb[:, 0:n], ub[:, 1, 1:n + 1], AF.Identity, bias=cb[:, cbi:cbi + 1], scale=cw[:, 1, cbi:cbi + 1])
                k.stt(vb[:, 0:n], ub[:, 1, 0:n], cw[:, 0, cbi:cbi + 1], vb[:, 0:n], ALU.mult, ALU.add)
                k.stt(vb[:, 0:n], ub[:, 1, 2:n + 2], cw[:, 2, cbi:cbi + 1], vb[:, 0:n], ALU.mult, ALU.add)
                k.act(sa[:, 0:n], va[:, 0:n], AF.Silu)
                k.tt(tT[:, c, 0:n], sa[:, 0:n], vb[:, 0:n], ALU.mult)
        for fc in range(8):
            p = nextP()
            for c in range(NFF):
                k.matmul(p[:, 0:n], wdn[:, c, fc * 128:(fc + 1) * 128], tT[:, c, 0:n], start=(c == 0), stop=(c == NFF - 1))
            k.stt(x1[:, fc, 1:n + 1], p[:, 0:n], modv[:, 24 + fc, j:j + 1], x1[:, fc, 1:n + 1], ALU.mult, ALU.add)
        if last:
            for kk in range(8):
                k.act(sq[:, kk, 0:n], x1[:, kk, 1:n + 1], AF.Square)
            p = nextP()
            for kk in range(8):
                k.matmul(p[:, 0:n], ones_bf[:], sq[:, kk, 0:n], start=(kk == 0), stop=(kk == 7))
            k.act(tmp[:, 0:n], p[:, 0:n], AF.Sqrt, bias=EPSB[0], scale=1.0 / D)
            k.recip(rstd[:, 0:n], tmp[:, 0:n])
            for kk in range(8):
                k.stt(x1[:, kk, 1:n + 1], x1[:, kk, 1:n + 1], fg[:, kk:kk + 1], rstd[:, 0:n], ALU.mult, ALU.mult)
        k.dma(d_y[si].rearrange("(k p) t -> p k t", p=128), x1[:, :, 1:n + 1])
    nc = k.finish(d_y)
    return nc, k

EPSB = [EPS]


NT = 34


def emit_mod(k, nextP, d_cvec, d_wmod, d_bmod, ncols, wmb, name="m"):
    ncc = ncols // 128
    cvec = k.sb(name + "cvec", [128, 8, 2]); scb = k.sb(name + "scb", [128, 8, 2], BF16)
    bmod = k.sb(name + "bmod", [128, ncc]); modv = k.sb(name + "modv", [128, ncc, 2])
    k.dma(cvec[:], d_cvec); k.dma(bmod[:], d_bmod)
    k.act(scb[:], cvec[:], AF.Silu)
    pm = nextP()
    for g in range(ncols // 512):
        wb = wmb[g % len(wmb)]
        k.dma(wb[:], d_wmod.rearrange("(k p) c -> p k c", p=128)[:, :, g * 512:(g + 1) * 512], q="pool")
        for c4 in range(4):
            cc = g * 4 + c4
            for kk in range(8):
                k.matmul(pm[:, cc * 2:cc * 2 + 2], wb[:, kk, c4 * 128:(c4 + 1) * 128], scb[:, kk, :],
                         start=(kk == 0), stop=(kk == 7))
    pmv = pm[:, 0:2 * ncc].rearrange("p (c j) -> p c j", j=2)
    for j in range(2):
        k.tt(modv[:, :, j], pmv[:, :, j], bmod[:], ALU.add)
    return modv


def emit_hT(k, nextP, hT, d_xT, d_xcT, modv, n1g, ones_bf, xt, sq, rstd, tmp, name=""):
    gm1 = k.sb(name + "gm1", [128, 8, 2]); tmp2 = k.sb(name + "tmp2", [128, 256])
    for j in range(2):
        k.ts(gm1[:, :, j], modv[:, 8:16, j], 1.0, ALU.add)
        k.tt(gm1[:, :, j], gm1[:, :, j], n1g[:], ALU.mult)
    W = 256
    jobs = [(d_xcT, 0, 0, 1)] + [(d_xT, i * W, 256 + i * W, 0) for i in range(4096 // W)]
    for ji, (src, c0, h0, j) in enumerate(jobs):
        x = xt[ji % len(xt)]
        k.dma(x[:], src.rearrange("(k p) t -> p k t", p=128)[:, :, c0:c0 + W])
        for kk in range(8):
            k.act(sq[:, kk, :], x[:, kk, :], AF.Square)
        p = nextP()
        for kk in range(8):
            k.matmul(p[:, 0:W], ones_bf[:], sq[:, kk, :], start=(kk == 0), stop=(kk == 7))
        k.ts(tmp[:, 0:W], p[:, 0:W], 1.0 / D, ALU.mult, EPS, ALU.add)
        k.act(tmp[:, 0:W], tmp[:, 0:W], AF.Sqrt)
        k.recip(rstd[:, 0:W], tmp[:, 0:W])
        for kk in range(8):
            tb = tmp if kk % 2 == 0 else sq[:, 0:4, :].bitcast(F32).rearrange("p a w -> p (a w)") if False else (tmp if kk % 2 == 0 else tmp2)
            k.stt(tb[:, 0:W], x[:, kk, :], gm1[:, kk, j:j + 1], rstd[:, 0:W], ALU.mult, ALU.mult)
            k.act(hT[:, kk, h0:h0 + W], tb[:, 0:W], AF.Identity, bias=modv[:, kk, j:j + 1], scale=1.0)


def rope(k, out_bf, x, cos_t, sin_t, tA, tB):
    xv = x.rearrange("p (a h f) -> p a h f", a=2, h=2)
    ov = out_bf.rearrange("p (a h f) -> p a h f", a=2, h=2)
    Av = tA.rearrange("p (a h f) -> p a h f", a=2, h=2)
    Bv = tB.rearrange("p (a h f) -> p a h f", a=2, h=2)
    cb = cos_t.rearrange("p (a f) -> p a f", a=2).unsqueeze(2).to_broadcast([128, 2, 2, 32])
    sv = sin_t.rearrange("p (a f) -> p a f", a=2)
    k.tt(Av, xv, cb, ALU.mult)
    k.tt(Bv[:, :, 0, :], xv[:, :, 1, :], sv, ALU.mult)
    k.tt(Bv[:, :, 1, :], xv[:, :, 0, :], sv, ALU.mult)
    k.tt(ov[:, :, 0, :], Av[:, :, 0, :], Bv[:, :, 0, :], ALU.subtract)
    k.tt(ov[:, :, 1, :], Av[:, :, 1, :], Bv[:, :, 1, :], ALU.add)


def make_consts():
    j = np.arange(128)[:, None].astype(np.float32); i = np.arange(128)[None, :].astype(np.float32)
    c = {}
    c["mask_f"] = (i >= j).astype(np.float32)
    c["mask_b"] = (j >= i).astype(np.float32)
    c["dm_f"] = np.maximum(i - j, 0.0); c["dm_b"] = np.maximum(j - i, 0.0)
    c["row_f"] = np.broadcast_to(i + 1.0, (128, 128)).copy()
    c["row_b"] = np.broadcast_to(128.0 - i, (128, 128)).copy()
    pc = np.zeros((128, 4), np.float32)
    pc[:, 0] = 127.0 - np.arange(128)
    pc[:, 1] = np.arange(128)
    pc[:, 2] = 128.0
    c["pcols"] = pc
    return {kk: np.ascontiguousarray(v.astype(np.float32)) for kk, v in c.items()}


def rope_tables():
    pos = np.arange(4096)
    row = (pos // 64).astype(np.float32); col = (pos % 64).astype(np.float32)
    inv = (10000.0 ** (-np.arange(0, 64, 2, dtype=np.float32) / 64.0)).astype(np.float32)
    ang = np.concatenate([row[:, None] * inv, col[:, None] * inv], axis=-1).astype(np.float32)
    cos = np.cos(ang).astype(np.float32); sin = np.sin(ang).astype(np.float32)
    cs = np.ascontiguousarray(cos.reshape(32, 128, 64).transpose(1, 0, 2))
    sn = np.ascontiguousarray(sin.reshape(32, 128, 64).transpose(1, 0, 2))
    return cs, sn


def build_M1():
    k = KB()
    DK = 128; DV = 256; NH = 4
    d_xT = k.dram("xT", [D, 4096]); d_xcT = k.dram("xcT", [D, 256])
    d_cvec = k.dram("cvec", [128, 8, 2]); d_wmod = k.dram("wmod", [D, 2048]); d_bmod = k.dram("bmod", [128, 16])
    d_n1g = k.dram("n1g", [128, 8])
    d_win = k.dram("win", [NH, D, 768])
    d_dec = k.dram("dec", [128, 8])
    d_ng = k.dram("ng", [128, NH, DV])
    d_cos = k.dram("cos", [128, 32, 64]); d_sin = k.dram("sin", [128, 32, 64])
    cn = {nm: k.dram(nm, [128, 128]) for nm in ("mask_f", "mask_b", "dm_f", "dm_b", "row_f", "row_b")}
    d_pcols = k.dram("pcols", [128, 4]); d_ident = k.dram("ident", [128, 128])
    d_oT = k.dram("oT", [NH * DV, 4096], BF16, kind="ExternalOutput")

    P = [k.ps("P%d" % i, [128, 512]) for i in range(6)]
    PT = [k.ps("PT%d" % i, [128, 1024], BF16) for i in range(2)]
    pc = [0, 0]

    def nextP():
        p = P[pc[0] % 6]; pc[0] += 1; return p

    def nextPT():
        p = PT[pc[1] % 2]; pc[1] += 1; return p

    ones_bf = k.sb("ones_bf", [128, 128], BF16); k.memset(ones_bf[:], 1.0)
    identf = k.sb("identf", [128, 128]); ident = k.sb("ident_s", [128, 128], BF16)
    k.dma(identf[:], d_ident); k.copy(ident[:], identf[:])
    n1g = k.sb("n1g_s", [128, 8]); k.dma(n1g[:], d_n1g)
    whd = [k.sb("whd0", [128, 8, 768], BF16)]
    wmb = [whd[0][:, :, 0:512]]
    import os
    if 'nomod' in os.environ.get('M1_SKIP', ''):
        modv = k.sb("mmodv", [128, 16, 2]); k.memset(modv[:], 0.1)
    else:
        modv = emit_mod(k, nextP, d_cvec, d_wmod, d_bmod, 2048, wmb)
    hT = k.sb("hT", [128, 8, NT * 128], BF16)
    xt = [k.sb("xt0", [128, 8, 256])]
    sq = k.sb("sq", [128, 8, 256], BF16); rstd = k.sb("rstd", [128, 256]); tmp = k.sb("tmp", [128, 256])
    import os
    if 'nohT' in os.environ.get('M1_SKIP', ''):
        k.memset(hT[:, :, 0:512], 0.01)
    else:
        emit_hT(k, nextP, hT, d_xT, d_xcT, modv, n1g, ones_bf, xt, sq, rstd, tmp)

    cs = {nm: k.sb(nm + "_s", [128, 128]) for nm in cn}
    for nm in cn:
        k.dma(cs[nm][:], cn[nm])
    pcols = k.sb("pcols_s", [128, 4]); k.dma(pcols[:], d_pcols)
    cos = k.sb("cos_s", [128, 32, 64]); sin = k.sb("sin_s", [128, 32, 64])
    ng = k.sb("ng_s", [128, NH, DV])
    if 'nocs' not in os.environ.get('M1_SKIP', ''):
        k.dma(cos[:], d_cos); k.dma(sin[:], d_sin)
        k.dma(ng[:], d_ng)
    dec = k.sb("dec_s", [128, 8]); k.dma(dec[:], d_dec)
    lg = k.sb("lg", [128, 8]); e1 = k.sb("e1", [128, 8])
    k.act(e1[:], dec[:], AF.Exp, scale=-1.0)
    k.act(e1[:], e1[:], AF.Ln, bias=1.0, scale=1.0)
    k.ts(lg[:], e1[:], -1.0, ALU.mult)

    kb = k.sb("kb", [128, NT, DK], BF16); qT = k.sb("qT", [128, NT, 128], BF16); kT = k.sb("kT", [128, NT, 128], BF16)
    vb = k.sb("vb", [128, NT, DV], BF16); gs = k.sb("gs", [128, 32, DV], BF16)
    oacc = k.sb("oacc", [128, 32, DV], BF16)
    oTh = [k.sb("oTh%d" % i, [128, 2, 512], BF16) for i in range(2)]
    qf = k.sb("qf", [128, 128]); kf = k.sb("kf", [128, 128]); ksc = k.sb("ksc", [128, 128])
    tA = k.sb("tA", [128, 128]); tB = k.sb("tB", [128, 128])
    DM = [k.sb("DM%d" % i, [128, 128]) for i in range(2)]
    EBr = [k.sb("EBr%d" % i, [128, 128]) for i in range(2)]
    ERc = k.sb("ERc", [128, 2]); dcol = k.sb("dcol", [128, 2])
    S = k.sb("S", [128, DV]); Sb = k.sb("Sb", [128, DV], BF16)
    AT = k.sb("AT", [128, 128], BF16); qin = k.sb("qin", [128, 128], BF16); kk_ = k.sb("kk", [128, 128], BF16)
    st6 = k.sb("st6", [128, 6]); mv = k.sb("mv", [128, 2]); rs = k.sb("rs", [128, 1]); on = k.sb("on", [128, DV])
    obf = k.sb("obf", [128, DV], BF16)

    import os
    STOP = float(os.environ.get('M1_STOP', '99')); NTL = int(os.environ.get('M1_NT', '34'))
    gf = k.sb("gf", [128, DV])
    for hl in range(NH):
        w = whd[0]
        k.dma(w[:], d_win[hl].rearrange("(k p) c -> p k c", p=128), q="pool")
        for di, (dmn, mkn, rown) in enumerate((("dm_f", "mask_f", "row_f"), ("dm_b", "mask_b", "row_b"))):
            lgc = lg[:, di * 4 + hl:di * 4 + hl + 1]
            k.act(DM[di][:], cs[dmn][:], AF.Exp, scale=lgc)
            k.tt(DM[di][:], DM[di][:], cs[mkn][:], ALU.mult)
            k.act(EBr[di][:], cs[rown][:], AF.Exp, scale=lgc)
            k.act(ERc[:, di:di + 1], pcols[:, di:di + 1], AF.Exp, scale=lgc)
            k.act(dcol[:, di:di + 1], pcols[:, 2:3], AF.Exp, scale=lgc)
        for t in range(NT):
            pa = nextP(); pb = nextP()
            for kk in range(8):
                k.matmul(pa[:, 0:512], hT[:, kk, t * 128:(t + 1) * 128], w[:, kk, 0:512], start=(kk == 0), stop=(kk == 7))
            for kk in range(8):
                k.matmul(pb[:, 0:256], hT[:, kk, t * 128:(t + 1) * 128], w[:, kk, 512:768], start=(kk == 0), stop=(kk == 7))
            k.ts(ksc[:], pa[:, 128:256], float(DK) ** -0.5, ALU.mult)
            if t >= 2:
                rope(k, qf[:], pa[:, 0:128], cos[:, t - 2, :], sin[:, t - 2, :], tA[:], tB[:])
                rope(k, kf[:], ksc[:], cos[:, t - 2, :], sin[:, t - 2, :], tA[:], tB[:])
                ksrc = kf
                k.copy(gf[:], pa[:, 256:512])
                k.act(gs[:, t - 2, :], gf[:], AF.Silu)
            else:
                k.copy(qf[:], pa[:, 0:128])
                ksrc = ksc
            k.copy(kb[:, t, :], ksrc[:], eng="pool")
            k.copy(vb[:, t, :], pb[:, 0:256])
            pt = nextP()
            k.transpose(pt[:, 0:128], qf[:], identf[:])
            k.transpose(pt[:, 128:256], ksrc[:], identf[:])
            k.copy(qT[:, t, :], pt[:, 0:128])
            k.copy(kT[:, t, :], pt[:, 128:256])
        for di in (1, 0):
            order = [1, 0] + list(range(33, 1, -1)) if di == 1 else list(range(NT))
            k.memset(S[:], 0.0); k.memset(Sb[:], 0.0)
            for t in order:
                if t >= 2:
                    pat = nextP()
                    k.matmul(pat[:, 0:128], kT[:, t, :], qT[:, t, :])
                    k.tt(AT[:], pat[:, 0:128], DM[di][:], ALU.mult)
                    k.tt(qin[:], qT[:, t, :], EBr[di][:], ALU.mult, eng="pool")
                    po = nextP()
                    k.matmul(po[:, 0:DV], AT[:], vb[:, t, :], start=True, stop=False)
                    k.matmul(po[:, 0:DV], qin[:], Sb[:], start=False, stop=True)
                k.ts(kk_[:], kb[:, t, :], ERc[:, di:di + 1], ALU.mult, eng="pool")
                pkv = nextP()
                k.matmul(pkv[:, 0:DV], kk_[:], vb[:, t, :])
                if t >= 2:
                    if di == 1:
                        k.copy(oacc[:, t - 2, :], po[:, 0:DV])
                    else:
                        k.tt(on[:], po[:, 0:DV], oacc[:, t - 2, :], ALU.add)
                        k.op("dve", lambda e, a=st6, b=on: e.bn_stats(a[:], b[:]), [on[:]], [st6[:]])
                        k.op("dve", lambda e, a=mv, b=st6: e.bn_aggr(a[:], b[:]), [st6[:]], [mv[:]])
                        k.ts(rs[:], mv[:, 1:2], EPS, ALU.add)
                        k.act(rs[:], rs[:], AF.Sqrt)
                        k.recip(rs[:], rs[:])
                        k.ts(on[:], on[:], mv[:, 0:1], ALU.subtract, rs[:, 0:1], ALU.mult)
                        k.tt(on[:], on[:], ng[:, hl, :], ALU.mult, eng="pool")
                        k.tt(obf[:], on[:], gs[:, t - 2, :], ALU.mult, eng="pool")
                        pt = nextPT()
                        k.transpose(pt[:, 0:128], obf[:, 0:128], ident[:])
                        k.transpose(pt[:, 128:256], obf[:, 128:256], ident[:])
                        lt = t - 2
                        ob = oTh[(lt // 4) % 2]
                        k.copy(ob[:, :, (lt % 4) * 128:(lt % 4 + 1) * 128], pt[:, 0:256].rearrange("p (c t) -> p c t", c=2))
                        if lt % 4 == 3:
                            k.dma(d_oT[hl * DV:(hl + 1) * DV, (lt // 4) * 512:(lt // 4 + 1) * 512].rearrange("(c p) t -> p c t", p=128), ob[:])
                k.stt(S[:], S[:], dcol[:, di:di + 1], pkv[:, 0:DV], ALU.mult, ALU.add)
                k.copy(Sb[:], S[:], eng="act")
    nc = k.finish([d_oT])
    return nc, k


NEG = -30000.0


def na_configs():
    cfgs = []; plan = {}
    for g in range(32):
        lst = []
        for u in range(32):
            key = []
            for kh in range(2):
                for qh in range(2):
                    r = 2 * g + qh; kr = 2 * u + kh
                    r0 = min(max(r - 4, 0), 56)
                    key.append(kr - r + 7 if r0 <= kr <= r0 + 7 else None)
            key = tuple(key)
            if all(x is None for x in key):
                continue
            if key not in cfgs:
                cfgs.append(key)
            lst.append((u, cfgs.index(key)))
        plan[g] = lst
    return cfgs, plan


def build_M0():
    k = KB()
    d_xT = k.dram("xT", [D, 4096]); d_xcT = k.dram("xcT", [D, 256])
    d_cvec = k.dram("cvec", [128, 8, 2]); d_wmod = k.dram("wmod", [D, 2048]); d_bmod = k.dram("bmod", [128, 16])
    d_n1g = k.dram("n1g", [128, 8])
    d_wna = k.dram("wna", [D, 768]); d_wgl = k.dram("wgl", [D, 768]); d_wa = k.dram("wa", [D, 32])
    d_waaug = k.dram("waaug", [17, 2, 128])
    d_rpbT = k.dram("rpbT", [128, 4, 15, 64]); d_colmask = k.dram("colmask", [128, 64])
    d_gng = k.dram("gng", [128, 128])
    d_maskf = k.dram("mask_f", [128, 128]); d_maskb = k.dram("mask_b", [128, 128])
    d_ident = k.dram("ident", [128, 128])
    d_oT = k.dram("oT", [512, NT * 128], BF16, kind="ExternalOutput")

    P = [k.ps("P%d" % i, [128, 512]) for i in range(6)]
    PT = [k.ps("PT%d" % i, [128, 1024], BF16) for i in range(2)]
    pc = [0, 0]

    NRR = [6]

    def nextP():
        p = P[pc[0] % NRR[0]]; pc[0] += 1; return p

    def nextPT():
        p = PT[pc[1] % 2]; pc[1] += 1; return p

    ones_bf = k.sb("ones_bf", [128, 128], BF16); k.memset(ones_bf[:], 1.0)
    identf = k.sb("identf", [128, 128]); ident = k.sb("ident_s", [128, 128], BF16)
    k.dma(identf[:], d_ident); k.copy(ident[:], identf[:])
    n1g = k.sb("n1g_s", [128, 8]); k.dma(n1g[:], d_n1g)
    hT = k.sb("hT", [128, 8, NT * 128], BF16)
    oTs = [k.sb("oTs%d" % i, [128, 2, 512], BF16) for i in range(2)]
    k.open_scope()
    wmb = [k.sb("wmb0", [128, 8, 512], BF16)]
    modv = emit_mod(k, nextP, d_cvec, d_wmod, d_bmod, 2048, wmb)
    xt = [k.sb("xt0", [128, 8, 256]), k.sb("xt1", [128, 8, 256])]
    sq = k.sb("sq", [128, 8, 256], BF16); rstd = k.sb("rstd", [128, 256]); tmp = k.sb("tmp", [128, 256])
    emit_hT(k, nextP, hT, d_xT, d_xcT, modv, n1g, ones_bf, xt, sq, rstd, tmp)
    k.close_scope()

    cfgs, plan = na_configs()
    k.open_scope()
    wna = k.sb("wna_s", [128, 8, 768], BF16)
    k.dma(wna[:], d_wna.rearrange("(k p) c -> p k c", p=128), q="pool")
    QT = k.sb("QT", [64, 4, NT * 128], BF16); KT = k.sb("KT", [64, 4, NT * 128], BF16)
    Vaug = k.sb("Vaug", [128, NT, 4, 65], BF16)
    k.memset(Vaug[:], 1.0)
    BT = k.sb("BT", [128, len(cfgs), 4, 128])
    k.open_scope()
    Btab = k.sb("Btab", [128, 4, 15, 64]); cmask = k.sb("cmask", [128, 64])
    k.dma(Btab[:], d_rpbT); k.dma(cmask[:], d_colmask)
    for h in range(4):
        k.tt(Btab[:, h, :, :], Btab[:, h, :, :], cmask[:].unsqueeze(1).to_broadcast([128, 15, 64]), ALU.add)
    for ci, key in enumerate(cfgs):
        bi = 0
        for kh in range(2):
            for qh in range(2):
                roff = key[bi]; bi += 1
                dst = BT[kh * 64:(kh + 1) * 64, ci, :, qh * 64:(qh + 1) * 64]
                if roff is None:
                    k.memset(dst, NEG, eng="pool")
                else:
                    k.copy(dst, Btab[kh * 64:(kh + 1) * 64, :, roff, :], eng="pool")
    k.close_scope()
    qs = k.sb("qs", [128, 256]); ks_ = k.sb("ks", [128, 256])
    for t in range(NT):
        pa = nextP(); pb = nextP()
        for kk in range(8):
            k.matmul(pa[:, 0:512], hT[:, kk, t * 128:(t + 1) * 128], wna[:, kk, 0:512], start=(kk == 0), stop=(kk == 7))
        for kk in range(8):
            k.matmul(pb[:, 0:256], hT[:, kk, t * 128:(t + 1) * 128], wna[:, kk, 512:768], start=(kk == 0), stop=(kk == 7))
        k.ts(qs[:], pa[:, 0:256], 0.125, ALU.mult)
        k.copy(ks_[:], pa[:, 256:512])
        k.copy(Vaug[:, t, :, 0:64], pb[:, 0:256].rearrange("p (h d) -> p h d", h=4))
        pt = nextP(); pt2 = nextP()
        for h in range(4):
            k.transpose(pt[0:64, h * 128:(h + 1) * 128], qs[:, h * 64:(h + 1) * 64], identf[:])
        for h in range(4):
            k.transpose(pt2[0:64, h * 128:(h + 1) * 128], ks_[:, h * 64:(h + 1) * 64], identf[:])
        k.copy(QT[:, :, t * 128:(t + 1) * 128], pt[0:64, 0:512].rearrange("p (c t) -> p c t", c=4))
        k.copy(KT[:, :, t * 128:(t + 1) * 128], pt2[0:64, 0:512].rearrange("p (c t) -> p c t", c=4))
    import os
    STOP = float(os.environ.get("M0_STOP", "99"))
    if STOP <= 2:
        k.close_scope(); return k.finish([d_oT]), k
    sc = [k.sb("sc%d" % i, [128, 512]) for i in range(2)]
    PTb = [k.sb("PTb%d" % i, [128, 512], BF16) for i in range(2)]
    rden = k.sb("rden", [128, 4, 1]); obf = k.sb("obf", [128, 256], BF16)
    it = [0]
    NRR[0] = 4
    for qt in range(NT if STOP > 2.5 else int(os.environ.get("M0_NQ", "1"))):
        if qt < 2:
            keys = [(0, None), (1, None)]
        else:
            keys = [(0, None), (1, None)] + [(u + 2, ci) for (u, ci) in plan[qt - 2]]
        po = P[4 + qt % 2]
        for ki, (kt, ci) in enumerate(keys):
            ps = nextP()
            for h in range(4):
                k.matmul(ps[:, h * 128:(h + 1) * 128], KT[:, h, kt * 128:(kt + 1) * 128],
                         QT[:, h, qt * 128:(qt + 1) * 128])
            s_ = sc[it[0] % 2]; p_ = PTb[it[0] % 2]; it[0] += 1
            if ci is None:
                k.copy(s_[:], ps[:, 0:512])
            else:
                k.tt(s_[:], ps[:, 0:512], BT[:, ci, :, :].rearrange("p h q -> p (h q)"), ALU.add)
            k.act(p_[:], s_[:], AF.Exp)
            for h in range(4):
                k.matmul(po[:, h * 65:(h + 1) * 65], p_[:, h * 128:(h + 1) * 128], Vaug[:, kt, h, :],
                         start=(ki == 0 and h == 0), stop=(ki == len(keys) - 1 and h == 3))
        pov = po[:, 0:260].rearrange("p (h e) -> p h e", e=65)
        k.recip(rden[:], pov[:, :, 64:65])
        k.tt(obf[:].rearrange("p (h d) -> p h d", h=4), pov[:, :, 0:64], rden[:].to_broadcast([128, 4, 64]), ALU.mult)
        ptt = nextPT()
        k.transpose(ptt[:, 0:128], obf[:, 0:128], ident[:])
        k.transpose(ptt[:, 128:256], obf[:, 128:256], ident[:])
        ob = oTs[(qt // 4) % 2]
        k.copy(ob[:, :, (qt % 4) * 128:(qt % 4 + 1) * 128], ptt[:, 0:256].rearrange("p (c t) -> p c t", c=2))
        if qt % 4 == 3 or qt == NT - 1:
            q0 = (qt // 4) * 4; n = qt - q0 + 1
            k.dma(d_oT[0:256, q0 * 128:(qt + 1) * 128].rearrange("(c p) t -> p c t", p=128), ob[:, :, 0:n * 128])
    NRR[0] = 6
    k.close_scope()
    if STOP <= 3:
        return k.finish([d_oT]), k

    k.open_scope()
    waaugf = k.sb("waaugf", [17, 2, 128]); waaug = k.sb("waaug_s", [17, 2, 128], BF16)
    k.dma(waaugf[:], d_waaug); k.copy(waaug[:], waaugf[:])
    gng = k.sb("gng_s", [128, 128]); k.dma(gng[:], d_gng)
    mk = [k.sb("mkf", [128, 128]), k.sb("mkb", [128, 128])]
    k.dma(mk[0][:], d_maskf); k.dma(mk[1][:], d_maskb)
    mks = [k.sb("mksf", [128, 128]), k.sb("mksb", [128, 128])]
    k.ts(mks[0][:], mk[0][:], -1.0, ALU.mult, 1.0, ALU.add)
    k.ts(mks[1][:], mk[1][:], -1.0, ALU.mult, 1.0, ALU.add)
    qTg = k.sb("qTg", [64, NT, 128], BF16); kTg = k.sb("kTg", [64, NT, 128], BF16)
    kbg = k.sb("kbg", [128, NT, 64], BF16); vg = k.sb("vg", [128, NT, 128], BF16)
    rsl = k.sb("rsl", [128, NT, 128], BF16); sp = k.sb("sp", [128, NT, 2, 64])
    oacc = k.sb("oaccg", [128, NT, 128], BF16)
    wgl = k.sb("wgl_s", [128, 8, 384], BF16); wa = k.sb("wa_s", [128, 8, 32], BF16)
    k.dma(wa[:], d_wa.rearrange("(k p) c -> p k c", p=128), q="pool")
    aT = k.sb("aT", [17, 2, 128], BF16); k.memset(aT[:], 1.0)
    qf = k.sb("qfg", [128, 64]); kf = k.sb("kfg", [128, 64]); rf = k.sb("rfg", [128, 128]); zf = k.sb("zfg", [128, 128])
    S = k.sb("Sg", [64, 128]); Sb = k.sb("Sbg", [64, 128], BF16)
    EBT = k.sb("EBT", [64, 128]); ENBT = k.sb("ENBT", [64, 128]); ER = k.sb("ERg", [128, 64])
    bcs = k.sb("bcs", [64, 128]); rsb = k.sb("rsb", [128, 64])
    qin = k.sb("qing", [64, 128], BF16); kin = k.sb("king", [64, 128], BF16); kkg = k.sb("kkg", [128, 64], BF16)
    AT = k.sb("ATg", [128, 128], BF16)
    on = k.sb("ong", [128, 128]); st6 = k.sb("st6g", [128, 6]); mv = k.sb("mvg", [128, 2]); ms = k.sb("msg", [128, 1])
    obg = k.sb("obg", [128, 128], BF16)
    oTg = [k.sb("oTg%d" % i, [128, 512], BF16) for i in range(2)]
    dwg = d_wgl.rearrange("(k p) c -> p k c", p=128)
    for gh in range(2):
        k.dma(wgl[:, :, 0:64], dwg[:, :, gh * 64:(gh + 1) * 64], q="pool")
        k.dma(wgl[:, :, 64:128], dwg[:, :, 128 + gh * 64:128 + (gh + 1) * 64], q="pool")
        k.dma(wgl[:, :, 128:256], dwg[:, :, 256 + gh * 128:256 + (gh + 1) * 128], q="pool")
        k.dma(wgl[:, :, 256:384], dwg[:, :, 512 + gh * 128:512 + (gh + 1) * 128], q="pool")
        for t in range(NT):
            pa = nextP(); pz = nextP()
            for kk in range(8):
                k.matmul(pa[:, 0:384], hT[:, kk, t * 128:(t + 1) * 128], wgl[:, kk, 0:384], start=(kk == 0), stop=(kk == 7))
            for di in range(2):
                for kk in range(8):
                    k.matmul(pz[0:16, di * 128:(di + 1) * 128], wa[:, kk, di * 16:(di + 1) * 16], hT[:, kk, t * 128:(t + 1) * 128],
                             start=(kk == 0), stop=(kk == 7))
            k.copy(aT[0:16, :, :], pz[0:16, 0:256].rearrange("p (a t) -> p a t", a=2))
            pz2 = nextP()
            for di in range(2):
                k.matmul(pz2[:, di * 64:(di + 1) * 64], aT[:, di, :], waaug[:, di, gh * 64:(gh + 1) * 64])
            k.copy(zf[:], pz2[:, 0:128])
            k.act(zf[:], zf[:], AF.Exp, scale=-1.0)
            k.act(sp[:, t, :, :].rearrange("p a d -> p (a d)"), zf[:], AF.Ln, bias=1.0, scale=1.0)
            k.ts(qf[:], pa[:, 0:64], 0.125, ALU.mult)
            k.copy(kf[:], pa[:, 64:128])
            k.copy(kbg[:, t, :], kf[:], eng="pool")
            k.copy(rf[:], pa[:, 128:256])
            k.act(rsl[:, t, :], rf[:], AF.Silu)
            k.copy(vg[:, t, :], pa[:, 256:384])
            pt = nextP()
            k.transpose(pt[0:64, 0:128], qf[:], identf[:])
            k.transpose(pt[0:64, 128:256], kf[:], identf[:])
            k.copy(qTg[:, t, :], pt[0:64, 0:128])
            k.copy(kTg[:, t, :], pt[0:64, 128:256])
        for di in (1, 0):
            order = ([1, 0] + list(range(33, 1, -1))) if di == 1 else list(range(NT))
            k.memset(S[:], 0.0); k.memset(Sb[:], 0.0)
            for t in order:
                spt = sp[:, t, di, :]
                pbc = nextP()
                k.matmul(pbc[0:64, 0:128], spt, mk[di][:])
                k.matmul(pbc[:, 128:192], mks[di][:], spt)
                k.copy(bcs[:], pbc[0:64, 0:128]); k.copy(rsb[:], pbc[:, 128:192])
                k.act(EBT[:], bcs[:], AF.Exp, scale=-1.0 / 16.0)
                k.act(ENBT[:], bcs[:], AF.Exp, scale=1.0 / 16.0)
                k.act(ER[:], rsb[:], AF.Exp, scale=-1.0 / 16.0)
                k.tt(qin[:], qTg[:, t, :], EBT[:], ALU.mult)
                k.tt(kin[:], kTg[:, t, :], ENBT[:], ALU.mult, eng="pool")
                k.tt(kkg[:], kbg[:, t, :], ER[:], ALU.mult, eng="pool")
                pat = nextP()
                k.matmul(pat[:, 0:128], kin[:], qin[:])
                k.tt(AT[:], pat[:, 0:128], mk[di][:], ALU.mult)
                po = nextP()
                k.matmul(po[:, 0:128], AT[:], vg[:, t, :], start=True, stop=False)
                k.matmul(po[:, 0:128], qin[:], Sb[:], start=False, stop=True)
                pkv = nextP()
                k.matmul(pkv[0:64, 0:128], kkg[:], vg[:, t, :])
                if di == 1:
                    k.copy(oacc[:, t, :], po[:, 0:128])
                else:
                    k.tt(on[:], po[:, 0:128], oacc[:, t, :], ALU.add)
                    k.op("dve", lambda e, a=st6, b=on: e.bn_stats(a[:], b[:]), [on[:]], [st6[:]])
                    k.op("dve", lambda e, a=mv, b=st6: e.bn_aggr(a[:], b[:]), [st6[:]], [mv[:]])
                    k.stt(ms[:], mv[:, 0:1], mv[:, 0:1], mv[:, 1:2], ALU.mult, ALU.add)
                    k.ts(ms[:], ms[:], EPS, ALU.add)
                    k.act(ms[:], ms[:], AF.Sqrt)
                    k.recip(ms[:], ms[:])
                    k.stt(on[:], on[:], ms[:, 0:1], gng[:], ALU.mult, ALU.mult)
                    k.tt(obg[:], on[:], rsl[:, t, :], ALU.mult, eng="pool")
                    ptt = nextPT()
                    k.transpose(ptt[:, 0:128], obg[:], ident[:])
                    ob = oTg[(t // 4) % 2]
                    k.copy(ob[:, (t % 4) * 128:(t % 4 + 1) * 128], ptt[:, 0:128])
                    if t % 4 == 3 or t == NT - 1:
                        q0 = (t // 4) * 4; n = t - q0 + 1
                        k.dma(d_oT[256 + gh * 128:256 + (gh + 1) * 128, q0 * 128:(t + 1) * 128], ob[:, 0:n * 128])
                dci = 127 if di == 0 else 0
                k.stt(S[:], S[:], EBT[:, dci:dci + 1], pkv[0:64, 0:128], ALU.mult, ALU.add)
                k.copy(Sb[:], S[:], eng="act")
    k.close_scope()
    nc = k.finish([d_oT])
    return nc, k

BF = ml_dtypes.bfloat16

def pk(v):
    return np.ascontiguousarray(v.reshape(-1, 128).T)

def seg_cols(arrT, t0, n, T):
    out = np.zeros((arrT.shape[0], n + 2), arrT.dtype)
    lo = max(t0 - 1, 0); hi = min(t0 + n + 1, T)
    out[:, lo - (t0 - 1): hi - (t0 - 1)] = arrT[:, lo:hi]
    hm = np.array([1.0 if t0 - 1 >= 0 else 0.0, 1.0 if t0 + n < T else 0.0], np.float32)
    return out, np.ascontiguousarray(np.broadcast_to(hm, (128, 2)))

def prep_F(inp, L, b, th, xT, oT, xcT=None, ocT=None, wout=None):
    m = {}
    segs = []
    for s in range(4):
        t0 = th * 2048 + s * 512
        m["xT_%d" % s], m["hm_%d" % s] = seg_cols(xT, t0, 512, 4096)
        m["oT_%d" % s], _ = seg_cols(oT, t0, 512, 4096)
        segs.append((512, 0))
    if xcT is not None:
        m["xT_4"], m["hm_4"] = seg_cols(xcT, 0, 256, 256)
        m["oT_4"], _ = seg_cols(ocT, 0, 256, 256)
        segs.append((256, 1))
    cv = np.stack([pk(inp["c"][b]), pk(inp["c_ctx"])], axis=-1)
    m["cvec"] = np.ascontiguousarray(cv.astype(np.float32))
    m["wmod"] = np.ascontiguousarray(inp["w_mod"][L][:, 2048:6144])
    m["bmod"] = pk(inp["b_mod"][L][2048:6144])
    m["n2g"] = pk(inp["norm2_g"][L]); m["fg"] = pk(inp["final_norm_g"])
    cw = inp["ffn_conv_w"][L]
    m["convw"] = np.ascontiguousarray(np.stack([pk(cw[i]) for i in range(3)], axis=1))
    m["convb"] = pk(inp["ffn_conv_b"][L])
    m["wout"] = wout
    m["wup"] = inp["ffn_w_up"][L]; m["wdn"] = inp["ffn_w_down"][L]
    return m, segs


_C = make_consts(); _COS, _SIN = rope_tables()

def prep_M_common(inp, L, b, xT, xcT):
    m = {"xT": np.ascontiguousarray(xT), "xcT": np.ascontiguousarray(xcT)}
    cv = np.stack([pk(inp["c"][b]), pk(inp["c_ctx"])], axis=-1)
    m["cvec"] = np.ascontiguousarray(cv.astype(np.float32))
    m["wmod"] = np.ascontiguousarray(inp["w_mod"][L][:, 0:2048])
    m["bmod"] = pk(inp["b_mod"][L][0:2048])
    m["n1g"] = pk(inp["norm1_g"][L])
    m["ident"] = np.eye(128, dtype=np.float32)
    return m

def prep_M1(inp, b, hh, xT, xcT):
    m = prep_M_common(inp, 1, b, xT, xcT)
    w = inp["ret_w_in"][0]
    heads = [hh * 4 + i for i in range(4)]
    m["win"] = np.ascontiguousarray(np.stack([np.concatenate([
        w[:, h * 128:(h + 1) * 128], w[:, 1024 + h * 128:1024 + (h + 1) * 128],
        w[:, 4096 + h * 256:4096 + (h + 1) * 256], w[:, 2048 + h * 256:2048 + (h + 1) * 256]], axis=1) for h in heads]))
    dec = np.concatenate([inp["ret_decay_fwd"][0][heads], inp["ret_decay_bwd"][0][heads]])
    m["dec"] = np.ascontiguousarray(np.broadcast_to(dec, (128, 8)).astype(np.float32))
    m["ng"] = np.ascontiguousarray(np.broadcast_to(inp["ret_norm_g"][0][heads], (128, 4, 256)).astype(np.float32))
    m["cos"] = _COS; m["sin"] = _SIN
    for kk in ("mask_f", "mask_b", "dm_f", "dm_b", "row_f", "row_b", "pcols"):
        m[kk] = _C[kk]
    return m

def _na_tables(rpb_heads):
    c = np.arange(64); c0 = np.clip(c - 8, 0, 48)
    kc = np.arange(64)
    allowed = (kc[:, None] >= c0[None, :]) & (kc[:, None] < c0[None, :] + 16)
    off = np.clip(kc[:, None] - c[None, :] + 15, 0, 30)
    g = rpb_heads[:, :, off]
    g = np.where(allowed[None, None], g, 0.0).astype(np.float32)
    g = np.transpose(g, (2, 0, 1, 3))
    rpbT = np.ascontiguousarray(np.concatenate([g, g], axis=0))
    cm = np.where(allowed, 0.0, -30000.0).astype(np.float32)
    return rpbT, np.ascontiguousarray(np.concatenate([cm, cm], axis=0))

def prep_M0(inp, b, hh, xT, xcT):
    m = prep_M_common(inp, 0, b, xT, xcT)
    w = inp["na_gla_w_in"][0]
    nh = [hh * 4 + i for i in range(4)]; gh = [hh * 2 + i for i in range(2)]
    m["wna"] = np.ascontiguousarray(np.concatenate(
        [w[:, h * 64:(h + 1) * 64] for h in nh] + [w[:, 512 + h * 64:512 + (h + 1) * 64] for h in nh] +
        [w[:, 1024 + h * 64:1024 + (h + 1) * 64] for h in nh], axis=1))
    m["wgl"] = np.ascontiguousarray(np.concatenate(
        [w[:, 1536 + h * 64:1536 + (h + 1) * 64] for h in gh] + [w[:, 1792 + h * 64:1792 + (h + 1) * 64] for h in gh] +
        [w[:, 2560 + h * 128:2560 + (h + 1) * 128] for h in gh] + [w[:, 2048 + h * 128:2048 + (h + 1) * 128] for h in gh], axis=1))
    m["wa"] = np.ascontiguousarray(w[:, 3072:3104])
    gc = slice(hh * 128, (hh + 1) * 128)
    wa = np.zeros((17, 2, 128), np.float32)
    wa[0:16, 0] = inp["gla_w_a_fwd"][0][:, gc]; wa[16, 0] = inp["gla_b_a_fwd"][0][gc]
    wa[0:16, 1] = inp["gla_w_a_bwd"][0][:, gc]; wa[16, 1] = inp["gla_b_a_bwd"][0][gc]
    m["waaug"] = wa
    m["rpbT"], m["colmask"] = _na_tables(inp["na_rpb"][0][nh])
    m["gng"] = np.ascontiguousarray(np.broadcast_to(inp["gla_norm_g"][0], (128, 128)).astype(np.float32))
    m["mask_f"] = _C["mask_f"]; m["mask_b"] = _C["mask_b"]
    return m


class Env:
    pass


def make_env(k):
    e = Env(); e.k = k
    e.P = [k.ps("P%d" % i, [128, 512]) for i in range(6)]
    e.PT = [k.ps("PT%d" % i, [128, 1024], BF16) for i in range(2)]
    e.pc = [0, 0]; e.NRR = [6]

    def nextP():
        p = e.P[e.pc[0] % e.NRR[0]]; e.pc[0] += 1; return p

    def nextPT():
        p = e.PT[e.pc[1] % 2]; e.pc[1] += 1; return p
    e.nextP = nextP; e.nextPT = nextPT
    e.ones_bf = k.sb("ones_bf", [128, 128], BF16); k.memset(e.ones_bf[:], 1.0)
    e.identf = k.sb("identf", [128, 128]); e.ident = k.sb("ident_s", [128, 128], BF16)
    d_ident = k.dram("ident", [128, 128])
    k.dma(e.identf[:], d_ident); k.copy(e.ident[:], e.identf[:])
    return e


def phase_hT(e, pfx, d_xT, d_xcT, d_cvec, d_wmod, d_bmod, d_n1g, hT):
    k = e.k
    k.open_scope()
    n1g = k.sb(pfx + "n1g_s", [128, 8]); k.dma(n1g[:], d_n1g)
    wmb = [k.sb(pfx + "wmb0", [128, 8, 512], BF16)]
    modv = emit_mod(k, e.nextP, d_cvec, d_wmod, d_bmod, 2048, wmb, name=pfx + "m")
    xt = [k.sb(pfx + "xt0", [128, 8, 256]), k.sb(pfx + "xt1", [128, 8, 256])]
    sq = k.sb(pfx + "sq", [128, 8, 256], BF16); rstd = k.sb(pfx + "rstd", [128, 256]); tmp = k.sb(pfx + "tmp", [128, 256])
    emit_hT(k, e.nextP, hT, d_xT, d_xcT, modv, n1g, e.ones_bf, xt, sq, rstd, tmp, name=pfx)
    k.close_scope()


def phase_NA(e, pfx, hT, d_wna, d_rpbT, d_colmask, d_oT, row0):
    k = e.k; nextP = e.nextP; nextPT = e.nextPT; identf = e.identf; ident = e.ident; P = e.P
    cfgs, plan = na_configs()
    k.open_scope()
    oTs = [k.sb(pfx + "oTs%d" % i, [128, 2, 512], BF16) for i in range(2)]
    wna = k.sb(pfx + "wna_s", [128, 8, 768], BF16)
    k.dma(wna[:], d_wna.rearrange("(k p) c -> p k c", p=128), q="pool")
    QT = k.sb(pfx + "QT", [64, 4, NT * 128], BF16); KT = k.sb(pfx + "KT", [64, 4, NT * 128], BF16)
    Vaug = k.sb(pfx + "Vaug", [128, NT, 4, 65], BF16)
    k.memset(Vaug[:], 1.0)
    BT = k.sb(pfx + "BT", [128, len(cfgs), 4, 128])
    k.open_scope()
    Btab = k.sb(pfx + "Btab", [128, 4, 15, 64]); cmask = k.sb(pfx + "cmask", [128, 64])
    k.dma(Btab[:], d_rpbT); k.dma(cmask[:], d_colmask)
    for h in range(4):
        k.tt(Btab[:, h, :, :], Btab[:, h, :, :], cmask[:].unsqueeze(1).to_broadcast([128, 15, 64]), ALU.add)
    for ci, key in enumerate(cfgs):
        bi = 0
        for kh in range(2):
            for qh in range(2):
                roff = key[bi]; bi += 1
                dst = BT[kh * 64:(kh + 1) * 64, ci, :, qh * 64:(qh + 1) * 64]
                if roff is None:
                    k.memset(dst, NEG, eng="pool")
                else:
                    k.copy(dst, Btab[kh * 64:(kh + 1) * 64, :, roff, :], eng="pool")
    k.close_scope()
    qs = k.sb(pfx + "qs", [128, 256]); ks_ = k.sb(pfx + "ks", [128, 256])
    for t in range(NT):
        pa = nextP(); pb = nextP()
        for kk in range(8):
            k.matmul(pa[:, 0:512], hT[:, kk, t * 128:(t + 1) * 128], wna[:, kk, 0:512], start=(kk == 0), stop=(kk == 7))
        for kk in range(8):
            k.matmul(pb[:, 0:256], hT[:, kk, t * 128:(t + 1) * 128], wna[:, kk, 512:768], start=(kk == 0), stop=(kk == 7))
        k.ts(qs[:], pa[:, 0:256], 0.125, ALU.mult)
        k.copy(ks_[:], pa[:, 256:512])
        k.copy(Vaug[:, t, :, 0:64], pb[:, 0:256].rearrange("p (h d) -> p h d", h=4))
        pt = nextP(); pt2 = nextP()
        for h in range(4):
            k.transpose(pt[0:64, h * 128:(h + 1) * 128], qs[:, h * 64:(h + 1) * 64], identf[:])
        for h in range(4):
            k.transpose(pt2[0:64, h * 128:(h + 1) * 128], ks_[:, h * 64:(h + 1) * 64], identf[:])
        k.copy(QT[:, :, t * 128:(t + 1) * 128], pt[0:64, 0:512].rearrange("p (c t) -> p c t", c=4))
        k.copy(KT[:, :, t * 128:(t + 1) * 128], pt2[0:64, 0:512].rearrange("p (c t) -> p c t", c=4))
    sc = [k.sb(pfx + "sc%d" % i, [128, 512]) for i in range(2)]
    PTb = [k.sb(pfx + "PTb%d" % i, [128, 512], BF16) for i in range(2)]
    rden = k.sb(pfx + "rden", [128, 4, 1]); obf = k.sb(pfx + "obf", [128, 256], BF16)
    it = [0]
    e.NRR[0] = 4
    obfL = [obf, k.sb(pfx + "obf2", [128, 256], BF16)]
    items = []
    for qt in range(NT):
        if qt < 2:
            keys = [(0, None), (1, None)]
        else:
            keys = [(0, None), (1, None)] + [(u + 2, ci) for (u, ci) in plan[qt - 2]]
        for ki, (kt, ci) in enumerate(keys):
            items.append((qt, ki, kt, ci, len(keys)))

    def stage1(qt, ki, kt, ci, nk):
        ps = nextP()
        for h in range(4):
            k.matmul(ps[:, h * 128:(h + 1) * 128], KT[:, h, kt * 128:(kt + 1) * 128], QT[:, h, qt * 128:(qt + 1) * 128])
        s_ = sc[it[0] % 2]; p_ = PTb[it[0] % 2]; it[0] += 1
        if ci is None:
            k.copy(s_[:], ps[:, 0:512])
        else:
            k.tt(s_[:], ps[:, 0:512], BT[:, ci, :, :].rearrange("p h q -> p (h q)"), ALU.add)
        k.act(p_[:], s_[:], AF.Exp)
        return p_

    def stage2(qt, ki, kt, ci, nk, p_):
        po = P[4 + qt % 2]
        for h in range(4):
            k.matmul(po[:, h * 65:(h + 1) * 65], p_[:, h * 128:(h + 1) * 128], Vaug[:, kt, h, :],
                     start=(ki == 0 and h == 0), stop=(ki == nk - 1 and h == 3))
        if ki == nk - 1:
            ob_ = obfL[qt % 2]
            pov = po[:, 0:260].rearrange("p (h e) -> p h e", e=65)
            k.recip(rden[:], pov[:, :, 64:65])
            k.tt(ob_[:].rearrange("p (h d) -> p h d", h=4), pov[:, :, 0:64], rden[:].to_broadcast([128, 4, 64]), ALU.mult)
            return (qt, ob_)
        return None

    def stage3(qt, ob_):
        ptt = nextPT()
        k.transpose(ptt[:, 0:128], ob_[:, 0:128], ident[:])
        k.transpose(ptt[:, 128:256], ob_[:, 128:256], ident[:])
        ob = oTs[(qt // 4) % 2]
        k.copy(ob[:, :, (qt % 4) * 128:(qt % 4 + 1) * 128], ptt[:, 0:256].rearrange("p (c t) -> p c t", c=2))
        if qt % 4 == 3 or qt == NT - 1:
            q0 = (qt // 4) * 4; n = qt - q0 + 1
            k.dma(d_oT[row0:row0 + 256, q0 * 128:(qt + 1) * 128].rearrange("(c p) t -> p c t", p=128), ob[:, :, 0:n * 128])

    prev = None; fin = []
    for idx in range(len(items) + 1):
        cur = None
        if idx < len(items):
            cur = (items[idx], stage1(*items[idx]))
        while fin and fin[0][0] <= idx - 1:
            _, f3 = fin.pop(0); stage3(*f3)
        if prev is not None:
            r = stage2(*prev[0], prev[1])
            if r is not None:
                fin.append((idx, r))
        prev = cur
    for _, f3 in fin:
        stage3(*f3)
    e.NRR[0] = 6
    k.close_scope()


def phase_GLA(e, pfx, hT, d_wgl, d_wa, d_waaug, d_gng, d_maskf, d_maskb, d_oT, row0, ghs):
    k = e.k; nextP = e.nextP; nextPT = e.nextPT; identf = e.identf; ident = e.ident
    k.open_scope()
    waaugf = k.sb(pfx + "waaugf", [17, 2, 256]); waaug = k.sb(pfx + "waaug_s", [17, 2, 256], BF16)
    k.dma(waaugf[:], d_waaug); k.copy(waaug[:], waaugf[:])
    gng = k.sb(pfx + "gng_s", [128, 128]); k.dma(gng[:], d_gng)
    mk = [k.sb(pfx + "mkf", [128, 128]), k.sb(pfx + "mkb", [128, 128])]
    k.dma(mk[0][:], d_maskf); k.dma(mk[1][:], d_maskb)
    mks = [k.sb(pfx + "mksf", [128, 128]), k.sb(pfx + "mksb", [128, 128])]
    k.ts(mks[0][:], mk[0][:], -1.0, ALU.mult, 1.0, ALU.add)
    k.ts(mks[1][:], mk[1][:], -1.0, ALU.mult, 1.0, ALU.add)
    qTg = k.sb(pfx + "qTg", [64, NT, 128], BF16); kTg = k.sb(pfx + "kTg", [64, NT, 128], BF16)
    kbg = k.sb(pfx + "kbg", [128, NT, 64], BF16); vg = k.sb(pfx + "vg", [128, NT, 128], BF16)
    rsl = k.sb(pfx + "rsl", [128, NT, 128], BF16); sp = k.sb(pfx + "sp", [128, NT, 2, 64])
    oacc = k.sb(pfx + "oaccg", [128, NT, 128], BF16)
    wgl = k.sb(pfx + "wgl_s", [128, 8, 384], BF16); wa = k.sb(pfx + "wa_s", [128, 8, 32], BF16)
    k.dma(wa[:], d_wa.rearrange("(k p) c -> p k c", p=128), q="pool")
    aT = k.sb(pfx + "aT", [17, 2, 128], BF16); k.memset(aT[:], 1.0)
    qfL = [k.sb(pfx + "qfg%d" % i, [128, 64]) for i in range(2)]; kfL = [k.sb(pfx + "kfg%d" % i, [128, 64]) for i in range(2)]; rf = k.sb(pfx + "rfg", [128, 128]); zf = k.sb(pfx + "zfg", [128, 128])
    R2 = range(2)
    S2 = [k.sb(pfx + "Sg%d" % i, [64, 128]) for i in R2]; Sb2 = [k.sb(pfx + "Sbg%d" % i, [64, 128], BF16) for i in R2]
    EBT2 = [k.sb(pfx + "EBT%d" % i, [64, 128]) for i in R2]; ENBT2 = [k.sb(pfx + "ENBT%d" % i, [64, 128]) for i in R2]
    ER2 = [k.sb(pfx + "ERg%d" % i, [128, 64]) for i in R2]
    bcs2 = [k.sb(pfx + "bcs%d" % i, [64, 128]) for i in R2]; rsb2 = [k.sb(pfx + "rsb%d" % i, [128, 64]) for i in R2]
    qin2 = [k.sb(pfx + "qing%d" % i, [64, 128], BF16) for i in R2]; kin2 = [k.sb(pfx + "king%d" % i, [64, 128], BF16) for i in R2]
    kkg2 = [k.sb(pfx + "kkg%d" % i, [128, 64], BF16) for i in R2]
    AT2 = [k.sb(pfx + "ATg%d" % i, [128, 128], BF16) for i in R2]
    oT4 = [k.sb(pfx + "oT4g%d" % i, [128, 128], BF16) for i in range(4)]; oc = [0]
    Q2 = range(2)
    onL = [k.sb(pfx + "ong%d" % i, [128, 128]) for i in Q2]; st6L = [k.sb(pfx + "st6g%d" % i, [128, 6]) for i in Q2]
    mvL = [k.sb(pfx + "mvg%d" % i, [128, 2]) for i in Q2]; msL = [k.sb(pfx + "msg%d" % i, [128, 1]) for i in Q2]
    obgL = [k.sb(pfx + "obg%d" % i, [128, 128], BF16) for i in Q2]
    pend = [None]; pendB = [None]; rc = [0]
    dcl2 = [k.sb(pfx + "dcl%d" % i, [64, 1]) for i in range(2)]
    dwg = d_wgl.rearrange("(k p) c -> p k c", p=128)
    for (gl, gglob, rofs) in ghs:
        k.dma(wgl[:, :, 0:64], dwg[:, :, gl * 64:(gl + 1) * 64], q="pool")
        k.dma(wgl[:, :, 64:128], dwg[:, :, 128 + gl * 64:128 + (gl + 1) * 64], q="pool")
        k.dma(wgl[:, :, 128:256], dwg[:, :, 256 + gl * 128:256 + (gl + 1) * 128], q="pool")
        k.dma(wgl[:, :, 256:384], dwg[:, :, 512 + gl * 128:512 + (gl + 1) * 128], q="pool")
        trq = []
        for t in range(NT):
            qf = qfL[t % 2]; kf = kfL[t % 2]
            pa = nextP(); pz = nextP()
            for kk in range(8):
                k.matmul(pa[:, 0:384], hT[:, kk, t * 128:(t + 1) * 128], wgl[:, kk, 0:384], start=(kk == 0), stop=(kk == 7))
            while trq:
                trq.pop(0)()
            for di in range(2):
                for kk in range(8):
                    k.matmul(pz[0:16, di * 128:(di + 1) * 128], wa[:, kk, di * 16:(di + 1) * 16], hT[:, kk, t * 128:(t + 1) * 128],
                             start=(kk == 0), stop=(kk == 7))
            k.copy(aT[0:16, :, :], pz[0:16, 0:256].rearrange("p (a t) -> p a t", a=2))
            pz2 = nextP()
            for di in range(2):
                k.matmul(pz2[:, di * 64:(di + 1) * 64], aT[:, di, :], waaug[:, di, gglob * 64:(gglob + 1) * 64])
            k.copy(zf[:], pz2[:, 0:128])
            k.act(zf[:], zf[:], AF.Exp, scale=-1.0)
            k.act(sp[:, t, :, :].rearrange("p a d -> p (a d)"), zf[:], AF.Ln, bias=1.0, scale=1.0)
            k.ts(qf[:], pa[:, 0:64], 0.125, ALU.mult)
            k.copy(kf[:], pa[:, 64:128])
            k.copy(kbg[:, t, :], kf[:], eng="pool")
            k.copy(rsl[:, t, :], pa[:, 128:256])
            k.copy(vg[:, t, :], pa[:, 256:384])
            def trf(qf=qf, kf=kf, t=t):
                pt = nextP()
                k.transpose(pt[0:64, 0:128], qf[:], identf[:])
                k.transpose(pt[0:64, 128:256], kf[:], identf[:])
                k.copy(qTg[:, t, :], pt[0:64, 0:128])
                k.copy(kTg[:, t, :], pt[0:64, 128:256])
            trq.append(trf)
        while trq:
            trq.pop(0)()
        for t in range(NT):
            k.act(rsl[:, t, :], rsl[:, t, :], AF.Silu)
        ordB = [1, 0] + list(range(33, 1, -1)); ordF = list(range(NT))
        for di in range(2):
            k.memset(S2[di][:], 0.0); k.memset(Sb2[di][:], 0.0)
        have = set()
        steps = [(di, t) for i in range(NT) for (di, t) in ((1, ordB[i]), (0, ordF[i]))]

        def stage1(di, t):
            qin = qin2[di]; kin = kin2[di]; kkg = kkg2[di]
            EBT = EBT2[di]; ENBT = ENBT2[di]; ER = ER2[di]; bcs = bcs2[di]; rsb = rsb2[di]
            spt = sp[:, t, di, :]
            pbc = nextP()
            k.matmul(pbc[0:64, 0:128], spt, mk[di][:])
            k.matmul(pbc[:, 128:192], mks[di][:], spt)
            k.copy(bcs[:], pbc[0:64, 0:128]); k.copy(rsb[:], pbc[:, 128:192])
            k.act(EBT[:], bcs[:], AF.Exp, scale=-1.0 / 16.0)
            k.act(ENBT[:], bcs[:], AF.Exp, scale=1.0 / 16.0)
            k.act(ER[:], rsb[:], AF.Exp, scale=-1.0 / 16.0)
            k.tt(qin[:], qTg[:, t, :], EBT[:], ALU.mult)
            k.tt(kin[:], kTg[:, t, :], ENBT[:], ALU.mult)
            k.tt(kkg[:], kbg[:, t, :], ER[:], ALU.mult)
            k.copy(dcl2[di][:], EBT[:, (127 if di == 0 else 0):(128 if di == 0 else 1)], eng="pool")

        stage1(*steps[0])
        for si_ in range(len(steps)):
            if True:
                di, t = steps[si_]
                S = S2[di]; Sb = Sb2[di]; AT = AT2[di]; qin = qin2[di]; kin = kin2[di]; kkg = kkg2[di]
                EBT = EBT2[di]
                if si_ + 1 < len(steps) and steps[si_ + 1][0] != di:
                    stage1(*steps[si_ + 1])
                pat = nextP()
                k.matmul(pat[:, 0:128], kin[:], qin[:])
                k.tt(AT[:], pat[:, 0:128], mk[di][:], ALU.mult)
                po = nextP()
                k.matmul(po[:, 0:128], AT[:], vg[:, t, :], start=True, stop=False)
                k.matmul(po[:, 0:128], qin[:], Sb[:], start=False, stop=True)
                pkv = nextP()
                k.matmul(pkv[0:64, 0:128], kkg[:], vg[:, t, :])
                k.stt(S[:], S[:], dcl2[di][:, 0:1], pkv[0:64, 0:128], ALU.mult, ALU.add)
                k.copy(Sb[:], S[:], eng="act")
                if si_ + 1 < len(steps) and steps[si_ + 1][0] == di:
                    stage1(*steps[si_ + 1])
                if pendB[0] is not None:
                    pendB[0](); pendB[0] = None
                if pend[0] is not None:
                    pendB[0] = pend[0](); pend[0] = None
                if t not in have:
                    k.copy(oacc[:, t, :], po[:, 0:128]); have.add(t)
                else:
                    par = rc[0] % 2; rc[0] += 1
                    onb = onL[par]
                    k.tt(onb[:], po[:, 0:128], oacc[:, t, :], ALU.add)

                    def readout(onb=onb, par=par, t=t, rofs=rofs):
                        st6_ = st6L[par]; mv_ = mvL[par]; ms_ = msL[par]; obg_ = obgL[par]
                        k.op("dve", lambda e_, a=st6_, b_=onb: e_.bn_stats(a[:], b_[:]), [onb[:]], [st6_[:]])
                        k.op("dve", lambda e_, a=mv_, b_=st6_: e_.bn_aggr(a[:], b_[:]), [st6_[:]], [mv_[:]])
                        k.stt(ms_[:], mv_[:, 0:1], mv_[:, 0:1], mv_[:, 1:2], ALU.mult, ALU.add)
                        k.ts(ms_[:], ms_[:], EPS, ALU.add)
                        k.act(ms_[:], ms_[:], AF.Ln)
                        k.act(ms_[:], ms_[:], AF.Exp, scale=-0.5)
                        k.stt(onb[:], onb[:], ms_[:, 0:1], gng[:], ALU.mult, ALU.mult)
                        k.tt(obg_[:], onb[:], rsl[:, t, :], ALU.mult, eng="pool")
                        return lambda: partB(obg_, t, rofs)

                    def partB(obg_, t, rofs):
                        ptt = nextPT()
                        k.transpose(ptt[:, 0:128], obg_[:], ident[:])
                        ob = oT4[oc[0] % 4]; oc[0] += 1
                        k.copy(ob[:], ptt[:, 0:128])
                        k.dma(d_oT[row0 + rofs:row0 + rofs + 128, t * 128:(t + 1) * 128], ob[:])
                    pend[0] = readout
        if pendB[0] is not None:
            pendB[0](); pendB[0] = None
        if pend[0] is not None:
            pend[0]()(); pend[0] = None
    k.close_scope()


def phase_RET(e, pfx, hT, d_win, d_dec, d_ng, d_cos, d_sin, cn, d_pcols, d_oT, NH):
    k = e.k; nextP = e.nextP; nextPT = e.nextPT; identf = e.identf; ident = e.ident
    DK = 128; DV = 256
    k.open_scope()
    cs = {nm: k.sb(pfx + nm + "_s", [128, 128]) for nm in cn}
    for nm in cn:
        k.dma(cs[nm][:], cn[nm])
    pcols = k.sb(pfx + "pcols_s", [128, 4]); k.dma(pcols[:], d_pcols)
    cos = k.sb(pfx + "cos_s", [128, 32, 128]); sin = k.sb(pfx + "sin_s", [128, 32, 128])
    k.dma(cos[:], d_cos); k.dma(sin[:], d_sin)
    ng = k.sb(pfx + "ng_s", [128, 2, DV])
    dec = k.sb(pfx + "dec_s", [128, 2 * NH]); k.dma(dec[:], d_dec)
    lg = k.sb(pfx + "lg", [128, 2 * NH]); e1 = k.sb(pfx + "e1", [128, 2 * NH])
    k.act(e1[:], dec[:], AF.Exp, scale=-1.0)
    k.act(e1[:], e1[:], AF.Ln, bias=1.0, scale=1.0)
    k.ts(lg[:], e1[:], -1.0, ALU.mult)
    w = k.sb(pfx + "whd0", [128, 8, 768], BF16)
    kb = k.sb(pfx + "kb", [128, NT, DK], BF16); qT = k.sb(pfx + "qT", [128, NT, 128], BF16); kT = k.sb(pfx + "kT", [128, NT, 128], BF16)
    vb = k.sb(pfx + "vb", [128, NT, DV], BF16); gs = k.sb(pfx + "gs", [128, 32, DV], BF16)
    oacc = k.sb(pfx + "oacc", [128, 32, DV], BF16)
    qkL = [k.sb(pfx + "qk%d" % i, [128, 256]) for i in range(2)]
    tAL = [k.sb(pfx + "tA%d" % i, [128, 256]) for i in range(1)] * 2; tBL = [k.sb(pfx + "tB%d" % i, [128, 256]) for i in range(1)] * 2
    DM = [k.sb(pfx + "DM%d" % i, [128, 128]) for i in range(2)]
    EBr = [k.sb(pfx + "EBr%d" % i, [128, 128]) for i in range(2)]
    ERc = k.sb(pfx + "ERc", [128, 2]); dcol = k.sb(pfx + "dcol", [128, 2])
    S2 = [k.sb(pfx + "S%d" % i, [128, DV]) for i in range(2)]; Sb2 = [k.sb(pfx + "Sb%d" % i, [128, DV], BF16) for i in range(2)]
    AT2 = [k.sb(pfx + "AT%d" % i, [128, 128], BF16) for i in range(2)]; qin2 = [k.sb(pfx + "qin%d" % i, [128, 128], BF16) for i in range(2)]
    kk2 = [k.sb(pfx + "kk%d" % i, [128, 128], BF16) for i in range(2)]
    oT4 = [k.sb(pfx + "oT4%d" % i, [128, 2, 128], BF16) for i in range(2)] * 2; oc = [0]
    R2_ = range(2)
    st6L = [k.sb(pfx + "st6%d" % i, [128, 6]) for i in R2_]; mvL = [k.sb(pfx + "mv%d" % i, [128, 2]) for i in R2_]
    rsL = [k.sb(pfx + "rs%d" % i, [128, 1]) for i in R2_]; nbL = [k.sb(pfx + "nb%d" % i, [128, 1]) for i in R2_]
    onL = [k.sb(pfx + "on%d" % i, [128, DV]) for i in R2_]; obfL = [k.sb(pfx + "obf%d" % i, [128, DV], BF16) for i in R2_]
    pend = [None]; pendB = [None]; rc = [0]
    gfL = tBL
    for hl in range(NH):
        k.dma(w[:], d_win[hl].rearrange("(k p) c -> p k c", p=128), q="pool")
        k.dma(ng[:, hl % 2, :], d_ng[:, hl, :])
        for di, (dmn, mkn, rown) in enumerate((("dm_f", "mask_f", "row_f"), ("dm_b", "mask_b", "row_b"))):
            lgc = lg[:, di * NH + hl:di * NH + hl + 1]
            k.act(DM[di][:], cs[dmn][:], AF.Exp, scale=lgc)
            k.tt(DM[di][:], DM[di][:], cs[mkn][:], ALU.mult)
            k.act(EBr[di][:], cs[rown][:], AF.Exp, scale=lgc)
            k.act(ERc[:, di:di + 1], pcols[:, di:di + 1], AF.Exp, scale=lgc)
            k.act(dcol[:, di:di + 1], pcols[:, 2:3], AF.Exp, scale=lgc)
        trq = []
        for t in range(NT):
            pa = nextP(); pb = nextP()
            for kk in range(8):
                k.matmul(pa[:, 0:512], hT[:, kk, t * 128:(t + 1) * 128], w[:, kk, 0:512], start=(kk == 0), stop=(kk == 7))
            for kk in range(8):
                k.matmul(pb[:, 0:256], hT[:, kk, t * 128:(t + 1) * 128], w[:, kk, 512:768], start=(kk == 0), stop=(kk == 7))
            while trq:
                trq.pop(0)()
            qk = qkL[t % 2]; gf = gfL[t % 2]
            SC = float(DK) ** -0.5
            if t >= 2:
                xv = pa[:, 0:256].rearrange("p (g h f) -> p g h f", g=4, h=2)
                ov = qk[:].rearrange("p (g h f) -> p g h f", g=4, h=2)
                Av = tAL[t % 2][:].rearrange("p (g h f) -> p g h f", g=4, h=2)
                Bv = tBL[t % 2][:].rearrange("p (g h f) -> p g h f", g=4, h=2)
                cb_ = cos[:, t - 2, :].rearrange("p (g f) -> p g f", g=4).unsqueeze(2).to_broadcast([128, 4, 2, 32])
                sv = sin[:, t - 2, :].rearrange("p (g f) -> p g f", g=4)
                k.tt(Av, xv, cb_, ALU.mult)
                k.tt(Bv[:, :, 0, :], xv[:, :, 1, :], sv, ALU.mult)
                k.tt(Bv[:, :, 1, :], xv[:, :, 0, :], sv, ALU.mult)
                k.tt(ov[:, :, 0, :], Av[:, :, 0, :], Bv[:, :, 0, :], ALU.subtract)
                k.tt(ov[:, :, 1, :], Av[:, :, 1, :], Bv[:, :, 1, :], ALU.add)
                k.copy(gf[:], pa[:, 256:512])
                k.act(gs[:, t - 2, :], gf[:], AF.Silu)
            else:
                k.copy(qk[:], pa[:, 0:256])
            k.ts(kb[:, t, :], qk[:, 128:256], SC, ALU.mult, eng="pool")
            k.copy(vb[:, t, :], pb[:, 0:256])
            def trf(qk=qk, t=t, SC=SC):
                pt = nextP()
                k.transpose(pt[:, 0:128], qk[:, 0:128], identf[:])
                k.transpose(pt[:, 128:256], qk[:, 128:256], identf[:])
                k.copy(qT[:, t, :], pt[:, 0:128])
                k.ts(kT[:, t, :], pt[:, 128:256], SC, ALU.mult)
            trq.append(trf)
        while trq:
            trq.pop(0)()
        ordB = [1, 0] + list(range(33, 1, -1)); ordF = list(range(NT))
        for di in range(2):
            k.memset(S2[di][:], 0.0); k.memset(Sb2[di][:], 0.0)
        have = set()
        for i in range(NT):
            for di, t in ((1, ordB[i]), (0, ordF[i])):
                S = S2[di]; Sb = Sb2[di]; AT = AT2[di]; qin = qin2[di]; kk_ = kk2[di]
                if t >= 2:
                    pat = nextP()
                    k.matmul(pat[:, 0:128], kT[:, t, :], qT[:, t, :])
                    k.tt(AT[:], pat[:, 0:128], DM[di][:], ALU.mult)
                    k.tt(qin[:], qT[:, t, :], EBr[di][:], ALU.mult)
                    po = nextP()
                    k.matmul(po[:, 0:DV], AT[:], vb[:, t, :], start=True, stop=False)
                    k.matmul(po[:, 0:DV], qin[:], Sb[:], start=False, stop=True)
                k.ts(kk_[:], kb[:, t, :], ERc[:, di:di + 1], ALU.mult)
                pkv = nextP()
                k.matmul(pkv[:, 0:DV], kk_[:], vb[:, t, :])
                k.stt(S[:], S[:], dcol[:, di:di + 1], pkv[:, 0:DV], ALU.mult, ALU.add)
                k.copy(Sb[:], S[:], eng="act")
                if pendB[0] is not None:
                    pendB[0](); pendB[0] = None
                if pend[0] is not None:
                    pendB[0] = pend[0](); pend[0] = None
                if t >= 2:
                    if t not in have:
                        k.copy(oacc[:, t - 2, :], po[:, 0:DV]); have.add(t)
                    else:
                        par = rc[0] % 2; rc[0] += 1
                        onb = onL[par]
                        k.tt(onb[:], po[:, 0:DV], oacc[:, t - 2, :], ALU.add)

                        def readout(onb=onb, par=par, t=t, hl=hl):
                            st6_ = st6L[par]; mv_ = mvL[par]; rs_ = rsL[par]; nb_ = nbL[par]; obf_ = obfL[par]
                            k.op("dve", lambda e_, a=st6_, b_=onb: e_.bn_stats(a[:], b_[:]), [onb[:]], [st6_[:]])
                            k.op("dve", lambda e_, a=mv_, b_=st6_: e_.bn_aggr(a[:], b_[:]), [st6_[:]], [mv_[:]])
                            k.ts(rs_[:], mv_[:, 1:2], EPS, ALU.add)
                            k.act(rs_[:], rs_[:], AF.Sqrt)
                            k.recip(rs_[:], rs_[:])
                            k.stt(nb_[:], mv_[:, 0:1], -1.0, rs_[:], ALU.mult, ALU.mult)
                            k.act(onb[:], onb[:], AF.Identity, bias=nb_[:, 0:1], scale=rs_[:, 0:1])
                            k.tt(onb[:], onb[:], ng[:, hl % 2, :], ALU.mult, eng="pool")
                            k.tt(obf_[:], onb[:], gs[:, t - 2, :], ALU.mult, eng="pool")
                            return lambda: partB(obf_, t, hl)

                        def partB(obf_, t, hl):
                            ptt = nextPT()
                            k.transpose(ptt[:, 0:128], obf_[:, 0:128], ident[:])
                            k.transpose(ptt[:, 128:256], obf_[:, 128:256], ident[:])
                            lt = t - 2
                            ob = oT4[oc[0] % 4]; oc[0] += 1
                            k.copy(ob[:], ptt[:, 0:256].rearrange("p (c t) -> p c t", c=2))
                            k.dma(d_oT[hl * DV:(hl + 1) * DV, lt * 128:(lt + 1) * 128].rearrange("(c p) t -> p c t", p=128), ob[:])
                        pend[0] = readout
        if pendB[0] is not None:
            pendB[0](); pendB[0] = None
        if pend[0] is not None:
            pend[0]()(); pend[0] = None
    k.close_scope()


def phase_F(e, pfx, Fdim, last, segs, xsrc, osrc, ydst, d_cvec, d_wmod, d_bmod, d_n2g, d_fg, d_cw, d_cb, d_wout, d_wup, d_wdn):
    k = e.k; nextP = e.nextP; ones_bf = e.ones_bf
    KF = Fdim // 128
    NMAX = max(s[0] for s in segs); CMAX = NMAX + 2
    k.open_scope()
    cvec = k.sb(pfx + "cvec_s", [128, 8, 2]); scb = k.sb(pfx + "scb", [128, 8, 2], BF16)
    bmod = k.sb(pfx + "bmod_s", [128, 32]); modv = k.sb(pfx + "modv", [128, 32, 2])
    n2g = k.sb(pfx + "n2g_s", [128, 8]); fg = k.sb(pfx + "fg_s", [128, 8]); gm2 = k.sb(pfx + "gm2", [128, 8, 2])
    cw = k.sb(pfx + "cw_s", [128, 3, 44]); cb = k.sb(pfx + "cb_s", [128, 44])
    wout = k.sb(pfx + "wout_s", [128, KF, D], BF16)
    wdn = k.sb(pfx + "wdn_s", [128, NFF, D], BF16)
    oT = k.sb(pfx + "oT_s", [128, KF, CMAX], BF16)
    x1 = k.sb(pfx + "x1T", [128, 8, CMAX])
    sq = k.sb(pfx + "sq", [128, 8, CMAX], BF16)
    rstd = k.sb(pfx + "rstd", [128, CMAX]); tmp = k.sb(pfx + "tmp", [128, CMAX]); tmpb = k.sb(pfx + "tmpb", [128, CMAX])
    h2 = k.sb(pfx + "h2T", [128, 8, CMAX], BF16)
    wua = [k.sb(pfx + "wua%d" % i, [128, 8, 256], BF16) for i in range(2)]
    wub = [k.sb(pfx + "wub%d" % i, [128, 8, 256], BF16) for i in range(2)]
    u = [k.sb(pfx + "u%d" % i, [128, 2, CMAX]) for i in range(2)]
    vaL = [k.sb(pfx + "va%d" % i, [128, NMAX]) for i in range(2)]; vbL = [k.sb(pfx + "vb%d" % i, [128, NMAX]) for i in range(2)]
    saL = [k.sb(pfx + "sa%d" % i, [128, NMAX]) for i in range(2)]
    tT = k.sb(pfx + "tT", [128, NFF, NMAX], BF16)
    k.dma(cvec[:], d_cvec); k.dma(bmod[:], d_bmod); k.dma(n2g[:], d_n2g); k.dma(fg[:], d_fg)
    k.dma(cw[:], d_cw); k.dma(cb[:], d_cb)
    k.act(scb[:], cvec[:], AF.Silu)
    pm = nextP()
    for g in range(8):
        wb = wua[g % 2][:, :, :]
        wb2 = wub[g % 2][:, :, :]
        k.dma(wb, d_wmod.rearrange("(k p) c -> p k c", p=128)[:, :, g * 512:g * 512 + 256], q="pool")
        k.dma(wb2, d_wmod.rearrange("(k p) c -> p k c", p=128)[:, :, g * 512 + 256:(g + 1) * 512], q="pool")
        for c4 in range(4):
            cc = g * 4 + c4
            src = wb if c4 < 2 else wb2
            for kk in range(8):
                k.matmul(pm[:, cc * 2:cc * 2 + 2], src[:, kk, (c4 % 2) * 128:(c4 % 2 + 1) * 128], scb[:, kk, :],
                         start=(kk == 0), stop=(kk == 7))
    pmv = pm[:, 0:64].rearrange("p (c j) -> p c j", j=2)
    for j in range(2):
        k.tt(modv[:, :, j], pmv[:, :, j], bmod[:], ALU.add)
    for j in range(2):
        k.ts(gm2[:, :, j], modv[:, 16:24, j], 1.0, ALU.add)
        k.tt(gm2[:, :, j], gm2[:, :, j], n2g[:], ALU.mult)
    load_cast_rows(k, wout, d_wout, KF, D, split=4)
    load_cast_rows(k, wdn, d_wdn, NFF, D, split=4)
    wupv = d_wup.rearrange("(k p) c -> p k c", p=128)
    wctr = [0]
    k.dma(wua[0][:], wupv[:, :, 0:256], q="pool")
    k.dma(wub[0][:], wupv[:, :, DFF:DFF + 256], q="pool")

    for si, (n, j, kind, t0) in enumerate(segs):
        cols = n + 2
        tiles = col_tiles(cols)
        xs, T = xsrc[kind]; os_, oc0, _ = osrc[kind]
        lo = max(t0 - 1, 0); hi = min(t0 + n + 1, T)
        c_lo = lo - (t0 - 1); c_hi = hi - (t0 - 1)
        if c_lo > 0:
            k.memset(x1[:, :, 0:1], 0.0); k.memset(oT[:, :, 0:1], 0.0)
        if c_hi < cols:
            k.memset(x1[:, :, cols - 1:cols], 0.0); k.memset(oT[:, :, cols - 1:cols], 0.0)
        k.dma(x1[:, :, c_lo:c_hi], xs.rearrange("(k p) t -> p k t", p=128)[:, :, lo:hi])
        k.dma(oT[:, :, c_lo:c_hi], os_.rearrange("(k p) t -> p k t", p=128)[:, :, oc0 + lo:oc0 + hi])
        for fc in range(8):
            for (a, b) in tiles:
                p = nextP()
                for kk in range(KF):
                    k.matmul(p[:, 0:b - a], wout[:, kk, fc * 128:(fc + 1) * 128], oT[:, kk, a:b],
                             start=(kk == 0), stop=(kk == KF - 1))
                k.stt(x1[:, fc, a:b], p[:, 0:b - a], modv[:, 0 + fc, j:j + 1], x1[:, fc, a:b], ALU.mult, ALU.add)
        for kk in range(8):
            k.act(sq[:, kk, 0:cols], x1[:, kk, 0:cols], AF.Square)
        for (a, b) in tiles:
            p = nextP()
            for kk in range(8):
                k.matmul(p[:, 0:b - a], ones_bf[:], sq[:, kk, a:b], start=(kk == 0), stop=(kk == 7))
            k.ts(tmp[:, a:b], p[:, 0:b - a], 1.0 / D, ALU.mult, EPS, ALU.add)
        k.act(tmp[:, 0:cols], tmp[:, 0:cols], AF.Sqrt)
        k.recip(rstd[:, 0:cols], tmp[:, 0:cols])
        for kk in range(8):
            tb = tmp if kk % 2 == 0 else tmpb
            k.stt(tb[:, 0:cols], x1[:, kk, 0:cols], gm2[:, kk, j:j + 1], rstd[:, 0:cols], ALU.mult, ALU.mult)
            k.act(h2[:, kk, 0:cols], tb[:, 0:cols], AF.Identity, bias=modv[:, 8 + kk, j:j + 1], scale=1.0)
        if c_lo > 0:
            k.memset(h2[:, :, 0:1], 0.0)
        if c_hi < cols:
            k.memset(h2[:, :, cols - 1:cols], 0.0)
        for g in range(11):
            wa = wua[wctr[0] % 2]; wb_ = wub[wctr[0] % 2]
            wctr[0] += 1
            gn = g + 1 if g < 10 else (0 if si + 1 < len(segs) else None)
            if gn is not None:
                k.dma(wua[wctr[0] % 2][:], wupv[:, :, gn * 256:(gn + 1) * 256], q="pool")
                k.dma(wub[wctr[0] % 2][:], wupv[:, :, DFF + gn * 256:DFF + (gn + 1) * 256], q="pool")
            for c2 in range(2):
                c = g * 2 + c2
                ub = u[c % 2]
                for half, w in ((0, wa), (1, wb_)):
                    for (a, b) in tiles:
                        p = nextP()
                        for kk in range(8):
                            k.matmul(p[:, 0:b - a], w[:, kk, c2 * 128:(c2 + 1) * 128], h2[:, kk, a:b],
                                     start=(kk == 0), stop=(kk == 7))
                        k.copy(ub[:, half, a:b], p[:, 0:b - a])
                ca = c; cbi = NFF + c
                va = vaL[c % 2]; vb = vbL[c % 2]; sa = saL[c % 2]
                k.act(va[:, 0:n], ub[:, 0, 1:n + 1], AF.Identity, bias=cb[:, ca:ca + 1], scale=cw[:, 1, ca:ca + 1])
                k.stt(va[:, 0:n], ub[:, 0, 0:n], cw[:, 0, ca:ca + 1], va[:, 0:n], ALU.mult, ALU.add)
                k.stt(va[:, 0:n], ub[:, 0, 2:n + 2], cw[:, 2, ca:ca + 1], va[:, 0:n], ALU.mult, ALU.add)
                k.act(vb[:, 0:n], ub[:, 1, 1:n + 1], AF.Identity, bias=cb[:, cbi:cbi + 1], scale=cw[:, 1, cbi:cbi + 1])
                k.stt(vb[:, 0:n], ub[:, 1, 0:n], cw[:, 0, cbi:cbi + 1], vb[:, 0:n], ALU.mult, ALU.add)
                k.stt(vb[:, 0:n], ub[:, 1, 2:n + 2], cw[:, 2, cbi:cbi + 1], vb[:, 0:n], ALU.mult, ALU.add)
                k.act(sa[:, 0:n], va[:, 0:n], AF.Silu)
                k.tt(tT[:, c, 0:n], sa[:, 0:n], vb[:, 0:n], ALU.mult, eng="pool")
        for fc in range(8):
            p = nextP()
            for c in range(NFF):
                k.matmul(p[:, 0:n], wdn[:, c, fc * 128:(fc + 1) * 128], tT[:, c, 0:n], start=(c == 0), stop=(c == NFF - 1))
            k.stt(x1[:, fc, 1:n + 1], p[:, 0:n], modv[:, 24 + fc, j:j + 1], x1[:, fc, 1:n + 1], ALU.mult, ALU.add)
        if last:
            for kk in range(8):
                k.act(sq[:, kk, 0:n], x1[:, kk, 1:n + 1], AF.Square)
            p = nextP()
            for kk in range(8):
                k.matmul(p[:, 0:n], ones_bf[:], sq[:, kk, 0:n], start=(kk == 0), stop=(kk == 7))
            k.ts(tmp[:, 0:n], p[:, 0:n], 1.0 / D, ALU.mult, EPS, ALU.add)
            k.act(tmp[:, 0:n], tmp[:, 0:n], AF.Sqrt)
            k.recip(rstd[:, 0:n], tmp[:, 0:n])
            for kk in range(8):
                k.stt(x1[:, kk, 1:n + 1], x1[:, kk, 1:n + 1], fg[:, kk:kk + 1], rstd[:, 0:n], ALU.mult, ALU.mult)
        k.dma(ydst[kind].rearrange("(k p) t -> p k t", p=128)[:, :, t0:t0 + n], x1[:, :, 1:n + 1])
    k.close_scope()


def build_fused():
    k = KB(); nc = k.nc
    e = make_env(k)
    d_xT = k.dram("xT", [D, 4096]); d_xcT = k.dram("xcT", [D, 256]); d_cvec = k.dram("cvec", [128, 8, 2])
    cn = {nm: k.dram(nm, [128, 128]) for nm in ("mask_f", "mask_b", "dm_f", "dm_b", "row_f", "row_b")}
    d_pcols = k.dram("pcols", [128, 4]); d_cos = k.dram("cos", [128, 32, 128]); d_sin = k.dram("sin", [128, 32, 128])
    L = []
    for l in range(2):
        L.append(dict(wmodA=k.dram("wmodA%d" % l, [D, 2048]), bmodA=k.dram("bmodA%d" % l, [128, 16]), n1g=k.dram("n1g%d" % l, [128, 8]),
                      wmodB=k.dram("wmodB%d" % l, [D, 4096]), bmodB=k.dram("bmodB%d" % l, [128, 32]), n2g=k.dram("n2g%d" % l, [128, 8]),
                      cw=k.dram("convw%d" % l, [128, 3, 44]), cb=k.dram("convb%d" % l, [128, 44]),
                      wout=k.dram("wout%d" % l, [1024 * (l + 1), D]), wup=k.dram("wup%d" % l, [D, 2 * DFF]), wdn=k.dram("wdn%d" % l, [DFF, D])))
    d_fg = k.dram("fg", [128, 8])
    d_wna = k.dram("wna", [2, D, 768]); d_wgl = k.dram("wgl", [2, D, 768]); d_wa = k.dram("wa", [D, 32])
    d_waaug = k.dram("waaug", [17, 2, 256]); d_rpbT = k.dram("rpbT", [2, 128, 4, 15, 64]); d_colmask = k.dram("colmask", [128, 64])
    d_gng = k.dram("gng", [128, 128])
    d_win = k.dram("win", [8, D, 768]); d_dec = k.dram("dec", [128, 16]); d_ng = k.dram("ng", [128, 8, 256])
    d_y = k.dram("yT", [D, 4096], kind="ExternalOutput")
    o0T = nc.dram_tensor("o0T", [1024, NT * 128], BF16, kind="Internal").ap()
    x2T = nc.dram_tensor("x2T", [D, 4096], F32, kind="Internal").ap()
    xc2T = nc.dram_tensor("xc2T", [D, 256], F32, kind="Internal").ap()
    o1T = nc.dram_tensor("o1T", [2048, 4096], BF16, kind="Internal").ap()
    xcdump = nc.dram_tensor("xcdump", [D, 256], F32, kind="Internal").ap()

    k.open_scope()
    hT = k.sb("hT0", [128, 8, NT * 128], BF16)
    phase_hT(e, "a0", d_xT, d_xcT, d_cvec, L[0]["wmodA"], L[0]["bmodA"], L[0]["n1g"], hT)
    for hh in range(2):
        phase_NA(e, "na%d" % hh, hT, d_wna[hh], d_rpbT[hh], d_colmask, o0T, hh * 512)
        phase_GLA(e, "gl%d" % hh, hT, d_wgl[hh], d_wa, d_waaug, d_gng, cn["mask_f"], cn["mask_b"], o0T, hh * 512 + 256,
                  [(0, hh * 2, 0), (1, hh * 2 + 1, 128)])
    k.close_scope()
    segs0 = [(512, 0, "lat", s * 512) for s in range(8)] + [(256, 1, "ctx", 0)]
    phase_F(e, "f0", 1024, False, segs0,
            {"lat": (d_xT, 4096), "ctx": (d_xcT, 256)}, {"lat": (o0T, 256, 4096), "ctx": (o0T, 0, 256)},
            {"lat": x2T, "ctx": xc2T}, d_cvec, L[0]["wmodB"], L[0]["bmodB"], L[0]["n2g"], d_fg, L[0]["cw"], L[0]["cb"],
            L[0]["wout"], L[0]["wup"], L[0]["wdn"])
    k.open_scope()
    hT = k.sb("hT1", [128, 8, NT * 128], BF16)
    phase_hT(e, "a1", x2T, xc2T, d_cvec, L[1]["wmodA"], L[1]["bmodA"], L[1]["n1g"], hT)
    phase_RET(e, "rt", hT, d_win, d_dec, d_ng, d_cos, d_sin, cn, d_pcols, o1T, 8)
    k.close_scope()
    segs1 = [(512, 0, "lat", s * 512) for s in range(8)]
    phase_F(e, "f1", 2048, True, segs1,
            {"lat": (x2T, 4096)}, {"lat": (o1T, 0, 4096)}, {"lat": d_y},
            d_cvec, L[1]["wmodB"], L[1]["bmodB"], L[1]["n2g"], d_fg, L[1]["cw"], L[1]["cb"],
            L[1]["wout"], L[1]["wup"], L[1]["wdn"])
    ncc = k.finish([d_y])
    return ncc, k


_PROG = []


def _maps(inp):
    B = 4
    shared = {}
    shared["ident"] = np.eye(128, dtype=np.float32)
    for kk in ("mask_f", "mask_b", "dm_f", "dm_b", "row_f", "row_b", "pcols"):
        shared[kk] = _C[kk]
    shared["cos"] = np.ascontiguousarray(np.concatenate([_COS, _COS], axis=2)); shared["sin"] = np.ascontiguousarray(np.concatenate([_SIN, _SIN], axis=2))
    for l in range(2):
        shared["wmodA%d" % l] = np.ascontiguousarray(inp["w_mod"][l][:, 0:2048])
        shared["bmodA%d" % l] = pk(inp["b_mod"][l][0:2048])
        shared["n1g%d" % l] = pk(inp["norm1_g"][l])
        shared["wmodB%d" % l] = np.ascontiguousarray(inp["w_mod"][l][:, 2048:6144])
        shared["bmodB%d" % l] = pk(inp["b_mod"][l][2048:6144])
        shared["n2g%d" % l] = pk(inp["norm2_g"][l])
        cw = inp["ffn_conv_w"][l]
        shared["convw%d" % l] = np.ascontiguousarray(np.stack([pk(cw[i]) for i in range(3)], axis=1))
        shared["convb%d" % l] = pk(inp["ffn_conv_b"][l])
        shared["wup%d" % l] = inp["ffn_w_up"][l]; shared["wdn%d" % l] = inp["ffn_w_down"][l]
    perm = np.concatenate([np.arange(0, 256), np.arange(512, 768), np.arange(256, 512), np.arange(768, 1024)])
    shared["wout0"] = np.ascontiguousarray(inp["na_gla_w_out"][0][perm])
    shared["wout1"] = np.ascontiguousarray(inp["ret_w_out"][0])
    shared["fg"] = pk(inp["final_norm_g"])
    w = inp["na_gla_w_in"][0]
    wna = []; wgl = []; rp = []
    for hh in range(2):
        nh = [hh * 4 + i for i in range(4)]; gh = [hh * 2 + i for i in range(2)]
        wna.append(np.concatenate([w[:, h * 64:(h + 1) * 64] for h in nh] + [w[:, 512 + h * 64:512 + (h + 1) * 64] for h in nh] +
                                  [w[:, 1024 + h * 64:1024 + (h + 1) * 64] for h in nh], axis=1))
        wgl.append(np.concatenate([w[:, 1536 + h * 64:1536 + (h + 1) * 64] for h in gh] + [w[:, 1792 + h * 64:1792 + (h + 1) * 64] for h in gh] +
                                  [w[:, 2560 + h * 128:2560 + (h + 1) * 128] for h in gh] + [w[:, 2048 + h * 128:2048 + (h + 1) * 128] for h in gh], axis=1))
        r_, cm = _na_tables(inp["na_rpb"][0][nh]); rp.append(r_)
    shared["wna"] = np.ascontiguousarray(np.stack(wna)); shared["wgl"] = np.ascontiguousarray(np.stack(wgl))
    shared["rpbT"] = np.ascontiguousarray(np.stack(rp)); shared["colmask"] = cm
    shared["wa"] = np.ascontiguousarray(w[:, 3072:3104])
    wa = np.zeros((17, 2, 256), np.float32)
    wa[0:16, 0] = inp["gla_w_a_fwd"][0]; wa[16, 0] = inp["gla_b_a_fwd"][0]
    wa[0:16, 1] = inp["gla_w_a_bwd"][0]; wa[16, 1] = inp["gla_b_a_bwd"][0]
    shared["waaug"] = wa
    shared["gng"] = np.ascontiguousarray(np.broadcast_to(inp["gla_norm_g"][0], (128, 128)).astype(np.float32))
    wr = inp["ret_w_in"][0]
    shared["win"] = np.ascontiguousarray(np.stack([np.concatenate([
        wr[:, h * 128:(h + 1) * 128], wr[:, 1024 + h * 128:1024 + (h + 1) * 128],
        wr[:, 4096 + h * 256:4096 + (h + 1) * 256], wr[:, 2048 + h * 256:2048 + (h + 1) * 256]], axis=1) for h in range(8)]))
    dec = np.concatenate([inp["ret_decay_fwd"][0], inp["ret_decay_bwd"][0]])
    shared["dec"] = np.ascontiguousarray(np.broadcast_to(dec, (128, 16)).astype(np.float32))
    shared["ng"] = np.ascontiguousarray(np.broadcast_to(inp["ret_norm_g"][0], (128, 8, 256)).astype(np.float32))
    maps = []
    for core in range(8):
        b = core // 2
        m = dict(shared)
        m["xT"] = np.ascontiguousarray(inp["x"][b].T); m["xcT"] = np.ascontiguousarray(inp["ctx"][b].T)
        m["cvec"] = np.ascontiguousarray(np.stack([pk(inp["c"][b]), pk(inp["c_ctx"])], axis=-1).astype(np.float32))
        maps.append(m)
    return maps


def kernel(**inp):
    inp = {k_: np.asarray(v) for k_, v in inp.items()}
    if not _PROG:
        _PROG.append(build_fused()[0])
    res = run_bass_kernel_spmd(_PROG[0], _maps(inp), core_ids=list(range(8))).results
    out = np.empty((4, 4096, 1024), np.float32)
    for b in range(4):
        out[b, 0:2048] = res[2 * b]["yT"][:, 0:2048].T
        out[b, 2048:4096] = res[2 * b + 1]["yT"][:, 2048:4096].T
    return out
```

```python
import os
import ml_dtypes
from concourse.bass_utils import run_bass_kernel_spmd

from contextlib import ExitStack
import numpy as np
import concourse.bass as bass
import concourse.mybir as mybir

F32 = mybir.dt.float32
BF16 = mybir.dt.bfloat16
AF = mybir.ActivationFunctionType
ALU = mybir.AluOpType
AX = mybir.AxisListType

ENGS = ("pe", "act", "dve", "pool", "sp")
NDSEM = 12


def _region(ap):
    t = ap.tensor
    name = t.name
    dims = list(ap.ap)
    off = int(ap.offset)
    sp = str(ap.space) if hasattr(ap, "space") else ""
    if "DRAM" in sp.upper() or type(t).__name__.startswith("DRam"):
        ext = sum((int(c) - 1) * abs(int(s)) for s, c in dims)
        return (name, 0, 1, off, off + ext + 1)
    if type(t).__name__.startswith("PSum"):
        return (name, 0, 128, 0, 1 << 40)
    pstep, pcnt = int(dims[0][0]), int(dims[0][1])
    if pstep == 0:
        pstep = 1 << 40
    p0 = off // pstep
    f0 = off % pstep
    ext = sum((int(c) - 1) * abs(int(s)) for s, c in dims[1:])
    return (name, p0, p0 + pcnt, f0, f0 + ext + 1)


def _overlap(a, b):
    return a[1] < b[2] and b[1] < a[2] and a[3] < b[4] and b[3] < a[4]


def _covers(a, b):
    return a[1] <= b[1] and a[2] >= b[2] and a[3] <= b[3] and a[4] >= b[4]


class KB:
    def __init__(self):
        self.nc = bass.Bass("TRN2", target_bir_lowering=False)
        self.es = ExitStack()
        self.ops = []
        self.recs = {}
        self.n_alloc = 0
        self.fence = None
        self.fenced = set()
        self.stack = [self.es]

    def sb(self, name, shape, dt=F32):
        return self.stack[-1].enter_context(self.nc.sbuf_tensor(name, list(shape), dt))

    def barrier(self):
        last = {}
        f = set()
        for i, o in enumerate(self.ops):
            if o["dma"]:
                f.add(i)
            else:
                last[o["eng"]] = i
        f.update(last.values())
        if self.fence is not None:
            f = {i for i in f if i > self.fence_at or not self.ops[i]["dma"]}
        self.fence = f
        self.fence_at = len(self.ops)
        self.fenced = set()

    def open_scope(self):
        self.stack.append(ExitStack())

    def close_scope(self):
        self.barrier()
        self.stack.pop().close()

    def ps(self, name, shape, dt=F32):
        return self.es.enter_context(self.nc.psum_tensor(name, list(shape), dt))

    def dram(self, name, shape, dt=F32, kind="ExternalInput"):
        return self.nc.dram_tensor(name, list(shape), dt, kind=kind).ap()

    def op(self, eng, fn, reads, writes, dma=False):
        idx = len(self.ops)
        deps = set()
        rr = [_region(a) for a in reads if a is not None and hasattr(a, "tensor")]
        ww = [_region(a) for a in writes if a is not None and hasattr(a, "tensor")]
        for r in rr:
            for (g, oi, isw) in self.recs.get(r[0], ()):
                if isw and _overlap(r, g):
                    deps.add(oi)
        for w in ww:
            for (g, oi, isw) in self.recs.get(w[0], ()):
                if _overlap(w, g):
                    deps.add(oi)
        for w in ww:
            lst = self.recs.setdefault(w[0], [])
            lst[:] = [x for x in lst if not _covers(w, x[0])]
            lst.append((w, idx, True))
        for r in rr:
            lst = self.recs.setdefault(r[0], [])
            lst[:] = [x for x in lst if not ((not x[2]) and x[1] < idx and self.ops[x[1]]["eng"] == eng
                                             and not self.ops[x[1]]["dma"] and not dma and _covers(r, x[0]))]
            lst.append((r, idx, False))
        if self.fence is not None and eng not in self.fenced:
            deps.update(self.fence)
            self.fenced.add(eng)
        deps.discard(idx)
        self.ops.append(dict(eng=eng, fn=fn, deps=deps, dma=dma, rr=rr, ww=ww))
        return idx

    def dma(self, out, in_, q="sp"):
        return self.op(q, lambda e: e.dma_start(out=out, in_=in_), [in_], [out], dma=True)

    def matmul(self, out, lhsT, rhs, start=True, stop=True):
        return self.op("pe", lambda e: e.matmul(out, lhsT, rhs, start=start, stop=stop), [lhsT, rhs], [out])

    def transpose(self, out, in_, ident):
        return self.op("pe", lambda e: e.transpose(out, in_, ident), [in_, ident], [out])

    def act(self, out, in_, func, bias=None, scale=None, accum_out=None, eng="act"):
        kw = {}
        if bias is not None:
            kw["bias"] = bias
        if scale is not None:
            kw["scale"] = scale
        if accum_out is not None:
            kw["accum_out"] = accum_out
        return self.op(eng, lambda e: e.activation(out, in_, func, **kw), [in_, bias, scale], [out, accum_out])

    def tt(self, out, in0, in1, op, eng="dve"):
        return self.op(eng, lambda e: e.tensor_tensor(out, in0, in1, op), [in0, in1], [out])

    def ts(self, out, in0, s1, op0, s2=None, op1=None, accum_out=None, eng="dve"):
        def f(e):
            kw = {}
            if accum_out is not None:
                kw["accum_out"] = accum_out
            if op1 is None:
                return e.tensor_scalar(out, in0, s1, None, op0, **kw)
            return e.tensor_scalar(out, in0, s1, s2, op0, op1, **kw)
        return self.op(eng, f, [in0, s1, s2], [out, accum_out])

    def stt(self, out, in0, scalar, in1, op0, op1, eng="dve"):
        return self.op(eng, lambda e: e.scalar_tensor_tensor(out, in0, scalar, in1, op0, op1), [in0, scalar, in1], [out])

    def copy(self, out, in_, eng="dve"):
        if eng == "act":
            return self.op("act", lambda e: e.copy(out, in_), [in_], [out])
        return self.op(eng, lambda e: e.tensor_copy(out, in_), [in_], [out])

    def memset(self, ap, val, eng="dve"):
        return self.op(eng, lambda e: e.memset(ap, val), [], [ap])

    def recip(self, out, in_):
        return self.op("dve", lambda e: e.reciprocal(out, in_), [in_], [out])

    def reduce(self, out, in_, op=ALU.add, axis=AX.X, eng="dve"):
        return self.op(eng, lambda e: e.tensor_reduce(out, in_, axis, op), [in_], [out])

    def finish(self, out_aps):
        nc = self.nc
        ops = self.ops
        out_names = {a.tensor.name for a in out_aps}
        final_deps = set()
        for i, o in enumerate(ops):
            if o["dma"] and any(w[0] in out_names for w in o["ww"]):
                final_deps.add(i)
        ops.append(dict(eng="sp", fn=None, deps=final_deps, dma=False, rr=[], ww=[]))
        needs_sig = [False] * len(ops)
        for o in ops:
            for d in o["deps"]:
                if ops[d]["dma"]:
                    continue
                if ops[d]["eng"] == o["eng"] and not o["dma"] and o["eng"] == "pe":
                    continue
                needs_sig[d] = True
        sems = {e: self.es.enter_context(nc.semaphore("s_" + e)) for e in ENGS}
        dsems = {e: [self.es.enter_context(nc.semaphore("d_%s_%d" % (e, i))) for i in range(NDSEM)]
                 for e in ("sp", "act", "pool")}
        cnt = {e: 0 for e in ENGS}
        dcnt = {e: 0 for e in dsems}
        sig = [None] * len(ops)
        prevdma = [None] * len(ops)
        for i, o in enumerate(ops):
            if o["dma"]:
                q = o["eng"]
                n = dcnt[q]
                dcnt[q] += 1
                s = dsems[q][n % NDSEM]
                sig[i] = (s, 16 * (n // NDSEM + 1))
                if n >= NDSEM:
                    prevdma[i] = (s, 16 * (n // NDSEM))
            elif needs_sig[i]:
                cnt[o["eng"]] += 1
                sig[i] = (sems[o["eng"]], cnt[o["eng"]])
        per_eng = {e: [] for e in ENGS}
        for i, o in enumerate(ops):
            per_eng[o["eng"]].append(i)
        self.stats = {e: len(per_eng[e]) for e in ENGS}
        self.stats["sig"] = dict(cnt)

        def emit(ename):
            def body(eng):
                seen = {}
                for i in per_eng[ename]:
                    o = ops[i]
                    waits = {}
                    for d in o["deps"]:
                        po = ops[d]
                        if (not po["dma"]) and po["eng"] == ename and ename == "pe" and not o["dma"]:
                            continue
                        s, v = sig[d]
                        key = id(s)
                        if waits.get(key, (None, 0))[1] < v:
                            waits[key] = (s, v)
                    if prevdma[i] is not None:
                        s, v = prevdma[i]
                        key = id(s)
                        if waits.get(key, (None, 0))[1] < v:
                            waits[key] = (s, v)
                    for key, (s, v) in waits.items():
                        if seen.get(key, 0) >= v:
                            continue
                        eng.wait_ge(s, v)
                        seen[key] = v
                    if o["fn"] is None:
                        continue
                    ins = o["fn"](eng)
                    if sig[i] is not None:
                        s, v = sig[i]
                        ins.then_inc(s, 16 if o["dma"] else 1)
            return body

        with nc.Block() as block:
            block.tensor(emit("pe"))
            block.scalar(emit("act"))
            block.vector(emit("dve"))
            block.gpsimd(emit("pool"))
            block.sync(emit("sp"))
        self.es.close()
        return nc

D = 1024; DFF = 2816; NFF = 22; EPS = 1e-6; NT = 34


def col_tiles(cols):
    nt = (cols + 511) // 512
    base = cols // nt
    res = []; s = 0
    for i in range(nt):
        e = s + base + (1 if i < cols % nt else 0)
        res.append((s, e)); s = e
    return res


def load_cast_rows(k, dst, src, nk, width, q="pool", split=1):
    v = src.rearrange("(k p) c -> p k c", p=128)
    step = max(1, nk // split)
    for a in range(0, nk, step):
        b = min(nk, a + step)
        k.dma(dst[:, a:b, :], v[:, a:b, :], q=q)


def build_F(layer_has_ctx, Fdim, last, segs):
    k = KB()
    KF = Fdim // 128
    NMAX = max(n for n, _ in segs); CMAX = NMAX + 2
    d_x = [k.dram("xT_%d" % i, [D, n + 2]) for i, (n, _) in enumerate(segs)]
    d_o = [k.dram("oT_%d" % i, [Fdim, n + 2], BF16) for i, (n, _) in enumerate(segs)]
    d_hm = [k.dram("hm_%d" % i, [128, 2]) for i, (n, _) in enumerate(segs)]
    d_y = [k.dram("yT_%d" % i, [D, n], kind="ExternalOutput") for i, (n, _) in enumerate(segs)]
    d_cvec = k.dram("cvec", [128, 8, 2])
    d_wmod = k.dram("wmod", [D, 4096]); d_bmod = k.dram("bmod", [128, 32])
    d_n2g = k.dram("n2g", [128, 8]); d_fg = k.dram("fg", [128, 8])
    d_cw = k.dram("convw", [128, 3, 44]); d_cb = k.dram("convb", [128, 44])
    d_wout = k.dram("wout", [Fdim, D]); d_wup = k.dram("wup", [D, 2 * DFF]); d_wdn = k.dram("wdn", [DFF, D])
    ones_bf = k.sb("ones_bf", [128, 128], BF16)
    cvec = k.sb("cvec_s", [128, 8, 2]); scb = k.sb("scb", [128, 8, 2], BF16)
    bmod = k.sb("bmod_s", [128, 32]); modv = k.sb("modv", [128, 32, 2])
    n2g = k.sb("n2g_s", [128, 8]); fg = k.sb("fg_s", [128, 8]); gm2 = k.sb("gm2", [128, 8, 2])
    cw = k.sb("cw_s", [128, 3, 44]); cb = k.sb("cb_s", [128, 44])
    wmb = [k.sb("wmb%d" % i, [128, 8, 512], BF16) for i in range(2)]
    wout = k.sb("wout_s", [128, KF, D], BF16)
    wdn = k.sb("wdn_s", [128, NFF, D], BF16)
    oT = k.sb("oT_s", [128, KF, CMAX], BF16)
    x1 = k.sb("x1T", [128, 8, CMAX])
    sq = k.sb("sq", [128, 8, CMAX], BF16)
    rstd = k.sb("rstd", [128, CMAX]); tmp = k.sb("tmp", [128, CMAX])
    h2 = k.sb("h2T", [128, 8, CMAX], BF16)
    hm = k.sb("hm_s", [128, 2])
    wua = [k.sb("wua%d" % i, [128, 8, 256], BF16) for i in range(2)]
    wub = [k.sb("wub%d" % i, [128, 8, 256], BF16) for i in range(2)]
    u = [k.sb("u%d" % i, [128, 2, CMAX]) for i in range(2)]
    va = k.sb("va", [128, NMAX]); vb = k.sb("vb", [128, NMAX]); sa = k.sb("sa", [128, NMAX])
    tT = k.sb("tT", [128, NFF, NMAX], BF16)
    P = [k.ps("P%d" % i, [128, 512]) for i in range(8)]
    pctr = [0]

    def nextP():
        p = P[pctr[0] % 8]; pctr[0] += 1
        return p

    k.memset(ones_bf[:], 1.0)
    k.dma(cvec[:], d_cvec); k.dma(bmod[:], d_bmod); k.dma(n2g[:], d_n2g); k.dma(fg[:], d_fg)
    k.dma(cw[:], d_cw); k.dma(cb[:], d_cb)
    k.act(scb[:], cvec[:], AF.Silu)
    pm = nextP()
    for g in range(8):
        wb = wmb[g % 2]
        k.dma(wb[:], d_wmod.rearrange("(k p) c -> p k c", p=128)[:, :, g * 512:(g + 1) * 512], q="pool")
        for c4 in range(4):
            cc = g * 4 + c4
            for kk in range(8):
                k.matmul(pm[:, cc * 2:cc * 2 + 2], wb[:, kk, c4 * 128:(c4 + 1) * 128], scb[:, kk, :],
                         start=(kk == 0), stop=(kk == 7))
    pmv = pm[:, 0:64].rearrange("p (c j) -> p c j", j=2)
    for j in range(2):
        k.tt(modv[:, :, j], pmv[:, :, j], bmod[:], ALU.add)
    for j in range(2):
        k.ts(gm2[:, :, j], modv[:, 16:24, j], 1.0, ALU.add)
        k.tt(gm2[:, :, j], gm2[:, :, j], n2g[:], ALU.mult)
    load_cast_rows(k, wout, d_wout, KF, D, split=4)
    load_cast_rows(k, wdn, d_wdn, NFF, D, split=4)
    wupv = d_wup.rearrange("(k p) c -> p k c", p=128)

    for si, (n, j) in enumerate(segs):
        cols = n + 2
        tiles = col_tiles(cols)
        k.dma(x1[:, :, 0:cols], d_x[si].rearrange("(k p) t -> p k t", p=128))
        k.dma(oT[:, :, 0:cols], d_o[si].rearrange("(k p) t -> p k t", p=128))
        k.dma(hm[:], d_hm[si])
        for fc in range(8):
            for (a, b) in tiles:
                p = nextP()
                for kk in range(KF):
                    k.matmul(p[:, 0:b - a], wout[:, kk, fc * 128:(fc + 1) * 128], oT[:, kk, a:b],
                             start=(kk == 0), stop=(kk == KF - 1))
                k.stt(x1[:, fc, a:b], p[:, 0:b - a], modv[:, 0 + fc, j:j + 1], x1[:, fc, a:b], ALU.mult, ALU.add)
        for kk in range(8):
            k.act(sq[:, kk, 0:cols], x1[:, kk, 0:cols], AF.Square)
        for (a, b) in tiles:
            p = nextP()
            for kk in range(8):
                k.matmul(p[:, 0:b - a], ones_bf[:], sq[:, kk, a:b], start=(kk == 0), stop=(kk == 7))
            k.act(tmp[:, a:b], p[:, 0:b - a], AF.Sqrt, bias=EPSB[0], scale=1.0 / D)
        k.recip(rstd[:, 0:cols], tmp[:, 0:cols])
        for kk in range(8):
            k.stt(tmp[:, 0:cols], x1[:, kk, 0:cols], gm2[:, kk, j:j + 1], rstd[:, 0:cols], ALU.mult, ALU.mult)
            k.act(h2[:, kk, 0:cols], tmp[:, 0:cols], AF.Identity, bias=modv[:, 8 + kk, j:j + 1], scale=1.0)
        k.ts(h2[:, :, 0:1], h2[:, :, 0:1], hm[:, 0:1], ALU.mult)
        k.ts(h2[:, :, cols - 1:cols], h2[:, :, cols - 1:cols], hm[:, 1:2], ALU.mult)
        for g in range(11):
            wa = wua[g % 2]; wb_ = wub[g % 2]
            k.dma(wa[:], wupv[:, :, g * 256:(g + 1) * 256], q="pool")
            k.dma(wb_[:], wupv[:, :, DFF + g * 256:DFF + (g + 1) * 256], q="pool")
            for c2 in range(2):
                c = g * 2 + c2
                ub = u[c % 2]
                for half, w in ((0, wa), (1, wb_)):
                    for (a, b) in tiles:
                        p = nextP()
                        for kk in range(8):
                            k.matmul(p[:, 0:b - a], w[:, kk, c2 * 128:(c2 + 1) * 128], h2[:, kk, a:b],
                                     start=(kk == 0), stop=(kk == 7))
                        k.copy(ub[:, half, a:b], p[:, 0:b - a], eng="act")
                ca = c; cbi = NFF + c
                k.act(va[:, 0:n], ub[:, 0, 1:n + 1], AF.Identity, bias=cb[:, ca:ca + 1], scale=cw[:, 1, ca:ca + 1])
                k.stt(va[:, 0:n], ub[:, 0, 0:n], cw[:, 0, ca:ca + 1], va[:, 0:n], ALU.mult, ALU.add)
                k.stt(va[:, 0:n], ub[:, 0, 2:n + 2], cw[:, 2, ca:ca + 1], va[:, 0:n], ALU.mult, ALU.add)
                k.act(vb[:, 0:n], ub[:, 1, 1:n + 1], AF.Identity, bias=cb[:, cbi:cbi + 1], scale=cw[:, 1, cbi:cbi + 1])
                k.stt(vb[:, 0:n], ub[:, 1, 0:n], cw[:, 0, cbi:cbi + 1], vb[:, 0:n], ALU.mult, ALU.add)
                k.stt(vb[:, 0:n], ub[:, 1, 2:n + 2], cw[:, 2, cbi:cbi + 1], vb[:, 0:n], ALU.mult, ALU.add)
                k.act(sa[:, 0:n], va[:, 0:n], AF.Silu)
                k.tt(tT[:, c, 0:n], sa[:, 0:n], vb[:, 0:n], ALU.mult)
        for fc in range(8):
            p = nextP()
            for c in range(NFF):
                k.matmul(p[:, 0:n], wdn[:, c, fc * 128:(fc + 1) * 128], tT[:, c, 0:n], start=(c == 0), stop=(c == NFF - 1))
            k.stt(x1[:, fc, 1:n + 1], p[:, 0:n], modv[:, 24 + fc, j:j + 1], x1[:, fc, 1:n + 1], ALU.mult, ALU.add)
        if last:
            for kk in range(8):
                k.act(sq[:, kk, 0:n], x1[:, kk, 1:n + 1], AF.Square)
            p = nextP()
            for kk in range(8):
                k.matmul(p[:, 0:n], ones_bf[:], sq[:, kk, 0:n], start=(kk == 0), stop=(kk == 7))
            k.act(tmp[:, 0:n], p[:, 0:n], AF.Sqrt, bias=EPSB[0], scale=1.0 / D)
            k.recip(rstd[:, 0:n], tmp[:, 0:n])
            for kk in range(8):
                k.stt(x1[:, kk, 1:n + 1], x1[:, kk, 1:n + 1], fg[:, kk:kk + 1], rstd[:, 0:n], ALU.mult, ALU.mult)
        k.dma(d_y[si].rearrange("(k p) t -> p k t", p=128), x1[:, :, 1:n + 1])
    nc = k.finish(d_y)
    return nc, k

EPSB = [EPS]


NT = 34


def emit_mod(k, nextP, d_cvec, d_wmod, d_bmod, ncols, wmb, name="m"):
    ncc = ncols // 128
    cvec = k.sb(name + "cvec", [128, 8, 2]); scb = k.sb(name + "scb", [128, 8, 2], BF16)
    bmod = k.sb(name + "bmod", [128, ncc]); modv = k.sb(name + "modv", [128, ncc, 2])
    k.dma(cvec[:], d_cvec); k.dma(bmod[:], d_bmod)
    k.act(scb[:], cvec[:], AF.Silu)
    pm = nextP()
    for g in range(ncols // 512):
        wb = wmb[g % len(wmb)]
        k.dma(wb[:], d_wmod.rearrange("(k p) c -> p k c", p=128)[:, :, g * 512:(g + 1) * 512], q="pool")
        for c4 in range(4):
            cc = g * 4 + c4
            for kk in range(8):
                k.matmul(pm[:, cc * 2:cc * 2 + 2], wb[:, kk, c4 * 128:(c4 + 1) * 128], scb[:, kk, :],
                         start=(kk == 0), stop=(kk == 7))
    pmv = pm[:, 0:2 * ncc].rearrange("p (c j) -> p c j", j=2)
    for j in range(2):
        k.tt(modv[:, :, j], pmv[:, :, j], bmod[:], ALU.add)
    return modv


def emit_hT(k, nextP, hT, d_xT, d_xcT, modv, n1g, ones_bf, xt, sq, rstd, tmp, name=""):
    gm1 = k.sb(name + "gm1", [128, 8, 2]); tmp2 = k.sb(name + "tmp2", [128, 256])
    for j in range(2):
        k.ts(gm1[:, :, j], modv[:, 8:16, j], 1.0, ALU.add)
        k.tt(gm1[:, :, j], gm1[:, :, j], n1g[:], ALU.mult)
    W = 256
    jobs = [(d_xcT, 0, 0, 1)] + [(d_xT, i * W, 256 + i * W, 0) for i in range(4096 // W)]
    sqL = [sq, k.sb(name + "sqB", [128, 8, 256], BF16)]; rstdL = [rstd, k.sb(name + "rstdB", [128, 256])]
    tmpL = [(tmp, tmp2), (k.sb(name + "tmpC", [128, 256]), k.sb(name + "tmpD", [128, 256]))]
    for ji, (src, c0, h0, j) in enumerate(jobs):
        sq = sqL[ji % 2]; rstd = rstdL[ji % 2]; tmp, tmp2 = tmpL[ji % 2]
        x = xt[ji % len(xt)]
        k.dma(x[:], src.rearrange("(k p) t -> p k t", p=128)[:, :, c0:c0 + W])
        for kk in range(8):
            k.act(sq[:, kk, :], x[:, kk, :], AF.Square)
        p = nextP()
        for kk in range(8):
            k.matmul(p[:, 0:W], ones_bf[:], sq[:, kk, :], start=(kk == 0), stop=(kk == 7))
        k.ts(tmp[:, 0:W], p[:, 0:W], 1.0 / D, ALU.mult, EPS, ALU.add)
        k.act(tmp[:, 0:W], tmp[:, 0:W], AF.Sqrt)
        k.recip(rstd[:, 0:W], tmp[:, 0:W])
        for kk in range(8):
            tb = tmp if kk % 2 == 0 else sq[:, 0:4, :].bitcast(F32).rearrange("p a w -> p (a w)") if False else (tmp if kk % 2 == 0 else tmp2)
            k.stt(tb[:, 0:W], x[:, kk, :], gm1[:, kk, j:j + 1], rstd[:, 0:W], ALU.mult, ALU.mult)
            k.act(hT[:, kk, h0:h0 + W], tb[:, 0:W], AF.Identity, bias=modv[:, kk, j:j + 1], scale=1.0)


def rope(k, out_bf, x, cos_t, sin_t, tA, tB):
    xv = x.rearrange("p (a h f) -> p a h f", a=2, h=2)
    ov = out_bf.rearrange("p (a h f) -> p a h f", a=2, h=2)
    Av = tA.rearrange("p (a h f) -> p a h f", a=2, h=2)
    Bv = tB.rearrange("p (a h f) -> p a h f", a=2, h=2)
    cb = cos_t.rearrange("p (a f) -> p a f", a=2).unsqueeze(2).to_broadcast([128, 2, 2, 32])
    sv = sin_t.rearrange("p (a f) -> p a f", a=2)
    k.tt(Av, xv, cb, ALU.mult)
    k.tt(Bv[:, :, 0, :], xv[:, :, 1, :], sv, ALU.mult)
    k.tt(Bv[:, :, 1, :], xv[:, :, 0, :], sv, ALU.mult)
    k.tt(ov[:, :, 0, :], Av[:, :, 0, :], Bv[:, :, 0, :], ALU.subtract)
    k.tt(ov[:, :, 1, :], Av[:, :, 1, :], Bv[:, :, 1, :], ALU.add)


def make_consts():
    j = np.arange(128)[:, None].astype(np.float32); i = np.arange(128)[None, :].astype(np.float32)
    c = {}
    c["mask_f"] = (i >= j).astype(np.float32)
    c["mask_b"] = (j >= i).astype(np.float32)
    c["dm_f"] = np.maximum(i - j, 0.0); c["dm_b"] = np.maximum(j - i, 0.0)
    c["row_f"] = np.broadcast_to(i + 1.0, (128, 128)).copy()
    c["row_b"] = np.broadcast_to(128.0 - i, (128, 128)).copy()
    pc = np.zeros((128, 4), np.float32)
    pc[:, 0] = 127.0 - np.arange(128)
    pc[:, 1] = np.arange(128)
    pc[:, 2] = 128.0
    c["pcols"] = pc
    return {kk: np.ascontiguousarray(v.astype(np.float32)) for kk, v in c.items()}


def rope_tables():
    pos = np.arange(4096)
    row = (pos // 64).astype(np.float32); col = (pos % 64).astype(np.float32)
    inv = (10000.0 ** (-np.arange(0, 64, 2, dtype=np.float32) / 64.0)).astype(np.float32)
    ang = np.concatenate([row[:, None] * inv, col[:, None] * inv], axis=-1).astype(np.float32)
    cos = np.cos(ang).astype(np.float32); sin = np.sin(ang).astype(np.float32)
    cs = np.ascontiguousarray(cos.reshape(32, 128, 64).transpose(1, 0, 2))
    sn = np.ascontiguousarray(sin.reshape(32, 128, 64).transpose(1, 0, 2))
    return cs, sn


def build_M1():
    k = KB()
    DK = 128; DV = 256; NH = 4
    d_xT = k.dram("xT", [D, 4096]); d_xcT = k.dram("xcT", [D, 256])
    d_cvec = k.dram("cvec", [128, 8, 2]); d_wmod = k.dram("wmod", [D, 2048]); d_bmod = k.dram("bmod", [128, 16])
    d_n1g = k.dram("n1g", [128, 8])
    d_win = k.dram("win", [NH, D, 768])
    d_dec = k.dram("dec", [128, 8])
    d_ng = k.dram("ng", [128, NH, DV])
    d_cos = k.dram("cos", [128, 32, 64]); d_sin = k.dram("sin", [128, 32, 64])
    cn = {nm: k.dram(nm, [128, 128]) for nm in ("mask_f", "mask_b", "dm_f", "dm_b", "row_f", "row_b")}
    d_pcols = k.dram("pcols", [128, 4]); d_ident = k.dram("ident", [128, 128])
    d_oT = k.dram("oT", [NH * DV, 4096], BF16, kind="ExternalOutput")

    P = [k.ps("P%d" % i, [128, 512]) for i in range(6)]
    PT = [k.ps("PT%d" % i, [128, 1024], BF16) for i in range(2)]
    pc = [0, 0]

    def nextP():
        p = P[pc[0] % 6]; pc[0] += 1; return p

    def nextPT():
        p = PT[pc[1] % 2]; pc[1] += 1; return p

    ones_bf = k.sb("ones_bf", [128, 128], BF16); k.memset(ones_bf[:], 1.0)
    identf = k.sb("identf", [128, 128]); ident = k.sb("ident_s", [128, 128], BF16)
    k.dma(identf[:], d_ident); k.copy(ident[:], identf[:])
    n1g = k.sb("n1g_s", [128, 8]); k.dma(n1g[:], d_n1g)
    whd = [k.sb("whd0", [128, 8, 768], BF16)]
    wmb = [whd[0][:, :, 0:512]]
    import os
    if 'nomod' in os.environ.get('M1_SKIP', ''):
        modv = k.sb("mmodv", [128, 16, 2]); k.memset(modv[:], 0.1)
    else:
        modv = emit_mod(k, nextP, d_cvec, d_wmod, d_bmod, 2048, wmb)
    hT = k.sb("hT", [128, 8, NT * 128], BF16)
    xt = [k.sb("xt0", [128, 8, 256])]
    sq = k.sb("sq", [128, 8, 256], BF16); rstd = k.sb("rstd", [128, 256]); tmp = k.sb("tmp", [128, 256])
    import os
    if 'nohT' in os.environ.get('M1_SKIP', ''):
        k.memset(hT[:, :, 0:512], 0.01)
    else:
        emit_hT(k, nextP, hT, d_xT, d_xcT, modv, n1g, ones_bf, xt, sq, rstd, tmp)

    cs = {nm: k.sb(nm + "_s", [128, 128]) for nm in cn}
    for nm in cn:
        k.dma(cs[nm][:], cn[nm])
    pcols = k.sb("pcols_s", [128, 4]); k.dma(pcols[:], d_pcols)
    cos = k.sb("cos_s", [128, 32, 64]); sin = k.sb("sin_s", [128, 32, 64])
    ng = k.sb("ng_s", [128, NH, DV])
    if 'nocs' not in os.environ.get('M1_SKIP', ''):
        k.dma(cos[:], d_cos); k.dma(sin[:], d_sin)
        k.dma(ng[:], d_ng)
    dec = k.sb("dec_s", [128, 8]); k.dma(dec[:], d_dec)
    lg = k.sb("lg", [128, 8]); e1 = k.sb("e1", [128, 8])
    k.act(e1[:], dec[:], AF.Exp, scale=-1.0)
    k.act(e1[:], e1[:], AF.Ln, bias=1.0, scale=1.0)
    k.ts(lg[:], e1[:], -1.0, ALU.mult)

    kb = k.sb("kb", [128, NT, DK], BF16); qT = k.sb("qT", [128, NT, 128], BF16); kT = k.sb("kT", [128, NT, 128], BF16)
    vb = k.sb("vb", [128, NT, DV], BF16); gs = k.sb("gs", [128, 32, DV], BF16)
    oacc = k.sb("oacc", [128, 32, DV], BF16)
    oTh = [k.sb("oTh%d" % i, [128, 2, 512], BF16) for i in range(2)]
    qf = k.sb("qf", [128, 128]); kf = k.sb("kf", [128, 128]); ksc = k.sb("ksc", [128, 128])
    tA = k.sb("tA", [128, 128]); tB = k.sb("tB", [128, 128])
    DM = [k.sb("DM%d" % i, [128, 128]) for i in range(2)]
    EBr = [k.sb("EBr%d" % i, [128, 128]) for i in range(2)]
    ERc = k.sb("ERc", [128, 2]); dcol = k.sb("dcol", [128, 2])
    S = k.sb("S", [128, DV]); Sb = k.sb("Sb", [128, DV], BF16)
    AT = k.sb("AT", [128, 128], BF16); qin = k.sb("qin", [128, 128], BF16); kk_ = k.sb("kk", [128, 128], BF16)
    st6 = k.sb("st6", [128, 6]); mv = k.sb("mv", [128, 2]); rs = k.sb("rs", [128, 1]); on = k.sb("on", [128, DV])
    obf = k.sb("obf", [128, DV], BF16)

    import os
    STOP = float(os.environ.get('M1_STOP', '99')); NTL = int(os.environ.get('M1_NT', '34'))
    gf = k.sb("gf", [128, DV])
    for hl in range(NH):
        w = whd[0]
        k.dma(w[:], d_win[hl].rearrange("(k p) c -> p k c", p=128), q="pool")
        for di, (dmn, mkn, rown) in enumerate((("dm_f", "mask_f", "row_f"), ("dm_b", "mask_b", "row_b"))):
            lgc = lg[:, di * 4 + hl:di * 4 + hl + 1]
            k.act(DM[di][:], cs[dmn][:], AF.Exp, scale=lgc)
            k.tt(DM[di][:], DM[di][:], cs[mkn][:], ALU.mult)
            k.act(EBr[di][:], cs[rown][:], AF.Exp, scale=lgc)
            k.act(ERc[:, di:di + 1], pcols[:, di:di + 1], AF.Exp, scale=lgc)
            k.act(dcol[:, di:di + 1], pcols[:, 2:3], AF.Exp, scale=lgc)
        for t in range(NT):
            pa = nextP(); pb = nextP()
            for kk in range(8):
                k.matmul(pa[:, 0:512], hT[:, kk, t * 128:(t + 1) * 128], w[:, kk, 0:512], start=(kk == 0), stop=(kk == 7))
            for kk in range(8):
                k.matmul(pb[:, 0:256], hT[:, kk, t * 128:(t + 1) * 128], w[:, kk, 512:768], start=(kk == 0), stop=(kk == 7))
            k.ts(ksc[:], pa[:, 128:256], float(DK) ** -0.5, ALU.mult)
            if t >= 2:
                rope(k, qf[:], pa[:, 0:128], cos[:, t - 2, :], sin[:, t - 2, :], tA[:], tB[:])
                rope(k, kf[:], ksc[:], cos[:, t - 2, :], sin[:, t - 2, :], tA[:], tB[:])
                ksrc = kf
                k.copy(gf[:], pa[:, 256:512])
                k.act(gs[:, t - 2, :], gf[:], AF.Silu)
            else:
                k.copy(qf[:], pa[:, 0:128])
                ksrc = ksc
            k.copy(kb[:, t, :], ksrc[:], eng="pool")
            k.copy(vb[:, t, :], pb[:, 0:256])
            pt = nextP()
            k.transpose(pt[:, 0:128], qf[:], identf[:])
            k.transpose(pt[:, 128:256], ksrc[:], identf[:])
            k.copy(qT[:, t, :], pt[:, 0:128])
            k.copy(kT[:, t, :], pt[:, 128:256])
        for di in (1, 0):
            order = [1, 0] + list(range(33, 1, -1)) if di == 1 else list(range(NT))
            k.memset(S[:], 0.0); k.memset(Sb[:], 0.0)
            for t in order:
                if t >= 2:
                    pat = nextP()
                    k.matmul(pat[:, 0:128], kT[:, t, :], qT[:, t, :])
                    k.tt(AT[:], pat[:, 0:128], DM[di][:], ALU.mult)
                    k.tt(qin[:], qT[:, t, :], EBr[di][:], ALU.mult, eng="pool")
                    po = nextP()
                    k.matmul(po[:, 0:DV], AT[:], vb[:, t, :], start=True, stop=False)
                    k.matmul(po[:, 0:DV], qin[:], Sb[:], start=False, stop=True)
                k.ts(kk_[:], kb[:, t, :], ERc[:, di:di + 1], ALU.mult, eng="pool")
                pkv = nextP()
                k.matmul(pkv[:, 0:DV], kk_[:], vb[:, t, :])
                if t >= 2:
                    if di == 1:
                        k.copy(oacc[:, t - 2, :], po[:, 0:DV])
                    else:
                        k.tt(on[:], po[:, 0:DV], oacc[:, t - 2, :], ALU.add)
                        k.op("dve", lambda e, a=st6, b=on: e.bn_stats(a[:], b[:]), [on[:]], [st6[:]])
                        k.op("dve", lambda e, a=mv, b=st6: e.bn_aggr(a[:], b[:]), [st6[:]], [mv[:]])
                        k.ts(rs[:], mv[:, 1:2], EPS, ALU.add)
                        k.act(rs[:], rs[:], AF.Sqrt)
                        k.recip(rs[:], rs[:])
                        k.ts(on[:], on[:], mv[:, 0:1], ALU.subtract, rs[:, 0:1], ALU.mult)
                        k.tt(on[:], on[:], ng[:, hl, :], ALU.mult, eng="pool")
                        k.tt(obf[:], on[:], gs[:, t - 2, :], ALU.mult, eng="pool")
                        pt = nextPT()
                        k.transpose(pt[:, 0:128], obf[:, 0:128], ident[:])
                        k.transpose(pt[:, 128:256], obf[:, 128:256], ident[:])
                        lt = t - 2
                        ob = oTh[(lt // 4) % 2]
                        k.copy(ob[:, :, (lt % 4) * 128:(lt % 4 + 1) * 128], pt[:, 0:256].rearrange("p (c t) -> p c t", c=2))
                        if lt % 4 == 3:
                            k.dma(d_oT[hl * DV:(hl + 1) * DV, (lt // 4) * 512:(lt // 4 + 1) * 512].rearrange("(c p) t -> p c t", p=128), ob[:])
                k.stt(S[:], S[:], dcol[:, di:di + 1], pkv[:, 0:DV], ALU.mult, ALU.add)
                k.copy(Sb[:], S[:], eng="act")
    nc = k.finish([d_oT])
    return nc, k


NEG = -30000.0


def na_configs():
    cfgs = []; plan = {}
    for g in range(32):
        lst = []
        for u in range(32):
            key = []
            for kh in range(2):
                for qh in range(2):
                    r = 2 * g + qh; kr = 2 * u + kh
                    r0 = min(max(r - 4, 0), 56)
                    key.append(kr - r + 7 if r0 <= kr <= r0 + 7 else None)
            key = tuple(key)
            if all(x is None for x in key):
                continue
            if key not in cfgs:
                cfgs.append(key)
            lst.append((u, cfgs.index(key)))
        plan[g] = lst
    return cfgs, plan


def build_M0():
    k = KB()
    d_xT = k.dram("xT", [D, 4096]); d_xcT = k.dram("xcT", [D, 256])
    d_cvec = k.dram("cvec", [128, 8, 2]); d_wmod = k.dram("wmod", [D, 2048]); d_bmod = k.dram("bmod", [128, 16])
    d_n1g = k.dram("n1g", [128, 8])
    d_wna = k.dram("wna", [D, 768]); d_wgl = k.dram("wgl", [D, 768]); d_wa = k.dram("wa", [D, 32])
    d_waaug = k.dram("waaug", [17, 2, 128])
    d_rpbT = k.dram("rpbT", [128, 4, 15, 64]); d_colmask = k.dram("colmask", [128, 64])
    d_gng = k.dram("gng", [128, 128])
    d_maskf = k.dram("mask_f", [128, 128]); d_maskb = k.dram("mask_b", [128, 128])
    d_ident = k.dram("ident", [128, 128])
    d_oT = k.dram("oT", [512, NT * 128], BF16, kind="ExternalOutput")

    P = [k.ps("P%d" % i, [128, 512]) for i in range(6)]
    PT = [k.ps("PT%d" % i, [128, 1024], BF16) for i in range(2)]
    pc = [0, 0]

    NRR = [6]

    def nextP():
        p = P[pc[0] % NRR[0]]; pc[0] += 1; return p

    def nextPT():
        p = PT[pc[1] % 2]; pc[1] += 1; return p

    ones_bf = k.sb("ones_bf", [128, 128], BF16); k.memset(ones_bf[:], 1.0)
    identf = k.sb("identf", [128, 128]); ident = k.sb("ident_s", [128, 128], BF16)
    k.dma(identf[:], d_ident); k.copy(ident[:], identf[:])
    n1g = k.sb("n1g_s", [128, 8]); k.dma(n1g[:], d_n1g)
    hT = k.sb("hT", [128, 8, NT * 128], BF16)
    oTs = [k.sb("oTs%d" % i, [128, 2, 512], BF16) for i in range(2)]
    k.open_scope()
    wmb = [k.sb("wmb0", [128, 8, 512], BF16)]
    modv = emit_mod(k, nextP, d_cvec, d_wmod, d_bmod, 2048, wmb)
    xt = [k.sb("xt0", [128, 8, 256]), k.sb("xt1", [128, 8, 256])]
    sq = k.sb("sq", [128, 8, 256], BF16); rstd = k.sb("rstd", [128, 256]); tmp = k.sb("tmp", [128, 256])
    emit_hT(k, nextP, hT, d_xT, d_xcT, modv, n1g, ones_bf, xt, sq, rstd, tmp)
    k.close_scope()

    cfgs, plan = na_configs()
    k.open_scope()
    wna = k.sb("wna_s", [128, 8, 768], BF16)
    k.dma(wna[:], d_wna.rearrange("(k p) c -> p k c", p=128), q="pool")
    QT = k.sb("QT", [64, 4, NT * 128], BF16); KT = k.sb("KT", [64, 4, NT * 128], BF16)
    Vaug = k.sb("Vaug", [128, NT, 4, 65], BF16)
    k.memset(Vaug[:], 1.0)
    BT = k.sb("BT", [128, len(cfgs), 4, 128])
    k.open_scope()
    Btab = k.sb("Btab", [128, 4, 15, 64]); cmask = k.sb("cmask", [128, 64])
    k.dma(Btab[:], d_rpbT); k.dma(cmask[:], d_colmask)
    for h in range(4):
        k.tt(Btab[:, h, :, :], Btab[:, h, :, :], cmask[:].unsqueeze(1).to_broadcast([128, 15, 64]), ALU.add)
    for ci, key in enumerate(cfgs):
        bi = 0
        for kh in range(2):
            for qh in range(2):
                roff = key[bi]; bi += 1
                dst = BT[kh * 64:(kh + 1) * 64, ci, :, qh * 64:(qh + 1) * 64]
                if roff is None:
                    k.memset(dst, NEG, eng="pool")
                else:
                    k.copy(dst, Btab[kh * 64:(kh + 1) * 64, :, roff, :], eng="pool")
    k.close_scope()
    qs = k.sb("qs", [128, 256]); ks_ = k.sb("ks", [128, 256])
    for t in range(NT):
        pa = nextP(); pb = nextP()
        for kk in range(8):
            k.matmul(pa[:, 0:512], hT[:, kk, t * 128:(t + 1) * 128], wna[:, kk, 0:512], start=(kk == 0), stop=(kk == 7))
        for kk in range(8):
            k.matmul(pb[:, 0:256], hT[:, kk, t * 128:(t + 1) * 128], wna[:, kk, 512:768], start=(kk == 0), stop=(kk == 7))
        k.ts(qs[:], pa[:, 0:256], 0.125, ALU.mult)
        k.copy(ks_[:], pa[:, 256:512])
        k.copy(Vaug[:, t, :, 0:64], pb[:, 0:256].rearrange("p (h d) -> p h d", h=4))
        pt = nextP(); pt2 = nextP()
        for h in range(4):
            k.transpose(pt[0:64, h * 128:(h + 1) * 128], qs[:, h * 64:(h + 1) * 64], identf[:])
        for h in range(4):
            k.transpose(pt2[0:64, h * 128:(h + 1) * 128], ks_[:, h * 64:(h + 1) * 64], identf[:])
        k.copy(QT[:, :, t * 128:(t + 1) * 128], pt[0:64, 0:512].rearrange("p (c t) -> p c t", c=4))
        k.copy(KT[:, :, t * 128:(t + 1) * 128], pt2[0:64, 0:512].rearrange("p (c t) -> p c t", c=4))
    import os
    STOP = float(os.environ.get("M0_STOP", "99"))
    if STOP <= 2:
        k.close_scope(); return k.finish([d_oT]), k
    sc = [k.sb("sc%d" % i, [128, 512]) for i in range(2)]
    PTb = [k.sb("PTb%d" % i, [128, 512], BF16) for i in range(2)]
    rden = k.sb("rden", [128, 4, 1]); obf = k.sb("obf", [128, 256], BF16)
    it = [0]
    NRR[0] = 4
    for qt in range(NT if STOP > 2.5 else int(os.environ.get("M0_NQ", "1"))):
        if qt < 2:
            keys = [(0, None), (1, None)]
        else:
            keys = [(0, None), (1, None)] + [(u + 2, ci) for (u, ci) in plan[qt - 2]]
        po = P[4 + qt % 2]
        for ki, (kt, ci) in enumerate(keys):
            ps = nextP()
            for h in range(4):
                k.matmul(ps[:, h * 128:(h + 1) * 128], KT[:, h, kt * 128:(kt + 1) * 128],
                         QT[:, h, qt * 128:(qt + 1) * 128])
            s_ = sc[it[0] % 2]; p_ = PTb[it[0] % 2]; it[0] += 1
            if ci is None:
                k.copy(s_[:], ps[:, 0:512])
            else:
                k.tt(s_[:], ps[:, 0:512], BT[:, ci, :, :].rearrange("p h q -> p (h q)"), ALU.add)
            k.act(p_[:], s_[:], AF.Exp)
            for h in range(4):
                k.matmul(po[:, h * 65:(h + 1) * 65], p_[:, h * 128:(h + 1) * 128], Vaug[:, kt, h, :],
                         start=(ki == 0 and h == 0), stop=(ki == len(keys) - 1 and h == 3))
        pov = po[:, 0:260].rearrange("p (h e) -> p h e", e=65)
        k.recip(rden[:], pov[:, :, 64:65])
        k.tt(obf[:].rearrange("p (h d) -> p h d", h=4), pov[:, :, 0:64], rden[:].to_broadcast([128, 4, 64]), ALU.mult)
        ptt = nextPT()
        k.transpose(ptt[:, 0:128], obf[:, 0:128], ident[:])
        k.transpose(ptt[:, 128:256], obf[:, 128:256], ident[:])
        ob = oTs[(qt // 4) % 2]
        k.copy(ob[:, :, (qt % 4) * 128:(qt % 4 + 1) * 128], ptt[:, 0:256].rearrange("p (c t) -> p c t", c=2))
        if qt % 4 == 3 or qt == NT - 1:
            q0 = (qt // 4) * 4; n = qt - q0 + 1
            k.dma(d_oT[0:256, q0 * 128:(qt + 1) * 128].rearrange("(c p) t -> p c t", p=128), ob[:, :, 0:n * 128])
    NRR[0] = 6
    k.close_scope()
    if STOP <= 3:
        return k.finish([d_oT]), k

    k.open_scope()
    waaugf = k.sb("waaugf", [17, 2, 128]); waaug = k.sb("waaug_s", [17, 2, 128], BF16)
    k.dma(waaugf[:], d_waaug); k.copy(waaug[:], waaugf[:])
    gng = k.sb("gng_s", [128, 128]); k.dma(gng[:], d_gng)
    mk = [k.sb("mkf", [128, 128]), k.sb("mkb", [128, 128])]
    k.dma(mk[0][:], d_maskf); k.dma(mk[1][:], d_maskb)
    mks = [k.sb("mksf", [128, 128]), k.sb("mksb", [128, 128])]
    k.ts(mks[0][:], mk[0][:], -1.0, ALU.mult, 1.0, ALU.add)
    k.ts(mks[1][:], mk[1][:], -1.0, ALU.mult, 1.0, ALU.add)
    qTg = k.sb("qTg", [64, NT, 128], BF16); kTg = k.sb("kTg", [64, NT, 128], BF16)
    kbg = k.sb("kbg", [128, NT, 64], BF16); vg = k.sb("vg", [128, NT, 128], BF16)
    rsl = k.sb("rsl", [128, NT, 128], BF16); sp = k.sb("sp", [128, NT, 2, 64])
    oacc = k.sb("oaccg", [128, NT, 128], BF16)
    wgl = k.sb("wgl_s", [128, 8, 384], BF16); wa = k.sb("wa_s", [128, 8, 32], BF16)
    k.dma(wa[:], d_wa.rearrange("(k p) c -> p k c", p=128), q="pool")
    aT = k.sb("aT", [17, 2, 128], BF16); k.memset(aT[:], 1.0)
    qf = k.sb("qfg", [128, 64]); kf = k.sb("kfg", [128, 64]); rf = k.sb("rfg", [128, 128]); zf = k.sb("zfg", [128, 128])
    S = k.sb("Sg", [64, 128]); Sb = k.sb("Sbg", [64, 128], BF16)
    EBT = k.sb("EBT", [64, 128]); ENBT = k.sb("ENBT", [64, 128]); ER = k.sb("ERg", [128, 64])
    bcs = k.sb("bcs", [64, 128]); rsb = k.sb("rsb", [128, 64])
    qin = k.sb("qing", [64, 128], BF16); kin = k.sb("king", [64, 128], BF16); kkg = k.sb("kkg", [128, 64], BF16)
    AT = k.sb("ATg", [128, 128], BF16)
    on = k.sb("ong", [128, 128]); st6 = k.sb("st6g", [128, 6]); mv = k.sb("mvg", [128, 2]); ms = k.sb("msg", [128, 1])
    obg = k.sb("obg", [128, 128], BF16)
    oTg = [k.sb("oTg%d" % i, [128, 512], BF16) for i in range(2)]
    dwg = d_wgl.rearrange("(k p) c -> p k c", p=128)
    for gh in range(2):
        k.dma(wgl[:, :, 0:64], dwg[:, :, gh * 64:(gh + 1) * 64], q="pool")
        k.dma(wgl[:, :, 64:128], dwg[:, :, 128 + gh * 64:128 + (gh + 1) * 64], q="pool")
        k.dma(wgl[:, :, 128:256], dwg[:, :, 256 + gh * 128:256 + (gh + 1) * 128], q="pool")
        k.dma(wgl[:, :, 256:384], dwg[:, :, 512 + gh * 128:512 + (gh + 1) * 128], q="pool")
        for t in range(NT):
            pa = nextP(); pz = nextP()
            for kk in range(8):
                k.matmul(pa[:, 0:384], hT[:, kk, t * 128:(t + 1) * 128], wgl[:, kk, 0:384], start=(kk == 0), stop=(kk == 7))
            for di in range(2):
                for kk in range(8):
                    k.matmul(pz[0:16, di * 128:(di + 1) * 128], wa[:, kk, di * 16:(di + 1) * 16], hT[:, kk, t * 128:(t + 1) * 128],
                             start=(kk == 0), stop=(kk == 7))
            k.copy(aT[0:16, :, :], pz[0:16, 0:256].rearrange("p (a t) -> p a t", a=2))
            pz2 = nextP()
            for di in range(2):
                k.matmul(pz2[:, di * 64:(di + 1) * 64], aT[:, di, :], waaug[:, di, gh * 64:(gh + 1) * 64])
            k.copy(zf[:], pz2[:, 0:128])
            k.act(zf[:], zf[:], AF.Exp, scale=-1.0)
            k.act(sp[:, t, :, :].rearrange("p a d -> p (a d)"), zf[:], AF.Ln, bias=1.0, scale=1.0)
            k.ts(qf[:], pa[:, 0:64], 0.125, ALU.mult)
            k.copy(kf[:], pa[:, 64:128])
            k.copy(kbg[:, t, :], kf[:], eng="pool")
            k.copy(rf[:], pa[:, 128:256])
            k.act(rsl[:, t, :], rf[:], AF.Silu)
            k.copy(vg[:, t, :], pa[:, 256:384])
            pt = nextP()
            k.transpose(pt[0:64, 0:128], qf[:], identf[:])
            k.transpose(pt[0:64, 128:256], kf[:], identf[:])
            k.copy(qTg[:, t, :], pt[0:64, 0:128])
            k.copy(kTg[:, t, :], pt[0:64, 128:256])
        for di in (1, 0):
            order = ([1, 0] + list(range(33, 1, -1))) if di == 1 else list(range(NT))
            k.memset(S[:], 0.0); k.memset(Sb[:], 0.0)
            for t in order:
                spt = sp[:, t, di, :]
                pbc = nextP()
                k.matmul(pbc[0:64, 0:128], spt, mk[di][:])
                k.matmul(pbc[:, 128:192], mks[di][:], spt)
                k.copy(bcs[:], pbc[0:64, 0:128]); k.copy(rsb[:], pbc[:, 128:192])
                k.act(EBT[:], bcs[:], AF.Exp, scale=-1.0 / 16.0)
                k.act(ENBT[:], bcs[:], AF.Exp, scale=1.0 / 16.0)
                k.act(ER[:], rsb[:], AF.Exp, scale=-1.0 / 16.0)
                k.tt(qin[:], qTg[:, t, :], EBT[:], ALU.mult)
                k.tt(kin[:], kTg[:, t, :], ENBT[:], ALU.mult, eng="pool")
                k.tt(kkg[:], kbg[:, t, :], ER[:], ALU.mult, eng="pool")
                pat = nextP()
                k.matmul(pat[:, 0:128], kin[:], qin[:])
                k.tt(AT[:], pat[:, 0:128], mk[di][:], ALU.mult)
                po = nextP()
                k.matmul(po[:, 0:128], AT[:], vg[:, t, :], start=True, stop=False)
                k.matmul(po[:, 0:128], qin[:], Sb[:], start=False, stop=True)
                pkv = nextP()
                k.matmul(pkv[0:64, 0:128], kkg[:], vg[:, t, :])
                if di == 1:
                    k.copy(oacc[:, t, :], po[:, 0:128])
                else:
                    k.tt(on[:], po[:, 0:128], oacc[:, t, :], ALU.add)
                    k.op("dve", lambda e, a=st6, b=on: e.bn_stats(a[:], b[:]), [on[:]], [st6[:]])
                    k.op("dve", lambda e, a=mv, b=st6: e.bn_aggr(a[:], b[:]), [st6[:]], [mv[:]])
                    k.stt(ms[:], mv[:, 0:1], mv[:, 0:1], mv[:, 1:2], ALU.mult, ALU.add)
                    k.ts(ms[:], ms[:], EPS, ALU.add)
                    k.act(ms[:], ms[:], AF.Sqrt)
                    k.recip(ms[:], ms[:])
                    k.stt(on[:], on[:], ms[:, 0:1], gng[:], ALU.mult, ALU.mult)
                    k.tt(obg[:], on[:], rsl[:, t, :], ALU.mult, eng="pool")
                    ptt = nextPT()
                    k.transpose(ptt[:, 0:128], obg[:], ident[:])
                    ob = oTg[(t // 4) % 2]
                    k.copy(ob[:, (t % 4) * 128:(t % 4 + 1) * 128], ptt[:, 0:128])
                    if t % 4 == 3 or t == NT - 1:
                        q0 = (t // 4) * 4; n = t - q0 + 1
                        k.dma(d_oT[256 + gh * 128:256 + (gh + 1) * 128, q0 * 128:(t + 1) * 128], ob[:, 0:n * 128])
                dci = 127 if di == 0 else 0
                k.stt(S[:], S[:], EBT[:, dci:dci + 1], pkv[0:64, 0:128], ALU.mult, ALU.add)
                k.copy(Sb[:], S[:], eng="act")
    k.close_scope()
    nc = k.finish([d_oT])
    return nc, k

BF = ml_dtypes.bfloat16

def pk(v):
    return np.ascontiguousarray(v.reshape(-1, 128).T)

def seg_cols(arrT, t0, n, T):
    out = np.zeros((arrT.shape[0], n + 2), arrT.dtype)
    lo = max(t0 - 1, 0); hi = min(t0 + n + 1, T)
    out[:, lo - (t0 - 1): hi - (t0 - 1)] = arrT[:, lo:hi]
    hm = np.array([1.0 if t0 - 1 >= 0 else 0.0, 1.0 if t0 + n < T else 0.0], np.float32)
    return out, np.ascontiguousarray(np.broadcast_to(hm, (128, 2)))

def prep_F(inp, L, b, th, xT, oT, xcT=None, ocT=None, wout=None):
    m = {}
    segs = []
    for s in range(4):
        t0 = th * 2048 + s * 512
        m["xT_%d" % s], m["hm_%d" % s] = seg_cols(xT, t0, 512, 4096)
        m["oT_%d" % s], _ = seg_cols(oT, t0, 512, 4096)
        segs.append((512, 0))
    if xcT is not None:
        m["xT_4"], m["hm_4"] = seg_cols(xcT, 0, 256, 256)
        m["oT_4"], _ = seg_cols(ocT, 0, 256, 256)
        segs.append((256, 1))
    cv = np.stack([pk(inp["c"][b]), pk(inp["c_ctx"])], axis=-1)
    m["cvec"] = np.ascontiguousarray(cv.astype(np.float32))
    m["wmod"] = np.ascontiguousarray(inp["w_mod"][L][:, 2048:6144])
    m["bmod"] = pk(inp["b_mod"][L][2048:6144])
    m["n2g"] = pk(inp["norm2_g"][L]); m["fg"] = pk(inp["final_norm_g"])
    cw = inp["ffn_conv_w"][L]
    m["convw"] = np.ascontiguousarray(np.stack([pk(cw[i]) for i in range(3)], axis=1))
    m["convb"] = pk(inp["ffn_conv_b"][L])
    m["wout"] = wout
    m["wup"] = inp["ffn_w_up"][L]; m["wdn"] = inp["ffn_w_down"][L]
    return m, segs


_C = make_consts(); _COS, _SIN = rope_tables()

def prep_M_common(inp, L, b, xT, xcT):
    m = {"xT": np.ascontiguousarray(xT), "xcT": np.ascontiguousarray(xcT)}
    cv = np.stack([pk(inp["c"][b]), pk(inp["c_ctx"])], axis=-1)
    m["cvec"] = np.ascontiguousarray(cv.astype(np.float32))
    m["wmod"] = np.ascontiguousarray(inp["w_mod"][L][:, 0:2048])
    m["bmod"] = pk(inp["b_mod"][L][0:2048])
    m["n1g"] = pk(inp["norm1_g"][L])
    m["ident"] = np.eye(128, dtype=np.float32)
    return m

def prep_M1(inp, b, hh, xT, xcT):
    m = prep_M_common(inp, 1, b, xT, xcT)
    w = inp["ret_w_in"][0]
    heads = [hh * 4 + i for i in range(4)]
    m["win"] = np.ascontiguousarray(np.stack([np.concatenate([
        w[:, h * 128:(h + 1) * 128], w[:, 1024 + h * 128:1024 + (h + 1) * 128],
        w[:, 4096 + h * 256:4096 + (h + 1) * 256], w[:, 2048 + h * 256:2048 + (h + 1) * 256]], axis=1) for h in heads]))
    dec = np.concatenate([inp["ret_decay_fwd"][0][heads], inp["ret_decay_bwd"][0][heads]])
    m["dec"] = np.ascontiguousarray(np.broadcast_to(dec, (128, 8)).astype(np.float32))
    m["ng"] = np.ascontiguousarray(np.broadcast_to(inp["ret_norm_g"][0][heads], (128, 4, 256)).astype(np.float32))
    m["cos"] = _COS; m["sin"] = _SIN
    for kk in ("mask_f", "mask_b", "dm_f", "dm_b", "row_f", "row_b", "pcols"):
        m[kk] = _C[kk]
    return m

def _na_tables(rpb_heads):
    c = np.arange(64); c0 = np.clip(c - 8, 0, 48)
    kc = np.arange(64)
    allowed = (kc[:, None] >= c0[None, :]) & (kc[:, None] < c0[None, :] + 16)
    off = np.clip(kc[:, None] - c[None, :] + 15, 0, 30)
    g = rpb_heads[:, :, off]
    g = np.where(allowed[None, None], g, 0.0).astype(np.float32)
    g = np.transpose(g, (2, 0, 1, 3))
    rpbT = np.ascontiguousarray(np.concatenate([g, g], axis=0))
    cm = np.where(allowed, 0.0, -30000.0).astype(np.float32)
    return rpbT, np.ascontiguousarray(np.concatenate([cm, cm], axis=0))

def prep_M0(inp, b, hh, xT, xcT):
    m = prep_M_common(inp, 0, b, xT, xcT)
    w = inp["na_gla_w_in"][0]
    nh = [hh * 4 + i for i in range(4)]; gh = [hh * 2 + i for i in range(2)]
    m["wna"] = np.ascontiguousarray(np.concatenate(
        [w[:, h * 64:(h + 1) * 64] for h in nh] + [w[:, 512 + h * 64:512 + (h + 1) * 64] for h in nh] +
        [w[:, 1024 + h * 64:1024 + (h + 1) * 64] for h in nh], axis=1))
    m["wgl"] = np.ascontiguousarray(np.concatenate(
        [w[:, 1536 + h * 64:1536 + (h + 1) * 64] for h in gh] + [w[:, 1792 + h * 64:1792 + (h + 1) * 64] for h in gh] +
        [w[:, 2560 + h * 128:2560 + (h + 1) * 128] for h in gh] + [w[:, 2048 + h * 128:2048 + (h + 1) * 128] for h in gh], axis=1))
    m["wa"] = np.ascontiguousarray(w[:, 3072:3104])
    gc = slice(hh * 128, (hh + 1) * 128)
    wa = np.zeros((17, 2, 128), np.float32)
    wa[0:16, 0] = inp["gla_w_a_fwd"][0][:, gc]; wa[16, 0] = inp["gla_b_a_fwd"][0][gc]
    wa[0:16, 1] = inp["gla_w_a_bwd"][0][:, gc]; wa[16, 1] = inp["gla_b_a_bwd"][0][gc]
    m["waaug"] = wa
    m["rpbT"], m["colmask"] = _na_tables(inp["na_rpb"][0][nh])
    m["gng"] = np.ascontiguousarray(np.broadcast_to(inp["gla_norm_g"][0], (128, 128)).astype(np.float32))
    m["mask_f"] = _C["mask_f"]; m["mask_b"] = _C["mask_b"]
    return m


class Env:
    pass


def make_env(k):
    e = Env(); e.k = k
    e.P = [k.ps("P%d" % i, [128, 512]) for i in range(6)]
    e.PT = [k.ps("PT%d" % i, [128, 1024], BF16) for i in range(2)]
    e.pc = [0, 0]; e.NRR = [6]

    def nextP():
        p = e.P[e.pc[0] % e.NRR[0]]; e.pc[0] += 1; return p

    def nextPT():
        p = e.PT[e.pc[1] % 2]; e.pc[1] += 1; return p
    e.nextP = nextP; e.nextPT = nextPT
    e.ones_bf = k.sb("ones_bf", [128, 128], BF16); k.memset(e.ones_bf[:], 1.0)
    e.identf = k.sb("identf", [128, 128]); e.ident = k.sb("ident_s", [128, 128], BF16)
    d_ident = k.dram("ident", [128, 128])
    k.dma(e.identf[:], d_ident); k.copy(e.ident[:], e.identf[:])
    return e


def phase_hT(e, pfx, d_xT, d_xcT, d_cvec, d_wmod, d_bmod, d_n1g, hT):
    k = e.k
    k.open_scope()
    n1g = k.sb(pfx + "n1g_s", [128, 8]); k.dma(n1g[:], d_n1g)
    wmb = [k.sb(pfx + "wmb0", [128, 8, 512], BF16)]
    modv = emit_mod(k, e.nextP, d_cvec, d_wmod, d_bmod, 2048, wmb, name=pfx + "m")
    xt = [k.sb(pfx + "xt0", [128, 8, 256]), k.sb(pfx + "xt1", [128, 8, 256])]
    sq = k.sb(pfx + "sq", [128, 8, 256], BF16); rstd = k.sb(pfx + "rstd", [128, 256]); tmp = k.sb(pfx + "tmp", [128, 256])
    emit_hT(k, e.nextP, hT, d_xT, d_xcT, modv, n1g, e.ones_bf, xt, sq, rstd, tmp, name=pfx)
    k.close_scope()


def phase_NA(e, pfx, hT, d_wna, d_rpbT, d_colmask, d_oT, row0):
    k = e.k; nextP = e.nextP; nextPT = e.nextPT; identf = e.identf; ident = e.ident; P = e.P
    cfgs, plan = na_configs()
    k.open_scope()
    oTs = [k.sb(pfx + "oTs%d" % i, [128, 2, 512], BF16) for i in range(2)]
    wna = k.sb(pfx + "wna_s", [128, 8, 768], BF16)
    k.dma(wna[:], d_wna.rearrange("(k p) c -> p k c", p=128), q="pool")
    QT = k.sb(pfx + "QT", [64, 4, NT * 128], BF16); KT = k.sb(pfx + "KT", [64, 4, NT * 128], BF16)
    Vaug = k.sb(pfx + "Vaug", [128, NT, 4, 65], BF16)
    k.memset(Vaug[:], 1.0)
    BT = k.sb(pfx + "BT", [128, len(cfgs), 4, 128])
    k.open_scope()
    Btab = k.sb(pfx + "Btab", [128, 4, 15, 64]); cmask = k.sb(pfx + "cmask", [128, 64])
    k.dma(Btab[:], d_rpbT); k.dma(cmask[:], d_colmask)
    for h in range(4):
        k.tt(Btab[:, h, :, :], Btab[:, h, :, :], cmask[:].unsqueeze(1).to_broadcast([128, 15, 64]), ALU.add)
    for ci, key in enumerate(cfgs):
        bi = 0
        for kh in range(2):
            for qh in range(2):
                roff = key[bi]; bi += 1
                dst = BT[kh * 64:(kh + 1) * 64, ci, :, qh * 64:(qh + 1) * 64]
                if roff is None:
                    k.memset(dst, NEG, eng="pool")
                else:
                    k.copy(dst, Btab[kh * 64:(kh + 1) * 64, :, roff, :], eng="pool")
    k.close_scope()
    qsL = [k.sb(pfx + "qs%d" % i, [128, 256]) for i in range(2)]; ksL = [k.sb(pfx + "ks%d" % i, [128, 256]) for i in range(2)]
    trq = []
    for t in range(NT):
        qs = qsL[t % 2]; ks_ = ksL[t % 2]
        pa = nextP(); pb = nextP()
        for kk in range(8):
            k.matmul(pa[:, 0:512], hT[:, kk, t * 128:(t + 1) * 128], wna[:, kk, 0:512], start=(kk == 0), stop=(kk == 7))
        for kk in range(8):
            k.matmul(pb[:, 0:256], hT[:, kk, t * 128:(t + 1) * 128], wna[:, kk, 512:768], start=(kk == 0), stop=(kk == 7))
        while trq:
            trq.pop(0)()
        k.ts(qs[:], pa[:, 0:256], 0.125, ALU.mult)
        k.copy(ks_[:], pa[:, 256:512])
        k.copy(Vaug[:, t, :, 0:64], pb[:, 0:256].rearrange("p (h d) -> p h d", h=4))

        def trf(qs=qs, ks_=ks_, t=t):
            pt = nextP(); pt2 = nextP()
            for h in range(4):
                k.transpose(pt[0:64, h * 128:(h + 1) * 128], qs[:, h * 64:(h + 1) * 64], identf[:])
            for h in range(4):
                k.transpose(pt2[0:64, h * 128:(h + 1) * 128], ks_[:, h * 64:(h + 1) * 64], identf[:])
            k.copy(QT[:, :, t * 128:(t + 1) * 128], pt[0:64, 0:512].rearrange("p (c t) -> p c t", c=4))
            k.copy(KT[:, :, t * 128:(t + 1) * 128], pt2[0:64, 0:512].rearrange("p (c t) -> p c t", c=4))
        trq.append(trf)
    while trq:
        trq.pop(0)()
    sc = [k.sb(pfx + "sc%d" % i, [128, 512]) for i in range(2)]
    PTb = [k.sb(pfx + "PTb%d" % i, [128, 512], BF16) for i in range(2)]
    rden = k.sb(pfx + "rden", [128, 4, 1]); obf = k.sb(pfx + "obf", [128, 256], BF16)
    it = [0]
    e.NRR[0] = 4
    obfL = [obf, k.sb(pfx + "obf2", [128, 256], BF16)]
    items = []
    for qt in range(NT):
        if qt < 2:
            keys = [(0, None), (1, None)]
        else:
            keys = [(0, None), (1, None)] + [(u + 2, ci) for (u, ci) in plan[qt - 2]]
        for ki, (kt, ci) in enumerate(keys):
            items.append((qt, ki, kt, ci, len(keys)))

    def stage1(qt, ki, kt, ci, nk):
        ps = nextP()
        for h in range(4):
            k.matmul(ps[:, h * 128:(h + 1) * 128], KT[:, h, kt * 128:(kt + 1) * 128], QT[:, h, qt * 128:(qt + 1) * 128])
        s_ = sc[it[0] % 2]; p_ = PTb[it[0] % 2]; it[0] += 1
        if ci is None:
            k.copy(s_[:], ps[:, 0:512])
        else:
            k.tt(s_[:], ps[:, 0:512], BT[:, ci, :, :].rearrange("p h q -> p (h q)"), ALU.add)
        k.act(p_[:], s_[:], AF.Exp)
        return p_

    def stage2(qt, ki, kt, ci, nk, p_):
        po = P[4 + qt % 2]
        for h in range(4):
            k.matmul(po[:, h * 65:(h + 1) * 65], p_[:, h * 128:(h + 1) * 128], Vaug[:, kt, h, :],
                     start=(ki == 0 and h == 0), stop=(ki == nk - 1 and h == 3))
        if ki == nk - 1:
            ob_ = obfL[qt % 2]
            pov = po[:, 0:260].rearrange("p (h e) -> p h e", e=65)
            k.recip(rden[:], pov[:, :, 64:65])
            k.tt(ob_[:].rearrange("p (h d) -> p h d", h=4), pov[:, :, 0:64], rden[:].to_broadcast([128, 4, 64]), ALU.mult)
            return (qt, ob_)
        return None

    def stage3(qt, ob_):
        ptt = nextPT()
        k.transpose(ptt[:, 0:128], ob_[:, 0:128], ident[:])
        k.transpose(ptt[:, 128:256], ob_[:, 128:256], ident[:])
        ob = oTs[(qt // 4) % 2]
        k.copy(ob[:, :, (qt % 4) * 128:(qt % 4 + 1) * 128], ptt[:, 0:256].rearrange("p (c t) -> p c t", c=2))
        if qt % 4 == 3 or qt == NT - 1:
            q0 = (qt // 4) * 4; n = qt - q0 + 1
            k.dma(d_oT[row0:row0 + 256, q0 * 128:(qt + 1) * 128].rearrange("(c p) t -> p c t", p=128), ob[:, :, 0:n * 128])

    prev = None; fin = []
    for idx in range(len(items) + 1):
        cur = None
        if idx < len(items):
            cur = (items[idx], stage1(*items[idx]))
        while fin and fin[0][0] <= idx - 1:
            _, f3 = fin.pop(0); stage3(*f3)
        if prev is not None:
            r = stage2(*prev[0], prev[1])
            if r is not None:
                fin.append((idx, r))
        prev = cur
    for _, f3 in fin:
        stage3(*f3)
    e.NRR[0] = 6
    k.close_scope()


def phase_GLA(e, pfx, hT, d_wgl, d_wa, d_waaug, d_gng, d_maskf, d_maskb, d_oT, row0, ghs):
    k = e.k; nextP = e.nextP; nextPT = e.nextPT; identf = e.identf; ident = e.ident
    k.open_scope()
    waaugf = k.sb(pfx + "waaugf", [17, 2, 256]); waaug = k.sb(pfx + "waaug_s", [17, 2, 256], BF16)
    k.dma(waaugf[:], d_waaug); k.copy(waaug[:], waaugf[:])
    gng = k.sb(pfx + "gng_s", [128, 128]); k.dma(gng[:], d_gng)
    mk = [k.sb(pfx + "mkf", [128, 128]), k.sb(pfx + "mkb", [128, 128])]
    k.dma(mk[0][:], d_maskf); k.dma(mk[1][:], d_maskb)
    mks = [k.sb(pfx + "mksf", [128, 128]), k.sb(pfx + "mksb", [128, 128])]
    k.ts(mks[0][:], mk[0][:], -1.0, ALU.mult, 1.0, ALU.add)
    k.ts(mks[1][:], mk[1][:], -1.0, ALU.mult, 1.0, ALU.add)
    qTg = k.sb(pfx + "qTg", [64, NT, 128], BF16); kTg = k.sb(pfx + "kTg", [64, NT, 128], BF16)
    kbg = k.sb(pfx + "kbg", [128, NT, 64], BF16); vg = k.sb(pfx + "vg", [128, NT, 128], BF16)
    rsl = k.sb(pfx + "rsl", [128, NT, 128], BF16); sp = k.sb(pfx + "sp", [128, NT, 2, 64])
    oacc = k.sb(pfx + "oaccg", [128, NT, 128], BF16)
    wgl = k.sb(pfx + "wgl_s", [128, 8, 384], BF16); wa = k.sb(pfx + "wa_s", [128, 8, 32], BF16)
    k.dma(wa[:], d_wa.rearrange("(k p) c -> p k c", p=128), q="pool")
    aT = k.sb(pfx + "aT", [17, 2, 128], BF16); k.memset(aT[:], 1.0)
    qfL = [k.sb(pfx + "qfg%d" % i, [128, 64]) for i in range(2)]; kfL = [k.sb(pfx + "kfg%d" % i, [128, 64]) for i in range(2)]; rf = k.sb(pfx + "rfg", [128, 128]); zf = k.sb(pfx + "zfg", [128, 128])
    R2 = range(2)
    S2 = [k.sb(pfx + "Sg%d" % i, [64, 128]) for i in R2]; Sb2 = [k.sb(pfx + "Sbg%d" % i, [64, 128], BF16) for i in R2]
    EBT2 = [k.sb(pfx + "EBT%d" % i, [64, 128]) for i in R2]; ENBT2 = [k.sb(pfx + "ENBT%d" % i, [64, 128]) for i in R2]
    ER2 = [k.sb(pfx + "ERg%d" % i, [128, 64]) for i in R2]
    bcs2 = [k.sb(pfx + "bcs%d" % i, [64, 128]) for i in R2]; rsb2 = [k.sb(pfx + "rsb%d" % i, [128, 64]) for i in R2]
    qin2 = [k.sb(pfx + "qing%d" % i, [64, 128], BF16) for i in R2]; kin2 = [k.sb(pfx + "king%d" % i, [64, 128], BF16) for i in R2]
    kkg2 = [k.sb(pfx + "kkg%d" % i, [128, 64], BF16) for i in R2]
    AT2 = [k.sb(pfx + "ATg%d" % i, [128, 128], BF16) for i in R2]
    oT4 = [k.sb(pfx + "oT4g%d" % i, [128, 128], BF16) for i in range(4)]; oc = [0]
    Q2 = range(2)
    onL = [k.sb(pfx + "ong%d" % i, [128, 128]) for i in Q2]; st6L = [k.sb(pfx + "st6g%d" % i, [128, 6]) for i in Q2]
    mvL = [k.sb(pfx + "mvg%d" % i, [128, 2]) for i in Q2]; msL = [k.sb(pfx + "msg%d" % i, [128, 1]) for i in Q2]
    obgL = [k.sb(pfx + "obg%d" % i, [128, 128], BF16) for i in Q2]
    pend = [None]; pendB = [None]; rc = [0]
    dcl2 = [k.sb(pfx + "dcl%d" % i, [64, 1]) for i in range(2)]
    dwg = d_wgl.rearrange("(k p) c -> p k c", p=128)
    for (gl, gglob, rofs) in ghs:
        k.dma(wgl[:, :, 0:64], dwg[:, :, gl * 64:(gl + 1) * 64], q="pool")
        k.dma(wgl[:, :, 64:128], dwg[:, :, 128 + gl * 64:128 + (gl + 1) * 64], q="pool")
        k.dma(wgl[:, :, 128:256], dwg[:, :, 256 + gl * 128:256 + (gl + 1) * 128], q="pool")
        k.dma(wgl[:, :, 256:384], dwg[:, :, 512 + gl * 128:512 + (gl + 1) * 128], q="pool")
        trq = []
        for t in range(NT):
            qf = qfL[t % 2]; kf = kfL[t % 2]
            pa = nextP(); pz = nextP()
            for kk in range(8):
                k.matmul(pa[:, 0:384], hT[:, kk, t * 128:(t + 1) * 128], wgl[:, kk, 0:384], start=(kk == 0), stop=(kk == 7))
            while trq:
                trq.pop(0)()
            for di in range(2):
                for kk in range(8):
                    k.matmul(pz[0:16, di * 128:(di + 1) * 128], wa[:, kk, di * 16:(di + 1) * 16], hT[:, kk, t * 128:(t + 1) * 128],
                             start=(kk == 0), stop=(kk == 7))
            k.copy(aT[0:16, :, :], pz[0:16, 0:256].rearrange("p (a t) -> p a t", a=2))
            pz2 = nextP()
            for di in range(2):
                k.matmul(pz2[:, di * 64:(di + 1) * 64], aT[:, di, :], waaug[:, di, gglob * 64:(gglob + 1) * 64])
            k.copy(zf[:], pz2[:, 0:128])
            k.act(zf[:], zf[:], AF.Exp, scale=-1.0)
            k.act(sp[:, t, :, :].rearrange("p a d -> p (a d)"), zf[:], AF.Ln, bias=1.0, scale=1.0)
            k.ts(qf[:], pa[:, 0:64], 0.125, ALU.mult)
            k.copy(kf[:], pa[:, 64:128])
            k.copy(kbg[:, t, :], kf[:], eng="pool")
            k.copy(rsl[:, t, :], pa[:, 128:256])
            k.copy(vg[:, t, :], pa[:, 256:384])
            def trf(qf=qf, kf=kf, t=t):
                pt = nextP()
                k.transpose(pt[0:64, 0:128], qf[:], identf[:])
                k.transpose(pt[0:64, 128:256], kf[:], identf[:])
                k.copy(qTg[:, t, :], pt[0:64, 0:128])
                k.copy(kTg[:, t, :], pt[0:64, 128:256])
            trq.append(trf)
        while trq:
            trq.pop(0)()
        for t in range(NT):
            k.act(rsl[:, t, :], rsl[:, t, :], AF.Silu)
        ordB = [1, 0] + list(range(33, 1, -1)); ordF = list(range(NT))
        for di in range(2):
            k.memset(S2[di][:], 0.0); k.memset(Sb2[di][:], 0.0)
        have = set()
        steps = [(di, t) for i in range(NT) for (di, t) in ((1, ordB[i]), (0, ordF[i]))]

        def stage1(di, t):
            qin = qin2[di]; kin = kin2[di]; kkg = kkg2[di]
            EBT = EBT2[di]; ENBT = ENBT2[di]; ER = ER2[di]; bcs = bcs2[di]; rsb = rsb2[di]
            spt = sp[:, t, di, :]
            pbc = nextP()
            k.matmul(pbc[0:64, 0:128], spt, mk[di][:])
            k.matmul(pbc[:, 128:192], mks[di][:], spt)
            k.copy(bcs[:], pbc[0:64, 0:128]); k.copy(rsb[:], pbc[:, 128:192])
            k.act(EBT[:], bcs[:], AF.Exp, scale=-1.0 / 16.0)
            k.act(ENBT[:], bcs[:], AF.Exp, scale=1.0 / 16.0)
            k.act(ER[:], rsb[:], AF.Exp, scale=-1.0 / 16.0)
            k.tt(qin[:], qTg[:, t, :], EBT[:], ALU.mult)
            k.tt(kin[:], kTg[:, t, :], ENBT[:], ALU.mult)
            k.tt(kkg[:], kbg[:, t, :], ER[:], ALU.mult)
            k.copy(dcl2[di][:], EBT[:, (127 if di == 0 else 0):(128 if di == 0 else 1)], eng="pool")

        stage1(*steps[0])
        for si_ in range(len(steps)):
            if True:
                di, t = steps[si_]
                S = S2[di]; Sb = Sb2[di]; AT = AT2[di]; qin = qin2[di]; kin = kin2[di]; kkg = kkg2[di]
                EBT = EBT2[di]
                if si_ + 1 < len(steps) and steps[si_ + 1][0] != di:
                    stage1(*steps[si_ + 1])
                pat = nextP()
                k.matmul(pat[:, 0:128], kin[:], qin[:])
                k.tt(AT[:], pat[:, 0:128], mk[di][:], ALU.mult)
                po = nextP()
                k.matmul(po[:, 0:128], AT[:], vg[:, t, :], start=True, stop=False)
                k.matmul(po[:, 0:128], qin[:], Sb[:], start=False, stop=True)
                pkv = nextP()
                k.matmul(pkv[0:64, 0:128], kkg[:], vg[:, t, :])
                k.stt(S[:], S[:], dcl2[di][:, 0:1], pkv[0:64, 0:128], ALU.mult, ALU.add)
                k.copy(Sb[:], S[:], eng="act")
                if si_ + 1 < len(steps) and steps[si_ + 1][0] == di:
                    stage1(*steps[si_ + 1])
                if pendB[0] is not None:
                    pendB[0](); pendB[0] = None
                if pend[0] is not None:
                    pendB[0] = pend[0](); pend[0] = None
                if t not in have:
                    k.copy(oacc[:, t, :], po[:, 0:128]); have.add(t)
                else:
                    par = rc[0] % 2; rc[0] += 1
                    onb = onL[par]
                    k.tt(onb[:], po[:, 0:128], oacc[:, t, :], ALU.add)

                    def readout(onb=onb, par=par, t=t, rofs=rofs):
                        st6_ = st6L[par]; mv_ = mvL[par]; ms_ = msL[par]; obg_ = obgL[par]
                        k.op("dve", lambda e_, a=st6_, b_=onb: e_.bn_stats(a[:], b_[:]), [onb[:]], [st6_[:]])
                        k.op("dve", lambda e_, a=mv_, b_=st6_: e_.bn_aggr(a[:], b_[:]), [st6_[:]], [mv_[:]])
                        k.stt(ms_[:], mv_[:, 0:1], mv_[:, 0:1], mv_[:, 1:2], ALU.mult, ALU.add)
                        k.ts(ms_[:], ms_[:], EPS, ALU.add)
                        k.act(ms_[:], ms_[:], AF.Ln)
                        k.act(ms_[:], ms_[:], AF.Exp, scale=-0.5)
                        k.stt(onb[:], onb[:], ms_[:, 0:1], gng[:], ALU.mult, ALU.mult)
                        k.tt(obg_[:], onb[:], rsl[:, t, :], ALU.mult, eng="pool")
                        return lambda: partB(obg_, t, rofs)

                    def partB(obg_, t, rofs):
                        ptt = nextPT()
                        k.transpose(ptt[:, 0:128], obg_[:], ident[:])
                        ob = oT4[oc[0] % 4]; oc[0] += 1
                        k.copy(ob[:], ptt[:, 0:128])
                        k.dma(d_oT[row0 + rofs:row0 + rofs + 128, t * 128:(t + 1) * 128], ob[:])
                    pend[0] = readout
        if pendB[0] is not None:
            pendB[0](); pendB[0] = None
        if pend[0] is not None:
            pend[0]()(); pend[0] = None
    k.close_scope()


def phase_RET(e, pfx, hT, d_win, d_dec, d_ng, d_cos, d_sin, cn, d_pcols, d_oT, NH):
    k = e.k; nextP = e.nextP; nextPT = e.nextPT; identf = e.identf; ident = e.ident
    DK = 128; DV = 256
    k.open_scope()
    cs = {nm: k.sb(pfx + nm + "_s", [128, 128]) for nm in cn}
    for nm in cn:
        k.dma(cs[nm][:], cn[nm])
    pcols = k.sb(pfx + "pcols_s", [128, 4]); k.dma(pcols[:], d_pcols)
    cos = k.sb(pfx + "cos_s", [128, 32, 128]); sin = k.sb(pfx + "sin_s", [128, 32, 128])
    k.dma(cos[:], d_cos); k.dma(sin[:], d_sin)
    ng = k.sb(pfx + "ng_s", [128, 2, DV])
    dec = k.sb(pfx + "dec_s", [128, 2 * NH]); k.dma(dec[:], d_dec)
    lg = k.sb(pfx + "lg", [128, 2 * NH]); e1 = k.sb(pfx + "e1", [128, 2 * NH])
    k.act(e1[:], dec[:], AF.Exp, scale=-1.0)
    k.act(e1[:], e1[:], AF.Ln, bias=1.0, scale=1.0)
    k.ts(lg[:], e1[:], -1.0, ALU.mult)
    w = k.sb(pfx + "whd0", [128, 8, 768], BF16)
    kb = k.sb(pfx + "kb", [128, NT, DK], BF16); qT = k.sb(pfx + "qT", [128, NT, 128], BF16); kT = k.sb(pfx + "kT", [128, NT, 128], BF16)
    vb = k.sb(pfx + "vb", [128, NT, DV], BF16); gs = k.sb(pfx + "gs", [128, 32, DV], BF16)
    oacc = k.sb(pfx + "oacc", [128, 32, DV], BF16)
    qkL = [k.sb(pfx + "qk%d" % i, [128, 256]) for i in range(2)]
    tAL = [k.sb(pfx + "tA%d" % i, [128, 256]) for i in range(1)] * 2; tBL = [k.sb(pfx + "tB%d" % i, [128, 256]) for i in range(1)] * 2
    DM = [k.sb(pfx + "DM%d" % i, [128, 128]) for i in range(2)]
    EBr = [k.sb(pfx + "EBr%d" % i, [128, 128]) for i in range(2)]
    ERc = k.sb(pfx + "ERc", [128, 2]); dcol = k.sb(pfx + "dcol", [128, 2])
    S2 = [k.sb(pfx + "S%d" % i, [128, DV]) for i in range(2)]; Sb2 = [k.sb(pfx + "Sb%d" % i, [128, DV], BF16) for i in range(2)]
    AT2 = [k.sb(pfx + "AT%d" % i, [128, 128], BF16) for i in range(2)]; qin2 = [k.sb(pfx + "qin%d" % i, [128, 128], BF16) for i in range(2)]
    kk2 = [k.sb(pfx + "kk%d" % i, [128, 128], BF16) for i in range(2)]
    oT4 = [k.sb(pfx + "oT4%d" % i, [128, 2, 128], BF16) for i in range(2)] * 2; oc = [0]
    R2_ = range(2)
    st6L = [k.sb(pfx + "st6%d" % i, [128, 6]) for i in R2_]; mvL = [k.sb(pfx + "mv%d" % i, [128, 2]) for i in R2_]
    rsL = [k.sb(pfx + "rs%d" % i, [128, 1]) for i in R2_]; nbL = [k.sb(pfx + "nb%d" % i, [128, 1]) for i in R2_]
    onL = [k.sb(pfx + "on%d" % i, [128, DV]) for i in R2_]; obfL = [k.sb(pfx + "obf%d" % i, [128, DV], BF16) for i in R2_]
    pend = [None]; pendB = [None]; rc = [0]
    gfL = tBL
    for hl in range(NH):
        k.dma(w[:], d_win[hl].rearrange("(k p) c -> p k c", p=128), q="pool")
        k.dma(ng[:, hl % 2, :], d_ng[:, hl, :])
        for di, (dmn, mkn, rown) in enumerate((("dm_f", "mask_f", "row_f"), ("dm_b", "mask_b", "row_b"))):
            lgc = lg[:, di * NH + hl:di * NH + hl + 1]
            k.act(DM[di][:], cs[dmn][:], AF.Exp, scale=lgc)
            k.tt(DM[di][:], DM[di][:], cs[mkn][:], ALU.mult)
            k.act(EBr[di][:], cs[rown][:], AF.Exp, scale=lgc)
            k.act(ERc[:, di:di + 1], pcols[:, di:di + 1], AF.Exp, scale=lgc)
            k.act(dcol[:, di:di + 1], pcols[:, 2:3], AF.Exp, scale=lgc)
        trq = []
        for t in range(NT):
            pa = nextP(); pb = nextP()
            for kk in range(8):
                k.matmul(pa[:, 0:512], hT[:, kk, t * 128:(t + 1) * 128], w[:, kk, 0:512], start=(kk == 0), stop=(kk == 7))
            for kk in range(8):
                k.matmul(pb[:, 0:256], hT[:, kk, t * 128:(t + 1) * 128], w[:, kk, 512:768], start=(kk == 0), stop=(kk == 7))
            while trq:
                trq.pop(0)()
            qk = qkL[t % 2]; gf = gfL[t % 2]
            SC = float(DK) ** -0.5
            if t >= 2:
                xv = pa[:, 0:256].rearrange("p (g h f) -> p g h f", g=4, h=2)
                ov = qk[:].rearrange("p (g h f) -> p g h f", g=4, h=2)
                Av = tAL[t % 2][:].rearrange("p (g h f) -> p g h f", g=4, h=2)
                Bv = tBL[t % 2][:].rearrange("p (g h f) -> p g h f", g=4, h=2)
                cb_ = cos[:, t - 2, :].rearrange("p (g f) -> p g f", g=4).unsqueeze(2).to_broadcast([128, 4, 2, 32])
                sv = sin[:, t - 2, :].rearrange("p (g f) -> p g f", g=4)
                k.tt(Av, xv, cb_, ALU.mult)
                k.tt(Bv[:, :, 0, :], xv[:, :, 1, :], sv, ALU.mult)
                k.tt(Bv[:, :, 1, :], xv[:, :, 0, :], sv, ALU.mult)
                k.tt(ov[:, :, 0, :], Av[:, :, 0, :], Bv[:, :, 0, :], ALU.subtract)
                k.tt(ov[:, :, 1, :], Av[:, :, 1, :], Bv[:, :, 1, :], ALU.add)
                k.copy(gf[:], pa[:, 256:512])
                k.act(gs[:, t - 2, :], gf[:], AF.Silu)
            else:
                k.copy(qk[:], pa[:, 0:256])
            k.ts(kb[:, t, :], qk[:, 128:256], SC, ALU.mult, eng="pool")
            k.copy(vb[:, t, :], pb[:, 0:256])
            def trf(qk=qk, t=t, SC=SC):
                pt = nextP()
                k.transpose(pt[:, 0:128], qk[:, 0:128], identf[:])
                k.transpose(pt[:, 128:256], qk[:, 128:256], identf[:])
                k.copy(qT[:, t, :], pt[:, 0:128])
                k.ts(kT[:, t, :], pt[:, 128:256], SC, ALU.mult)
            trq.append(trf)
        while trq:
            trq.pop(0)()
        ordB = [1, 0] + list(range(33, 1, -1)); ordF = list(range(NT))
        for di in range(2):
            k.memset(S2[di][:], 0.0); k.memset(Sb2[di][:], 0.0)
        have = set()
        for i in range(NT):
            for di, t in ((1, ordB[i]), (0, ordF[i])):
                S = S2[di]; Sb = Sb2[di]; AT = AT2[di]; qin = qin2[di]; kk_ = kk2[di]
                if t >= 2:
                    pat = nextP()
                    k.matmul(pat[:, 0:128], kT[:, t, :], qT[:, t, :])
                    k.tt(AT[:], pat[:, 0:128], DM[di][:], ALU.mult)
                    k.tt(qin[:], qT[:, t, :], EBr[di][:], ALU.mult)
                    po = nextP()
                    k.matmul(po[:, 0:DV], AT[:], vb[:, t, :], start=True, stop=False)
                    k.matmul(po[:, 0:DV], qin[:], Sb[:], start=False, stop=True)
                k.ts(kk_[:], kb[:, t, :], ERc[:, di:di + 1], ALU.mult)
                pkv = nextP()
                k.matmul(pkv[:, 0:DV], kk_[:], vb[:, t, :])
                k.stt(S[:], S[:], dcol[:, di:di + 1], pkv[:, 0:DV], ALU.mult, ALU.add)
                k.copy(Sb[:], S[:], eng="act")
                if pendB[0] is not None:
                    pendB[0](); pendB[0] = None
                if pend[0] is not None:
                    pendB[0] = pend[0](); pend[0] = None
                if t >= 2:
                    if t not in have:
                        k.copy(oacc[:, t - 2, :], po[:, 0:DV]); have.add(t)
                    else:
                        par = rc[0] % 2; rc[0] += 1
                        onb = onL[par]
                        k.tt(onb[:], po[:, 0:DV], oacc[:, t - 2, :], ALU.add)

                        def readout(onb=onb, par=par, t=t, hl=hl):
                            st6_ = st6L[par]; mv_ = mvL[par]; rs_ = rsL[par]; nb_ = nbL[par]; obf_ = obfL[par]
                            k.op("dve", lambda e_, a=st6_, b_=onb: e_.bn_stats(a[:], b_[:]), [onb[:]], [st6_[:]])
                            k.op("dve", lambda e_, a=mv_, b_=st6_: e_.bn_aggr(a[:], b_[:]), [st6_[:]], [mv_[:]])
                            k.ts(rs_[:], mv_[:, 1:2], EPS, ALU.add)
                            k.act(rs_[:], rs_[:], AF.Sqrt)
                            k.recip(rs_[:], rs_[:])
                            k.stt(nb_[:], mv_[:, 0:1], -1.0, rs_[:], ALU.mult, ALU.mult)
                            k.act(onb[:], onb[:], AF.Identity, bias=nb_[:, 0:1], scale=rs_[:, 0:1])
                            k.tt(onb[:], onb[:], ng[:, hl % 2, :], ALU.mult, eng="pool")
                            k.tt(obf_[:], onb[:], gs[:, t - 2, :], ALU.mult, eng="pool")
                            return lambda: partB(obf_, t, hl)

                        def partB(obf_, t, hl):
                            ptt = nextPT()
                            k.transpose(ptt[:, 0:128], obf_[:, 0:128], ident[:])
                            k.transpose(ptt[:, 128:256], obf_[:, 128:256], ident[:])
                            lt = t - 2
                            ob = oT4[oc[0] % 4]; oc[0] += 1
                            k.copy(ob[:], ptt[:, 0:256].rearrange("p (c t) -> p c t", c=2))
                            k.dma(d_oT[hl * DV:(hl + 1) * DV, lt * 128:(lt + 1) * 128].rearrange("(c p) t -> p c t", p=128), ob[:])
                        pend[0] = readout
        if pendB[0] is not None:
            pendB[0](); pendB[0] = None
        if pend[0] is not None:
            pend[0]()(); pend[0] = None
    k.close_scope()


def phase_F(e, pfx, Fdim, last, segs, xsrc, osrc, ydst, d_cvec, d_wmod, d_bmod, d_n2g, d_fg, d_cw, d_cb, d_wout, d_wup, d_wdn):
    k = e.k; nextP = e.nextP; ones_bf = e.ones_bf
    KF = Fdim // 128
    NMAX = max(s[0] for s in segs); CMAX = NMAX + 2
    k.open_scope()
    cvec = k.sb(pfx + "cvec_s", [128, 8, 2]); scb = k.sb(pfx + "scb", [128, 8, 2], BF16)
    bmod = k.sb(pfx + "bmod_s", [128, 32]); modv = k.sb(pfx + "modv", [128, 32, 2])
    n2g = k.sb(pfx + "n2g_s", [128, 8]); fg = k.sb(pfx + "fg_s", [128, 8]); gm2 = k.sb(pfx + "gm2", [128, 8, 2])
    cw = k.sb(pfx + "cw_s", [128, 3, 44]); cb = k.sb(pfx + "cb_s", [128, 44])
    wout = k.sb(pfx + "wout_s", [128, KF, D], BF16)
    wdn = k.sb(pfx + "wdn_s", [128, NFF, D], BF16)
    oT = k.sb(pfx + "oT_s", [128, KF, CMAX], BF16)
    x1 = k.sb(pfx + "x1T", [128, 8, CMAX])
    sq = k.sb(pfx + "sq", [128, 8, CMAX], BF16)
    rstd = k.sb(pfx + "rstd", [128, CMAX]); tmp = k.sb(pfx + "tmp", [128, CMAX]); tmpb = k.sb(pfx + "tmpb", [128, CMAX])
    h2 = k.sb(pfx + "h2T", [128, 8, CMAX], BF16)
    wua = [k.sb(pfx + "wua%d" % i, [128, 8, 256], BF16) for i in range(2)]
    wub = [k.sb(pfx + "wub%d" % i, [128, 8, 256], BF16) for i in range(2)]
    u = [k.sb(pfx + "u%d" % i, [128, 2, CMAX]) for i in range(2)]
    vaL = [k.sb(pfx + "va%d" % i, [128, NMAX]) for i in range(2)]; vbL = [k.sb(pfx + "vb%d" % i, [128, NMAX]) for i in range(2)]
    saL = [k.sb(pfx + "sa%d" % i, [128, NMAX]) for i in range(2)]
    tT = k.sb(pfx + "tT", [128, NFF, NMAX], BF16)
    k.dma(cvec[:], d_cvec); k.dma(bmod[:], d_bmod); k.dma(n2g[:], d_n2g); k.dma(fg[:], d_fg)
    k.dma(cw[:], d_cw); k.dma(cb[:], d_cb)
    k.act(scb[:], cvec[:], AF.Silu)
    pm = nextP()
    for g in range(8):
        wb = wua[g % 2][:, :, :]
        wb2 = wub[g % 2][:, :, :]
        k.dma(wb, d_wmod.rearrange("(k p) c -> p k c", p=128)[:, :, g * 512:g * 512 + 256], q="pool")
        k.dma(wb2, d_wmod.rearrange("(k p) c -> p k c", p=128)[:, :, g * 512 + 256:(g + 1) * 512], q="pool")
        for c4 in range(4):
            cc = g * 4 + c4
            src = wb if c4 < 2 else wb2
            for kk in range(8):
                k.matmul(pm[:, cc * 2:cc * 2 + 2], src[:, kk, (c4 % 2) * 128:(c4 % 2 + 1) * 128], scb[:, kk, :],
                         start=(kk == 0), stop=(kk == 7))
    pmv = pm[:, 0:64].rearrange("p (c j) -> p c j", j=2)
    for j in range(2):
        k.tt(modv[:, :, j], pmv[:, :, j], bmod[:], ALU.add)
    for j in range(2):
        k.ts(gm2[:, :, j], modv[:, 16:24, j], 1.0, ALU.add)
        k.tt(gm2[:, :, j], gm2[:, :, j], n2g[:], ALU.mult)
    load_cast_rows(k, wout, d_wout, KF, D, split=4)
    load_cast_rows(k, wdn, d_wdn, NFF, D, split=4)
    wupv = d_wup.rearrange("(k p) c -> p k c", p=128)
    wctr = [0]
    k.dma(wua[0][:], wupv[:, :, 0:256], q="pool")
    k.dma(wub[0][:], wupv[:, :, DFF:DFF + 256], q="pool")

    for si, (n, j, kind, t0) in enumerate(segs):
        cols = n + 2
        tiles = col_tiles(cols)
        xs, T = xsrc[kind]; os_, oc0, _ = osrc[kind]
        lo = max(t0 - 1, 0); hi = min(t0 + n + 1, T)
        c_lo = lo - (t0 - 1); c_hi = hi - (t0 - 1)
        if c_lo > 0:
            k.memset(x1[:, :, 0:1], 0.0); k.memset(oT[:, :, 0:1], 0.0)
        if c_hi < cols:
            k.memset(x1[:, :, cols - 1:cols], 0.0); k.memset(oT[:, :, cols - 1:cols], 0.0)
        k.dma(x1[:, :, c_lo:c_hi], xs.rearrange("(k p) t -> p k t", p=128)[:, :, lo:hi])
        k.dma(oT[:, :, c_lo:c_hi], os_.rearrange("(k p) t -> p k t", p=128)[:, :, oc0 + lo:oc0 + hi])
        for fc in range(8):
            for (a, b) in tiles:
                p = nextP()
                for kk in range(KF):
                    k.matmul(p[:, 0:b - a], wout[:, kk, fc * 128:(fc + 1) * 128], oT[:, kk, a:b],
                             start=(kk == 0), stop=(kk == KF - 1))
                k.stt(x1[:, fc, a:b], p[:, 0:b - a], modv[:, 0 + fc, j:j + 1], x1[:, fc, a:b], ALU.mult, ALU.add)
        for kk in range(8):
            k.act(sq[:, kk, 0:cols], x1[:, kk, 0:cols], AF.Square)
        for (a, b) in tiles:
            p = nextP()
            for kk in range(8):
                k.matmul(p[:, 0:b - a], ones_bf[:], sq[:, kk, a:b], start=(kk == 0), stop=(kk == 7))
            k.ts(tmp[:, a:b], p[:, 0:b - a], 1.0 / D, ALU.mult, EPS, ALU.add)
        k.act(tmp[:, 0:cols], tmp[:, 0:cols], AF.Sqrt)
        k.recip(rstd[:, 0:cols], tmp[:, 0:cols])
        for kk in range(8):
            tb = tmp if kk % 2 == 0 else tmpb
            k.stt(tb[:, 0:cols], x1[:, kk, 0:cols], gm2[:, kk, j:j + 1], rstd[:, 0:cols], ALU.mult, ALU.mult)
            k.act(h2[:, kk, 0:cols], tb[:, 0:cols], AF.Identity, bias=modv[:, 8 + kk, j:j + 1], scale=1.0)
        if c_lo > 0:
            k.memset(h2[:, :, 0:1], 0.0)
        if c_hi < cols:
            k.memset(h2[:, :, cols - 1:cols], 0.0)
        for g in range(11):
            wa = wua[wctr[0] % 2]; wb_ = wub[wctr[0] % 2]
            wctr[0] += 1
            gn = g + 1 if g < 10 else (0 if si + 1 < len(segs) else None)
            if gn is not None:
                k.dma(wua[wctr[0] % 2][:], wupv[:, :, gn * 256:(gn + 1) * 256], q="pool")
                k.dma(wub[wctr[0] % 2][:], wupv[:, :, DFF + gn * 256:DFF + (gn + 1) * 256], q="pool")
            for c2 in range(2):
                c = g * 2 + c2
                ub = u[c % 2]
                for half, w in ((0, wa), (1, wb_)):
                    for (a, b) in tiles:
                        p = nextP()
                        for kk in range(8):
                            k.matmul(p[:, 0:b - a], w[:, kk, c2 * 128:(c2 + 1) * 128], h2[:, kk, a:b],
                                     start=(kk == 0), stop=(kk == 7))
                        k.copy(ub[:, half, a:b], p[:, 0:b - a])
                ca = c; cbi = NFF + c
                va = vaL[c % 2]; vb = vbL[c % 2]; sa = saL[c % 2]
                k.act(va[:, 0:n], ub[:, 0, 1:n + 1], AF.Identity, bias=cb[:, ca:ca + 1], scale=cw[:, 1, ca:ca + 1])
                k.stt(va[:, 0:n], ub[:, 0, 0:n], cw[:, 0, ca:ca + 1], va[:, 0:n], ALU.mult, ALU.add)
                k.stt(va[:, 0:n], ub[:, 0, 2:n + 2], cw[:, 2, ca:ca + 1], va[:, 0:n], ALU.mult, ALU.add)
                k.act(vb[:, 0:n], ub[:, 1, 1:n + 1], AF.Identity, bias=cb[:, cbi:cbi + 1], scale=cw[:, 1, cbi:cbi + 1])
                k.stt(vb[:, 0:n], ub[:, 1, 0:n], cw[:, 0, cbi:cbi + 1], vb[:, 0:n], ALU.mult, ALU.add)
                k.stt(vb[:, 0:n], ub[:, 1, 2:n + 2], cw[:, 2, cbi:cbi + 1], vb[:, 0:n], ALU.mult, ALU.add)
                k.act(sa[:, 0:n], va[:, 0:n], AF.Silu)
                k.tt(tT[:, c, 0:n], sa[:, 0:n], vb[:, 0:n], ALU.mult, eng="pool")
        for fc in range(8):
            p = nextP()
            for c in range(NFF):
                k.matmul(p[:, 0:n], wdn[:, c, fc * 128:(fc + 1) * 128], tT[:, c, 0:n], start=(c == 0), stop=(c == NFF - 1))
            k.stt(x1[:, fc, 1:n + 1], p[:, 0:n], modv[:, 24 + fc, j:j + 1], x1[:, fc, 1:n + 1], ALU.mult, ALU.add)
        if last:
            for kk in range(8):
                k.act(sq[:, kk, 0:n], x1[:, kk, 1:n + 1], AF.Square)
            p = nextP()
            for kk in range(8):
                k.matmul(p[:, 0:n], ones_bf[:], sq[:, kk, 0:n], start=(kk == 0), stop=(kk == 7))
            k.ts(tmp[:, 0:n], p[:, 0:n], 1.0 / D, ALU.mult, EPS, ALU.add)
            k.act(tmp[:, 0:n], tmp[:, 0:n], AF.Sqrt)
            k.recip(rstd[:, 0:n], tmp[:, 0:n])
            for kk in range(8):
                k.stt(x1[:, kk, 1:n + 1], x1[:, kk, 1:n + 1], fg[:, kk:kk + 1], rstd[:, 0:n], ALU.mult, ALU.mult)
        k.dma(ydst[kind].rearrange("(k p) t -> p k t", p=128)[:, :, t0:t0 + n], x1[:, :, 1:n + 1])
    k.close_scope()


def build_fused():
    k = KB(); nc = k.nc
    e = make_env(k)
    d_xT = k.dram("xT", [D, 4096]); d_xcT = k.dram("xcT", [D, 256]); d_cvec = k.dram("cvec", [128, 8, 2])
    cn = {nm: k.dram(nm, [128, 128]) for nm in ("mask_f", "mask_b", "dm_f", "dm_b", "row_f", "row_b")}
    d_pcols = k.dram("pcols", [128, 4]); d_cos = k.dram("cos", [128, 32, 128]); d_sin = k.dram("sin", [128, 32, 128])
    L = []
    for l in range(2):
        L.append(dict(wmodA=k.dram("wmodA%d" % l, [D, 2048]), bmodA=k.dram("bmodA%d" % l, [128, 16]), n1g=k.dram("n1g%d" % l, [128, 8]),
                      wmodB=k.dram("wmodB%d" % l, [D, 4096]), bmodB=k.dram("bmodB%d" % l, [128, 32]), n2g=k.dram("n2g%d" % l, [128, 8]),
                      cw=k.dram("convw%d" % l, [128, 3, 44]), cb=k.dram("convb%d" % l, [128, 44]),
                      wout=k.dram("wout%d" % l, [1024 * (l + 1), D]), wup=k.dram("wup%d" % l, [D, 2 * DFF]), wdn=k.dram("wdn%d" % l, [DFF, D])))
    d_fg = k.dram("fg", [128, 8])
    d_wna = k.dram("wna", [2, D, 768]); d_wgl = k.dram("wgl", [2, D, 768]); d_wa = k.dram("wa", [D, 32])
    d_waaug = k.dram("waaug", [17, 2, 256]); d_rpbT = k.dram("rpbT", [2, 128, 4, 15, 64]); d_colmask = k.dram("colmask", [128, 64])
    d_gng = k.dram("gng", [128, 128])
    d_win = k.dram("win", [8, D, 768]); d_dec = k.dram("dec", [128, 16]); d_ng = k.dram("ng", [128, 8, 256])
    d_y = k.dram("yT", [D, 4096], kind="ExternalOutput")
    o0T = nc.dram_tensor("o0T", [1024, NT * 128], BF16, kind="Internal").ap()
    x2T = nc.dram_tensor("x2T", [D, 4096], F32, kind="Internal").ap()
    xc2T = nc.dram_tensor("xc2T", [D, 256], F32, kind="Internal").ap()
    o1T = nc.dram_tensor("o1T", [2048, 4096], BF16, kind="Internal").ap()
    xcdump = nc.dram_tensor("xcdump", [D, 256], F32, kind="Internal").ap()

    k.open_scope()
    hT = k.sb("hT0", [128, 8, NT * 128], BF16)
    phase_hT(e, "a0", d_xT, d_xcT, d_cvec, L[0]["wmodA"], L[0]["bmodA"], L[0]["n1g"], hT)
    for hh in range(2):
        phase_NA(e, "na%d" % hh, hT, d_wna[hh], d_rpbT[hh], d_colmask, o0T, hh * 512)
        phase_GLA(e, "gl%d" % hh, hT, d_wgl[hh], d_wa, d_waaug, d_gng, cn["mask_f"], cn["mask_b"], o0T, hh * 512 + 256,
                  [(0, hh * 2, 0), (1, hh * 2 + 1, 128)])
    k.close_scope()
    segs0 = [(512, 0, "lat", s * 512) for s in range(8)] + [(256, 1, "ctx", 0)]
    phase_F(e, "f0", 1024, False, segs0,
            {"lat": (d_xT, 4096), "ctx": (d_xcT, 256)}, {"lat": (o0T, 256, 4096), "ctx": (o0T, 0, 256)},
            {"lat": x2T, "ctx": xc2T}, d_cvec, L[0]["wmodB"], L[0]["bmodB"], L[0]["n2g"], d_fg, L[0]["cw"], L[0]["cb"],
            L[0]["wout"], L[0]["wup"], L[0]["wdn"])
    k.open_scope()
    hT = k.sb("hT1", [128, 8, NT * 128], BF16)
    phase_hT(e, "a1", x2T, xc2T, d_cvec, L[1]["wmodA"], L[1]["bmodA"], L[1]["n1g"], hT)
    phase_RET(e, "rt", hT, d_win, d_dec, d_ng, d_cos, d_sin, cn, d_pcols, o1T, 8)
    k.close_scope()
    segs1 = [(512, 0, "lat", s * 512) for s in range(8)]
    phase_F(e, "f1", 2048, True, segs1,
            {"lat": (x2T, 4096)}, {"lat": (o1T, 0, 4096)}, {"lat": d_y},
            d_cvec, L[1]["wmodB"], L[1]["bmodB"], L[1]["n2g"], d_fg, L[1]["cw"], L[1]["cb"],
            L[1]["wout"], L[1]["wup"], L[1]["wdn"])
    ncc = k.finish([d_y])
    return ncc, k


_PROG = []


def _maps(inp):
    B = 4
    shared = {}
    shared["ident"] = np.eye(128, dtype=np.float32)
    for kk in ("mask_f", "mask_b", "dm_f", "dm_b", "row_f", "row_b", "pcols"):
        shared[kk] = _C[kk]
    shared["cos"] = np.ascontiguousarray(np.concatenate([_COS, _COS], axis=2)); shared["sin"] = np.ascontiguousarray(np.concatenate([_SIN, _SIN], axis=2))
    for l in range(2):
        shared["wmodA%d" % l] = np.ascontiguousarray(inp["w_mod"][l][:, 0:2048])
        shared["bmodA%d" % l] = pk(inp["b_mod"][l][0:2048])
        shared["n1g%d" % l] = pk(inp["norm1_g"][l])
        shared["wmodB%d" % l] = np.ascontiguousarray(inp["w_mod"][l][:, 2048:6144])
        shared["bmodB%d" % l] = pk(inp["b_mod"][l][2048:6144])
        shared["n2g%d" % l] = pk(inp["norm2_g"][l])
        cw = inp["ffn_conv_w"][l]
        shared["convw%d" % l] = np.ascontiguousarray(np.stack([pk(cw[i]) for i in range(3)], axis=1))
        shared["convb%d" % l] = pk(inp["ffn_conv_b"][l])
        shared["wup%d" % l] = inp["ffn_w_up"][l]; shared["wdn%d" % l] = inp["ffn_w_down"][l]
    perm = np.concatenate([np.arange(0, 256), np.arange(512, 768), np.arange(256, 512), np.arange(768, 1024)])
    shared["wout0"] = np.ascontiguousarray(inp["na_gla_w_out"][0][perm])
    shared["wout1"] = np.ascontiguousarray(inp["ret_w_out"][0])
    shared["fg"] = pk(inp["final_norm_g"])
    w = inp["na_gla_w_in"][0]
    wna = []; wgl = []; rp = []
    for hh in range(2):
        nh = [hh * 4 + i for i in range(4)]; gh = [hh * 2 + i for i in range(2)]
        wna.append(np.concatenate([w[:, h * 64:(h + 1) * 64] for h in nh] + [w[:, 512 + h * 64:512 + (h + 1) * 64] for h in nh] +
                                  [w[:, 1024 + h * 64:1024 + (h + 1) * 64] for h in nh], axis=1))
        wgl.append(np.concatenate([w[:, 1536 + h * 64:1536 + (h + 1) * 64] for h in gh] + [w[:, 1792 + h * 64:1792 + (h + 1) * 64] for h in gh] +
                                  [w[:, 2560 + h * 128:2560 + (h + 1) * 128] for h in gh] + [w[:, 2048 + h * 128:2048 + (h + 1) * 128] for h in gh], axis=1))
        r_, cm = _na_tables(inp["na_rpb"][0][nh]); rp.append(r_)
    shared["wna"] = np.ascontiguousarray(np.stack(wna)); shared["wgl"] = np.ascontiguousarray(np.stack(wgl))
    shared["rpbT"] = np.ascontiguousarray(np.stack(rp)); shared["colmask"] = cm
    shared["wa"] = np.ascontiguousarray(w[:, 3072:3104])
    wa = np.zeros((17, 2, 256), np.float32)
    wa[0:16, 0] = inp["gla_w_a_fwd"][0]; wa[16, 0] = inp["gla_b_a_fwd"][0]
    wa[0:16, 1] = inp["gla_w_a_bwd"][0]; wa[16, 1] = inp["gla_b_a_bwd"][0]
    shared["waaug"] = wa
    shared["gng"] = np.ascontiguousarray(np.broadcast_to(inp["gla_norm_g"][0], (128, 128)).astype(np.float32))
    wr = inp["ret_w_in"][0]
    shared["win"] = np.ascontiguousarray(np.stack([np.concatenate([
        wr[:, h * 128:(h + 1) * 128], wr[:, 1024 + h * 128:1024 + (h + 1) * 128],
        wr[:, 4096 + h * 256:4096 + (h + 1) * 256], wr[:, 2048 + h * 256:2048 + (h + 1) * 256]], axis=1) for h in range(8)]))
    dec = np.concatenate([inp["ret_decay_fwd"][0], inp["ret_decay_bwd"][0]])
    shared["dec"] = np.ascontiguousarray(np.broadcast_to(dec, (128, 16)).astype(np.float32))
    shared["ng"] = np.ascontiguousarray(np.broadcast_to(inp["ret_norm_g"][0], (128, 8, 256)).astype(np.float32))
    maps = []
    for core in range(8):
        b = core // 2
        m = dict(shared)
        m["xT"] = np.ascontiguousarray(inp["x"][b].T); m["xcT"] = np.ascontiguousarray(inp["ctx"][b].T)
        m["cvec"] = np.ascontiguousarray(np.stack([pk(inp["c"][b]), pk(inp["c_ctx"])], axis=-1).astype(np.float32))
        maps.append(m)
    return maps


def kernel(**inp):
    inp = {k_: np.asarray(v) for k_, v in inp.items()}
    if not _PROG:
        _PROG.append(build_fused()[0])
    res = run_bass_kernel_spmd(_PROG[0], _maps(inp), core_ids=list(range(8))).results
    out = np.empty((4, 4096, 1024), np.float32)
    for b in range(4):
        out[b, 0:2048] = res[2 * b]["yT"][:, 0:2048].T
        out[b, 2048:4096] = res[2 * b + 1]["yT"][:, 2048:4096].T
    return out
```

```python
import os
import ml_dtypes
from concourse.bass_utils import run_bass_kernel_spmd

from contextlib import ExitStack
import numpy as np
import concourse.bass as bass
import concourse.mybir as mybir

F32 = mybir.dt.float32
BF16 = mybir.dt.bfloat16
AF = mybir.ActivationFunctionType
ALU = mybir.AluOpType
AX = mybir.AxisListType

ENGS = ("pe", "act", "dve", "pool", "sp")
NDSEM = 12


def _region(ap):
    t = ap.tensor
    name = t.name
    dims = list(ap.ap)
    off = int(ap.offset)
    sp = str(ap.space) if hasattr(ap, "space") else ""
    if "DRAM" in sp.upper() or type(t).__name__.startswith("DRam"):
        ext = sum((int(c) - 1) * abs(int(s)) for s, c in dims)
        return (name, 0, 1, off, off + ext + 1)
    if type(t).__name__.startswith("PSum"):
        return (name, 0, 128, 0, 1 << 40)
    pstep, pcnt = int(dims[0][0]), int(dims[0][1])
    if pstep == 0:
        pstep = 1 << 40
    p0 = off // pstep
    f0 = off % pstep
    ext = sum((int(c) - 1) * abs(int(s)) for s, c in dims[1:])
    return (name, p0, p0 + pcnt, f0, f0 + ext + 1)


def _overlap(a, b):
    return a[1] < b[2] and b[1] < a[2] and a[3] < b[4] and b[3] < a[4]


def _covers(a, b):
    return a[1] <= b[1] and a[2] >= b[2] and a[3] <= b[3] and a[4] >= b[4]


class KB:
    def __init__(self):
        self.nc = bass.Bass("TRN2", target_bir_lowering=False)
        self.es = ExitStack()
        self.ops = []
        self.recs = {}
        self.n_alloc = 0
        self.fence = None
        self.fenced = set()
        self.stack = [self.es]

    def sb(self, name, shape, dt=F32):
        return self.stack[-1].enter_context(self.nc.sbuf_tensor(name, list(shape), dt))

    def barrier(self):
        last = {}
        f = set()
        for i, o in enumerate(self.ops):
            if o["dma"]:
                f.add(i)
            else:
                last[o["eng"]] = i
        f.update(last.values())
        if self.fence is not None:
            f = {i for i in f if i > self.fence_at or not self.ops[i]["dma"]}
        self.fence = f
        self.fence_at = len(self.ops)
        self.fenced = set()

    def open_scope(self):
        self.stack.append(ExitStack())

    def close_scope(self):
        self.barrier()
        self.stack.pop().close()

    def ps(self, name, shape, dt=F32):
        return self.es.enter_context(self.nc.psum_tensor(name, list(shape), dt))

    def dram(self, name, shape, dt=F32, kind="ExternalInput"):
        return self.nc.dram_tensor(name, list(shape), dt, kind=kind).ap()

    def op(self, eng, fn, reads, writes, dma=False):
        idx = len(self.ops)
        deps = set()
        rr = [_region(a) for a in reads if a is not None and hasattr(a, "tensor")]
        ww = [_region(a) for a in writes if a is not None and hasattr(a, "tensor")]
        for r in rr:
            for (g, oi, isw) in self.recs.get(r[0], ()):
                if isw and _overlap(r, g):
                    deps.add(oi)
        for w in ww:
            for (g, oi, isw) in self.recs.get(w[0], ()):
                if _overlap(w, g):
                    deps.add(oi)
        for w in ww:
            lst = self.recs.setdefault(w[0], [])
            lst[:] = [x for x in lst if not _covers(w, x[0])]
            lst.append((w, idx, True))
        for r in rr:
            lst = self.recs.setdefault(r[0], [])
            lst[:] = [x for x in lst if not ((not x[2]) and x[1] < idx and self.ops[x[1]]["eng"] == eng
                                             and not self.ops[x[1]]["dma"] and not dma and _covers(r, x[0]))]
            lst.append((r, idx, False))
        if self.fence is not None and eng not in self.fenced:
            deps.update(self.fence)
            self.fenced.add(eng)
        deps.discard(idx)
        self.ops.append(dict(eng=eng, fn=fn, deps=deps, dma=dma, rr=rr, ww=ww))
        return idx

    def dma(self, out, in_, q="sp"):
        return self.op(q, lambda e: e.dma_start(out=out, in_=in_), [in_], [out], dma=True)

    def matmul(self, out, lhsT, rhs, start=True, stop=True):
        return self.op("pe", lambda e: e.matmul(out, lhsT, rhs, start=start, stop=stop), [lhsT, rhs], [out])

    def transpose(self, out, in_, ident):
        return self.op("pe", lambda e: e.transpose(out, in_, ident), [in_, ident], [out])

    def act(self, out, in_, func, bias=None, scale=None, accum_out=None, eng="act"):
        kw = {}
        if bias is not None:
            kw["bias"] = bias
        if scale is not None:
            kw["scale"] = scale
        if accum_out is not None:
            kw["accum_out"] = accum_out
        return self.op(eng, lambda e: e.activation(out, in_, func, **kw), [in_, bias, scale], [out, accum_out])

    def tt(self, out, in0, in1, op, eng="dve"):
        return self.op(eng, lambda e: e.tensor_tensor(out, in0, in1, op), [in0, in1], [out])

    def ts(self, out, in0, s1, op0, s2=None, op1=None, accum_out=None, eng="dve"):
        def f(e):
            kw = {}
            if accum_out is not None:
                kw["accum_out"] = accum_out
            if op1 is None:
                return e.tensor_scalar(out, in0, s1, None, op0, **kw)
            return e.tensor_scalar(out, in0, s1, s2, op0, op1, **kw)
        return self.op(eng, f, [in0, s1, s2], [out, accum_out])

    def stt(self, out, in0, scalar, in1, op0, op1, eng="dve"):
        return self.op(eng, lambda e: e.scalar_tensor_tensor(out, in0, scalar, in1, op0, op1), [in0, scalar, in1], [out])

    def copy(self, out, in_, eng="dve"):
        if eng == "act":
            return self.op("act", lambda e: e.copy(out, in_), [in_], [out])
        return self.op(eng, lambda e: e.tensor_copy(out, in_), [in_], [out])

    def memset(self, ap, val, eng="dve"):
        return self.op(eng, lambda e: e.memset(ap, val), [], [ap])

    def recip(self, out, in_):
        return self.op("dve", lambda e: e.reciprocal(out, in_), [in_], [out])

    def reduce(self, out, in_, op=ALU.add, axis=AX.X, eng="dve"):
        return self.op(eng, lambda e: e.tensor_reduce(out, in_, axis, op), [in_], [out])

    def finish(self, out_aps):
        nc = self.nc
        ops = self.ops
        out_names = {a.tensor.name for a in out_aps}
        final_deps = set()
        for i, o in enumerate(ops):
            if o["dma"] and any(w[0] in out_names for w in o["ww"]):
                final_deps.add(i)
        ops.append(dict(eng="sp", fn=None, deps=final_deps, dma=False, rr=[], ww=[]))
        needs_sig = [False] * len(ops)
        for o in ops:
            for d in o["deps"]:
                if ops[d]["dma"]:
                    continue
                if ops[d]["eng"] == o["eng"] and not o["dma"] and o["eng"] == "pe":
                    continue
                needs_sig[d] = True
        sems = {e: self.es.enter_context(nc.semaphore("s_" + e)) for e in ENGS}
        dsems = {e: [self.es.enter_context(nc.semaphore("d_%s_%d" % (e, i))) for i in range(NDSEM)]
                 for e in ("sp", "act", "pool")}
        cnt = {e: 0 for e in ENGS}
        dcnt = {e: 0 for e in dsems}
        sig = [None] * len(ops)
        prevdma = [None] * len(ops)
        for i, o in enumerate(ops):
            if o["dma"]:
                q = o["eng"]
                n = dcnt[q]
                dcnt[q] += 1
                s = dsems[q][n % NDSEM]
                sig[i] = (s, 16 * (n // NDSEM + 1))
                if n >= NDSEM:
                    prevdma[i] = (s, 16 * (n // NDSEM))
            elif needs_sig[i]:
                cnt[o["eng"]] += 1
                sig[i] = (sems[o["eng"]], cnt[o["eng"]])
        per_eng = {e: [] for e in ENGS}
        for i, o in enumerate(ops):
            per_eng[o["eng"]].append(i)
        self.stats = {e: len(per_eng[e]) for e in ENGS}
        self.stats["sig"] = dict(cnt)

        def emit(ename):
            def body(eng):
                seen = {}
                for i in per_eng[ename]:
                    o = ops[i]
                    waits = {}
                    for d in o["deps"]:
                        po = ops[d]
                        if (not po["dma"]) and po["eng"] == ename and ename == "pe" and not o["dma"]:
                            continue
                        s, v = sig[d]
                        key = id(s)
                        if waits.get(key, (None, 0))[1] < v:
                            waits[key] = (s, v)
                    if prevdma[i] is not None:
                        s, v = prevdma[i]
                        key = id(s)
                        if waits.get(key, (None, 0))[1] < v:
                            waits[key] = (s, v)
                    for key, (s, v) in waits.items():
                        if seen.get(key, 0) >= v:
                            continue
                        eng.wait_ge(s, v)
                        seen[key] = v
                    if o["fn"] is None:
                        continue
                    ins = o["fn"](eng)
                    if sig[i] is not None:
                        s, v = sig[i]
                        ins.then_inc(s, 16 if o["dma"] else 1)
            return body

        with nc.Block() as block:
            block.tensor(emit("pe"))
            block.scalar(emit("act"))
            block.vector(emit("dve"))
            block.gpsimd(emit("pool"))
            block.sync(emit("sp"))
        self.es.close()
        return nc

D = 1024; DFF = 2816; NFF = 22; EPS = 1e-6; NT = 34


def col_tiles(cols):
    nt = (cols + 511) // 512
    base = cols // nt
    res = []; s = 0
    for i in range(nt):
        e = s + base + (1 if i < cols % nt else 0)
        res.append((s, e)); s = e
    return res


def load_cast_rows(k, dst, src, nk, width, q="pool", split=1):
    v = src.rearrange("(k p) c -> p k c", p=128)
    step = max(1, nk // split)
    for a in range(0, nk, step):
        b = min(nk, a + step)
        k.dma(dst[:, a:b, :], v[:, a:b, :], q=q)


def build_F(layer_has_ctx, Fdim, last, segs):
    k = KB()
    KF = Fdim // 128
    NMAX = max(n for n, _ in segs); CMAX = NMAX + 2
    d_x = [k.dram("xT_%d" % i, [D, n + 2]) for i, (n, _) in enumerate(segs)]
    d_o = [k.dram("oT_%d" % i, [Fdim, n + 2], BF16) for i, (n, _) in enumerate(segs)]
    d_hm = [k.dram("hm_%d" % i, [128, 2]) for i, (n, _) in enumerate(segs)]
    d_y = [k.dram("yT_%d" % i, [D, n], kind="ExternalOutput") for i, (n, _) in enumerate(segs)]
    d_cvec = k.dram("cvec", [128, 8, 2])
    d_wmod = k.dram("wmod", [D, 4096]); d_bmod = k.dram("bmod", [128, 32])
    d_n2g = k.dram("n2g", [128, 8]); d_fg = k.dram("fg", [128, 8])
    d_cw = k.dram("convw", [128, 3, 44]); d_cb = k.dram("convb", [128, 44])
    d_wout = k.dram("wout", [Fdim, D]); d_wup = k.dram("wup", [D, 2 * DFF]); d_wdn = k.dram("wdn", [DFF, D])
    ones_bf = k.sb("ones_bf", [128, 128], BF16)
    cvec = k.sb("cvec_s", [128, 8, 2]); scb = k.sb("scb", [128, 8, 2], BF16)
    bmod = k.sb("bmod_s", [128, 32]); modv = k.sb("modv", [128, 32, 2])
    n2g = k.sb("n2g_s", [128, 8]); fg = k.sb("fg_s", [128, 8]); gm2 = k.sb("gm2", [128, 8, 2])
    cw = k.sb("cw_s", [128, 3, 44]); cb = k.sb("cb_s", [128, 44])
    wmb = [k.sb("wmb%d" % i, [128, 8, 512], BF16) for i in range(2)]
    wout = k.sb("wout_s", [128, KF, D], BF16)
    wdn = k.sb("wdn_s", [128, NFF, D], BF16)
    oT = k.sb("oT_s", [128, KF, CMAX], BF16)
    x1 = k.sb("x1T", [128, 8, CMAX])
    sq = k.sb("sq", [128, 8, CMAX], BF16)
    rstd = k.sb("rstd", [128, CMAX]); tmp = k.sb("tmp", [128, CMAX])
    h2 = k.sb("h2T", [128, 8, CMAX], BF16)
    hm = k.sb("hm_s", [128, 2])
    wua = [k.sb("wua%d" % i, [128, 8, 256], BF16) for i in range(2)]
    wub = [k.sb("wub%d" % i, [128, 8, 256], BF16) for i in range(2)]
    u = [k.sb("u%d" % i, [128, 2, CMAX]) for i in range(2)]
    va = k.sb("va", [128, NMAX]); vb = k.sb("vb", [128, NMAX]); sa = k.sb("sa", [128, NMAX])
    tT = k.sb("tT", [128, NFF, NMAX], BF16)
    P = [k.ps("P%d" % i, [128, 512]) for i in range(8)]
    pctr = [0]

    def nextP():
        p = P[pctr[0] % 8]; pctr[0] += 1
        return p

    k.memset(ones_bf[:], 1.0)
    k.dma(cvec[:], d_cvec); k.dma(bmod[:], d_bmod); k.dma(n2g[:], d_n2g); k.dma(fg[:], d_fg)
    k.dma(cw[:], d_cw); k.dma(cb[:], d_cb)
    k.act(scb[:], cvec[:], AF.Silu)
    pm = nextP()
    for g in range(8):
        wb = wmb[g % 2]
        k.dma(wb[:], d_wmod.rearrange("(k p) c -> p k c", p=128)[:, :, g * 512:(g + 1) * 512], q="pool")
        for c4 in range(4):
            cc = g * 4 + c4
            for kk in range(8):
                k.matmul(pm[:, cc * 2:cc * 2 + 2], wb[:, kk, c4 * 128:(c4 + 1) * 128], scb[:, kk, :],
                         start=(kk == 0), stop=(kk == 7))
    pmv = pm[:, 0:64].rearrange("p (c j) -> p c j", j=2)
    for j in range(2):
        k.tt(modv[:, :, j], pmv[:, :, j], bmod[:], ALU.add)
    for j in range(2):
        k.ts(gm2[:, :, j], modv[:, 16:24, j], 1.0, ALU.add)
        k.tt(gm2[:, :, j], gm2[:, :, j], n2g[:], ALU.mult)
    load_cast_rows(k, wout, d_wout, KF, D, split=4)
    load_cast_rows(k, wdn, d_wdn, NFF, D, split=4)
    wupv = d_wup.rearrange("(k p) c -> p k c", p=128)

    for si, (n, j) in enumerate(segs):
        cols = n + 2
        tiles = col_tiles(cols)
        k.dma(x1[:, :, 0:cols], d_x[si].rearrange("(k p) t -> p k t", p=128))
        k.dma(oT[:, :, 0:cols], d_o[si].rearrange("(k p) t -> p k t", p=128))
        k.dma(hm[:], d_hm[si])
        for fc in range(8):
            for (a, b) in tiles:
                p = nextP()
                for kk in range(KF):
                    k.matmul(p[:, 0:b - a], wout[:, kk, fc * 128:(fc + 1) * 128], oT[:, kk, a:b],
                             start=(kk == 0), stop=(kk == KF - 1))
                k.stt(x1[:, fc, a:b], p[:, 0:b - a], modv[:, 0 + fc, j:j + 1], x1[:, fc, a:b], ALU.mult, ALU.add)
        for kk in range(8):
            k.act(sq[:, kk, 0:cols], x1[:, kk, 0:cols], AF.Square)
        for (a, b) in tiles:
            p = nextP()
            for kk in range(8):
                k.matmul(p[:, 0:b - a], ones_bf[:], sq[:, kk, a:b], start=(kk == 0), stop=(kk == 7))
            k.act(tmp[:, a:b], p[:, 0:b - a], AF.Sqrt, bias=EPSB[0], scale=1.0 / D)
        k.recip(rstd[:, 0:cols], tmp[:, 0:cols])
        for kk in range(8):
            k.stt(tmp[:, 0:cols], x1[:, kk, 0:cols], gm2[:, kk, j:j + 1], rstd[:, 0:cols], ALU.mult, ALU.mult)
            k.act(h2[:, kk, 0:cols], tmp[:, 0:cols], AF.Identity, bias=modv[:, 8 + kk, j:j + 1], scale=1.0)
        k.ts(h2[:, :, 0:1], h2[:, :, 0:1], hm[:, 0:1], ALU.mult)
        k.ts(h2[:, :, cols - 1:cols], h2[:, :, cols - 1:cols], hm[:, 1:2], ALU.mult)
        for g in range(11):
            wa = wua[g % 2]; wb_ = wub[g % 2]
            k.dma(wa[:], wupv[:, :, g * 256:(g + 1) * 256], q="pool")
            k.dma(wb_[:], wupv[:, :, DFF + g * 256:DFF + (g + 1) * 256], q="pool")
            for c2 in range(2):
                c = g * 2 + c2
                ub = u[c % 2]
                for half, w in ((0, wa), (1, wb_)):
                    for (a, b) in tiles:
                        p = nextP()
                        for kk in range(8):
                            k.matmul(p[:, 0:b - a], w[:, kk, c2 * 128:(c2 + 1) * 128], h2[:, kk, a:b],
                                     start=(kk == 0), stop=(kk == 7))
                        k.copy(ub[:, half, a:b], p[:, 0:b - a], eng="act")
                ca = c; cbi = NFF + c
                k.act(va[:, 0:n], ub[:, 0, 1:n + 1], AF.Identity, bias=cb[:, ca:ca + 1], scale=cw[:, 1, ca:ca + 1])
                k.stt(va[:, 0:n], ub[:, 0, 0:n], cw[:, 0, ca:ca + 1], va[:, 0:n], ALU.mult, ALU.add)
                k.stt(va[:, 0:n], ub[:, 0, 2:n + 2], cw[:, 2, ca:ca + 1], va[:, 0:n], ALU.mult, ALU.add)
                k.act(vb[:, 0:n], ub[:, 1, 1:n + 1], AF.Identity, bias=cb[:, cbi:cbi + 1], scale=cw[:, 1, cbi:cbi + 1])
                k.stt(vb[:, 0:n], ub[:, 1, 0:n], cw[:, 0, cbi:cbi + 1], vb[:, 0:n], ALU.mult, ALU.add)
                k.stt(vb[:, 0:n], ub[:, 1, 2:n + 2], cw[:, 2, cbi:cbi + 1], vb[:, 0:n], ALU.mult, ALU.add)
                k.act(sa[:, 0:n], va[:, 0:n], AF.Silu)
                k.tt(tT[:, c, 0:n], sa[:, 0:n], vb[:, 0:n], ALU.mult)
        for fc in range(8):
            p = nextP()
            for c in range(NFF):
                k.matmul(p[:, 0:n], wdn[:, c, fc * 128:(fc + 1) * 128], tT[:, c, 0:n], start=(c == 0), stop=(c == NFF - 1))
            k.stt(x1[:, fc, 1:n + 1], p[:, 0:n], modv[:, 24 + fc, j:j + 1], x1[:, fc, 1:n + 1], ALU.mult, ALU.add)
        if last:
            for kk in range(8):
                k.act(sq[:, kk, 0:n], x1[:, kk, 1:n + 1], AF.Square)
            p = nextP()
            for kk in range(8):
                k.matmul(p[:, 0:n], ones_bf[:], sq[:, kk, 0:n], start=(kk == 0), stop=(kk == 7))
            k.act(tmp[:, 0:n], p[:, 0:n], AF.Sqrt, bias=EPSB[0], scale=1.0 / D)
            k.recip(rstd[:, 0:n], tmp[:, 0:n])
            for kk in range(8):
                k.stt(x1[:, kk, 1:n + 1], x1[:, kk, 1:n + 1], fg[:, kk:kk + 1], rstd[:, 0:n], ALU.mult, ALU.mult)
        k.dma(d_y[si].rearrange("(k p) t -> p k t", p=128), x1[:, :, 1:n + 1])
    nc = k.finish(d_y)
    return nc, k

EPSB = [EPS]


NT = 34


def emit_mod(k, nextP, d_cvec, d_wmod, d_bmod, ncols, wmb, name="m"):
    ncc = ncols // 128
    cvec = k.sb(name + "cvec", [128, 8, 2]); scb = k.sb(name + "scb", [128, 8, 2], BF16)
    bmod = k.sb(name + "bmod", [128, ncc]); modv = k.sb(name + "modv", [128, ncc, 2])
    k.dma(cvec[:], d_cvec); k.dma(bmod[:], d_bmod)
    k.act(scb[:], cvec[:], AF.Silu)
    pm = nextP()
    for g in range(ncols // 512):
        wb = wmb[g % len(wmb)]
        k.dma(wb[:], d_wmod.rearrange("(k p) c -> p k c", p=128)[:, :, g * 512:(g + 1) * 512], q="pool")
        for c4 in range(4):
            cc = g * 4 + c4
            for kk in range(8):
                k.matmul(pm[:, cc * 2:cc * 2 + 2], wb[:, kk, c4 * 128:(c4 + 1) * 128], scb[:, kk, :],
                         start=(kk == 0), stop=(kk == 7))
    pmv = pm[:, 0:2 * ncc].rearrange("p (c j) -> p c j", j=2)
    for j in range(2):
        k.tt(modv[:, :, j], pmv[:, :, j], bmod[:], ALU.add)
    return modv


def emit_hT(k, nextP, hT, d_xT, d_xcT, modv, n1g, ones_bf, xt, sq, rstd, tmp, name=""):
    gm1 = k.sb(name + "gm1", [128, 8, 2]); tmp2 = k.sb(name + "tmp2", [128, 256])
    for j in range(2):
        k.ts(gm1[:, :, j], modv[:, 8:16, j], 1.0, ALU.add)
        k.tt(gm1[:, :, j], gm1[:, :, j], n1g[:], ALU.mult)
    W = 256
    jobs = [(d_xcT, 0, 0, 1)] + [(d_xT, i * W, 256 + i * W, 0) for i in range(4096 // W)]
    sqL = [sq, k.sb(name + "sqB", [128, 8, 256], BF16)]; rstdL = [rstd, k.sb(name + "rstdB", [128, 256])]
    tmpL = [(tmp, tmp2), (k.sb(name + "tmpC", [128, 256]), k.sb(name + "tmpD", [128, 256]))]
    for ji, (src, c0, h0, j) in enumerate(jobs):
        sq = sqL[ji % 2]; rstd = rstdL[ji % 2]; tmp, tmp2 = tmpL[ji % 2]
        x = xt[ji % len(xt)]
        k.dma(x[:], src.rearrange("(k p) t -> p k t", p=128)[:, :, c0:c0 + W])
        for kk in range(8):
            k.act(sq[:, kk, :], x[:, kk, :], AF.Square)
        p = nextP()
        for kk in range(8):
            k.matmul(p[:, 0:W], ones_bf[:], sq[:, kk, :], start=(kk == 0), stop=(kk == 7))
        k.ts(tmp[:, 0:W], p[:, 0:W], 1.0 / D, ALU.mult, EPS, ALU.add)
        k.act(tmp[:, 0:W], tmp[:, 0:W], AF.Sqrt)
        k.recip(rstd[:, 0:W], tmp[:, 0:W])
        for kk in range(8):
            tb = tmp if kk % 2 == 0 else sq[:, 0:4, :].bitcast(F32).rearrange("p a w -> p (a w)") if False else (tmp if kk % 2 == 0 else tmp2)
            k.stt(tb[:, 0:W], x[:, kk, :], gm1[:, kk, j:j + 1], rstd[:, 0:W], ALU.mult, ALU.mult)
            k.act(hT[:, kk, h0:h0 + W], tb[:, 0:W], AF.Identity, bias=modv[:, kk, j:j + 1], scale=1.0)


def rope(k, out_bf, x, cos_t, sin_t, tA, tB):
    xv = x.rearrange("p (a h f) -> p a h f", a=2, h=2)
    ov = out_bf.rearrange("p (a h f) -> p a h f", a=2, h=2)
    Av = tA.rearrange("p (a h f) -> p a h f", a=2, h=2)
    Bv = tB.rearrange("p (a h f) -> p a h f", a=2, h=2)
    cb = cos_t.rearrange("p (a f) -> p a f", a=2).unsqueeze(2).to_broadcast([128, 2, 2, 32])
    sv = sin_t.rearrange("p (a f) -> p a f", a=2)
    k.tt(Av, xv, cb, ALU.mult)
    k.tt(Bv[:, :, 0, :], xv[:, :, 1, :], sv, ALU.mult)
    k.tt(Bv[:, :, 1, :], xv[:, :, 0, :], sv, ALU.mult)
    k.tt(ov[:, :, 0, :], Av[:, :, 0, :], Bv[:, :, 0, :], ALU.subtract)
    k.tt(ov[:, :, 1, :], Av[:, :, 1, :], Bv[:, :, 1, :], ALU.add)


def make_consts():
    j = np.arange(128)[:, None].astype(np.float32); i = np.arange(128)[None, :].astype(np.float32)
    c = {}
    c["mask_f"] = (i >= j).astype(np.float32)
    c["mask_b"] = (j >= i).astype(np.float32)
    c["dm_f"] = np.maximum(i - j, 0.0); c["dm_b"] = np.maximum(j - i, 0.0)
    c["row_f"] = np.broadcast_to(i + 1.0, (128, 128)).copy()
    c["row_b"] = np.broadcast_to(128.0 - i, (128, 128)).copy()
    pc = np.zeros((128, 4), np.float32)
    pc[:, 0] = 127.0 - np.arange(128)
    pc[:, 1] = np.arange(128)
    pc[:, 2] = 128.0
    c["pcols"] = pc
    return {kk: np.ascontiguousarray(v.astype(np.float32)) for kk, v in c.items()}


def rope_tables():
    pos = np.arange(4096)
    row = (pos // 64).astype(np.float32); col = (pos % 64).astype(np.float32)
    inv = (10000.0 ** (-np.arange(0, 64, 2, dtype=np.float32) / 64.0)).astype(np.float32)
    ang = np.concatenate([row[:, None] * inv, col[:, None] * inv], axis=-1).astype(np.float32)
    cos = np.cos(ang).astype(np.float32); sin = np.sin(ang).astype(np.float32)
    cs = np.ascontiguousarray(cos.reshape(32, 128, 64).transpose(1, 0, 2))
    sn = np.ascontiguousarray(sin.reshape(32, 128, 64).transpose(1, 0, 2))
    return cs, sn


def build_M1():
    k = KB()
    DK = 128; DV = 256; NH = 4
    d_xT = k.dram("xT", [D, 4096]); d_xcT = k.dram("xcT", [D, 256])
    d_cvec = k.dram("cvec", [128, 8, 2]); d_wmod = k.dram("wmod", [D, 2048]); d_bmod = k.dram("bmod", [128, 16])
    d_n1g = k.dram("n1g", [128, 8])
    d_win = k.dram("win", [NH, D, 768])
    d_dec = k.dram("dec", [128, 8])
    d_ng = k.dram("ng", [128, NH, DV])
    d_cos = k.dram("cos", [128, 32, 64]); d_sin = k.dram("sin", [128, 32, 64])
    cn = {nm: k.dram(nm, [128, 128]) for nm in ("mask_f", "mask_b", "dm_f", "dm_b", "row_f", "row_b")}
    d_pcols = k.dram("pcols", [128, 4]); d_ident = k.dram("ident", [128, 128])
    d_oT = k.dram("oT", [NH * DV, 4096], BF16, kind="ExternalOutput")

    P = [k.ps("P%d" % i, [128, 512]) for i in range(6)]
    PT = [k.ps("PT%d" % i, [128, 1024], BF16) for i in range(2)]
    pc = [0, 0]

    def nextP():
        p = P[pc[0] % 6]; pc[0] += 1; return p

    def nextPT():
        p = PT[pc[1] % 2]; pc[1] += 1; return p

    ones_bf = k.sb("ones_bf", [128, 128], BF16); k.memset(ones_bf[:], 1.0)
    identf = k.sb("identf", [128, 128]); ident = k.sb("ident_s", [128, 128], BF16)
    k.dma(identf[:], d_ident); k.copy(ident[:], identf[:])
    n1g = k.sb("n1g_s", [128, 8]); k.dma(n1g[:], d_n1g)
    whd = [k.sb("whd0", [128, 8, 768], BF16)]
    wmb = [whd[0][:, :, 0:512]]
    import os
    if 'nomod' in os.environ.get('M1_SKIP', ''):
        modv = k.sb("mmodv", [128, 16, 2]); k.memset(modv[:], 0.1)
    else:
        modv = emit_mod(k, nextP, d_cvec, d_wmod, d_bmod, 2048, wmb)
    hT = k.sb("hT", [128, 8, NT * 128], BF16)
    xt = [k.sb("xt0", [128, 8, 256])]
    sq = k.sb("sq", [128, 8, 256], BF16); rstd = k.sb("rstd", [128, 256]); tmp = k.sb("tmp", [128, 256])
    import os
    if 'nohT' in os.environ.get('M1_SKIP', ''):
        k.memset(hT[:, :, 0:512], 0.01)
    else:
        emit_hT(k, nextP, hT, d_xT, d_xcT, modv, n1g, ones_bf, xt, sq, rstd, tmp)

    cs = {nm: k.sb(nm + "_s", [128, 128]) for nm in cn}
    for nm in cn:
        k.dma(cs[nm][:], cn[nm])
    pcols = k.sb("pcols_s", [128, 4]); k.dma(pcols[:], d_pcols)
    cos = k.sb("cos_s", [128, 32, 64]); sin = k.sb("sin_s", [128, 32, 64])
    ng = k.sb("ng_s", [128, NH, DV])
    if 'nocs' not in os.environ.get('M1_SKIP', ''):
        k.dma(cos[:], d_cos); k.dma(sin[:], d_sin)
        k.dma(ng[:], d_ng)
    dec = k.sb("dec_s", [128, 8]); k.dma(dec[:], d_dec)
    lg = k.sb("lg", [128, 8]); e1 = k.sb("e1", [128, 8])
    k.act(e1[:], dec[:], AF.Exp, scale=-1.0)
    k.act(e1[:], e1[:], AF.Ln, bias=1.0, scale=1.0)
    k.ts(lg[:], e1[:], -1.0, ALU.mult)

    kb = k.sb("kb", [128, NT, DK], BF16); qT = k.sb("qT", [128, NT, 128], BF16); kT = k.sb("kT", [128, NT, 128], BF16)
    vb = k.sb("vb", [128, NT, DV], BF16); gs = k.sb("gs", [128, 32, DV], BF16)
    oacc = k.sb("oacc", [128, 32, DV], BF16)
    oTh = [k.sb("oTh%d" % i, [128, 2, 512], BF16) for i in range(2)]
    qf = k.sb("qf", [128, 128]); kf = k.sb("kf", [128, 128]); ksc = k.sb("ksc", [128, 128])
    tA = k.sb("tA", [128, 128]); tB = k.sb("tB", [128, 128])
    DM = [k.sb("DM%d" % i, [128, 128]) for i in range(2)]
    EBr = [k.sb("EBr%d" % i, [128, 128]) for i in range(2)]
    ERc = k.sb("ERc", [128, 2]); dcol = k.sb("dcol", [128, 2])
    S = k.sb("S", [128, DV]); Sb = k.sb("Sb", [128, DV], BF16)
    AT = k.sb("AT", [128, 128], BF16); qin = k.sb("qin", [128, 128], BF16); kk_ = k.sb("kk", [128, 128], BF16)
    st6 = k.sb("st6", [128, 6]); mv = k.sb("mv", [128, 2]); rs = k.sb("rs", [128, 1]); on = k.sb("on", [128, DV])
    obf = k.sb("obf", [128, DV], BF16)

    import os
    STOP = float(os.environ.get('M1_STOP', '99')); NTL = int(os.environ.get('M1_NT', '34'))
    gf = k.sb("gf", [128, DV])
    for hl in range(NH):
        w = whd[0]
        k.dma(w[:], d_win[hl].rearrange("(k p) c -> p k c", p=128), q="pool")
        for di, (dmn, mkn, rown) in enumerate((("dm_f", "mask_f", "row_f"), ("dm_b", "mask_b", "row_b"))):
            lgc = lg[:, di * 4 + hl:di * 4 + hl + 1]
            k.act(DM[di][:], cs[dmn][:], AF.Exp, scale=lgc)
            k.tt(DM[di][:], DM[di][:], cs[mkn][:], ALU.mult)
            k.act(EBr[di][:], cs[rown][:], AF.Exp, scale=lgc)
            k.act(ERc[:, di:di + 1], pcols[:, di:di + 1], AF.Exp, scale=lgc)
            k.act(dcol[:, di:di + 1], pcols[:, 2:3], AF.Exp, scale=lgc)
        for t in range(NT):
            pa = nextP(); pb = nextP()
            for kk in range(8):
                k.matmul(pa[:, 0:512], hT[:, kk, t * 128:(t + 1) * 128], w[:, kk, 0:512], start=(kk == 0), stop=(kk == 7))
            for kk in range(8):
                k.matmul(pb[:, 0:256], hT[:, kk, t * 128:(t + 1) * 128], w[:, kk, 512:768], start=(kk == 0), stop=(kk == 7))
            k.ts(ksc[:], pa[:, 128:256], float(DK) ** -0.5, ALU.mult)
            if t >= 2:
                rope(k, qf[:], pa[:, 0:128], cos[:, t - 2, :], sin[:, t - 2, :], tA[:], tB[:])
                rope(k, kf[:], ksc[:], cos[:, t - 2, :], sin[:, t - 2, :], tA[:], tB[:])
                ksrc = kf
                k.copy(gf[:], pa[:, 256:512])
                k.act(gs[:, t - 2, :], gf[:], AF.Silu)
            else:
                k.copy(qf[:], pa[:, 0:128])
                ksrc = ksc
            k.copy(kb[:, t, :], ksrc[:], eng="pool")
            k.copy(vb[:, t, :], pb[:, 0:256])
            pt = nextP()
            k.transpose(pt[:, 0:128], qf[:], identf[:])
            k.transpose(pt[:, 128:256], ksrc[:], identf[:])
            k.copy(qT[:, t, :], pt[:, 0:128])
            k.copy(kT[:, t, :], pt[:, 128:256])
        for di in (1, 0):
            order = [1, 0] + list(range(33, 1, -1)) if di == 1 else list(range(NT))
            k.memset(S[:], 0.0); k.memset(Sb[:], 0.0)
            for t in order:
                if t >= 2:
                    pat = nextP()
                    k.matmul(pat[:, 0:128], kT[:, t, :], qT[:, t, :])
                    k.tt(AT[:], pat[:, 0:128], DM[di][:], ALU.mult)
                    k.tt(qin[:], qT[:, t, :], EBr[di][:], ALU.mult, eng="pool")
                    po = nextP()
                    k.matmul(po[:, 0:DV], AT[:], vb[:, t, :], start=True, stop=False)
                    k.matmul(po[:, 0:DV], qin[:], Sb[:], start=False, stop=True)
                k.ts(kk_[:], kb[:, t, :], ERc[:, di:di + 1], ALU.mult, eng="pool")
                pkv = nextP()
                k.matmul(pkv[:, 0:DV], kk_[:], vb[:, t, :])
                if t >= 2:
                    if di == 1:
                        k.copy(oacc[:, t - 2, :], po[:, 0:DV])
                    else:
                        k.tt(on[:], po[:, 0:DV], oacc[:, t - 2, :], ALU.add)
                        k.op("dve", lambda e, a=st6, b=on: e.bn_stats(a[:], b[:]), [on[:]], [st6[:]])
                        k.op("dve", lambda e, a=mv, b=st6: e.bn_aggr(a[:], b[:]), [st6[:]], [mv[:]])
                        k.ts(rs[:], mv[:, 1:2], EPS, ALU.add)
                        k.act(rs[:], rs[:], AF.Sqrt)
                        k.recip(rs[:], rs[:])
                        k.ts(on[:], on[:], mv[:, 0:1], ALU.subtract, rs[:, 0:1], ALU.mult)
                        k.tt(on[:], on[:], ng[:, hl, :], ALU.mult, eng="pool")
                        k.tt(obf[:], on[:], gs[:, t - 2, :], ALU.mult, eng="pool")
                        pt = nextPT()
                        k.transpose(pt[:, 0:128], obf[:, 0:128], ident[:])
                        k.transpose(pt[:, 128:256], obf[:, 128:256], ident[:])
                        lt = t - 2
                        ob = oTh[(lt // 4) % 2]
                        k.copy(ob[:, :, (lt % 4) * 128:(lt % 4 + 1) * 128], pt[:, 0:256].rearrange("p (c t) -> p c t", c=2))
                        if lt % 4 == 3:
                            k.dma(d_oT[hl * DV:(hl + 1) * DV, (lt // 4) * 512:(lt // 4 + 1) * 512].rearrange("(c p) t -> p c t", p=128), ob[:])
                k.stt(S[:], S[:], dcol[:, di:di + 1], pkv[:, 0:DV], ALU.mult, ALU.add)
                k.copy(Sb[:], S[:], eng="act")
    nc = k.finish([d_oT])
    return nc, k


NEG = -30000.0


def na_configs():
    cfgs = []; plan = {}
    for g in range(32):
        lst = []
        for u in range(32):
            key = []
            for kh in range(2):
                for qh in range(2):
                    r = 2 * g + qh; kr = 2 * u + kh
                    r0 = min(max(r - 4, 0), 56)
                    key.append(kr - r + 7 if r0 <= kr <= r0 + 7 else None)
            key = tuple(key)
            if all(x is None for x in key):
                continue
            if key not in cfgs:
                cfgs.append(key)
            lst.append((u, cfgs.index(key)))
        plan[g] = lst
    return cfgs, plan


def build_M0():
    k = KB()
    d_xT = k.dram("xT", [D, 4096]); d_xcT = k.dram("xcT", [D, 256])
    d_cvec = k.dram("cvec", [128, 8, 2]); d_wmod = k.dram("wmod", [D, 2048]); d_bmod = k.dram("bmod", [128, 16])
    d_n1g = k.dram("n1g", [128, 8])
    d_wna = k.dram("wna", [D, 768]); d_wgl = k.dram("wgl", [D, 768]); d_wa = k.dram("wa", [D, 32])
    d_waaug = k.dram("waaug", [17, 2, 128])
    d_rpbT = k.dram("rpbT", [128, 4, 15, 64]); d_colmask = k.dram("colmask", [128, 64])
    d_gng = k.dram("gng", [128, 128])
    d_maskf = k.dram("mask_f", [128, 128]); d_maskb = k.dram("mask_b", [128, 128])
    d_ident = k.dram("ident", [128, 128])
    d_oT = k.dram("oT", [512, NT * 128], BF16, kind="ExternalOutput")

    P = [k.ps("P%d" % i, [128, 512]) for i in range(6)]
    PT = [k.ps("PT%d" % i, [128, 1024], BF16) for i in range(2)]
    pc = [0, 0]

    NRR = [6]

    def nextP():
        p = P[pc[0] % NRR[0]]; pc[0] += 1; return p

    def nextPT():
        p = PT[pc[1] % 2]; pc[1] += 1; return p

    ones_bf = k.sb("ones_bf", [128, 128], BF16); k.memset(ones_bf[:], 1.0)
    identf = k.sb("identf", [128, 128]); ident = k.sb("ident_s", [128, 128], BF16)
    k.dma(identf[:], d_ident); k.copy(ident[:], identf[:])
    n1g = k.sb("n1g_s", [128, 8]); k.dma(n1g[:], d_n1g)
    hT = k.sb("hT", [128, 8, NT * 128], BF16)
    oTs = [k.sb("oTs%d" % i, [128, 2, 512], BF16) for i in range(2)]
    k.open_scope()
    wmb = [k.sb("wmb0", [128, 8, 512], BF16)]
    modv = emit_mod(k, nextP, d_cvec, d_wmod, d_bmod, 2048, wmb)
    xt = [k.sb("xt0", [128, 8, 256]), k.sb("xt1", [128, 8, 256])]
    sq = k.sb("sq", [128, 8, 256], BF16); rstd = k.sb("rstd", [128, 256]); tmp = k.sb("tmp", [128, 256])
    emit_hT(k, nextP, hT, d_xT, d_xcT, modv, n1g, ones_bf, xt, sq, rstd, tmp)
    k.close_scope()

    cfgs, plan = na_configs()
    k.open_scope()
    wna = k.sb("wna_s", [128, 8, 768], BF16)
    k.dma(wna[:], d_wna.rearrange("(k p) c -> p k c", p=128), q="pool")
    QT = k.sb("QT", [64, 4, NT * 128], BF16); KT = k.sb("KT", [64, 4, NT * 128], BF16)
    Vaug = k.sb("Vaug", [128, NT, 4, 65], BF16)
    k.memset(Vaug[:], 1.0)
    BT = k.sb("BT", [128, len(cfgs), 4, 128])
    k.open_scope()
    Btab = k.sb("Btab", [128, 4, 15, 64]); cmask = k.sb("cmask", [128, 64])
    k.dma(Btab[:], d_rpbT); k.dma(cmask[:], d_colmask)
    for h in range(4):
        k.tt(Btab[:, h, :, :], Btab[:, h, :, :], cmask[:].unsqueeze(1).to_broadcast([128, 15, 64]), ALU.add)
    for ci, key in enumerate(cfgs):
        bi = 0
        for kh in range(2):
            for qh in range(2):
                roff = key[bi]; bi += 1
                dst = BT[kh * 64:(kh + 1) * 64, ci, :, qh * 64:(qh + 1) * 64]
                if roff is None:
                    k.memset(dst, NEG, eng="pool")
                else:
                    k.copy(dst, Btab[kh * 64:(kh + 1) * 64, :, roff, :], eng="pool")
    k.close_scope()
    qs = k.sb("qs", [128, 256]); ks_ = k.sb("ks", [128, 256])
    for t in range(NT):
        pa = nextP(); pb = nextP()
        for kk in range(8):
            k.matmul(pa[:, 0:512], hT[:, kk, t * 128:(t + 1) * 128], wna[:, kk, 0:512], start=(kk == 0), stop=(kk == 7))
        for kk in range(8):
            k.matmul(pb[:, 0:256], hT[:, kk, t * 128:(t + 1) * 128], wna[:, kk, 512:768], start=(kk == 0), stop=(kk == 7))
        k.ts(qs[:], pa[:, 0:256], 0.125, ALU.mult)
        k.copy(ks_[:], pa[:, 256:512])
        k.copy(Vaug[:, t, :, 0:64], pb[:, 0:256].rearrange("p (h d) -> p h d", h=4))
        pt = nextP(); pt2 = nextP()
        for h in range(4):
            k.transpose(pt[0:64, h * 128:(h + 1) * 128], qs[:, h * 64:(h + 1) * 64], identf[:])
        for h in range(4):
            k.transpose(pt2[0:64, h * 128:(h + 1) * 128], ks_[:, h * 64:(h + 1) * 64], identf[:])
        k.copy(QT[:, :, t * 128:(t + 1) * 128], pt[0:64, 0:512].rearrange("p (c t) -> p c t", c=4))
        k.copy(KT[:, :, t * 128:(t + 1) * 128], pt2[0:64, 0:512].rearrange("p (c t) -> p c t", c=4))
    import os
    STOP = float(os.environ.get("M0_STOP", "99"))
    if STOP <= 2:
        k.close_scope(); return k.finish([d_oT]), k
    sc = [k.sb("sc%d" % i, [128, 512]) for i in range(2)]
    PTb = [k.sb("PTb%d" % i, [128, 512], BF16) for i in range(2)]
    rden = k.sb("rden", [128, 4, 1]); obf = k.sb("obf", [128, 256], BF16)
    it = [0]
    NRR[0] = 4
    for qt in range(NT if STOP > 2.5 else int(os.environ.get("M0_NQ", "1"))):
        if qt < 2:
            keys = [(0, None), (1, None)]
        else:
            keys = [(0, None), (1, None)] + [(u + 2, ci) for (u, ci) in plan[qt - 2]]
        po = P[4 + qt % 2]
        for ki, (kt, ci) in enumerate(keys):
            ps = nextP()
            for h in range(4):
                k.matmul(ps[:, h * 128:(h + 1) * 128], KT[:, h, kt * 128:(kt + 1) * 128],
                         QT[:, h, qt * 128:(qt + 1) * 128])
            s_ = sc[it[0] % 2]; p_ = PTb[it[0] % 2]; it[0] += 1
            if ci is None:
                k.copy(s_[:], ps[:, 0:512])
            else:
                k.tt(s_[:], ps[:, 0:512], BT[:, ci, :, :].rearrange("p h q -> p (h q)"), ALU.add)
            k.act(p_[:], s_[:], AF.Exp)
            for h in range(4):
                k.matmul(po[:, h * 65:(h + 1) * 65], p_[:, h * 128:(h + 1) * 128], Vaug[:, kt, h, :],
                         start=(ki == 0 and h == 0), stop=(ki == len(keys) - 1 and h == 3))
        pov = po[:, 0:260].rearrange("p (h e) -> p h e", e=65)
        k.recip(rden[:], pov[:, :, 64:65])
        k.tt(obf[:].rearrange("p (h d) -> p h d", h=4), pov[:, :, 0:64], rden[:].to_broadcast([128, 4, 64]), ALU.mult)
        ptt = nextPT()
        k.transpose(ptt[:, 0:128], obf[:, 0:128], ident[:])
        k.transpose(ptt[:, 128:256], obf[:, 128:256], ident[:])
        ob = oTs[(qt // 4) % 2]
        k.copy(ob[:, :, (qt % 4) * 128:(qt % 4 + 1) * 128], ptt[:, 0:256].rearrange("p (c t) -> p c t", c=2))
        if qt % 4 == 3 or qt == NT - 1:
            q0 = (qt // 4) * 4; n = qt - q0 + 1
            k.dma(d_oT[0:256, q0 * 128:(qt + 1) * 128].rearrange("(c p) t -> p c t", p=128), ob[:, :, 0:n * 128])
    NRR[0] = 6
    k.close_scope()
    if STOP <= 3:
        return k.finish([d_oT]), k

    k.open_scope()
    waaugf = k.sb("waaugf", [17, 2, 128]); waaug = k.sb("waaug_s", [17, 2, 128], BF16)
    k.dma(waaugf[:], d_waaug); k.copy(waaug[:], waaugf[:])
    gng = k.sb("gng_s", [128, 128]); k.dma(gng[:], d_gng)
    mk = [k.sb("mkf", [128, 128]), k.sb("mkb", [128, 128])]
    k.dma(mk[0][:], d_maskf); k.dma(mk[1][:], d_maskb)
    mks = [k.sb("mksf", [128, 128]), k.sb("mksb", [128, 128])]
    k.ts(mks[0][:], mk[0][:], -1.0, ALU.mult, 1.0, ALU.add)
    k.ts(mks[1][:], mk[1][:], -1.0, ALU.mult, 1.0, ALU.add)
    qTg = k.sb("qTg", [64, NT, 128], BF16); kTg = k.sb("kTg", [64, NT, 128], BF16)
    kbg = k.sb("kbg", [128, NT, 64], BF16); vg = k.sb("vg", [128, NT, 128], BF16)
    rsl = k.sb("rsl", [128, NT, 128], BF16); sp = k.sb("sp", [128, NT, 2, 64])
    oacc = k.sb("oaccg", [128, NT, 128], BF16)
    wgl = k.sb("wgl_s", [128, 8, 384], BF16); wa = k.sb("wa_s", [128, 8, 32], BF16)
    k.dma(wa[:], d_wa.rearrange("(k p) c -> p k c", p=128), q="pool")
    aT = k.sb("aT", [17, 2, 128], BF16); k.memset(aT[:], 1.0)
    qf = k.sb("qfg", [128, 64]); kf = k.sb("kfg", [128, 64]); rf = k.sb("rfg", [128, 128]); zf = k.sb("zfg", [128, 128])
    S = k.sb("Sg", [64, 128]); Sb = k.sb("Sbg", [64, 128], BF16)
    EBT = k.sb("EBT", [64, 128]); ENBT = k.sb("ENBT", [64, 128]); ER = k.sb("ERg", [128, 64])
    bcs = k.sb("bcs", [64, 128]); rsb = k.sb("rsb", [128, 64])
    qin = k.sb("qing", [64, 128], BF16); kin = k.sb("king", [64, 128], BF16); kkg = k.sb("kkg", [128, 64], BF16)
    AT = k.sb("ATg", [128, 128], BF16)
    on = k.sb("ong", [128, 128]); st6 = k.sb("st6g", [128, 6]); mv = k.sb("mvg", [128, 2]); ms = k.sb("msg", [128, 1])
    obg = k.sb("obg", [128, 128], BF16)
    oTg = [k.sb("oTg%d" % i, [128, 512], BF16) for i in range(2)]
    dwg = d_wgl.rearrange("(k p) c -> p k c", p=128)
    for gh in range(2):
        k.dma(wgl[:, :, 0:64], dwg[:, :, gh * 64:(gh + 1) * 64], q="pool")
        k.dma(wgl[:, :, 64:128], dwg[:, :, 128 + gh * 64:128 + (gh + 1) * 64], q="pool")
        k.dma(wgl[:, :, 128:256], dwg[:, :, 256 + gh * 128:256 + (gh + 1) * 128], q="pool")
        k.dma(wgl[:, :, 256:384], dwg[:, :, 512 + gh * 128:512 + (gh + 1) * 128], q="pool")
        for t in range(NT):
            pa = nextP(); pz = nextP()
            for kk in range(8):
                k.matmul(pa[:, 0:384], hT[:, kk, t * 128:(t + 1) * 128], wgl[:, kk, 0:384], start=(kk == 0), stop=(kk == 7))
            for di in range(2):
                for kk in range(8):
                    k.matmul(pz[0:16, di * 128:(di + 1) * 128], wa[:, kk, di * 16:(di + 1) * 16], hT[:, kk, t * 128:(t + 1) * 128],
                             start=(kk == 0), stop=(kk == 7))
            k.copy(aT[0:16, :, :], pz[0:16, 0:256].rearrange("p (a t) -> p a t", a=2))
            pz2 = nextP()
            for di in range(2):
                k.matmul(pz2[:, di * 64:(di + 1) * 64], aT[:, di, :], waaug[:, di, gh * 64:(gh + 1) * 64])
            k.copy(zf[:], pz2[:, 0:128])
            k.act(zf[:], zf[:], AF.Exp, scale=-1.0)
            k.act(sp[:, t, :, :].rearrange("p a d -> p (a d)"), zf[:], AF.Ln, bias=1.0, scale=1.0)
            k.ts(qf[:], pa[:, 0:64], 0.125, ALU.mult)
            k.copy(kf[:], pa[:, 64:128])
            k.copy(kbg[:, t, :], kf[:], eng="pool")
            k.copy(rf[:], pa[:, 128:256])
            k.act(rsl[:, t, :], rf[:], AF.Silu)
            k.copy(vg[:, t, :], pa[:, 256:384])
            pt = nextP()
            k.transpose(pt[0:64, 0:128], qf[:], identf[:])
            k.transpose(pt[0:64, 128:256], kf[:], identf[:])
            k.copy(qTg[:, t, :], pt[0:64, 0:128])
            k.copy(kTg[:, t, :], pt[0:64, 128:256])
        for di in (1, 0):
            order = ([1, 0] + list(range(33, 1, -1))) if di == 1 else list(range(NT))
            k.memset(S[:], 0.0); k.memset(Sb[:], 0.0)
            for t in order:
                spt = sp[:, t, di, :]
                pbc = nextP()
                k.matmul(pbc[0:64, 0:128], spt, mk[di][:])
                k.matmul(pbc[:, 128:192], mks[di][:], spt)
                k.copy(bcs[:], pbc[0:64, 0:128]); k.copy(rsb[:], pbc[:, 128:192])
                k.act(EBT[:], bcs[:], AF.Exp, scale=-1.0 / 16.0)
                k.act(ENBT[:], bcs[:], AF.Exp, scale=1.0 / 16.0)
                k.act(ER[:], rsb[:], AF.Exp, scale=-1.0 / 16.0)
                k.tt(qin[:], qTg[:, t, :], EBT[:], ALU.mult)
                k.tt(kin[:], kTg[:, t, :], ENBT[:], ALU.mult, eng="pool")
                k.tt(kkg[:], kbg[:, t, :], ER[:], ALU.mult, eng="pool")
                pat = nextP()
                k.matmul(pat[:, 0:128], kin[:], qin[:])
                k.tt(AT[:], pat[:, 0:128], mk[di][:], ALU.mult)
                po = nextP()
                k.matmul(po[:, 0:128], AT[:], vg[:, t, :], start=True, stop=False)
                k.matmul(po[:, 0:128], qin[:], Sb[:], start=False, stop=True)
                pkv = nextP()
                k.matmul(pkv[0:64, 0:128], kkg[:], vg[:, t, :])
                if di == 1:
                    k.copy(oacc[:, t, :], po[:, 0:128])
                else:
                    k.tt(on[:], po[:, 0:128], oacc[:, t, :], ALU.add)
                    k.op("dve", lambda e, a=st6, b=on: e.bn_stats(a[:], b[:]), [on[:]], [st6[:]])
                    k.op("dve", lambda e, a=mv, b=st6: e.bn_aggr(a[:], b[:]), [st6[:]], [mv[:]])
                    k.stt(ms[:], mv[:, 0:1], mv[:, 0:1], mv[:, 1:2], ALU.mult, ALU.add)
                    k.ts(ms[:], ms[:], EPS, ALU.add)
                    k.act(ms[:], ms[:], AF.Sqrt)
                    k.recip(ms[:], ms[:])
                    k.stt(on[:], on[:], ms[:, 0:1], gng[:], ALU.mult, ALU.mult)
                    k.tt(obg[:], on[:], rsl[:, t, :], ALU.mult, eng="pool")
                    ptt = nextPT()
                    k.transpose(ptt[:, 0:128], obg[:], ident[:])
                    ob = oTg[(t // 4) % 2]
                    k.copy(ob[:, (t % 4) * 128:(t % 4 + 1) * 128], ptt[:, 0:128])
                    if t % 4 == 3 or t == NT - 1:
                        q0 = (t // 4) * 4; n = t - q0 + 1
                        k.dma(d_oT[256 + gh * 128:256 + (gh + 1) * 128, q0 * 128:(t + 1) * 128], ob[:, 0:n * 128])
                dci = 127 if di == 0 else 0
                k.stt(S[:], S[:], EBT[:, dci:dci + 1], pkv[0:64, 0:128], ALU.mult, ALU.add)
                k.copy(Sb[:], S[:], eng="act")
    k.close_scope()
    nc = k.finish([d_oT])
    return nc, k

BF = ml_dtypes.bfloat16

def pk(v):
    return np.ascontiguousarray(v.reshape(-1, 128).T)

def seg_cols(arrT, t0, n, T):
    out = np.zeros((arrT.shape[0], n + 2), arrT.dtype)
    lo = max(t0 - 1, 0); hi = min(t0 + n + 1, T)
    out[:, lo - (t0 - 1): hi - (t0 - 1)] = arrT[:, lo:hi]
    hm = np.array([1.0 if t0 - 1 >= 0 else 0.0, 1.0 if t0 + n < T else 0.0], np.float32)
    return out, np.ascontiguousarray(np.broadcast_to(hm, (128, 2)))

def prep_F(inp, L, b, th, xT, oT, xcT=None, ocT=None, wout=None):
    m = {}
    segs = []
    for s in range(4):
        t0 = th * 2048 + s * 512
        m["xT_%d" % s], m["hm_%d" % s] = seg_cols(xT, t0, 512, 4096)
        m["oT_%d" % s], _ = seg_cols(oT, t0, 512, 4096)
        segs.append((512, 0))
    if xcT is not None:
        m["xT_4"], m["hm_4"] = seg_cols(xcT, 0, 256, 256)
        m["oT_4"], _ = seg_cols(ocT, 0, 256, 256)
        segs.append((256, 1))
    cv = np.stack([pk(inp["c"][b]), pk(inp["c_ctx"])], axis=-1)
    m["cvec"] = np.ascontiguousarray(cv.astype(np.float32))
    m["wmod"] = np.ascontiguousarray(inp["w_mod"][L][:, 2048:6144])
    m["bmod"] = pk(inp["b_mod"][L][2048:6144])
    m["n2g"] = pk(inp["norm2_g"][L]); m["fg"] = pk(inp["final_norm_g"])
    cw = inp["ffn_conv_w"][L]
    m["convw"] = np.ascontiguousarray(np.stack([pk(cw[i]) for i in range(3)], axis=1))
    m["convb"] = pk(inp["ffn_conv_b"][L])
    m["wout"] = wout
    m["wup"] = inp["ffn_w_up"][L]; m["wdn"] = inp["ffn_w_down"][L]
    return m, segs


_C = make_consts(); _COS, _SIN = rope_tables()

def prep_M_common(inp, L, b, xT, xcT):
    m = {"xT": np.ascontiguousarray(xT), "xcT": np.ascontiguousarray(xcT)}
    cv = np.stack([pk(inp["c"][b]), pk(inp["c_ctx"])], axis=-1)
    m["cvec"] = np.ascontiguousarray(cv.astype(np.float32))
    m["wmod"] = np.ascontiguousarray(inp["w_mod"][L][:, 0:2048])
    m["bmod"] = pk(inp["b_mod"][L][0:2048])
    m["n1g"] = pk(inp["norm1_g"][L])
    m["ident"] = np.eye(128, dtype=np.float32)
    return m

def prep_M1(inp, b, hh, xT, xcT):
    m = prep_M_common(inp, 1, b, xT, xcT)
    w = inp["ret_w_in"][0]
    heads = [hh * 4 + i for i in range(4)]
    m["win"] = np.ascontiguousarray(np.stack([np.concatenate([
        w[:, h * 128:(h + 1) * 128], w[:, 1024 + h * 128:1024 + (h + 1) * 128],
        w[:, 4096 + h * 256:4096 + (h + 1) * 256], w[:, 2048 + h * 256:2048 + (h + 1) * 256]], axis=1) for h in heads]))
    dec = np.concatenate([inp["ret_decay_fwd"][0][heads], inp["ret_decay_bwd"][0][heads]])
    m["dec"] = np.ascontiguousarray(np.broadcast_to(dec, (128, 8)).astype(np.float32))
    m["ng"] = np.ascontiguousarray(np.broadcast_to(inp["ret_norm_g"][0][heads], (128, 4, 256)).astype(np.float32))
    m["cos"] = _COS; m["sin"] = _SIN
    for kk in ("mask_f", "mask_b", "dm_f", "dm_b", "row_f", "row_b", "pcols"):
        m[kk] = _C[kk]
    return m

def _na_tables(rpb_heads):
    c = np.arange(64); c0 = np.clip(c - 8, 0, 48)
    kc = np.arange(64)
    allowed = (kc[:, None] >= c0[None, :]) & (kc[:, None] < c0[None, :] + 16)
    off = np.clip(kc[:, None] - c[None, :] + 15, 0, 30)
    g = rpb_heads[:, :, off]
    g = np.where(allowed[None, None], g, 0.0).astype(np.float32)
    g = np.transpose(g, (2, 0, 1, 3))
    rpbT = np.ascontiguousarray(np.concatenate([g, g], axis=0))
    cm = np.where(allowed, 0.0, -30000.0).astype(np.float32)
    return rpbT, np.ascontiguousarray(np.concatenate([cm, cm], axis=0))

def prep_M0(inp, b, hh, xT, xcT):
    m = prep_M_common(inp, 0, b, xT, xcT)
    w = inp["na_gla_w_in"][0]
    nh = [hh * 4 + i for i in range(4)]; gh = [hh * 2 + i for i in range(2)]
    m["wna"] = np.ascontiguousarray(np.concatenate(
        [w[:, h * 64:(h + 1) * 64] for h in nh] + [w[:, 512 + h * 64:512 + (h + 1) * 64] for h in nh] +
        [w[:, 1024 + h * 64:1024 + (h + 1) * 64] for h in nh], axis=1))
    m["wgl"] = np.ascontiguousarray(np.concatenate(
        [w[:, 1536 + h * 64:1536 + (h + 1) * 64] for h in gh] + [w[:, 1792 + h * 64:1792 + (h + 1) * 64] for h in gh] +
        [w[:, 2560 + h * 128:2560 + (h + 1) * 128] for h in gh] + [w[:, 2048 + h * 128:2048 + (h + 1) * 128] for h in gh], axis=1))
    m["wa"] = np.ascontiguousarray(w[:, 3072:3104])
    gc = slice(hh * 128, (hh + 1) * 128)
    wa = np.zeros((17, 2, 128), np.float32)
    wa[0:16, 0] = inp["gla_w_a_fwd"][0][:, gc]; wa[16, 0] = inp["gla_b_a_fwd"][0][gc]
    wa[0:16, 1] = inp["gla_w_a_bwd"][0][:, gc]; wa[16, 1] = inp["gla_b_a_bwd"][0][gc]
    m["waaug"] = wa
    m["rpbT"], m["colmask"] = _na_tables(inp["na_rpb"][0][nh])
    m["gng"] = np.ascontiguousarray(np.broadcast_to(inp["gla_norm_g"][0], (128, 128)).astype(np.float32))
    m["mask_f"] = _C["mask_f"]; m["mask_b"] = _C["mask_b"]
    return m


class Env:
    pass


def make_env(k):
    e = Env(); e.k = k
    e.P = [k.ps("P%d" % i, [128, 512]) for i in range(6)]
    e.PT = [k.ps("PT%d" % i, [128, 1024], BF16) for i in range(2)]
    e.pc = [0, 0]; e.NRR = [6]

    def nextP():
        p = e.P[e.pc[0] % e.NRR[0]]; e.pc[0] += 1; return p

    def nextPT():
        p = e.PT[e.pc[1] % 2]; e.pc[1] += 1; return p
    e.nextP = nextP; e.nextPT = nextPT
    e.ones_bf = k.sb("ones_bf", [128, 128], BF16); k.memset(e.ones_bf[:], 1.0)
    e.identf = k.sb("identf", [128, 128]); e.ident = k.sb("ident_s", [128, 128], BF16)
    d_ident = k.dram("ident", [128, 128])
    k.dma(e.identf[:], d_ident); k.copy(e.ident[:], e.identf[:])
    return e


def phase_hT(e, pfx, d_xT, d_xcT, d_cvec, d_wmod, d_bmod, d_n1g, hT):
    k = e.k
    k.open_scope()
    n1g = k.sb(pfx + "n1g_s", [128, 8]); k.dma(n1g[:], d_n1g)
    wmb = [k.sb(pfx + "wmb0", [128, 8, 512], BF16)]
    modv = emit_mod(k, e.nextP, d_cvec, d_wmod, d_bmod, 2048, wmb, name=pfx + "m")
    xt = [k.sb(pfx + "xt0", [128, 8, 256]), k.sb(pfx + "xt1", [128, 8, 256])]
    sq = k.sb(pfx + "sq", [128, 8, 256], BF16); rstd = k.sb(pfx + "rstd", [128, 256]); tmp = k.sb(pfx + "tmp", [128, 256])
    emit_hT(k, e.nextP, hT, d_xT, d_xcT, modv, n1g, e.ones_bf, xt, sq, rstd, tmp, name=pfx)
    k.close_scope()


def phase_NA(e, pfx, hT, d_wna, d_rpbT, d_colmask, d_oT, row0):
    k = e.k; nextP = e.nextP; nextPT = e.nextPT; identf = e.identf; ident = e.ident; P = e.P
    cfgs, plan = na_configs()
    k.open_scope()
    oTs = [k.sb(pfx + "oTs%d" % i, [128, 2, 512], BF16) for i in range(2)]
    wna = k.sb(pfx + "wna_s", [128, 8, 768], BF16)
    k.dma(wna[:], d_wna.rearrange("(k p) c -> p k c", p=128), q="pool")
    QT = k.sb(pfx + "QT", [64, 4, NT * 128], BF16); KT = k.sb(pfx + "KT", [64, 4, NT * 128], BF16)
    Vaug = k.sb(pfx + "Vaug", [128, NT, 4, 65], BF16)
    k.memset(Vaug[:], 1.0)
    BT = k.sb(pfx + "BT", [128, len(cfgs), 4, 128])
    k.open_scope()
    Btab = k.sb(pfx + "Btab", [128, 4, 15, 64]); cmask = k.sb(pfx + "cmask", [128, 64])
    k.dma(Btab[:], d_rpbT); k.dma(cmask[:], d_colmask)
    for h in range(4):
        k.tt(Btab[:, h, :, :], Btab[:, h, :, :], cmask[:].unsqueeze(1).to_broadcast([128, 15, 64]), ALU.add)
    for ci, key in enumerate(cfgs):
        bi = 0
        for kh in range(2):
            for qh in range(2):
                roff = key[bi]; bi += 1
                dst = BT[kh * 64:(kh + 1) * 64, ci, :, qh * 64:(qh + 1) * 64]
                if roff is None:
                    k.memset(dst, NEG, eng="pool")
                else:
                    k.copy(dst, Btab[kh * 64:(kh + 1) * 64, :, roff, :], eng="pool")
    k.close_scope()
    qsL = [k.sb(pfx + "qs%d" % i, [128, 256]) for i in range(2)]; ksL = [k.sb(pfx + "ks%d" % i, [128, 256]) for i in range(2)]
    trq = []
    for t in range(NT):
        qs = qsL[t % 2]; ks_ = ksL[t % 2]
        pa = nextP(); pb = nextP()
        for kk in range(8):
            k.matmul(pa[:, 0:512], hT[:, kk, t * 128:(t + 1) * 128], wna[:, kk, 0:512], start=(kk == 0), stop=(kk == 7))
        for kk in range(8):
            k.matmul(pb[:, 0:256], hT[:, kk, t * 128:(t + 1) * 128], wna[:, kk, 512:768], start=(kk == 0), stop=(kk == 7))
        while trq:
            trq.pop(0)()
        k.ts(qs[:], pa[:, 0:256], 0.125, ALU.mult)
        k.copy(ks_[:], pa[:, 256:512])
        k.copy(Vaug[:, t, :, 0:64], pb[:, 0:256].rearrange("p (h d) -> p h d", h=4))

        def trf(qs=qs, ks_=ks_, t=t):
            pt = nextP(); pt2 = nextP()
            for h in range(4):
                k.transpose(pt[0:64, h * 128:(h + 1) * 128], qs[:, h * 64:(h + 1) * 64], identf[:])
            for h in range(4):
                k.transpose(pt2[0:64, h * 128:(h + 1) * 128], ks_[:, h * 64:(h + 1) * 64], identf[:])
            k.copy(QT[:, :, t * 128:(t + 1) * 128], pt[0:64, 0:512].rearrange("p (c t) -> p c t", c=4))
            k.copy(KT[:, :, t * 128:(t + 1) * 128], pt2[0:64, 0:512].rearrange("p (c t) -> p c t", c=4))
        trq.append(trf)
    while trq:
        trq.pop(0)()
    sc = [k.sb(pfx + "sc%d" % i, [128, 512]) for i in range(2)]
    PTb = [k.sb(pfx + "PTb%d" % i, [128, 512], BF16) for i in range(2)]
    rden = k.sb(pfx + "rden", [128, 4, 1]); obf = k.sb(pfx + "obf", [128, 256], BF16)
    it = [0]
    e.NRR[0] = 4
    obfL = [obf, k.sb(pfx + "obf2", [128, 256], BF16)]
    items = []
    for qt in range(NT):
        if qt < 2:
            keys = [(0, None), (1, None)]
        else:
            keys = [(0, None), (1, None)] + [(u + 2, ci) for (u, ci) in plan[qt - 2]]
        for ki, (kt, ci) in enumerate(keys):
            items.append((qt, ki, kt, ci, len(keys)))

    def stage1(qt, ki, kt, ci, nk):
        ps = nextP()
        for h in range(4):
            k.matmul(ps[:, h * 128:(h + 1) * 128], KT[:, h, kt * 128:(kt + 1) * 128], QT[:, h, qt * 128:(qt + 1) * 128])
        s_ = sc[it[0] % 2]; p_ = PTb[it[0] % 2]; it[0] += 1
        if ci is None:
            k.copy(s_[:], ps[:, 0:512])
        else:
            k.tt(s_[:], ps[:, 0:512], BT[:, ci, :, :].rearrange("p h q -> p (h q)"), ALU.add)
        k.act(p_[:], s_[:], AF.Exp)
        return p_

    def stage2(qt, ki, kt, ci, nk, p_):
        po = P[4 + qt % 2]
        for h in range(4):
            k.matmul(po[:, h * 65:(h + 1) * 65], p_[:, h * 128:(h + 1) * 128], Vaug[:, kt, h, :],
                     start=(ki == 0 and h == 0), stop=(ki == nk - 1 and h == 3))
        if ki == nk - 1:
            ob_ = obfL[qt % 2]
            pov = po[:, 0:260].rearrange("p (h e) -> p h e", e=65)
            k.recip(rden[:], pov[:, :, 64:65])
            k.tt(ob_[:].rearrange("p (h d) -> p h d", h=4), pov[:, :, 0:64], rden[:].to_broadcast([128, 4, 64]), ALU.mult)
            return (qt, ob_)
        return None

    def stage3(qt, ob_):
        ptt = nextPT()
        k.transpose(ptt[:, 0:128], ob_[:, 0:128], ident[:])
        k.transpose(ptt[:, 128:256], ob_[:, 128:256], ident[:])
        ob = oTs[(qt // 4) % 2]
        k.copy(ob[:, :, (qt % 4) * 128:(qt % 4 + 1) * 128], ptt[:, 0:256].rearrange("p (c t) -> p c t", c=2))
        if qt % 4 == 3 or qt == NT - 1:
            q0 = (qt // 4) * 4; n = qt - q0 + 1
            k.dma(d_oT[row0:row0 + 256, q0 * 128:(qt + 1) * 128].rearrange("(c p) t -> p c t", p=128), ob[:, :, 0:n * 128])

    prev = None; fin = []
    for idx in range(len(items) + 1):
        cur = None
        if idx < len(items):
            cur = (items[idx], stage1(*items[idx]))
        while fin and fin[0][0] <= idx - 1:
            _, f3 = fin.pop(0); stage3(*f3)
        if prev is not None:
            r = stage2(*prev[0], prev[1])
            if r is not None:
                fin.append((idx, r))
        prev = cur
    for _, f3 in fin:
        stage3(*f3)
    e.NRR[0] = 6
    k.close_scope()


def phase_GLA(e, pfx, hT, d_wgl, d_wa, d_waaug, d_gng, d_maskf, d_maskb, d_oT, row0, ghs):
    k = e.k; nextP = e.nextP; nextPT = e.nextPT; identf = e.identf; ident = e.ident
    k.open_scope()
    waaugf = k.sb(pfx + "waaugf", [17, 2, 256]); waaug = k.sb(pfx + "waaug_s", [17, 2, 256], BF16)
    k.dma(waaugf[:], d_waaug); k.copy(waaug[:], waaugf[:])
    gng = k.sb(pfx + "gng_s", [128, 128]); k.dma(gng[:], d_gng)
    mk = [k.sb(pfx + "mkf", [128, 128]), k.sb(pfx + "mkb", [128, 128])]
    k.dma(mk[0][:], d_maskf); k.dma(mk[1][:], d_maskb)
    mks = [k.sb(pfx + "mksf", [128, 128]), k.sb(pfx + "mksb", [128, 128])]
    k.ts(mks[0][:], mk[0][:], -1.0, ALU.mult, 1.0, ALU.add)
    k.ts(mks[1][:], mk[1][:], -1.0, ALU.mult, 1.0, ALU.add)
    qTg = k.sb(pfx + "qTg", [64, NT, 128], BF16); kTg = k.sb(pfx + "kTg", [64, NT, 128], BF16)
    kbg = k.sb(pfx + "kbg", [128, NT, 64], BF16); vg = k.sb(pfx + "vg", [128, NT, 128], BF16)
    rsl = k.sb(pfx + "rsl", [128, NT, 128], BF16); sp = k.sb(pfx + "sp", [128, NT, 2, 64])
    oacc = k.sb(pfx + "oaccg", [128, NT, 128], BF16)
    wgl = k.sb(pfx + "wgl_s", [128, 8, 384], BF16); wa = k.sb(pfx + "wa_s", [128, 8, 32], BF16)
    k.dma(wa[:], d_wa.rearrange("(k p) c -> p k c", p=128), q="pool")
    aT = k.sb(pfx + "aT", [17, 2, 128], BF16); k.memset(aT[:], 1.0)
    qfL = [k.sb(pfx + "qfg%d" % i, [128, 64]) for i in range(2)]; kfL = [k.sb(pfx + "kfg%d" % i, [128, 64]) for i in range(2)]; rf = k.sb(pfx + "rfg", [128, 128]); zf = k.sb(pfx + "zfg", [128, 128])
    R2 = range(2)
    S2 = [k.sb(pfx + "Sg%d" % i, [64, 128]) for i in R2]; Sb2 = [k.sb(pfx + "Sbg%d" % i, [64, 128], BF16) for i in R2]
    EBT2 = [k.sb(pfx + "EBT%d" % i, [64, 128]) for i in R2]; ENBT2 = [k.sb(pfx + "ENBT%d" % i, [64, 128]) for i in R2]
    ER2 = [k.sb(pfx + "ERg%d" % i, [128, 64]) for i in R2]
    bcs2 = [k.sb(pfx + "bcs%d" % i, [64, 128]) for i in R2]; rsb2 = [k.sb(pfx + "rsb%d" % i, [128, 64]) for i in R2]
    qin2 = [k.sb(pfx + "qing%d" % i, [64, 128], BF16) for i in R2]; kin2 = [k.sb(pfx + "king%d" % i, [64, 128], BF16) for i in R2]
    kkg2 = [k.sb(pfx + "kkg%d" % i, [128, 64], BF16) for i in R2]
    AT2 = [k.sb(pfx + "ATg%d" % i, [128, 128], BF16) for i in R2]
    oT4 = [k.sb(pfx + "oT4g%d" % i, [128, 128], BF16) for i in range(4)]; oc = [0]
    Q2 = range(2)
    onL = [k.sb(pfx + "ong%d" % i, [128, 128]) for i in Q2]; st6L = [k.sb(pfx + "st6g%d" % i, [128, 6]) for i in Q2]
    mvL = [k.sb(pfx + "mvg%d" % i, [128, 2]) for i in Q2]; msL = [k.sb(pfx + "msg%d" % i, [128, 1]) for i in Q2]
    obgL = [k.sb(pfx + "obg%d" % i, [128, 128], BF16) for i in Q2]
    pend = [None]; pendB = [None]; rc = [0]
    dcl2 = [k.sb(pfx + "dcl%d" % i, [64, 1]) for i in range(2)]
    dwg = d_wgl.rearrange("(k p) c -> p k c", p=128)
    for (gl, gglob, rofs) in ghs:
        k.dma(wgl[:, :, 0:64], dwg[:, :, gl * 64:(gl + 1) * 64], q="pool")
        k.dma(wgl[:, :, 64:128], dwg[:, :, 128 + gl * 64:128 + (gl + 1) * 64], q="pool")
        k.dma(wgl[:, :, 128:256], dwg[:, :, 256 + gl * 128:256 + (gl + 1) * 128], q="pool")
        k.dma(wgl[:, :, 256:384], dwg[:, :, 512 + gl * 128:512 + (gl + 1) * 128], q="pool")
        trq = []
        for t in range(NT):
            qf = qfL[t % 2]; kf = kfL[t % 2]
            pa = nextP(); pz = nextP()
            for kk in range(8):
                k.matmul(pa[:, 0:384], hT[:, kk, t * 128:(t + 1) * 128], wgl[:, kk, 0:384], start=(kk == 0), stop=(kk == 7))
            while trq:
                trq.pop(0)()
            for di in range(2):
                for kk in range(8):
                    k.matmul(pz[0:16, di * 128:(di + 1) * 128], wa[:, kk, di * 16:(di + 1) * 16], hT[:, kk, t * 128:(t + 1) * 128],
                             start=(kk == 0), stop=(kk == 7))
            k.copy(aT[0:16, :, :], pz[0:16, 0:256].rearrange("p (a t) -> p a t", a=2))
            pz2 = nextP()
            for di in range(2):
                k.matmul(pz2[:, di * 64:(di + 1) * 64], aT[:, di, :], waaug[:, di, gglob * 64:(gglob + 1) * 64])
            k.copy(zf[:], pz2[:, 0:128])
            k.act(zf[:], zf[:], AF.Exp, scale=-1.0)
            k.act(sp[:, t, :, :].rearrange("p a d -> p (a d)"), zf[:], AF.Ln, bias=1.0, scale=1.0)
            k.ts(qf[:], pa[:, 0:64], 0.125, ALU.mult)
            k.copy(kf[:], pa[:, 64:128])
            k.copy(kbg[:, t, :], kf[:], eng="pool")
            k.copy(rsl[:, t, :], pa[:, 128:256])
            k.copy(vg[:, t, :], pa[:, 256:384])
            def trf(qf=qf, kf=kf, t=t):
                pt = nextP()
                k.transpose(pt[0:64, 0:128], qf[:], identf[:])
                k.transpose(pt[0:64, 128:256], kf[:], identf[:])
                k.copy(qTg[:, t, :], pt[0:64, 0:128])
                k.copy(kTg[:, t, :], pt[0:64, 128:256])
            trq.append(trf)
        while trq:
            trq.pop(0)()
        for t in range(NT):
            k.act(rsl[:, t, :], rsl[:, t, :], AF.Silu)
        ordB = [1, 0] + list(range(33, 1, -1)); ordF = list(range(NT))
        for di in range(2):
            k.memset(S2[di][:], 0.0); k.memset(Sb2[di][:], 0.0)
        have = set()
        steps = [(di, t) for i in range(NT) for (di, t) in ((1, ordB[i]), (0, ordF[i]))]

        def stage1(di, t):
            qin = qin2[di]; kin = kin2[di]; kkg = kkg2[di]
            EBT = EBT2[di]; ENBT = ENBT2[di]; ER = ER2[di]; bcs = bcs2[di]; rsb = rsb2[di]
            spt = sp[:, t, di, :]
            pbc = nextP()
            k.matmul(pbc[0:64, 0:128], spt, mk[di][:])
            k.matmul(pbc[:, 128:192], mks[di][:], spt)
            k.copy(bcs[:], pbc[0:64, 0:128]); k.copy(rsb[:], pbc[:, 128:192])
            k.act(EBT[:], bcs[:], AF.Exp, scale=-1.0 / 16.0)
            k.act(ENBT[:], bcs[:], AF.Exp, scale=1.0 / 16.0)
            k.act(ER[:], rsb[:], AF.Exp, scale=-1.0 / 16.0)
            k.tt(qin[:], qTg[:, t, :], EBT[:], ALU.mult)
            k.tt(kin[:], kTg[:, t, :], ENBT[:], ALU.mult)
            k.tt(kkg[:], kbg[:, t, :], ER[:], ALU.mult)
            k.copy(dcl2[di][:], EBT[:, (127 if di == 0 else 0):(128 if di == 0 else 1)], eng="pool")

        stage1(*steps[0])
        for si_ in range(len(steps)):
            if True:
                di, t = steps[si_]
                S = S2[di]; Sb = Sb2[di]; AT = AT2[di]; qin = qin2[di]; kin = kin2[di]; kkg = kkg2[di]
                EBT = EBT2[di]
                if si_ + 1 < len(steps) and steps[si_ + 1][0] != di:
                    stage1(*steps[si_ + 1])
                pat = nextP()
                k.matmul(pat[:, 0:128], kin[:], qin[:])
                k.tt(AT[:], pat[:, 0:128], mk[di][:], ALU.mult)
                po = nextP()
                k.matmul(po[:, 0:128], AT[:], vg[:, t, :], start=True, stop=False)
                k.matmul(po[:, 0:128], qin[:], Sb[:], start=False, stop=True)
                pkv = nextP()
                k.matmul(pkv[0:64, 0:128], kkg[:], vg[:, t, :])
                k.stt(S[:], S[:], dcl2[di][:, 0:1], pkv[0:64, 0:128], ALU.mult, ALU.add)
                k.copy(Sb[:], S[:], eng="act")
                if si_ + 1 < len(steps) and steps[si_ + 1][0] == di:
                    stage1(*steps[si_ + 1])
                if pendB[0] is not None:
                    pendB[0](); pendB[0] = None
                if pend[0] is not None:
                    pendB[0] = pend[0](); pend[0] = None
                if t not in have:
                    k.copy(oacc[:, t, :], po[:, 0:128]); have.add(t)
                else:
                    par = rc[0] % 2; rc[0] += 1
                    onb = onL[par]
                    k.tt(onb[:], po[:, 0:128], oacc[:, t, :], ALU.add)

                    def readout(onb=onb, par=par, t=t, rofs=rofs):
                        st6_ = st6L[par]; mv_ = mvL[par]; ms_ = msL[par]; obg_ = obgL[par]
                        k.op("dve", lambda e_, a=st6_, b_=onb: e_.bn_stats(a[:], b_[:]), [onb[:]], [st6_[:]])
                        k.op("dve", lambda e_, a=mv_, b_=st6_: e_.bn_aggr(a[:], b_[:]), [st6_[:]], [mv_[:]])
                        k.stt(ms_[:], mv_[:, 0:1], mv_[:, 0:1], mv_[:, 1:2], ALU.mult, ALU.add)
                        k.ts(ms_[:], ms_[:], EPS, ALU.add)
                        k.act(ms_[:], ms_[:], AF.Ln)
                        k.act(ms_[:], ms_[:], AF.Exp, scale=-0.5)
                        k.stt(onb[:], onb[:], ms_[:, 0:1], gng[:], ALU.mult, ALU.mult)
                        k.tt(obg_[:], onb[:], rsl[:, t, :], ALU.mult, eng="pool")
                        return lambda: partB(obg_, t, rofs)

                    def partB(obg_, t, rofs):
                        ptt = nextPT()
                        k.transpose(ptt[:, 0:128], obg_[:], ident[:])
                        ob = oT4[oc[0] % 4]; oc[0] += 1
                        k.copy(ob[:], ptt[:, 0:128])
                        k.dma(d_oT[row0 + rofs:row0 + rofs + 128, t * 128:(t + 1) * 128], ob[:])
                    pend[0] = readout
        if pendB[0] is not None:
            pendB[0](); pendB[0] = None
        if pend[0] is not None:
            pend[0]()(); pend[0] = None
    k.close_scope()


def phase_RET(e, pfx, hT, d_win, d_dec, d_ng, d_cos, d_sin, cn, d_pcols, d_oT, NH):
    k = e.k; nextP = e.nextP; nextPT = e.nextPT; identf = e.identf; ident = e.ident
    DK = 128; DV = 256
    k.open_scope()
    cs = {nm: k.sb(pfx + nm + "_s", [128, 128]) for nm in cn}
    for nm in cn:
        k.dma(cs[nm][:], cn[nm])
    pcols = k.sb(pfx + "pcols_s", [128, 4]); k.dma(pcols[:], d_pcols)
    cos = k.sb(pfx + "cos_s", [128, 32, 128]); sin = k.sb(pfx + "sin_s", [128, 32, 128])
    k.dma(cos[:], d_cos); k.dma(sin[:], d_sin)
    ng = k.sb(pfx + "ng_s", [128, 2, DV])
    dec = k.sb(pfx + "dec_s", [128, 2 * NH]); k.dma(dec[:], d_dec)
    lg = k.sb(pfx + "lg", [128, 2 * NH]); e1 = k.sb(pfx + "e1", [128, 2 * NH])
    k.act(e1[:], dec[:], AF.Exp, scale=-1.0)
    k.act(e1[:], e1[:], AF.Ln, bias=1.0, scale=1.0)
    k.ts(lg[:], e1[:], -1.0, ALU.mult)
    w = k.sb(pfx + "whd0", [128, 8, 768], BF16)
    kb = k.sb(pfx + "kb", [128, NT, DK], BF16); qT = k.sb(pfx + "qT", [128, NT, 128], BF16); kT = k.sb(pfx + "kT", [128, NT, 128], BF16)
    vb = k.sb(pfx + "vb", [128, NT, DV], BF16); gs = k.sb(pfx + "gs", [128, 32, DV], BF16)
    oacc = k.sb(pfx + "oacc", [128, 32, DV], BF16)
    qkL = [k.sb(pfx + "qk%d" % i, [128, 256]) for i in range(2)]
    tAL = [k.sb(pfx + "tA%d" % i, [128, 256]) for i in range(1)] * 2; tBL = [k.sb(pfx + "tB%d" % i, [128, 256]) for i in range(1)] * 2
    DM = [k.sb(pfx + "DM%d" % i, [128, 128]) for i in range(2)]
    EBr = [k.sb(pfx + "EBr%d" % i, [128, 128]) for i in range(2)]
    ERc = k.sb(pfx + "ERc", [128, 2]); dcol = k.sb(pfx + "dcol", [128, 2])
    S2 = [k.sb(pfx + "S%d" % i, [128, DV]) for i in range(2)]; Sb2 = [k.sb(pfx + "Sb%d" % i, [128, DV], BF16) for i in range(2)]
    AT2 = [k.sb(pfx + "AT%d" % i, [128, 128], BF16) for i in range(2)]; qin2 = [k.sb(pfx + "qin%d" % i, [128, 128], BF16) for i in range(2)]
    kk2 = [k.sb(pfx + "kk%d" % i, [128, 128], BF16) for i in range(2)]
    oT4 = [k.sb(pfx + "oT4%d" % i, [128, 2, 128], BF16) for i in range(2)] * 2; oc = [0]
    R2_ = range(2)
    st6L = [k.sb(pfx + "st6%d" % i, [128, 6]) for i in R2_]; mvL = [k.sb(pfx + "mv%d" % i, [128, 2]) for i in R2_]
    rsL = [k.sb(pfx + "rs%d" % i, [128, 1]) for i in R2_]; nbL = [k.sb(pfx + "nb%d" % i, [128, 1]) for i in R2_]
    onL = [k.sb(pfx + "on%d" % i, [128, DV]) for i in R2_]; obfL = [k.sb(pfx + "obf%d" % i, [128, DV], BF16) for i in R2_]
    pend = [None]; pendB = [None]; rc = [0]
    gfL = tBL
    for hl in range(NH):
        k.dma(w[:], d_win[hl].rearrange("(k p) c -> p k c", p=128), q="pool")
        k.dma(ng[:, hl % 2, :], d_ng[:, hl, :])
        for di, (dmn, mkn, rown) in enumerate((("dm_f", "mask_f", "row_f"), ("dm_b", "mask_b", "row_b"))):
            lgc = lg[:, di * NH + hl:di * NH + hl + 1]
            k.act(DM[di][:], cs[dmn][:], AF.Exp, scale=lgc)
            k.tt(DM[di][:], DM[di][:], cs[mkn][:], ALU.mult)
            k.act(EBr[di][:], cs[rown][:], AF.Exp, scale=lgc)
            k.act(ERc[:, di:di + 1], pcols[:, di:di + 1], AF.Exp, scale=lgc)
            k.act(dcol[:, di:di + 1], pcols[:, 2:3], AF.Exp, scale=lgc)
        trq = []
        for t in range(NT):
            pa = nextP(); pb = nextP()
            for kk in range(8):
                k.matmul(pa[:, 0:512], hT[:, kk, t * 128:(t + 1) * 128], w[:, kk, 0:512], start=(kk == 0), stop=(kk == 7))
            for kk in range(8):
                k.matmul(pb[:, 0:256], hT[:, kk, t * 128:(t + 1) * 128], w[:, kk, 512:768], start=(kk == 0), stop=(kk == 7))
            while trq:
                trq.pop(0)()
            qk = qkL[t % 2]; gf = gfL[t % 2]
            SC = float(DK) ** -0.5
            if t >= 2:
                xv = pa[:, 0:256].rearrange("p (g h f) -> p g h f", g=4, h=2)
                ov = qk[:].rearrange("p (g h f) -> p g h f", g=4, h=2)
                Av = tAL[t % 2][:].rearrange("p (g h f) -> p g h f", g=4, h=2)
                Bv = tBL[t % 2][:].rearrange("p (g h f) -> p g h f", g=4, h=2)
                cb_ = cos[:, t - 2, :].rearrange("p (g f) -> p g f", g=4).unsqueeze(2).to_broadcast([128, 4, 2, 32])
                sv = sin[:, t - 2, :].rearrange("p (g f) -> p g f", g=4)
                k.tt(Av, xv, cb_, ALU.mult)
                k.tt(Bv[:, :, 0, :], xv[:, :, 1, :], sv, ALU.mult)
                k.tt(Bv[:, :, 1, :], xv[:, :, 0, :], sv, ALU.mult)
                k.tt(ov[:, :, 0, :], Av[:, :, 0, :], Bv[:, :, 0, :], ALU.subtract)
                k.tt(ov[:, :, 1, :], Av[:, :, 1, :], Bv[:, :, 1, :], ALU.add)
                k.copy(gf[:], pa[:, 256:512])
                k.act(gs[:, t - 2, :], gf[:], AF.Silu)
            else:
                k.copy(qk[:], pa[:, 0:256])
            k.ts(kb[:, t, :], qk[:, 128:256], SC, ALU.mult, eng="pool")
            k.copy(vb[:, t, :], pb[:, 0:256])
            def trf(qk=qk, t=t, SC=SC):
                pt = nextP()
                k.transpose(pt[:, 0:128], qk[:, 0:128], identf[:])
                k.transpose(pt[:, 128:256], qk[:, 128:256], identf[:])
                k.copy(qT[:, t, :], pt[:, 0:128])
                k.ts(kT[:, t, :], pt[:, 128:256], SC, ALU.mult)
            trq.append(trf)
        while trq:
            trq.pop(0)()
        ordB = [1, 0] + list(range(33, 1, -1)); ordF = list(range(NT))
        for di in range(2):
            k.memset(S2[di][:], 0.0); k.memset(Sb2[di][:], 0.0)
        have = set()
        for i in range(NT):
            for di, t in ((1, ordB[i]), (0, ordF[i])):
                S = S2[di]; Sb = Sb2[di]; AT = AT2[di]; qin = qin2[di]; kk_ = kk2[di]
                if t >= 2:
                    pat = nextP()
                    k.matmul(pat[:, 0:128], kT[:, t, :], qT[:, t, :])
                    k.tt(AT[:], pat[:, 0:128], DM[di][:], ALU.mult)
                    k.tt(qin[:], qT[:, t, :], EBr[di][:], ALU.mult)
                    po = nextP()
                    k.matmul(po[:, 0:DV], AT[:], vb[:, t, :], start=True, stop=False)
                    k.matmul(po[:, 0:DV], qin[:], Sb[:], start=False, stop=True)
                k.ts(kk_[:], kb[:, t, :], ERc[:, di:di + 1], ALU.mult)
                pkv = nextP()
                k.matmul(pkv[:, 0:DV], kk_[:], vb[:, t, :])
                k.stt(S[:], S[:], dcol[:, di:di + 1], pkv[:, 0:DV], ALU.mult, ALU.add)
                k.copy(Sb[:], S[:], eng="act")
                if pendB[0] is not None:
                    pendB[0](); pendB[0] = None
                if pend[0] is not None:
                    pendB[0] = pend[0](); pend[0] = None
                if t >= 2:
                    if t not in have:
                        k.copy(oacc[:, t - 2, :], po[:, 0:DV]); have.add(t)
                    else:
                        par = rc[0] % 2; rc[0] += 1
                        onb = onL[par]
                        k.tt(onb[:], po[:, 0:DV], oacc[:, t - 2, :], ALU.add)

                        def readout(onb=onb, par=par, t=t, hl=hl):
                            st6_ = st6L[par]; mv_ = mvL[par]; rs_ = rsL[par]; nb_ = nbL[par]; obf_ = obfL[par]
                            k.op("dve", lambda e_, a=st6_, b_=onb: e_.bn_stats(a[:], b_[:]), [onb[:]], [st6_[:]])
                            k.op("dve", lambda e_, a=mv_, b_=st6_: e_.bn_aggr(a[:], b_[:]), [st6_[:]], [mv_[:]])
                            k.ts(rs_[:], mv_[:, 1:2], EPS, ALU.add)
                            k.act(rs_[:], rs_[:], AF.Sqrt)
                            k.recip(rs_[:], rs_[:])
                            k.stt(nb_[:], mv_[:, 0:1], -1.0, rs_[:], ALU.mult, ALU.mult)
                            k.act(onb[:], onb[:], AF.Identity, bias=nb_[:, 0:1], scale=rs_[:, 0:1])
                            k.tt(onb[:], onb[:], ng[:, hl % 2, :], ALU.mult, eng="pool")
                            k.tt(obf_[:], onb[:], gs[:, t - 2, :], ALU.mult, eng="pool")
                            return lambda: partB(obf_, t, hl)

                        def partB(obf_, t, hl):
                            ptt = nextPT()
                            k.transpose(ptt[:, 0:128], obf_[:, 0:128], ident[:])
                            k.transpose(ptt[:, 128:256], obf_[:, 128:256], ident[:])
                            lt = t - 2
                            ob = oT4[oc[0] % 4]; oc[0] += 1
                            k.copy(ob[:], ptt[:, 0:256].rearrange("p (c t) -> p c t", c=2))
                            k.dma(d_oT[hl * DV:(hl + 1) * DV, lt * 128:(lt + 1) * 128].rearrange("(c p) t -> p c t", p=128), ob[:])
                        pend[0] = readout
        if pendB[0] is not None:
            pendB[0](); pendB[0] = None
        if pend[0] is not None:
            pend[0]()(); pend[0] = None
    k.close_scope()


def phase_F(e, pfx, Fdim, last, segs, xsrc, osrc, ydst, d_cvec, d_wmod, d_bmod, d_n2g, d_fg, d_cw, d_cb, d_wout, d_wup, d_wdn):
    k = e.k; nextP = e.nextP; ones_bf = e.ones_bf
    KF = Fdim // 128
    NMAX = max(s[0] for s in segs); CMAX = NMAX + 2
    k.open_scope()
    cvec = k.sb(pfx + "cvec_s", [128, 8, 2]); scb = k.sb(pfx + "scb", [128, 8, 2], BF16)
    bmod = k.sb(pfx + "bmod_s", [128, 32]); modv = k.sb(pfx + "modv", [128, 32, 2])
    n2g = k.sb(pfx + "n2g_s", [128, 8]); fg = k.sb(pfx + "fg_s", [128, 8]); gm2 = k.sb(pfx + "gm2", [128, 8, 2])
    cw = k.sb(pfx + "cw_s", [128, 3, 44]); cb = k.sb(pfx + "cb_s", [128, 44])
    wout = k.sb(pfx + "wout_s", [128, KF, D], BF16)
    wdn = k.sb(pfx + "wdn_s", [128, NFF, D], BF16)
    oTL = [k.sb(pfx + "oT_s%d" % i, [128, KF, CMAX], BF16) for i in range(2 if KF <= 8 else 1)] * (1 if KF <= 8 else 2)
    x1L = [k.sb(pfx + "x1T%d" % i, [128, 8, CMAX]) for i in range(2)]
    sq = k.sb(pfx + "sq", [128, 8, CMAX], BF16)
    rstd = k.sb(pfx + "rstd", [128, CMAX]); tmp = k.sb(pfx + "tmp", [128, CMAX]); tmpb = k.sb(pfx + "tmpb", [128, CMAX])
    h2 = k.sb(pfx + "h2T", [128, 8, CMAX], BF16)
    wua = [k.sb(pfx + "wua%d" % i, [128, 8, 256], BF16) for i in range(2)]
    wub = [k.sb(pfx + "wub%d" % i, [128, 8, 256], BF16) for i in range(2)]
    u = [k.sb(pfx + "u%d" % i, [128, 2, CMAX]) for i in range(2)]
    vaL = [k.sb(pfx + "va0", [128, NMAX])] * 2; vbL = [k.sb(pfx + "vb0", [128, NMAX])] * 2
    saL = [k.sb(pfx + "sa0", [128, NMAX])] * 2
    tT = k.sb(pfx + "tT", [128, NFF, NMAX], BF16)
    k.dma(cvec[:], d_cvec); k.dma(bmod[:], d_bmod); k.dma(n2g[:], d_n2g); k.dma(fg[:], d_fg)
    k.dma(cw[:], d_cw); k.dma(cb[:], d_cb)
    k.act(scb[:], cvec[:], AF.Silu)
    pm = nextP()
    for g in range(8):
        wb = wua[g % 2][:, :, :]
        wb2 = wub[g % 2][:, :, :]
        k.dma(wb, d_wmod.rearrange("(k p) c -> p k c", p=128)[:, :, g * 512:g * 512 + 256], q="pool")
        k.dma(wb2, d_wmod.rearrange("(k p) c -> p k c", p=128)[:, :, g * 512 + 256:(g + 1) * 512], q="pool")
        for c4 in range(4):
            cc = g * 4 + c4
            src = wb if c4 < 2 else wb2
            for kk in range(8):
                k.matmul(pm[:, cc * 2:cc * 2 + 2], src[:, kk, (c4 % 2) * 128:(c4 % 2 + 1) * 128], scb[:, kk, :],
                         start=(kk == 0), stop=(kk == 7))
    pmv = pm[:, 0:64].rearrange("p (c j) -> p c j", j=2)
    for j in range(2):
        k.tt(modv[:, :, j], pmv[:, :, j], bmod[:], ALU.add)
    for j in range(2):
        k.ts(gm2[:, :, j], modv[:, 16:24, j], 1.0, ALU.add)
        k.tt(gm2[:, :, j], gm2[:, :, j], n2g[:], ALU.mult)
    load_cast_rows(k, wout, d_wout, KF, D, split=4)
    load_cast_rows(k, wdn, d_wdn, NFF, D, split=4)
    wupv = d_wup.rearrange("(k p) c -> p k c", p=128)
    wctr = [0]
    k.dma(wua[0][:], wupv[:, :, 0:256], q="pool")
    k.dma(wub[0][:], wupv[:, :, DFF:DFF + 256], q="pool")

    def load_seg(sj, do_x=True, do_o=True):
        n_, j_, kind_, t0_ = segs[sj]
        cols_ = n_ + 2
        x1_ = x1L[sj % 2]; oT_ = oTL[sj % 2]
        xs, T = xsrc[kind_]; os_, oc0, _ = osrc[kind_]
        lo = max(t0_ - 1, 0); hi = min(t0_ + n_ + 1, T)
        c_lo = lo - (t0_ - 1); c_hi = hi - (t0_ - 1)
        if do_x:
            if c_lo > 0:
                k.memset(x1_[:, :, 0:1], 0.0)
            if c_hi < cols_:
                k.memset(x1_[:, :, cols_ - 1:cols_], 0.0)
            k.dma(x1_[:, :, c_lo:c_hi], xs.rearrange("(k p) t -> p k t", p=128)[:, :, lo:hi])
        if do_o:
            if c_lo > 0:
                k.memset(oT_[:, :, 0:1], 0.0)
            if c_hi < cols_:
                k.memset(oT_[:, :, cols_ - 1:cols_], 0.0)
            k.dma(oT_[:, :, c_lo:c_hi], os_.rearrange("(k p) t -> p k t", p=128)[:, :, oc0 + lo:oc0 + hi])

    for si, (n, j, kind, t0) in enumerate(segs):
        cols = n + 2
        tiles = col_tiles(cols)
        x1 = x1L[si % 2]; oT = oTL[si % 2]
        if si == 0:
            load_seg(0)
        if si + 1 < len(segs):
            load_seg(si + 1, do_x=True, do_o=(KF <= 8))
        lo = max(t0 - 1, 0); hi = min(t0 + n + 1, xsrc[kind][1])
        c_lo = lo - (t0 - 1); c_hi = hi - (t0 - 1)
        for fc in range(8):
            for (a, b) in tiles:
                p = nextP()
                for kk in range(KF):
                    k.matmul(p[:, 0:b - a], wout[:, kk, fc * 128:(fc + 1) * 128], oT[:, kk, a:b],
                             start=(kk == 0), stop=(kk == KF - 1))
                k.stt(x1[:, fc, a:b], p[:, 0:b - a], modv[:, 0 + fc, j:j + 1], x1[:, fc, a:b], ALU.mult, ALU.add)
        if si + 1 < len(segs) and KF > 8:
            load_seg(si + 1, do_x=False, do_o=True)
        for kk in range(8):
            k.act(sq[:, kk, 0:cols], x1[:, kk, 0:cols], AF.Square)
        for (a, b) in tiles:
            p = nextP()
            for kk in range(8):
                k.matmul(p[:, 0:b - a], ones_bf[:], sq[:, kk, a:b], start=(kk == 0), stop=(kk == 7))
            k.ts(tmp[:, a:b], p[:, 0:b - a], 1.0 / D, ALU.mult, EPS, ALU.add)
        k.act(tmp[:, 0:cols], tmp[:, 0:cols], AF.Sqrt)
        k.recip(rstd[:, 0:cols], tmp[:, 0:cols])
        for kk in range(8):
            tb = tmp if kk % 2 == 0 else tmpb
            k.stt(tb[:, 0:cols], x1[:, kk, 0:cols], gm2[:, kk, j:j + 1], rstd[:, 0:cols], ALU.mult, ALU.mult)
            k.act(h2[:, kk, 0:cols], tb[:, 0:cols], AF.Identity, bias=modv[:, 8 + kk, j:j + 1], scale=1.0)
        if c_lo > 0:
            k.memset(h2[:, :, 0:1], 0.0)
        if c_hi < cols:
            k.memset(h2[:, :, cols - 1:cols], 0.0)
        for g in range(11):
            wa = wua[wctr[0] % 2]; wb_ = wub[wctr[0] % 2]
            wctr[0] += 1
            gn = g + 1 if g < 10 else (0 if si + 1 < len(segs) else None)
            if gn is not None:
                k.dma(wua[wctr[0] % 2][:], wupv[:, :, gn * 256:(gn + 1) * 256], q="pool")
                k.dma(wub[wctr[0] % 2][:], wupv[:, :, DFF + gn * 256:DFF + (gn + 1) * 256], q="pool")
            for c2 in range(2):
                c = g * 2 + c2
                ub = u[c % 2]
                for half, w in ((0, wa), (1, wb_)):
                    for (a, b) in tiles:
                        p = nextP()
                        for kk in range(8):
                            k.matmul(p[:, 0:b - a], w[:, kk, c2 * 128:(c2 + 1) * 128], h2[:, kk, a:b],
                                     start=(kk == 0), stop=(kk == 7))
                        k.copy(ub[:, half, a:b], p[:, 0:b - a])
                ca = c; cbi = NFF + c
                va = vaL[c % 2]; vb = vbL[c % 2]; sa = saL[c % 2]
                k.act(va[:, 0:n], ub[:, 0, 1:n + 1], AF.Identity, bias=cb[:, ca:ca + 1], scale=cw[:, 1, ca:ca + 1])
                k.stt(va[:, 0:n], ub[:, 0, 0:n], cw[:, 0, ca:ca + 1], va[:, 0:n], ALU.mult, ALU.add)
                k.stt(va[:, 0:n], ub[:, 0, 2:n + 2], cw[:, 2, ca:ca + 1], va[:, 0:n], ALU.mult, ALU.add)
                k.act(vb[:, 0:n], ub[:, 1, 1:n + 1], AF.Identity, bias=cb[:, cbi:cbi + 1], scale=cw[:, 1, cbi:cbi + 1])
                k.stt(vb[:, 0:n], ub[:, 1, 0:n], cw[:, 0, cbi:cbi + 1], vb[:, 0:n], ALU.mult, ALU.add)
                k.stt(vb[:, 0:n], ub[:, 1, 2:n + 2], cw[:, 2, cbi:cbi + 1], vb[:, 0:n], ALU.mult, ALU.add)
                k.act(sa[:, 0:n], va[:, 0:n], AF.Silu)
                k.tt(tT[:, c, 0:n], sa[:, 0:n], vb[:, 0:n], ALU.mult, eng="pool")
        for fc in range(8):
            p = nextP()
            for c in range(NFF):
                k.matmul(p[:, 0:n], wdn[:, c, fc * 128:(fc + 1) * 128], tT[:, c, 0:n], start=(c == 0), stop=(c == NFF - 1))
            k.stt(x1[:, fc, 1:n + 1], p[:, 0:n], modv[:, 24 + fc, j:j + 1], x1[:, fc, 1:n + 1], ALU.mult, ALU.add)
        if last:
            for kk in range(8):
                k.act(sq[:, kk, 0:n], x1[:, kk, 1:n + 1], AF.Square)
            p = nextP()
            for kk in range(8):
                k.matmul(p[:, 0:n], ones_bf[:], sq[:, kk, 0:n], start=(kk == 0), stop=(kk == 7))
            k.ts(tmp[:, 0:n], p[:, 0:n], 1.0 / D, ALU.mult, EPS, ALU.add)
            k.act(tmp[:, 0:n], tmp[:, 0:n], AF.Sqrt)
            k.recip(rstd[:, 0:n], tmp[:, 0:n])
            for kk in range(8):
                k.stt(x1[:, kk, 1:n + 1], x1[:, kk, 1:n + 1], fg[:, kk:kk + 1], rstd[:, 0:n], ALU.mult, ALU.mult)
        k.dma(ydst[kind].rearrange("(k p) t -> p k t", p=128)[:, :, t0:t0 + n], x1[:, :, 1:n + 1])
    k.close_scope()


def build_fused():
    k = KB(); nc = k.nc
    e = make_env(k)
    d_xT = k.dram("xT", [D, 4096]); d_xcT = k.dram("xcT", [D, 256]); d_cvec = k.dram("cvec", [128, 8, 2])
    cn = {nm: k.dram(nm, [128, 128]) for nm in ("mask_f", "mask_b", "dm_f", "dm_b", "row_f", "row_b")}
    d_pcols = k.dram("pcols", [128, 4]); d_cos = k.dram("cos", [128, 32, 128]); d_sin = k.dram("sin", [128, 32, 128])
    L = []
    for l in range(2):
        L.append(dict(wmodA=k.dram("wmodA%d" % l, [D, 2048]), bmodA=k.dram("bmodA%d" % l, [128, 16]), n1g=k.dram("n1g%d" % l, [128, 8]),
                      wmodB=k.dram("wmodB%d" % l, [D, 4096]), bmodB=k.dram("bmodB%d" % l, [128, 32]), n2g=k.dram("n2g%d" % l, [128, 8]),
                      cw=k.dram("convw%d" % l, [128, 3, 44]), cb=k.dram("convb%d" % l, [128, 44]),
                      wout=k.dram("wout%d" % l, [1024 * (l + 1), D]), wup=k.dram("wup%d" % l, [D, 2 * DFF]), wdn=k.dram("wdn%d" % l, [DFF, D])))
    d_fg = k.dram("fg", [128, 8])
    d_wna = k.dram("wna", [2, D, 768]); d_wgl = k.dram("wgl", [2, D, 768]); d_wa = k.dram("wa", [D, 32])
    d_waaug = k.dram("waaug", [17, 2, 256]); d_rpbT = k.dram("rpbT", [2, 128, 4, 15, 64]); d_colmask = k.dram("colmask", [128, 64])
    d_gng = k.dram("gng", [128, 128])
    d_win = k.dram("win", [8, D, 768]); d_dec = k.dram("dec", [128, 16]); d_ng = k.dram("ng", [128, 8, 256])
    d_y = k.dram("yT", [D, 4096], kind="ExternalOutput")
    o0T = nc.dram_tensor("o0T", [1024, NT * 128], BF16, kind="Internal").ap()
    x2T = nc.dram_tensor("x2T", [D, 4096], F32, kind="Internal").ap()
    xc2T = nc.dram_tensor("xc2T", [D, 256], F32, kind="Internal").ap()
    o1T = nc.dram_tensor("o1T", [2048, 4096], BF16, kind="Internal").ap()
    xcdump = nc.dram_tensor("xcdump", [D, 256], F32, kind="Internal").ap()

    k.open_scope()
    hT = k.sb("hT0", [128, 8, NT * 128], BF16)
    phase_hT(e, "a0", d_xT, d_xcT, d_cvec, L[0]["wmodA"], L[0]["bmodA"], L[0]["n1g"], hT)
    for hh in range(2):
        phase_NA(e, "na%d" % hh, hT, d_wna[hh], d_rpbT[hh], d_colmask, o0T, hh * 512)
        phase_GLA(e, "gl%d" % hh, hT, d_wgl[hh], d_wa, d_waaug, d_gng, cn["mask_f"], cn["mask_b"], o0T, hh * 512 + 256,
                  [(0, hh * 2, 0), (1, hh * 2 + 1, 128)])
    k.close_scope()
    segs0 = [(512, 0, "lat", s * 512) for s in range(8)] + [(256, 1, "ctx", 0)]
    phase_F(e, "f0", 1024, False, segs0,
            {"lat": (d_xT, 4096), "ctx": (d_xcT, 256)}, {"lat": (o0T, 256, 4096), "ctx": (o0T, 0, 256)},
            {"lat": x2T, "ctx": xc2T}, d_cvec, L[0]["wmodB"], L[0]["bmodB"], L[0]["n2g"], d_fg, L[0]["cw"], L[0]["cb"],
            L[0]["wout"], L[0]["wup"], L[0]["wdn"])
    k.open_scope()
    hT = k.sb("hT1", [128, 8, NT * 128], BF16)
    phase_hT(e, "a1", x2T, xc2T, d_cvec, L[1]["wmodA"], L[1]["bmodA"], L[1]["n1g"], hT)
    phase_RET(e, "rt", hT, d_win, d_dec, d_ng, d_cos, d_sin, cn, d_pcols, o1T, 8)
    k.close_scope()
    segs1 = [(512, 0, "lat", s * 512) for s in range(8)]
    phase_F(e, "f1", 2048, True, segs1,
            {"lat": (x2T, 4096)}, {"lat": (o1T, 0, 4096)}, {"lat": d_y},
            d_cvec, L[1]["wmodB"], L[1]["bmodB"], L[1]["n2g"], d_fg, L[1]["cw"], L[1]["cb"],
            L[1]["wout"], L[1]["wup"], L[1]["wdn"])
    ncc = k.finish([d_y])
    return ncc, k


_PROG = []


def _maps(inp):
    B = 4
    shared = {}
    shared["ident"] = np.eye(128, dtype=np.float32)
    for kk in ("mask_f", "mask_b", "dm_f", "dm_b", "row_f", "row_b", "pcols"):
        shared[kk] = _C[kk]
    shared["cos"] = np.ascontiguousarray(np.concatenate([_COS, _COS], axis=2)); shared["sin"] = np.ascontiguousarray(np.concatenate([_SIN, _SIN], axis=2))
    for l in range(2):
        shared["wmodA%d" % l] = np.ascontiguousarray(inp["w_mod"][l][:, 0:2048])
        shared["bmodA%d" % l] = pk(inp["b_mod"][l][0:2048])
        shared["n1g%d" % l] = pk(inp["norm1_g"][l])
        shared["wmodB%d" % l] = np.ascontiguousarray(inp["w_mod"][l][:, 2048:6144])
        shared["bmodB%d" % l] = pk(inp["b_mod"][l][2048:6144])
        shared["n2g%d" % l] = pk(inp["norm2_g"][l])
        cw = inp["ffn_conv_w"][l]
        shared["convw%d" % l] = np.ascontiguousarray(np.stack([pk(cw[i]) for i in range(3)], axis=1))
        shared["convb%d" % l] = pk(inp["ffn_conv_b"][l])
        shared["wup%d" % l] = inp["ffn_w_up"][l]; shared["wdn%d" % l] = inp["ffn_w_down"][l]
    perm = np.concatenate([np.arange(0, 256), np.arange(512, 768), np.arange(256, 512), np.arange(768, 1024)])
    shared["wout0"] = np.ascontiguousarray(inp["na_gla_w_out"][0][perm])
    shared["wout1"] = np.ascontiguousarray(inp["ret_w_out"][0])
    shared["fg"] = pk(inp["final_norm_g"])
    w = inp["na_gla_w_in"][0]
    wna = []; wgl = []; rp = []
    for hh in range(2):
        nh = [hh * 4 + i for i in range(4)]; gh = [hh * 2 + i for i in range(2)]
        wna.append(np.concatenate([w[:, h * 64:(h + 1) * 64] for h in nh] + [w[:, 512 + h * 64:512 + (h + 1) * 64] for h in nh] +
                                  [w[:, 1024 + h * 64:1024 + (h + 1) * 64] for h in nh], axis=1))
        wgl.append(np.concatenate([w[:, 1536 + h * 64:1536 + (h + 1) * 64] for h in gh] + [w[:, 1792 + h * 64:1792 + (h + 1) * 64] for h in gh] +
                                  [w[:, 2560 + h * 128:2560 + (h + 1) * 128] for h in gh] + [w[:, 2048 + h * 128:2048 + (h + 1) * 128] for h in gh], axis=1))
        r_, cm = _na_tables(inp["na_rpb"][0][nh]); rp.append(r_)
    shared["wna"] = np.ascontiguousarray(np.stack(wna)); shared["wgl"] = np.ascontiguousarray(np.stack(wgl))
    shared["rpbT"] = np.ascontiguousarray(np.stack(rp)); shared["colmask"] = cm
    shared["wa"] = np.ascontiguousarray(w[:, 3072:3104])
    wa = np.zeros((17, 2, 256), np.float32)
    wa[0:16, 0] = inp["gla_w_a_fwd"][0]; wa[16, 0] = inp["gla_b_a_fwd"][0]
    wa[0:16, 1] = inp["gla_w_a_bwd"][0]; wa[16, 1] = inp["gla_b_a_bwd"][0]
    shared["waaug"] = wa
    shared["gng"] = np.ascontiguousarray(np.broadcast_to(inp["gla_norm_g"][0], (128, 128)).astype(np.float32))
    wr = inp["ret_w_in"][0]
    shared["win"] = np.ascontiguousarray(np.stack([np.concatenate([
        wr[:, h * 128:(h + 1) * 128], wr[:, 1024 + h * 128:1024 + (h + 1) * 128],
        wr[:, 4096 + h * 256:4096 + (h + 1) * 256], wr[:, 2048 + h * 256:2048 + (h + 1) * 256]], axis=1) for h in range(8)]))
    dec = np.concatenate([inp["ret_decay_fwd"][0], inp["ret_decay_bwd"][0]])
    shared["dec"] = np.ascontiguousarray(np.broadcast_to(dec, (128, 16)).astype(np.float32))
    shared["ng"] = np.ascontiguousarray(np.broadcast_to(inp["ret_norm_g"][0], (128, 8, 256)).astype(np.float32))
    maps = []
    for core in range(8):
        b = core // 2
        m = dict(shared)
        m["xT"] = np.ascontiguousarray(inp["x"][b].T); m["xcT"] = np.ascontiguousarray(inp["ctx"][b].T)
        m["cvec"] = np.ascontiguousarray(np.stack([pk(inp["c"][b]), pk(inp["c_ctx"])], axis=-1).astype(np.float32))
        maps.append(m)
    return maps


def kernel(**inp):
    inp = {k_: np.asarray(v) for k_, v in inp.items()}
    if not _PROG:
        _PROG.append(build_fused()[0])
    res = run_bass_kernel_spmd(_PROG[0], _maps(inp), core_ids=list(range(8))).results
    out = np.empty((4, 4096, 1024), np.float32)
    for b in range(4):
        out[b, 0:2048] = res[2 * b]["yT"][:, 0:2048].T
        out[b, 2048:4096] = res[2 * b + 1]["yT"][:, 2048:4096].T
    return out
```
